# Optimizing a Trainium2 kernel written in Bass

```python
import jax, jax.numpy as jnp
from jax import lax
import numpy as np

D_MODEL = 2048
BATCH = 2
SEQ = 4096
DEPTH = 4

CHUNK = 64
N_META = 16
META_PAD = CHUNK - N_META
N_MIXERS = 2
N_RET_LAYERS = (DEPTH + 1) // 2
N_SSD_LAYERS = DEPTH // 2
NORM_EPS = 1e-6

RET_HEADS = 8
RET_DK = D_MODEL // RET_HEADS
RET_DV = 2 * RET_DK
RET_QK = RET_HEADS * RET_DK
RET_VDIM = RET_HEADS * RET_DV
RET_PROJ = 2 * RET_QK + 2 * RET_VDIM
ROPE_BASE = 10000.0

SSD_DI = 2 * D_MODEL
SSD_HEADDIM = 64
SSD_HEADS = SSD_DI // SSD_HEADDIM
SSD_GROUPS = 8
SSD_HPG = SSD_HEADS // SSD_GROUPS
SSD_STATE = 128
SSD_CONV = 4
SSD_BC = SSD_GROUPS * SSD_STATE
SSD_CONV_DIM = SSD_DI + 2 * SSD_BC
SSD_PROJ = 2 * SSD_DI + 2 * SSD_BC + SSD_HEADS

FFN_DIM = 11 * D_MODEL // 4
FFN_CONV = 3

kernel_name = 'hybrid_retention_ssd_convffn_meta'


def rmsnorm(x, w):
    xf = x.astype(jnp.float32)
    y = xf * lax.rsqrt(jnp.mean(xf * xf, axis=-1, keepdims=True) + NORM_EPS)
    return (y * w.astype(jnp.float32)).astype(x.dtype)


def causal_dwconv(x, w, b):
    width, ch = w.shape
    out = lax.conv_general_dilated(
        x, w[:, None, :].astype(x.dtype), window_strides=(1,), padding=[(width - 1, 0)],
        dimension_numbers=('NWC', 'WIO', 'NWC'), feature_group_count=ch)
    return out + b.astype(x.dtype)


def to_chunks(a):
    return jnp.moveaxis(a.reshape(a.shape[0], -1, CHUNK, *a.shape[2:]), 1, 0)


def from_chunks(a):
    a = jnp.moveaxis(a, 0, 1)
    return a.reshape(a.shape[0], -1, *a.shape[3:])


def rotary(x, pos):
    half = x.shape[-1] // 2
    inv = ROPE_BASE ** (-jnp.arange(half, dtype=jnp.float32) / half)
    ang = pos[:, None] * inv[None, :]
    cos = jnp.cos(ang)[None, :, None, :]
    sin = jnp.sin(ang)[None, :, None, :]
    x1, x2 = x[..., :half], x[..., half:]
    return jnp.concatenate([x1 * cos - x2 * sin, x1 * sin + x2 * cos], axis=-1)


def retention_mixer(hn, pos, w_in, gn_w, w_out):
    bsz, length, _ = hn.shape
    f32 = jnp.float32
    proj = hn @ w_in
    q, k, v, g = jnp.split(proj, [RET_QK, 2 * RET_QK, 2 * RET_QK + RET_VDIM], axis=-1)
    q = rotary(q.reshape(bsz, length, RET_HEADS, RET_DK).astype(f32), pos)
    k = rotary(k.reshape(bsz, length, RET_HEADS, RET_DK).astype(f32), pos) * (RET_DK ** -0.5)
    v = v.reshape(bsz, length, RET_HEADS, RET_DV).astype(f32)
    log_gamma = jnp.log1p(-jnp.exp2(-5.0 - jnp.arange(RET_HEADS, dtype=f32)))
    idx = jnp.arange(CHUNK, dtype=f32)
    diff = idx[:, None] - idx[None, :]
    intra_decay = jnp.where(diff[None] >= 0,
                            jnp.exp(jnp.maximum(diff, 0.0)[None] * log_gamma[:, None, None]), 0.0)
    q_decay = jnp.exp((idx + 1.0)[:, None] * log_gamma[None, :])[None, :, :, None]
    k_decay = jnp.exp((CHUNK - 1.0 - idx)[:, None] * log_gamma[None, :])[None, :, :, None]
    chunk_decay = jnp.exp(CHUNK * log_gamma)[None, :, None, None]

    def step(state, inp):
        qc, kc, vc = inp
        scores = jnp.einsum('blhd,bshd->bhls', qc, kc) * intra_decay
        o = jnp.einsum('bhls,bshe->blhe', scores, vc)
        o = o + jnp.einsum('blhd,bhde->blhe', qc * q_decay, state)
        state = state * chunk_decay + jnp.einsum('bshd,bshe->bhde', kc * k_decay, vc)
        return state, o

    s0 = jnp.zeros((bsz, RET_HEADS, RET_DK, RET_DV), f32)
    _, o = lax.scan(step, s0, (to_chunks(q), to_chunks(k), to_chunks(v)))
    o = from_chunks(o)
    o = o * lax.rsqrt(jnp.mean(o * o, axis=-1, keepdims=True) + NORM_EPS)
    o = o.reshape(bsz, length, RET_VDIM) * gn_w.astype(f32)
    y = jax.nn.silu(g.astype(f32)) * o
    return y.astype(hn.dtype) @ w_out


def ssd_mixer(hn, valid, w_in, conv_w, conv_b, dt_bias, a_log, d_skip, gnorm_w, w_out):
    bsz, length, _ = hn.shape
    f32 = jnp.float32
    proj = hn @ w_in
    z, xbc, dt = jnp.split(proj, [SSD_DI, SSD_DI + SSD_CONV_DIM], axis=-1)
    xbc = jax.nn.silu(causal_dwconv(xbc, conv_w, conv_b))
    xs, bm, cm = jnp.split(xbc, [SSD_DI, SSD_DI + SSD_BC], axis=-1)
    xs = (xs * valid[None, :, None]).astype(f32).reshape(bsz, length, SSD_GROUPS, SSD_HPG, SSD_HEADDIM)
    bm = bm.astype(f32).reshape(bsz, length, SSD_GROUPS, SSD_STATE)
    cm = cm.astype(f32).reshape(bsz, length, SSD_GROUPS, SSD_STATE)
    dt = jax.nn.softplus(dt.astype(f32) + dt_bias.astype(f32)).reshape(bsz, length, SSD_GROUPS, SSD_HPG)
    da = dt * (-jnp.exp(a_log.astype(f32))).reshape(SSD_GROUPS, SSD_HPG)
    xdt = xs * dt[..., None]
    causal = jnp.tril(jnp.ones((CHUNK, CHUNK), dtype=bool))[None, :, :, None, None]

    def step(state, inp):
        xc, bc, cc, dac = inp
        cum = jnp.cumsum(dac, axis=1)
        seg = cum[:, :, None] - cum[:, None, :]
        decay = jnp.exp(jnp.where(causal, seg, -jnp.inf))
        attn = jnp.einsum('blgn,bsgn->blsg', cc, bc)[..., None] * decay
        y = jnp.einsum('blsgj,bsgjp->blgjp', attn, xc)
        y = y + jnp.einsum('blgn,bgjpn->blgjp', cc, state) * jnp.exp(cum)[..., None]
        to_end = jnp.exp(cum[:, -1:] - cum)
        state = state * jnp.exp(cum[:, -1])[..., None, None] + \
            jnp.einsum('bsgn,bsgj,bsgjp->bgjpn', bc, to_end, xc)
        return state, y

    s0 = jnp.zeros((bsz, SSD_GROUPS, SSD_HPG, SSD_HEADDIM, SSD_STATE), f32)
    _, ys = lax.scan(step, s0, (to_chunks(xdt), to_chunks(bm), to_chunks(cm), to_chunks(da)))
    y = from_chunks(ys) + xs * d_skip.astype(f32).reshape(SSD_GROUPS, SSD_HPG)[..., None]
    y = y.reshape(bsz, length, SSD_DI) * jax.nn.silu(z.astype(f32))
    yg = y.reshape(bsz, length, SSD_GROUPS, SSD_DI // SSD_GROUPS)
    yg = yg * lax.rsqrt(jnp.mean(yg * yg, axis=-1, keepdims=True) + NORM_EPS)
    y = yg.reshape(bsz, length, SSD_DI) * gnorm_w.astype(f32)
    return y.astype(hn.dtype) @ w_out


def conv_ffn(hn, w_up, conv_w, conv_b, w_down):
    gate, up = jnp.split(hn @ w_up, 2, axis=-1)
    gate = causal_dwconv(gate, conv_w, conv_b)
    return (jax.nn.silu(gate) * up) @ w_down


def setup_inputs(seed: int = 0) -> dict:
    key = jax.random.key(seed)
    ks = jax.random.split(key, 24)
    nrm = jax.random.normal
    f32 = jnp.float32
    D = D_MODEL
    dt0 = jnp.exp(jax.random.uniform(ks[10], (N_SSD_LAYERS, SSD_HEADS), f32)
                  * (np.log(0.1) - np.log(0.001)) + np.log(0.001)).astype(f32)
    return {
        'x': nrm(ks[0], (BATCH, SEQ, D), f32),
        'meta_tokens': nrm(ks[1], (N_META, D), f32),
        'ret_norm_w': 1.0 + 0.02 * nrm(ks[2], (N_RET_LAYERS, D), f32),
        'ret_w_in': nrm(ks[3], (N_RET_LAYERS, D, RET_PROJ), f32) * D ** -0.5,
        'ret_gn_w': 1.0 + 0.02 * nrm(ks[4], (N_RET_LAYERS, RET_VDIM), f32),
        'ret_w_out': nrm(ks[5], (N_RET_LAYERS, RET_VDIM, D), f32) * RET_VDIM ** -0.5,
        'ssd_norm_w': 1.0 + 0.02 * nrm(ks[6], (N_SSD_LAYERS, D), f32),
        'ssd_w_in': nrm(ks[7], (N_SSD_LAYERS, D, SSD_PROJ), f32) * D ** -0.5,
        'ssd_conv_w': nrm(ks[8], (N_SSD_LAYERS, SSD_CONV, SSD_CONV_DIM), f32) * SSD_CONV ** -0.5,
        'ssd_conv_b': 0.02 * nrm(ks[9], (N_SSD_LAYERS, SSD_CONV_DIM), f32),
        'ssd_dt_bias': dt0 + jnp.log(-jnp.expm1(-dt0)),
        'ssd_a_log': jnp.log(jax.random.uniform(ks[11], (N_SSD_LAYERS, SSD_HEADS), f32, 1.0, 16.0)),
        'ssd_d': 1.0 + 0.02 * nrm(ks[12], (N_SSD_LAYERS, SSD_HEADS), f32),
        'ssd_gnorm_w': 1.0 + 0.02 * nrm(ks[13], (N_SSD_LAYERS, SSD_DI), f32),
        'ssd_w_out': nrm(ks[14], (N_SSD_LAYERS, SSD_DI, D), f32) * SSD_DI ** -0.5,
        'ffn_norm_w': 1.0 + 0.02 * nrm(ks[15], (DEPTH, D), f32),
        'ffn_w_up': nrm(ks[16], (DEPTH, D, 2 * FFN_DIM), f32) * D ** -0.5,
        'ffn_conv_w': nrm(ks[17], (DEPTH, FFN_CONV, FFN_DIM), f32) * FFN_CONV ** -0.5,
        'ffn_conv_b': 0.02 * nrm(ks[18], (DEPTH, FFN_DIM), f32),
        'ffn_w_down': nrm(ks[19], (DEPTH, FFN_DIM, D), f32) * FFN_DIM ** -0.5,
        'final_norm_w': 1.0 + 0.02 * nrm(ks[20], (D,), f32),
    }


def reference(x, meta_tokens, ret_norm_w, ret_w_in, ret_gn_w, ret_w_out,
              ssd_norm_w, ssd_w_in, ssd_conv_w, ssd_conv_b, ssd_dt_bias, ssd_a_log, ssd_d,
              ssd_gnorm_w, ssd_w_out, ffn_norm_w, ffn_w_up, ffn_conv_w, ffn_conv_b, ffn_w_down,
              final_norm_w):
    bsz, seq, d = x.shape
    dtype = x.dtype
    length = META_PAD + N_META + seq
    meta = jnp.broadcast_to(meta_tokens.astype(dtype)[None], (bsz, N_META, d))
    h = jnp.concatenate([jnp.zeros((bsz, META_PAD, d), dtype), meta, x], axis=1)
    pos_i = jnp.arange(length) - META_PAD
    valid = (pos_i >= 0).astype(dtype)
    vmask = valid[None, :, None]
    pos = pos_i.astype(jnp.float32)
    for i in range(DEPTH):
        j = i // N_MIXERS
        if i % N_MIXERS == 0:
            mix = retention_mixer(rmsnorm(h, ret_norm_w[j]), pos, ret_w_in[j], ret_gn_w[j], ret_w_out[j])
        else:
            mix = ssd_mixer(rmsnorm(h, ssd_norm_w[j]), valid, ssd_w_in[j], ssd_conv_w[j], ssd_conv_b[j],
                            ssd_dt_bias[j], ssd_a_log[j], ssd_d[j], ssd_gnorm_w[j], ssd_w_out[j])
        h = (h + mix) * vmask
        ffn = conv_ffn(rmsnorm(h, ffn_norm_w[i]), ffn_w_up[i], ffn_conv_w[i], ffn_conv_b[i], ffn_w_down[i])
        h = (h + ffn) * vmask
    h = rmsnorm(h, final_norm_w)
    return h[:, META_PAD + N_META:]
```

```python
import contextlib
import numpy as np
import concourse.bass as bass
import concourse.mybir as mybir
from concourse.bass_utils import run_bass_kernel_spmd

F32 = mybir.dt.float32
BF16 = mybir.dt.bfloat16
AF = mybir.ActivationFunctionType
ALU = mybir.AluOpType

NCORES = 8
D = 2048
KT = 16
PRE = 16
OWN = 1024
T = PRE + OWN
CH = 104
NCH = T // CH
TTS = [(0, 347), (347, 694), (694, 1040)]
EPS = 1e-6
FFN = 5632
FT = FFN // 128
FG = 4
FGT = FT // FG
DEPTH = 4
WSLOT = 2048
NWSLOT = 4


class Buf:
    __slots__ = ("name", "writer", "readers")

    def __init__(self, name):
        self.name = name
        self.writer = None
        self.readers = {}


class Prog:
    ENGS = ("pe", "act", "dve", "pool", "sp")

    def __init__(self, nc):
        self.nc = nc
        self.ops = {e: [] for e in self.ENGS}
        self.seen = {e: {} for e in self.ENGS}
        self.dma_count = {}
        self.dma_unit = {}

    def op(self, eng, fn, reads=(), writes=(), dma=None, unit=16):
        deps = []
        for b in reads:
            if b.writer is not None:
                deps.append(b.writer)
        for b in writes:
            if b.writer is not None:
                deps.append(b.writer)
            deps.extend(b.readers.values())
        waits = []
        seen = self.seen[eng]
        for d in deps:
            if d[0] == "eng":
                _, e2, idx = d
                if e2 == eng and eng in ("pe", "sp"):
                    continue
                if seen.get(e2, -1) >= idx:
                    continue
                seen[e2] = idx
            else:
                _, key, cnt = d
                k = ("dma", key)
                if seen.get(k, -1) >= cnt:
                    continue
                seen[k] = cnt
            waits.append(d)
        idx = len(self.ops[eng])
        if dma is not None:
            c = self.dma_count.get(dma, 0) + 1
            self.dma_count[dma] = c
            self.dma_unit[dma] = unit
            tok = ("dma", dma, c)
            rkey = ("dma", dma)
        else:
            tok = ("eng", eng, idx)
            rkey = eng
        self.ops[eng].append(dict(fn=fn, waits=waits, tok=tok, inc=False))
        for b in reads:
            b.readers[rkey] = tok
        for b in writes:
            b.writer = tok
            b.readers = {}
        return tok

    def emit(self, final_bufs=()):
        nc = self.nc
        self.op("sp", None, reads=list(final_bufs))
        for e in self.ENGS:
            for o in self.ops[e]:
                for w in o["waits"]:
                    if w[0] == "eng":
                        self.ops[w[1]][w[2]]["inc"] = True
        rank = {}
        for e in self.ENGS:
            r = 0
            rk = []
            for o in self.ops[e]:
                if o["inc"] and o["tok"][0] == "eng":
                    r += 1
                rk.append(r)
            rank[e] = rk
        with contextlib.ExitStack() as st:
            esem = {e: st.enter_context(nc.semaphore("s_" + e)) for e in ("pe", "act", "dve", "pool")}
            dsem = {k: st.enter_context(nc.semaphore("d_%s" % str(k))) for k in self.dma_count}
            block = st.enter_context(nc.Block())
            handles = {"pe": block.tensor, "act": block.scalar, "dve": block.vector,
                       "pool": block.gpsimd, "sp": block.sync}

            def make(e):
                def body(h):
                    for o in self.ops[e]:
                        for w in o["waits"]:
                            if w[0] == "eng":
                                h.wait_ge(esem[w[1]], rank[w[1]][w[2]])
                            else:
                                h.wait_ge(dsem[w[1]], self.dma_unit[w[1]] * w[2])
                        if o["fn"] is None:
                            continue
                        ins = o["fn"](h)
                        tok = o["tok"]
                        if tok[0] == "dma":
                            if self.dma_unit[tok[1]] == 1:
                                ins.then_inc(dsem[tok[1]])
                            else:
                                ins.then_inc(dsem[tok[1]], self.dma_unit[tok[1]])
                        elif o["inc"]:
                            ins.then_inc(esem[e], 1)
                return body
            for e in self.ENGS:
                handles[e](make(e))


NF_ARENA = 9088
NB_ARENA = 28160
RH = 8
QB = 208
NQB = T // QB
GAMMA = [1.0 - 2.0 ** (-5.0 - h) for h in range(RH)]


class Builder:
    def __init__(self, sublayers, final_norm):
        self.sublayers = sublayers
        self.final_norm = final_norm
        self.nc = bass.Bass("TRN2", target_bir_lowering=False)
        self.P = Prog(self.nc)
        self.st = contextlib.ExitStack()
        self.dram = {}
        self.wblocks = []
        self.w_issued = 0
        self.w_next = 0
        self.ring = [0, 1, 2, 3, 4, 5]
        self.ring_i = 0
        self.n_cc = 0
        self.arena_bufs = []

    def din(self, name, shape, dt=F32):
        if name in self.dram:
            return self.dram[name]
        t = self.nc.dram_tensor(name, list(shape), dt, kind="ExternalInput").ap()
        self.dram[name] = t
        return t

    def sb(self, name, shape, dt):
        return self.st.enter_context(self.nc.sbuf_tensor(name, list(shape), dt))

    def set_ring(self, banks):
        self.ring = list(banks)
        self.ring_i = 0

    def next_bank(self):
        b = self.ring[self.ring_i % len(self.ring)]
        self.ring_i += 1
        return b

    def arena_begin(self):
        self.oF = 0
        self.oB = 0
        old = self.arena_bufs
        self.arena_bufs = []
        self._old_arena = old

    def arena_end(self):
        bufs = self._old_arena + self.arena_bufs
        self.P.op("dve", lambda h: h.memset(self.dummy[:], 0.0), writes=bufs + [self.b_dummy])

    def cF(self, n, parts=128):
        ap = self.arenaF[0:parts, self.oF:self.oF + n]
        self.oF += n
        assert self.oF <= NF_ARENA, ("fp32 arena overflow", self.oF)
        return ap

    def cB(self, n, parts=128):
        ap = self.arenaB[0:parts, self.oB:self.oB + n]
        self.oB += n
        assert self.oB <= NB_ARENA, ("bf16 arena overflow", self.oB)
        return ap

    def abuf(self, name):
        b = Buf(name)
        self.arena_bufs.append(b)
        return b

    def wreg(self, view, kt, n):
        assert kt * n <= WSLOT
        self.wblocks.append((view, kt, n))

    def _wissue(self, i):
        view, kt, n = self.wblocks[i]
        slot = i % NWSLOT
        dst = self.wsb[:, slot, 0:kt * n].rearrange("p (k n) -> p k n", n=n)
        self.P.op("pool", lambda h, dst=dst, view=view: h.dma_start(out=dst, in_=view),
                  writes=[self.wbuf[slot]], dma="w%d" % slot)

    def wprefetch(self):
        while self.w_issued < min(len(self.wblocks), self.w_next + NWSLOT):
            self._wissue(self.w_issued)
            self.w_issued += 1

    def wacquire(self, count=1):
        i = self.w_next
        self.w_next += count
        while self.w_issued < min(len(self.wblocks), i + NWSLOT):
            self._wissue(self.w_issued)
            self.w_issued += 1
        res = []
        for ii in range(i, i + count):
            view, kt, n = self.wblocks[ii]
            slot = ii % NWSLOT
            res.append((self.wsb[:, slot, 0:kt * n].rearrange("p (k n) -> p k n", n=n), self.wbuf[slot]))
        return res[0] if count == 1 else res

    def build(self):
        nc, P = self.nc, self.P
        self.xin = self.din("xin", [T, D])
        self.c_ident = self.din("c_ident", [128, 128])
        self.c_premask = self.din("c_premask", [128, PRE])
        self.c_halosel = self.din("c_halosel", [128, 4])
        self.out = nc.dram_tensor("out", [T, D], F32, kind="ExternalOutput").ap()
        for kind, idx in self.sublayers:
            getattr(self, kind + "_inputs")(idx)
        if self.final_norm:
            self.din("fin_nw", [128, KT])

        self.hT = self.sb("hT", [128, KT, T], F32)
        self.hnT = self.sb("hnT", [128, KT, T], BF16)
        self.wsb = self.sb("wsb", [128, NWSLOT, WSLOT], BF16)
        self.ident = self.sb("ident", [128, 128], F32)
        self.identb = self.sb("identb", [128, 128], BF16)
        self.ones_bf = self.sb("ones_bf", [128, 128], BF16)
        self.premask = self.sb("premask", [128, PRE], F32)
        self.halosel = self.sb("halosel", [128, 4], F32)
        self.nwcol = self.sb("nwcol", [128, KT], F32)
        self.hg = self.sb("hg", [128, 4, KT * 3], BF16)
        self.halo = self.sb("halo", [128, KT * 3], F32)
        self.dummy = self.sb("dummy_t", [128, 2], F32)
        self.arenaF = self.sb("arenaF", [128, NF_ARENA], F32)
        self.arenaB = self.sb("arenaB", [128, NB_ARENA], BF16)
        self.ps = self.st.enter_context(nc.psum_tensor("ps", [128, 6, 512], F32))
        self.psb = self.st.enter_context(nc.psum_tensor("psb", [128, 2, 1024], BF16))

        self.b_hT = [Buf("hT%d" % k) for k in range(KT)]
        self.b_hn = Buf("hn")
        self.wbuf = [Buf("w%d" % s) for s in range(NWSLOT)]
        self.b_const = Buf("const")
        self.b_nw = Buf("nw")
        self.b_ps = [Buf("ps%d" % i) for i in range(6)]
        self.b_psb = [Buf("psb0"), Buf("psb1")]
        self.b_hg = Buf("hg")
        self.b_halo = Buf("halo")
        self.b_out = Buf("out")
        self.b_dummy = Buf("dummy")

        for kind, idx in self.sublayers:
            getattr(self, kind + "_wreg")(idx)

        P.op("sp", lambda h: h.dma_start(out=self.ident[:], in_=self.c_ident[:, :]), writes=[self.b_const], dma="c")
        P.op("sp", lambda h: h.dma_start(out=self.premask[:], in_=self.c_premask[:, :]), writes=[self.b_const], dma="c")
        P.op("sp", lambda h: h.dma_start(out=self.halosel[:], in_=self.c_halosel[:, :]), writes=[self.b_const], dma="c")
        P.op("dve", lambda h: h.memset(self.ones_bf[:], 1.0), writes=[self.b_const])
        P.op("dve", lambda h: h.tensor_copy(out=self.identb[:], in_=self.ident[:]), reads=[self.b_const], writes=[self.b_const])

        self.load_input()
        for kind, idx in self.sublayers:
            getattr(self, kind)(idx)
        self.store_output()
        P.emit(final_bufs=[self.b_out])
        self.st.close()
        return nc

    def io_arena(self):
        self.arena_begin()
        self.stage = self.cF(D, parts=CH)
        self.b_stage = self.abuf("stage")
        self.rstd = self.cF(T)
        self.b_rstd = self.abuf("rstd")
        self.sq = self.cB(T)
        self.b_sq = self.abuf("sq")
        self.arena_end()
        self.set_ring([0, 1, 2, 3, 4, 5])

    def load_input(self):
        P = self.P
        self.io_arena()
        for c in range(NCH):
            P.op("sp", lambda h, c=c: h.dma_start(out=self.stage, in_=self.xin[c * CH:(c + 1) * CH, :]),
                 writes=[self.b_stage], dma="st")
            for q in range(4):
                bank = self.next_bank()

                def tr(h, c=c, q=q, bank=bank):
                    ins = None
                    for i in range(4):
                        k = 4 * q + i
                        ins = h.transpose(self.ps[:, bank, i * CH:(i + 1) * CH],
                                          self.stage[:, k * 128:(k + 1) * 128], self.ident[0:CH, 0:CH])
                    return ins
                P.op("pe", tr, reads=[self.b_stage, self.b_const], writes=[self.b_ps[bank]])
                src = self.ps[:, bank, 0:4 * CH].rearrange("p (i t) -> p i t", t=CH)
                dst = self.hT[:, 4 * q:4 * q + 4, c * CH:(c + 1) * CH]
                if q % 2 == 0:
                    P.op("act", lambda h, src=src, dst=dst: h.activation(out=dst, in_=src, func=AF.Copy),
                         reads=[self.b_ps[bank]], writes=self.b_hT[4 * q:4 * q + 4])
                else:
                    P.op("dve", lambda h, src=src, dst=dst: h.tensor_copy(out=dst, in_=src),
                         reads=[self.b_ps[bank]], writes=self.b_hT[4 * q:4 * q + 4])

    def store_output(self):
        P = self.P
        self.io_arena()
        if self.final_norm:
            self.rmsnorm(self.dram["fin_nw"], to_hn=False)
        for c in range(NCH):
            for q in range(4):
                bank = self.next_bank()

                def tr(h, c=c, q=q, bank=bank):
                    ins = None
                    for i in range(4):
                        k = 4 * q + i
                        ins = h.transpose(self.ps[0:CH, bank, i * 128:(i + 1) * 128],
                                          self.hT[:, k, c * CH:(c + 1) * CH], self.ident[:, :])
                    return ins
                P.op("pe", tr, reads=self.b_hT[4 * q:4 * q + 4] + [self.b_const], writes=[self.b_ps[bank]])
                src = self.ps[0:CH, bank, :]
                dst = self.stage[:, q * 512:(q + 1) * 512]
                if q % 2 == 0:
                    P.op("act", lambda h, src=src, dst=dst: h.activation(out=dst, in_=src, func=AF.Copy),
                         reads=[self.b_ps[bank]], writes=[self.b_stage])
                else:
                    P.op("dve", lambda h, src=src, dst=dst: h.tensor_copy(out=dst, in_=src),
                         reads=[self.b_ps[bank]], writes=[self.b_stage])
            P.op("sp", lambda h, c=c: h.dma_start(out=self.out[c * CH:(c + 1) * CH, :], in_=self.stage),
                 reads=[self.b_stage], writes=[self.b_out], dma="o")

    def rmsnorm(self, nw_dram, to_hn=True):
        P = self.P
        P.op("sp", lambda h: h.dma_start(out=self.nwcol[:], in_=nw_dram[:, :]), writes=[self.b_nw], dma="nw")
        P.op("dve", lambda h: h.tensor_scalar_mul(out=self.nwcol[:], in0=self.nwcol[:], scalar1=float(np.sqrt(D))),
             reads=[self.b_nw], writes=[self.b_nw])
        banks = [self.next_bank() for _ in TTS]
        for k in range(KT):
            P.op("act", lambda h, k=k: h.activation(out=self.sq, in_=self.hT[:, k, :], func=AF.Square),
                 reads=[self.b_hT[k]], writes=[self.b_sq])

            def mm(h, k=k):
                ins = None
                for ti, (a, b) in enumerate(TTS):
                    ins = h.matmul(self.ps[:, banks[ti], 0:b - a], lhsT=self.ones_bf[:, :], rhs=self.sq[:, a:b],
                                   start=(k == 0), stop=(k == KT - 1))
                return ins
            P.op("pe", mm, reads=[self.b_sq, self.b_const], writes=[self.b_ps[b] for b in banks])
        for ti, (a, b) in enumerate(TTS):
            P.op("act", lambda h, ti=ti, a=a, b=b: h.activation(
                out=self.rstd[:, a:b], in_=self.ps[:, banks[ti], 0:b - a], func=AF.Sqrt, bias=float(D * EPS), scale=1.0),
                reads=[self.b_ps[banks[ti]]], writes=[self.b_rstd])
        P.op("dve", lambda h: h.reciprocal(out=self.rstd, in_=self.rstd), reads=[self.b_rstd], writes=[self.b_rstd])
        for k in range(KT):
            dst = self.hnT[:, k, :] if to_hn else self.hT[:, k, :]
            P.op("dve", lambda h, k=k, dst=dst: h.scalar_tensor_tensor(
                out=dst, in0=self.hT[:, k, :], scalar=self.nwcol[:, k:k + 1], in1=self.rstd,
                op0=ALU.mult, op1=ALU.mult), reads=[self.b_hT[k], self.b_nw, self.b_rstd],
                writes=[self.b_hn if to_hn else self.b_hT[k]])

    def halo_exchange(self):
        nc, P = self.nc, self.P
        i = self.n_cc
        self.n_cc += 1
        bounce = nc.dram_tensor("cc_b%d" % i, [128, KT * 3], BF16)
        gathered = nc.dram_tensor("cc_g%d" % i, [4 * 128, KT * 3], BF16)
        b_b, b_g = Buf("ccb%d" % i), Buf("ccg%d" % i)
        P.op("sp", lambda h: h.dma_start(out=bounce.ap().rearrange("p (k t) -> p k t", t=3), in_=self.hnT[:, :, T - 3:T]),
             reads=[self.b_hn], writes=[b_b], dma="cc_in")
        self.wprefetch()
        P.op("pool", lambda h: h.collective_compute("AllGather", ALU.bypass, replica_groups=[[0, 1, 2, 3], [4, 5, 6, 7]],
                                                    ins=[bounce.ap().opt()], outs=[gathered.ap().opt()]),
             reads=[b_b], writes=[b_g], dma="cc", unit=1)
        P.op("sp", lambda h: h.dma_start(out=self.hg[:], in_=gathered.ap().rearrange("(r p) f -> p r f", p=128)),
             reads=[b_g], writes=[self.b_hg], dma="cc_out")
        P.op("dve", lambda h: h.tensor_scalar_mul(out=self.halo[:], in0=self.hg[:, 0, :], scalar1=self.halosel[:, 0:1]),
             reads=[self.b_hg, self.b_const], writes=[self.b_halo])
        for r in range(1, 4):
            P.op("dve", lambda h, r=r: h.scalar_tensor_tensor(
                out=self.halo[:], in0=self.hg[:, r, :], scalar=self.halosel[:, r:r + 1], in1=self.halo[:],
                op0=ALU.mult, op1=ALU.add), reads=[self.b_hg, self.b_halo], writes=[self.b_halo])
        P.op("dve", lambda h: h.tensor_tensor(out=self.hnT[:, :, PRE - 3:PRE], in0=self.hnT[:, :, PRE - 3:PRE],
                                              in1=self.halo[:].rearrange("p (k t) -> p k t", t=3), op=ALU.add),
             reads=[self.b_hn, self.b_halo], writes=[self.b_hn])

    def mask_pre(self):
        self.P.op("dve", lambda h: h.tensor_tensor(
            out=self.hT[:, :, 0:PRE], in0=self.hT[:, :, 0:PRE],
            in1=self.premask[:].unsqueeze(1).to_broadcast([128, KT, PRE]), op=ALU.mult),
            reads=self.b_hT + [self.b_const], writes=self.b_hT)

    def proj(self, w, wb, kt, c0, rhsT, rhs_bufs, a, b, bank):
        def mm(h):
            ins = None
            for k in range(kt):
                ins = h.matmul(self.ps[:, bank, 0:b - a], lhsT=w[:, k, c0:c0 + 128], rhs=rhsT[:, k, a:b],
                               start=(k == 0), stop=(k == kt - 1))
            return ins
        self.P.op("pe", mm, reads=[wb] + list(rhs_bufs), writes=[self.b_ps[bank]])

    def resid_add(self, n, a, b, bank):
        self.P.op("dve", lambda h: h.tensor_tensor(
            out=self.hT[:, n, a:b], in0=self.hT[:, n, a:b], in1=self.ps[:, bank, 0:b - a], op=ALU.add),
            reads=[self.b_hT[n], self.b_ps[bank]], writes=[self.b_hT[n]])

    def ffn_inputs(self, idx):
        self.din("ffn_nw%d" % idx, [128, KT])
        self.din("ffn_wup%d" % idx, [D, 2 * FFN])
        self.din("ffn_cw%d" % idx, [128, FT, 3])
        self.din("ffn_cb%d" % idx, [128, FT])
        self.din("ffn_wdn%d" % idx, [FFN, D])

    def ffn_wreg(self, idx):
        wup = self.dram["ffn_wup%d" % idx]
        wdn = self.dram["ffn_wdn%d" % idx]
        for g in range(FG):
            for jj in range(FGT):
                j = g * FGT + jj
                self.wreg(wup[:, j * 128:(j + 1) * 128].rearrange("(k p) n -> p k n", p=128), KT, 128)
                self.wreg(wup[:, FFN + j * 128:FFN + (j + 1) * 128].rearrange("(k p) n -> p k n", p=128), KT, 128)
            for nb in range(16):
                self.wreg(wdn[g * FGT * 128:(g + 1) * FGT * 128, nb * 128:(nb + 1) * 128].rearrange(
                    "(k p) n -> p k n", p=128), FGT, 128)

    def ffn(self, idx):
        P = self.P
        self.arena_begin()
        self.rstd = self.cF(T)
        self.b_rstd = self.abuf("rstd")
        self.sq = self.cB(T)
        self.b_sq = self.abuf("sq")
        actT = self.cB(FGT * T).rearrange("p (j t) -> p j t", t=T)
        gpre = self.cF(2 * (T + 2)).rearrange("p (s t) -> p s t", s=2)
        gacc = self.cF(T)
        sg = self.cF(T)
        cw = self.cF(FT * 3).rearrange("p (j w) -> p j w", w=3)
        cb = self.cF(FT)
        b_act, b_gacc, b_sg, b_cw = self.abuf("actT"), self.abuf("gacc"), self.abuf("sg"), self.abuf("cw")
        b_gpre = [self.abuf("gpre0"), self.abuf("gpre1")]
        self.arena_end()
        self.set_ring([0, 1, 2, 3, 4, 5])

        P.op("dve", lambda h: h.memset(gpre[:, :, 0:2], 0.0), writes=b_gpre)
        P.op("sp", lambda h: h.dma_start(out=cw, in_=self.dram["ffn_cw%d" % idx][:, :, :]), writes=[b_cw], dma="cw")
        P.op("sp", lambda h: h.dma_start(out=cb, in_=self.dram["ffn_cb%d" % idx][:, :]), writes=[b_cw], dma="cw")
        self.rmsnorm(self.dram["ffn_nw%d" % idx])
        self.halo_exchange()
        for g in range(FG):
            for jj in range(FGT):
                j = g * FGT + jj
                s = j % 2
                wg, wgb = self.wacquire()
                gb = [self.next_bank() for _ in TTS]
                for ti, (a, b) in enumerate(TTS):
                    self.proj(wg, wgb, KT, 0, self.hnT, [self.b_hn], a, b, gb[ti])
                    P.op("act", lambda h, a=a, b=b, s=s, bank=gb[ti]: h.activation(
                        out=gpre[:, s, 2 + a:2 + b], in_=self.ps[:, bank, 0:b - a], func=AF.Copy),
                        reads=[self.b_ps[gb[ti]]], writes=[b_gpre[s]])
                wu, wub = self.wacquire()
                ub = [self.next_bank() for _ in TTS]
                for ti, (a, b) in enumerate(TTS):
                    self.proj(wu, wub, KT, 0, self.hnT, [self.b_hn], a, b, ub[ti])
                P.op("dve", lambda h, j=j, s=s: h.tensor_scalar(
                    out=gacc, in0=gpre[:, s, 2:T + 2], scalar1=cw[:, j, 2:3], scalar2=cb[:, j:j + 1],
                    op0=ALU.mult, op1=ALU.add), reads=[b_gpre[s], b_cw, b_sg], writes=[b_gacc])
                P.op("dve", lambda h, j=j, s=s: h.scalar_tensor_tensor(
                    out=gacc, in0=gpre[:, s, 1:T + 1], scalar=cw[:, j, 1:2], in1=gacc, op0=ALU.mult, op1=ALU.add),
                    reads=[b_gpre[s], b_cw, b_gacc], writes=[b_gacc])
                P.op("dve", lambda h, j=j, s=s: h.scalar_tensor_tensor(
                    out=gacc, in0=gpre[:, s, 0:T], scalar=cw[:, j, 0:1], in1=gacc, op0=ALU.mult, op1=ALU.add),
                    reads=[b_gpre[s], b_cw, b_gacc], writes=[b_gacc])
                P.op("act", lambda h: h.activation(out=sg, in_=gacc, func=AF.Silu), reads=[b_gacc], writes=[b_sg])
                for ti, (a, b) in enumerate(TTS):
                    P.op("dve", lambda h, a=a, b=b, jj=jj, bank=ub[ti]: h.tensor_tensor(
                        out=actT[:, jj, a:b], in0=sg[:, a:b], in1=self.ps[:, bank, 0:b - a], op=ALU.mult),
                        reads=[b_sg, self.b_ps[ub[ti]]], writes=[b_act])
            for n in range(16):
                wd, wdb = self.wacquire()
                for ti, (a, b) in enumerate(TTS):
                    bank = self.next_bank()
                    self.proj(wd, wdb, FGT, 0, actT, [b_act], a, b, bank)
                    self.resid_add(n, a, b, bank)
        self.mask_pre()

    def ret_inputs(self, j):
        self.din("ret_nw%d" % j, [128, KT])
        self.din("ret_win%d" % j, [D, 12288])
        self.din("ret_gn%d" % j, [128, 32])
        self.din("ret_wout%d" % j, [4096, D])
        self.din("c_cos", [128, T], BF16)
        self.din("c_sin", [128, T], BF16)
        self.din("c_retG", [RH, CH, T], BF16)
        self.din("c_retQ", [128, RH * QB], BF16)
        self.din("c_retK", [CH, RH * NCH])
        self.din("c_retcoef", [128, 4 * RH])

    def ret_wreg(self, j):
        win = self.dram["ret_win%d" % j]
        wout = self.dram["ret_wout%d" % j]
        for h in range(RH):
            def blk(c0):
                self.wreg(win[:, c0:c0 + 128].rearrange("(k p) n -> p k n", p=128), KT, 128)
                self.wreg(win[:, c0 + 128:c0 + 256].rearrange("(k p) n -> p k n", p=128), KT, 128)
            blk(2048 + h * 256)
            blk(h * 256)
            blk(4096 + h * 512)
            blk(4096 + h * 512 + 256)
            blk(8192 + h * 512)
            blk(8192 + h * 512 + 256)
            for nb in range(4):
                self.wreg(wout[h * 512:(h + 1) * 512, nb * 512:(nb + 1) * 512].rearrange(
                    "(k p) n -> p k n", p=128), 4, 512)

    def ret(self, j):
        nc, P = self.nc, self.P
        self.arena_begin()
        self.rstd = self.cF(T)
        self.b_rstd = self.abuf("rstd")
        self.sq = self.cB(T)
        self.b_sq = self.abuf("sq")
        cosT, sinT = self.cB(T), self.cB(T)
        G = self.cB(T, parts=CH)
        qdecB = self.cB(RH * QB).rearrange("p (h l) -> p h l", l=QB)
        kfac = self.cF(RH * NCH, parts=CH)
        coef = self.cF(4 * RH)
        gnw = self.cF(32)
        kT = self.cB(2 * T).rearrange("p (d t) -> p d t", t=T)
        qT = self.cB(2 * T).rearrange("p (d t) -> p d t", t=T)
        qd = self.cB(2 * 2 * QB).rearrange("p (s d l) -> p s d l", s=2, d=2)
        vtmp = self.cB(T)
        v_tok = self.cB(NCH * 512, parts=CH).rearrange("p (c e) -> p c e", e=512)
        k_tok = self.cB(NCH * 256, parts=CH).rearrange("p (c d) -> p c d", d=256)
        yT = self.cB(4 * T).rearrange("p (e t) -> p e t", t=T)
        sqo = self.cB(4 * QB).rearrange("p (e l) -> p e l", l=QB)
        PT = self.cB(2 * QB, parts=CH).rearrange("p (s l) -> p s l", s=2)
        SinB = self.cB(1024).rearrange("p (d e) -> p d e", e=512)
        oT = self.cF(4 * QB).rearrange("p (e l) -> p e l", l=QB)
        rt = self.cF(2 * 347).rearrange("p (s l) -> p s l", s=2)
        sloc = self.cF(1024).rearrange("p (d e) -> p d e", e=512)
        sgath = self.cF(1024).rearrange("p (d e) -> p d e", e=512)
        SinF = self.cF(1024).rearrange("p (d e) -> p d e", e=512)
        rs = self.cF(QB)
        b_tab, b_G, b_kT, b_qT, b_vtmp = self.abuf("tab"), self.abuf("G"), self.abuf("kT"), self.abuf("qT"), self.abuf("vtmp")
        b_qd = [self.abuf("qd0"), self.abuf("qd1")]
        b_vtok, b_ktok, b_yT, b_sqo = self.abuf("vtok"), self.abuf("ktok"), self.abuf("yT"), self.abuf("sqo")
        b_PT = [self.abuf("PT0"), self.abuf("PT1")]
        b_SinB, b_oT, b_rt, b_sloc, b_sgath, b_SinF, b_rs = (self.abuf(n) for n in
                                                             ("SinB", "oT", "rt", "sloc", "sgath", "SinF", "rs"))
        self.arena_end()

        dr = self.dram
        for dst, src in ((cosT, dr["c_cos"][:, :]), (sinT, dr["c_sin"][:, :]),
                         (qdecB, dr["c_retQ"].rearrange("p (h l) -> p h l", l=QB)),
                         (kfac, dr["c_retK"][:, :]), (coef, dr["c_retcoef"][:, :]), (gnw, dr["ret_gn%d" % j][:, :])):
            P.op("sp", lambda h, dst=dst, src=src: h.dma_start(out=dst, in_=src), writes=[b_tab], dma="tab")
        self.set_ring([0, 1, 2, 3, 4, 5])
        self.rmsnorm(dr["ret_nw%d" % j])

        for hd in range(RH):
            gam = GAMMA[hd]
            P.op("sp", lambda h, hd=hd: h.dma_start(out=G, in_=dr["c_retG"][hd, :, :]), writes=[b_G], dma="G")
            self.set_ring([0, 1, 2, 3, 4, 5])
            for dstT, b_dst in ((kT, b_kT), (qT, b_qT)):
                (w, wb), (w2, wb2) = self.wacquire(2)
                for ti, (a, b) in enumerate(TTS):
                    b1, b2 = self.next_bank(), self.next_bank()
                    self.proj(w, wb, KT, 0, self.hnT, [self.b_hn], a, b, b1)
                    self.proj(w2, wb2, KT, 0, self.hnT, [self.b_hn], a, b, b2)
                    n = b - a
                    x1, x2 = self.ps[:, b1, 0:n], self.ps[:, b2, 0:n]
                    t1, t2 = rt[:, 0, 0:n], rt[:, 1, 0:n]
                    seq = [(t1, x1, cosT[:, a:b], None), (t2, x2, sinT[:, a:b], None), (dstT[:, 0, a:b], t1, t2, ALU.subtract),
                           (t1, x1, sinT[:, a:b], None), (t2, x2, cosT[:, a:b], None), (dstT[:, 1, a:b], t1, t2, ALU.add)]
                    for si, (o_, i0, i1, op_) in enumerate(seq):
                        fin = op_ is not None
                        rd = [b_rt] if fin else [self.b_ps[b1 if i0 is x1 else b2], b_tab]
                        P.op("dve", lambda h, o_=o_, i0=i0, i1=i1, op_=op_: h.tensor_tensor(
                            out=o_, in0=i0, in1=i1, op=(op_ or ALU.mult)), reads=rd + ([b_rt] if not fin else []),
                            writes=[b_dst] if fin else [b_rt])
            for half in range(2):
                for q in range(2):
                    w, wb = self.wacquire()
                    e = half * 2 + q
                    for ti, (a, b) in enumerate(TTS):
                        bank = self.next_bank()
                        self.proj(w, wb, KT, 0, self.hnT, [self.b_hn], a, b, bank)
                        P.op("act", lambda h, a=a, b=b, bank=bank: h.activation(
                            out=vtmp[:, a:b], in_=self.ps[:, bank, 0:b - a], func=AF.Copy),
                            reads=[self.b_ps[bank]], writes=[b_vtmp])
                    self.tok_transposes(vtmp, b_vtmp, v_tok, b_vtok, e * 128, None, None, b_tab)
            for dt in range(2):
                self.tok_transposes(kT[:, dt, :], b_kT, k_tok, b_ktok, dt * 128, kfac, hd, b_tab)
            sb_ = [0, 1]
            for dt in range(2):
                def mm(h, dt=dt):
                    ins = None
                    for c in range(NCH):
                        ins = h.matmul(self.ps[:, sb_[dt], :], lhsT=k_tok[:, c, dt * 128:(dt + 1) * 128], rhs=v_tok[:, c, :],
                                       start=(c == 0), stop=(c == NCH - 1))
                    return ins
                P.op("pe", mm, reads=[b_ktok, b_vtok], writes=[self.b_ps[sb_[dt]]])
                P.op("act", lambda h, dt=dt: h.activation(out=sloc[:, dt, :], in_=self.ps[:, sb_[dt], :], func=AF.Copy),
                     reads=[self.b_ps[sb_[dt]]], writes=[b_sloc])
            i = self.n_cc
            self.n_cc += 1
            bounce = nc.dram_tensor("cc_b%d" % i, [256, 512], F32)
            gathered = nc.dram_tensor("cc_g%d" % i, [4 * 256, 512], F32)
            b_b, b_g = Buf("ccb%d" % i), Buf("ccg%d" % i)
            P.op("sp", lambda h, bounce=bounce: h.dma_start(out=bounce.ap().rearrange("(d p) e -> p d e", p=128), in_=sloc),
                 reads=[b_sloc], writes=[b_b], dma="cc_in")
            self.wprefetch()
            P.op("pool", lambda h, bounce=bounce, gathered=gathered: h.collective_compute(
                "AllGather", ALU.bypass, replica_groups=[[0, 1, 2, 3], [4, 5, 6, 7]],
                ins=[bounce.ap().opt()], outs=[gathered.ap().opt()]), reads=[b_b], writes=[b_g], dma="cc", unit=1)
            for r in range(4):
                P.op("sp", lambda h, r=r, gathered=gathered: h.dma_start(
                    out=sgath, in_=gathered.ap()[r * 256:(r + 1) * 256, :].rearrange("(d p) e -> p d e", p=128)),
                    reads=[b_g], writes=[b_sgath], dma="sg")
                cf = coef[:, r * RH + hd:r * RH + hd + 1]
                if r == 0:
                    P.op("dve", lambda h, cf=cf: h.tensor_scalar_mul(out=SinF, in0=sgath, scalar1=cf),
                         reads=[b_sgath, b_tab], writes=[b_SinF])
                else:
                    P.op("dve", lambda h, cf=cf: h.scalar_tensor_tensor(out=SinF, in0=sgath, scalar=cf, in1=SinF,
                                                                        op0=ALU.mult, op1=ALU.add),
                         reads=[b_sgath, b_tab, b_SinF], writes=[b_SinF])
            P.op("act", lambda h: h.activation(out=SinB, in_=SinF, func=AF.Copy), reads=[b_SinF], writes=[b_SinB])
            self.set_ring([2, 3, 4, 5])
            for half in range(2):
                for q in range(2):
                    w, wb = self.wacquire()
                    e = half * 2 + q
                    for ti, (a, b) in enumerate(TTS):
                        bank = self.next_bank()
                        self.proj(w, wb, KT, 0, self.hnT, [self.b_hn], a, b, bank)
                        P.op("act", lambda h, e=e, a=a, b=b, bank=bank: h.activation(
                            out=yT[:, e, a:b], in_=self.ps[:, bank, 0:b - a], func=AF.Silu),
                            reads=[self.b_ps[bank]], writes=[b_yT])
                    P.op("dve", lambda h, e=e, hd=hd: h.tensor_scalar_mul(
                        out=yT[:, e, :], in0=yT[:, e, :], scalar1=gnw[:, hd * 4 + e:hd * 4 + e + 1]),
                        reads=[b_yT, b_tab], writes=[b_yT])
            for qb in range(NQB):
                s = qb % 2
                q0b = qb * QB
                oa = [0, 1] if s == 0 else [2, 3]
                for dt in range(2):
                    P.op("dve", lambda h, dt=dt, s=s, q0b=q0b, qb=qb, hd=hd, gam=gam: h.scalar_tensor_tensor(
                        out=qd[:, s, dt, :], in0=qT[:, dt, q0b:q0b + QB], scalar=float(gam ** (QB * qb)),
                        in1=qdecB[:, hd, :], op0=ALU.mult, op1=ALU.mult),
                        reads=[b_qT, b_tab], writes=[b_qd[s]])
                nkc = 2 * qb + 2

                def emit_sc(kc):
                    k0 = kc * CH
                    q0 = max(q0b, k0)
                    n = q0b + QB - q0
                    sbank = 4 + (kc % 2)

                    def sc(h, k0=k0, q0=q0, n=n, sbank=sbank):
                        ins = None
                        for dt in range(2):
                            ins = h.matmul(self.ps[0:CH, sbank, 0:n], lhsT=kT[:, dt, k0:k0 + CH], rhs=qT[:, dt, q0:q0 + n],
                                           start=(dt == 0), stop=(dt == 1))
                        return ins
                    P.op("pe", sc, reads=[b_kT, b_qT], writes=[self.b_ps[sbank]])

                def emit_pv(kc):
                    k0 = kc * CH
                    q0 = max(q0b, k0)
                    n = q0b + QB - q0
                    sbank = 4 + (kc % 2)
                    ps_ = kc % 2
                    P.op("dve", lambda h, n=n, sbank=sbank, ps_=ps_, q0=q0, k0=k0: h.tensor_tensor(
                        out=PT[:, ps_, 0:n], in0=self.ps[0:CH, sbank, 0:n], in1=G[:, q0 - k0:q0 - k0 + n], op=ALU.mult),
                        reads=[self.b_ps[sbank], b_G], writes=[b_PT[ps_]])

                    def pv(h, kc=kc, q0=q0, n=n, ps_=ps_, oa=oa, q0b=q0b):
                        ins = None
                        off = q0 - q0b
                        for e in range(4):
                            ins = h.matmul(self.ps[:, oa[e // 2], (e % 2) * QB + off:(e % 2) * QB + off + n],
                                           lhsT=v_tok[:, kc, e * 128:(e + 1) * 128], rhs=PT[:, ps_, 0:n],
                                           start=(kc == 0 and e % 2 == 0), stop=False, skip_group_check=True)
                        return ins
                    P.op("pe", pv, reads=[b_vtok, b_PT[ps_]], writes=[self.b_ps[oa[0]], self.b_ps[oa[1]]])
                emit_sc(0)
                for kc in range(nkc):
                    if kc + 1 < nkc:
                        emit_sc(kc + 1)
                    emit_pv(kc)

                def corr(h, s=s, oa=oa):
                    ins = None
                    for e in range(4):
                        for dt in range(2):
                            ins = h.matmul(self.ps[:, oa[e // 2], (e % 2) * QB:(e % 2) * QB + QB],
                                           lhsT=SinB[:, dt, e * 128:(e + 1) * 128], rhs=qd[:, s, dt, :],
                                           start=False, stop=(dt == 1), skip_group_check=True)
                    return ins
                P.op("pe", corr, reads=[b_SinB, b_qd[s]], writes=[self.b_ps[oa[0]], self.b_ps[oa[1]]])
                for i2 in range(2):
                    P.op("act", lambda h, i2=i2, oa=oa: h.activation(
                        out=oT[:, 2 * i2:2 * i2 + 2, :], in_=self.ps[:, oa[i2], 0:2 * QB].rearrange("p (e l) -> p e l", l=QB),
                        func=AF.Copy), reads=[self.b_ps[oa[i2]]], writes=[b_oT])
                P.op("act", lambda h: h.activation(out=sqo, in_=oT, func=AF.Square), reads=[b_oT], writes=[b_sqo])
                nbank = 4 + (qb % 2)

                def nsum(h, nbank=nbank):
                    ins = None
                    for e in range(4):
                        ins = h.matmul(self.ps[:, nbank, 0:QB], lhsT=self.ones_bf[:, :], rhs=sqo[:, e, :],
                                       start=(e == 0), stop=(e == 3))
                    return ins
                P.op("pe", nsum, reads=[b_sqo, self.b_const], writes=[self.b_ps[nbank]])
                P.op("act", lambda h, nbank=nbank: h.activation(out=rs, in_=self.ps[:, nbank, 0:QB], func=AF.Sqrt,
                                                                bias=float(EPS), scale=1.0 / 512.0),
                     reads=[self.b_ps[nbank]], writes=[b_rs])
                P.op("dve", lambda h: h.reciprocal(out=rs, in_=rs), reads=[b_rs], writes=[b_rs])
                P.op("dve", lambda h: h.tensor_tensor(out=oT, in0=oT, in1=rs.unsqueeze(1).to_broadcast([128, 4, QB]),
                                                      op=ALU.mult), reads=[b_oT, b_rs], writes=[b_oT])
                P.op("dve", lambda h, q0b=q0b: h.tensor_tensor(out=yT[:, :, q0b:q0b + QB], in0=yT[:, :, q0b:q0b + QB],
                                                               in1=oT, op=ALU.mult), reads=[b_oT, b_yT], writes=[b_yT])
            self.set_ring([0, 1, 2, 3, 4, 5])
            for nb in range(4):
                w, wb = self.wacquire()
                for q in range(4):
                    n_ = nb * 4 + q
                    for ti, (a, b) in enumerate(TTS):
                        bank = self.next_bank()
                        self.proj(w, wb, 4, q * 128, yT, [b_yT], a, b, bank)
                        self.resid_add(n_, a, b, bank)
        if getattr(self, "debug", False):
            for name, ap, shape, dt, bufs in (("dbg_kT", kT, [128, 2, T], BF16, [b_kT]), ("dbg_qT", qT, [128, 2, T], BF16, [b_qT]),
                                              ("dbg_vtok", v_tok, [CH, NCH, 512], BF16, [b_vtok]),
                                              ("dbg_ktok", k_tok, [CH, NCH, 256], BF16, [b_ktok]),
                                              ("dbg_yT", yT, [128, 4, T], BF16, [b_yT]),
                                              ("dbg_SinF", SinF, [128, 2, 512], F32, [b_SinF]),
                                              ("dbg_hn", self.hnT[:, :, :], [128, KT, T], BF16, [self.b_hn])):
                o_ = nc.dram_tensor(name, shape, dt, kind="ExternalOutput").ap()
                P.op("sp", lambda h, o_=o_, ap=ap: h.dma_start(out=o_, in_=ap), reads=bufs, writes=[self.b_out], dma="dbg")
        self.mask_pre()

    def ssd_inputs(self, j):
        self.din("ssd_nw%d" % j, [128, KT])
        self.din("ssd_win%d" % j, [D, 10304])
        self.din("ssd_cw%d" % j, [128, 48 * 4])
        self.din("ssd_cb%d" % j, [128, 48])
        self.din("ssd_dtb%d" % j, [8, 8])
        self.din("ssd_alog%d" % j, [8, 8])
        self.din("ssd_dcol%d" % j, [128, 32])
        self.din("ssd_gn%d" % j, [128, 32])
        self.din("ssd_wout%d" % j, [4096, D])
        self.din("c_tri", [CH, CH])
        self.din("c_ones", [CH, 128])
        self.din("c_sel", [8, 8 * 128])
        self.din("c_selr", [128, 4])

    def ssd_wreg(self, j):
        win = self.dram["ssd_win%d" % j]
        wout = self.dram["ssd_wout%d" % j]

        def blk(c0, n=128):
            self.wreg(win[:, c0:c0 + n].rearrange("(k p) n -> p k n", p=128), KT, n)
        for g in range(8):
            for q in range(4):
                blk(4096 + g * 512 + q * 128)
            blk(8192 + g * 128)
            blk(9216 + g * 128)
            blk(10240 + g * 8, 8)
            for q in range(4):
                blk(g * 512 + q * 128)
            for nb in range(4):
                self.wreg(wout[g * 512:(g + 1) * 512, nb * 512:(nb + 1) * 512].rearrange(
                    "(k p) n -> p k n", p=128), 4, 512)

    def ssd(self, jl):
        nc, P = self.nc, self.P
        dr = self.dram
        NC8 = NCH * 8
        self.arena_begin()
        self.rstd = self.cF(T)
        self.b_rstd = self.abuf("rstd")
        self.sq = self.cB(T)
        self.b_sq = self.abuf("sq")
        xpre = self.cF(T + 3)
        acc = self.cF(347)
        dda = self.cF(T, parts=8)
        cumT = self.cF(T, parts=8)
        dt_tok = self.cF(NC8, parts=CH)
        da_tok = self.cF(NC8, parts=CH)
        cum_tok = self.cF(NC8, parts=CH)
        negcum = self.cF(NC8, parts=CH)
        tot = self.cF(NC8)
        sfx = self.cF(NC8)
        eend = self.cF(NC8)
        wloc = self.cF(NC8, parts=CH)
        wend = self.cF(NC8, parts=CH)
        tmp80 = self.cF(NC8)
        S = self.cF(512)
        Sg = self.cF(520)
        dmul = self.cF(8)
        tri = self.cF(CH, parts=CH)
        ones = self.cF(128, parts=CH)
        sel = self.cF(8 * 128, parts=8)
        selr = self.cF(4)
        cw = self.cF(48 * 4).rearrange("p (t w) -> p t w", w=4)
        cb = self.cF(48)
        dtb = self.cF(8, parts=8)
        nega = self.cF(8, parts=8)
        dcol = self.cF(32)
        gnw = self.cF(32)
        xs = self.cB(T)
        xDT = self.cB(4 * T).rearrange("p (q t) -> p q t", t=T)
        x_tok = self.cB(NCH * 512, parts=CH).rearrange("p (c f) -> p c f", f=512)
        BT = self.cB(T)
        CT = self.cB(T)
        B_tok = self.cB(NCH * 128, parts=CH).rearrange("p (c n) -> p c n", n=128)
        E = self.cB(8 * CH, parts=CH).rearrange("p (j l) -> p j l", l=CH)
        eR = self.cB(8 * CH).rearrange("p (j l) -> p j l", l=CH)
        CBm = self.cB(CH, parts=CH)
        xdt = self.cB(512, parts=CH)
        xw = self.cB(2 * 512, parts=CH).rearrange("p (s f) -> p s f", s=2)
        S_bf = self.cB(512)
        yT = self.cB(4 * T).rearrange("p (q t) -> p q t", t=T)
        zs = self.cB(4 * T).rearrange("p (q t) -> p q t", t=T)
        names = ("tab", "xpre", "acc", "dda", "cumT", "dttok", "datok", "cumtok", "tot", "wl", "S", "Sg", "dmul", "xs", "xDT",
                 "xtok", "BT", "CT", "Btok", "E", "eR", "CBm", "xdt", "xw0", "xw1", "Sbf", "yT", "zs", "tmp80")
        B_ = {n: self.abuf(n) for n in names}
        self.arena_end()

        loads = ((tri, dr["c_tri"][:, :]), (ones, dr["c_ones"][:, :]), (sel, dr["c_sel"][:, :]), (selr, dr["c_selr"][:, :]),
                 (cw, dr["ssd_cw%d" % jl].rearrange("p (t w) -> p t w", w=4)), (cb, dr["ssd_cb%d" % jl][:, :]),
                 (dtb, dr["ssd_dtb%d" % jl][:, :]), (nega, dr["ssd_alog%d" % jl][:, :]),
                 (dcol, dr["ssd_dcol%d" % jl][:, :]), (gnw, dr["ssd_gn%d" % jl][:, :]))
        for dst, src in loads:
            P.op("sp", lambda h, dst=dst, src=src: h.dma_start(out=dst, in_=src), writes=[B_["tab"]], dma="tab")
        P.op("act", lambda h: h.activation(out=nega, in_=nega, func=AF.Exp), reads=[B_["tab"]], writes=[B_["tab"]])
        P.op("dve", lambda h: h.tensor_scalar_mul(out=nega, in0=nega, scalar1=-1.0), reads=[B_["tab"]], writes=[B_["tab"]])
        P.op("dve", lambda h: h.memset(xpre[:, 0:3], 0.0), writes=[B_["xpre"]])
        self.set_ring([0, 1, 2, 3, 4, 5])
        self.rmsnorm(dr["ssd_nw%d" % jl])
        self.halo_exchange()

        def conv_tile(tile, dst, b_dst, mask_pre):
            w, wb = self.wacquire()
            for ti, (a, b) in enumerate(TTS):
                bank = self.next_bank()
                self.proj(w, wb, KT, 0, self.hnT, [self.b_hn], a, b, bank)
                P.op("act", lambda h, a=a, b=b, bank=bank: h.activation(
                    out=xpre[:, 3 + a:3 + b], in_=self.ps[:, bank, 0:b - a], func=AF.Copy),
                    reads=[self.b_ps[bank]], writes=[B_["xpre"]])
            for ti, (a, b) in enumerate(TTS):
                n = b - a
                P.op("dve", lambda h, a=a, b=b, n=n: h.tensor_scalar(
                    out=acc[:, 0:n], in0=xpre[:, 3 + a:3 + b], scalar1=cw[:, tile, 3:4], scalar2=cb[:, tile:tile + 1],
                    op0=ALU.mult, op1=ALU.add), reads=[B_["xpre"], B_["tab"]], writes=[B_["acc"]])
                for wi in range(3):
                    P.op("dve", lambda h, a=a, b=b, n=n, wi=wi: h.scalar_tensor_tensor(
                        out=acc[:, 0:n], in0=xpre[:, wi + a:wi + b], scalar=cw[:, tile, wi:wi + 1], in1=acc[:, 0:n],
                        op0=ALU.mult, op1=ALU.add), reads=[B_["xpre"], B_["tab"], B_["acc"]], writes=[B_["acc"]])
                P.op("act", lambda h, a=a, b=b, n=n: h.activation(out=dst[:, a:b], in_=acc[:, 0:n], func=AF.Silu),
                     reads=[B_["acc"]], writes=[b_dst])
            if mask_pre:
                P.op("dve", lambda h: h.tensor_tensor(out=dst[:, 0:PRE], in0=dst[:, 0:PRE], in1=self.premask[:], op=ALU.mult),
                     reads=[b_dst, self.b_const], writes=[b_dst])

        for g in range(8):
            self.set_ring([0, 1, 2, 3, 4, 5])
            for q in range(4):
                conv_tile(g * 4 + q, xs, B_["xs"], True)
                self.tok_transposes(xs, B_["xs"], x_tok, B_["xtok"], q * 128, None, None, B_["tab"])
                P.op("dve", lambda h, q=q, g=g: h.tensor_scalar_mul(out=xDT[:, q, :], in0=xs, scalar1=dcol[:, g * 4 + q:g * 4 + q + 1]),
                     reads=[B_["xs"], B_["tab"]], writes=[B_["xDT"]])
            conv_tile(32 + g, BT, B_["BT"], False)
            self.tok_transposes(BT, B_["BT"], B_tok, B_["Btok"], 0, None, None, B_["tab"])
            conv_tile(40 + g, CT, B_["CT"], False)
            w, wb = self.wacquire()
            for ti, (a, b) in enumerate(TTS):
                bank = self.next_bank()

                def mm(h, a=a, b=b, bank=bank, w=w):
                    ins = None
                    for k in range(KT):
                        ins = h.matmul(self.ps[0:8, bank, 0:b - a], lhsT=w[:, k, 0:8], rhs=self.hnT[:, k, a:b],
                                       start=(k == 0), stop=(k == KT - 1))
                    return ins
                P.op("pe", mm, reads=[wb, self.b_hn], writes=[self.b_ps[bank]])
                P.op("act", lambda h, a=a, b=b, bank=bank, g=g: h.activation(
                    out=dda[:, a:b], in_=self.ps[0:8, bank, 0:b - a], func=AF.Exp, bias=dtb[:, g:g + 1], scale=1.0),
                    reads=[self.b_ps[bank], B_["tab"]], writes=[B_["dda"]])
            P.op("act", lambda h: h.activation(out=dda, in_=dda, func=AF.Ln, bias=1.0, scale=1.0),
                 reads=[B_["dda"]], writes=[B_["dda"]])

            def tok8(dst, b_dst):
                bank = self.next_bank()

                def tr(h, bank=bank):
                    ins = None
                    for c in range(NCH):
                        ins = h.transpose(self.ps[0:CH, bank, c * 8:(c + 1) * 8], dda[:, c * CH:(c + 1) * CH], self.ident[0:8, 0:8])
                    return ins
                P.op("pe", tr, reads=[B_["dda"], self.b_const], writes=[self.b_ps[bank]])
                P.op("act", lambda h, bank=bank: h.activation(out=dst, in_=self.ps[0:CH, bank, 0:NC8], func=AF.Copy),
                     reads=[self.b_ps[bank]], writes=[b_dst])
            tok8(dt_tok, B_["dttok"])
            P.op("dve", lambda h, g=g: h.tensor_scalar_mul(out=dda, in0=dda, scalar1=nega[:, g:g + 1]),
                 reads=[B_["dda"], B_["tab"]], writes=[B_["dda"]])
            P.op("dve", lambda h: h.tensor_tensor(out=dda[:, 0:PRE], in0=dda[:, 0:PRE], in1=self.premask[0:8, :], op=ALU.mult),
                 reads=[B_["dda"], self.b_const], writes=[B_["dda"]])
            tok8(da_tok, B_["datok"])
            b_cum, b_tot = self.next_bank(), self.next_bank()
            P.op("pe", lambda h, b_cum=b_cum: h.matmul(self.ps[0:CH, b_cum, 0:NC8], lhsT=tri, rhs=da_tok, start=True, stop=True),
                 reads=[B_["datok"], B_["tab"]], writes=[self.b_ps[b_cum]])
            P.op("pe", lambda h, b_tot=b_tot: h.matmul(self.ps[:, b_tot, 0:NC8], lhsT=ones, rhs=da_tok, start=True, stop=True),
                 reads=[B_["datok"], B_["tab"]], writes=[self.b_ps[b_tot]])
            P.op("act", lambda h, b_cum=b_cum: h.activation(out=cum_tok, in_=self.ps[0:CH, b_cum, 0:NC8], func=AF.Copy),
                 reads=[self.b_ps[b_cum]], writes=[B_["cumtok"]])
            P.op("act", lambda h, b_cum=b_cum: h.activation(out=negcum, in_=self.ps[0:CH, b_cum, 0:NC8], func=AF.Copy, scale=-1.0),
                 reads=[self.b_ps[b_cum]], writes=[B_["cumtok"]])
            P.op("act", lambda h, b_tot=b_tot: h.activation(out=tot, in_=self.ps[:, b_tot, 0:NC8], func=AF.Copy),
                 reads=[self.b_ps[b_tot]], writes=[B_["tot"]])
            P.op("act", lambda h, b_tot=b_tot: h.activation(out=eend, in_=self.ps[:, b_tot, 0:NC8], func=AF.Exp),
                 reads=[self.b_ps[b_tot]], writes=[B_["tot"]])
            P.op("dve", lambda h: h.tensor_copy(out=sfx[:, (NCH - 1) * 8:NC8], in_=tot[:, (NCH - 1) * 8:NC8]),
                 reads=[B_["tot"]], writes=[B_["tot"]])
            for c in range(NCH - 2, -1, -1):
                P.op("dve", lambda h, c=c: h.tensor_tensor(out=sfx[:, c * 8:(c + 1) * 8], in0=sfx[:, (c + 1) * 8:(c + 2) * 8],
                                                           in1=tot[:, c * 8:(c + 1) * 8], op=ALU.add),
                     reads=[B_["tot"]], writes=[B_["tot"]])
            for src, dst in ((sfx, wloc), (tot, wend)):
                P.op("dve", lambda h, src=src: h.tensor_tensor(out=tmp80[0:CH, :], in0=src[0:CH, :], in1=cum_tok, op=ALU.subtract),
                     reads=[B_["tot"], B_["cumtok"], B_["tmp80"]], writes=[B_["tmp80"]])
                P.op("act", lambda h: h.activation(out=tmp80[0:CH, :], in_=tmp80[0:CH, :], func=AF.Exp),
                     reads=[B_["tmp80"]], writes=[B_["tmp80"]])
                P.op("dve", lambda h, dst=dst: h.tensor_tensor(out=dst, in0=tmp80[0:CH, :], in1=dt_tok, op=ALU.mult),
                     reads=[B_["tmp80"], B_["dttok"]], writes=[B_["wl"]])
            cb_ = [self.next_bank() for _ in range(3)]
            for c in range(NCH):
                P.op("pe", lambda h, c=c: h.matmul(self.ps[0:8, cb_[c // 4], (c % 4) * CH:(c % 4 + 1) * CH],
                                                   lhsT=da_tok[:, c * 8:(c + 1) * 8], rhs=tri, start=(c % 4 == 0), stop=True,
                                                   skip_group_check=True),
                     reads=[B_["datok"], B_["tab"]], writes=[self.b_ps[cb_[c // 4]]])
            for i3 in range(3):
                nch = min(4, NCH - 4 * i3)
                P.op("act", lambda h, i3=i3, nch=nch: h.activation(
                    out=cumT[:, 4 * i3 * CH:(4 * i3 + nch) * CH], in_=self.ps[0:8, cb_[i3], 0:nch * CH], func=AF.Copy),
                    reads=[self.b_ps[cb_[i3]]], writes=[B_["cumT"]])
            ub = self.next_bank()
            for c in range(NCH):
                s2 = c % 2
                P.op("dve", lambda h, c=c, s2=s2: h.tensor_tensor(
                    out=xw[:, s2, :].rearrange("p (j d) -> p j d", d=64), in0=x_tok[:, c, :].rearrange("p (j d) -> p j d", d=64),
                    in1=wloc[:, c * 8:(c + 1) * 8].unsqueeze(2).to_broadcast([CH, 8, 64]), op=ALU.mult),
                    reads=[B_["xtok"], B_["wl"]], writes=[B_["xw%d" % s2]])
                P.op("pe", lambda h, c=c, s2=s2: h.matmul(self.ps[:, ub, :], lhsT=B_tok[:, c, :], rhs=xw[:, s2, :],
                                                          start=(c == 0), stop=(c == NCH - 1)),
                     reads=[B_["Btok"], B_["xw%d" % s2]], writes=[self.b_ps[ub]])
            P.op("act", lambda h: h.activation(out=Sg[:, 0:512], in_=self.ps[:, ub, :], func=AF.Copy),
                 reads=[self.b_ps[ub]], writes=[B_["Sg"]])
            P.op("act", lambda h: h.activation(out=Sg[:, 512:520], in_=sfx[:, 0:8], func=AF.Exp),
                 reads=[B_["tot"]], writes=[B_["Sg"]])
            i = self.n_cc
            self.n_cc += 1
            bounce = nc.dram_tensor("cc_b%d" % i, [128, 520], F32)
            gathered = nc.dram_tensor("cc_g%d" % i, [4 * 128, 520], F32)
            b_b, b_g = Buf("ccb%d" % i), Buf("ccg%d" % i)
            P.op("sp", lambda h, bounce=bounce: h.dma_start(out=bounce.ap(), in_=Sg), reads=[B_["Sg"]], writes=[b_b], dma="cc_in")
            self.wprefetch()
            P.op("pool", lambda h, bounce=bounce, gathered=gathered: h.collective_compute(
                "AllGather", ALU.bypass, replica_groups=[[0, 1, 2, 3], [4, 5, 6, 7]],
                ins=[bounce.ap().opt()], outs=[gathered.ap().opt()]), reads=[b_b], writes=[b_g], dma="cc", unit=1)
            P.op("dve", lambda h: h.memset(S, 0.0), writes=[B_["S"]])
            for r in range(4):
                P.op("sp", lambda h, r=r, gathered=gathered: h.dma_start(out=Sg, in_=gathered.ap()[r * 128:(r + 1) * 128, :]),
                     reads=[b_g], writes=[B_["Sg"]], dma="sg")
                P.op("dve", lambda h, r=r: h.tensor_scalar(out=dmul, in0=Sg[:, 512:520], scalar1=-1.0, scalar2=selr[:, r:r + 1],
                                                           op0=ALU.add, op1=ALU.mult),
                     reads=[B_["Sg"], B_["tab"]], writes=[B_["dmul"]])
                P.op("dve", lambda h: h.tensor_scalar_add(out=dmul, in0=dmul, scalar1=1.0), reads=[B_["dmul"]], writes=[B_["dmul"]])
                P.op("dve", lambda h: h.tensor_tensor(
                    out=S.rearrange("p (j d) -> p j d", d=64), in0=S.rearrange("p (j d) -> p j d", d=64),
                    in1=dmul.unsqueeze(2).to_broadcast([128, 8, 64]), op=ALU.mult), reads=[B_["S"], B_["dmul"]], writes=[B_["S"]])
                P.op("dve", lambda h, r=r: h.scalar_tensor_tensor(out=S, in0=Sg[:, 0:512], scalar=selr[:, r:r + 1], in1=S,
                                                                  op0=ALU.mult, op1=ALU.add),
                     reads=[B_["Sg"], B_["tab"], B_["S"]], writes=[B_["S"]])
            P.op("act", lambda h: h.activation(out=S_bf, in_=S, func=AF.Copy), reads=[B_["S"]], writes=[B_["Sbf"]])
            self.set_ring([0, 1, 2, 3, 4, 5])
            for q in range(4):
                w, wb = self.wacquire()
                for ti, (a, b) in enumerate(TTS):
                    bank = self.next_bank()
                    self.proj(w, wb, KT, 0, self.hnT, [self.b_hn], a, b, bank)
                    P.op("act", lambda h, q=q, a=a, b=b, bank=bank: h.activation(
                        out=zs[:, q, a:b], in_=self.ps[:, bank, 0:b - a], func=AF.Silu),
                        reads=[self.b_ps[bank]], writes=[B_["zs"]])
            rb, yb, cbk, ub = [0, 1], [2, 3], 4, 5
            for c in range(NCH):
                c0 = c * CH
                for bk in range(2):
                    def rmm(h, bk=bk, c0=c0):
                        ins = None
                        for jj in range(4):
                            j = bk * 4 + jj
                            ins = h.matmul(self.ps[:, rb[bk], jj * CH:(jj + 1) * CH], lhsT=sel[:, j * 128:(j + 1) * 128],
                                           rhs=cumT[:, c0:c0 + CH], start=(jj == 0), stop=True, skip_group_check=True)
                        return ins
                    P.op("pe", rmm, reads=[B_["cumT"], B_["tab"]], writes=[self.b_ps[rb[bk]]])
                P.op("pe", lambda h, c0=c0: h.matmul(self.ps[0:CH, cbk, 0:CH], lhsT=BT[:, c0:c0 + CH], rhs=CT[:, c0:c0 + CH],
                                                     start=True, stop=True), reads=[B_["BT"], B_["CT"]], writes=[self.b_ps[cbk]])
                for j in range(8):
                    P.op("act", lambda h, j=j, c=c: h.activation(
                        out=E[:, j, :], in_=self.ps[0:CH, rb[j // 4], (j % 4) * CH:(j % 4 + 1) * CH], func=AF.Exp,
                        bias=negcum[:, c * 8 + j:c * 8 + j + 1], scale=1.0),
                        reads=[self.b_ps[rb[j // 4]], B_["cumtok"]], writes=[B_["E"]])
                for bk in range(2):
                    P.op("act", lambda h, bk=bk: h.activation(
                        out=eR[:, bk * 4:bk * 4 + 4, :], in_=self.ps[:, rb[bk], 0:4 * CH].rearrange("p (j l) -> p j l", l=CH),
                        func=AF.Exp), reads=[self.b_ps[rb[bk]]], writes=[B_["eR"]])
                P.op("dve", lambda h: h.tensor_tensor(out=CBm, in0=self.ps[0:CH, cbk, 0:CH], in1=tri, op=ALU.mult),
                     reads=[self.b_ps[cbk], B_["tab"]], writes=[B_["CBm"]])
                P.op("dve", lambda h: h.scalar_tensor_tensor(
                    out=E, in0=E, scalar=1.0, in1=CBm.unsqueeze(1).to_broadcast([CH, 8, CH]), op0=ALU.min, op1=ALU.mult),
                    reads=[B_["E"], B_["CBm"]], writes=[B_["E"]])
                P.op("dve", lambda h, c0=c0: h.tensor_tensor(
                    out=eR, in0=eR, in1=CT[:, c0:c0 + CH].unsqueeze(1).to_broadcast([128, 8, CH]), op=ALU.mult),
                    reads=[B_["eR"], B_["CT"]], writes=[B_["eR"]])
                P.op("dve", lambda h, c=c: h.tensor_tensor(
                    out=xdt.rearrange("p (j d) -> p j d", d=64), in0=x_tok[:, c, :].rearrange("p (j d) -> p j d", d=64),
                    in1=dt_tok[:, c * 8:(c + 1) * 8].unsqueeze(2).to_broadcast([CH, 8, 64]), op=ALU.mult),
                    reads=[B_["xtok"], B_["dttok"]], writes=[B_["xdt"]])
                P.op("dve", lambda h, c=c: h.tensor_tensor(
                    out=xw[:, 0, :].rearrange("p (j d) -> p j d", d=64), in0=x_tok[:, c, :].rearrange("p (j d) -> p j d", d=64),
                    in1=wend[:, c * 8:(c + 1) * 8].unsqueeze(2).to_broadcast([CH, 8, 64]), op=ALU.mult),
                    reads=[B_["xtok"], B_["wl"]], writes=[B_["xw0"]])
                for bk in range(2):
                    def ymm(h, bk=bk):
                        ins = None
                        for jj in range(4):
                            j = bk * 4 + jj
                            o_ = self.ps[:, yb[bk], jj * CH:(jj + 1) * CH]
                            h.matmul(o_, lhsT=xdt[:, (j // 2) * 128:(j // 2 + 1) * 128], rhs=E[:, j, :],
                                     start=(jj == 0), stop=False, skip_group_check=True)
                            ins = h.matmul(o_, lhsT=S_bf[:, (j // 2) * 128:(j // 2 + 1) * 128], rhs=eR[:, j, :],
                                           start=False, stop=True, skip_group_check=True)
                        return ins
                    P.op("pe", ymm, reads=[B_["xdt"], B_["E"], B_["Sbf"], B_["eR"]], writes=[self.b_ps[yb[bk]]])
                P.op("pe", lambda h, c=c: h.matmul(self.ps[:, ub, :], lhsT=B_tok[:, c, :], rhs=xw[:, 0, :], start=True, stop=True),
                     reads=[B_["Btok"], B_["xw0"]], writes=[self.b_ps[ub]])
                for bk in range(2):
                    for par in range(2):
                        src = self.ps[64 * par:64 * par + 64, yb[bk], 0:4 * CH].rearrange("p (q r l) -> p q r l", r=2, l=CH)[:, :, par, :]
                        P.op("dve", lambda h, src=src, par=par, bk=bk, c0=c0: h.tensor_tensor(
                            out=yT[64 * par:64 * par + 64, 2 * bk:2 * bk + 2, c0:c0 + CH], in0=src,
                            in1=xDT[64 * par:64 * par + 64, 2 * bk:2 * bk + 2, c0:c0 + CH], op=ALU.add),
                            reads=[self.b_ps[yb[bk]], B_["xDT"]], writes=[B_["yT"]])
                P.op("dve", lambda h, c=c: h.tensor_tensor(
                    out=S.rearrange("p (j d) -> p j d", d=64), in0=S.rearrange("p (j d) -> p j d", d=64),
                    in1=eend[:, c * 8:(c + 1) * 8].unsqueeze(2).to_broadcast([128, 8, 64]), op=ALU.mult),
                    reads=[B_["S"], B_["tot"]], writes=[B_["S"]])
                P.op("dve", lambda h: h.tensor_tensor(out=S, in0=S, in1=self.ps[:, ub, :], op=ALU.add),
                     reads=[B_["S"], self.b_ps[ub]], writes=[B_["S"]])
                P.op("act", lambda h: h.activation(out=S_bf, in_=S, func=AF.Copy), reads=[B_["S"]], writes=[B_["Sbf"]])
            self.set_ring([0, 1, 2, 3, 4, 5])
            nb_ = [self.next_bank() for _ in TTS]
            for q in range(4):
                P.op("dve", lambda h, q=q: h.tensor_tensor(out=yT[:, q, :], in0=yT[:, q, :], in1=zs[:, q, :], op=ALU.mult),
                     reads=[B_["yT"], B_["zs"]], writes=[B_["yT"]])
                P.op("act", lambda h, q=q: h.activation(out=self.sq, in_=yT[:, q, :], func=AF.Square),
                     reads=[B_["yT"]], writes=[self.b_sq])

                def mm(h, q=q):
                    ins = None
                    for ti, (a, b) in enumerate(TTS):
                        ins = h.matmul(self.ps[:, nb_[ti], 0:b - a], lhsT=self.ones_bf[:, :], rhs=self.sq[:, a:b],
                                       start=(q == 0), stop=(q == 3))
                    return ins
                P.op("pe", mm, reads=[self.b_sq, self.b_const], writes=[self.b_ps[b] for b in nb_])
            for ti, (a, b) in enumerate(TTS):
                P.op("act", lambda h, ti=ti, a=a, b=b: h.activation(
                    out=self.rstd[:, a:b], in_=self.ps[:, nb_[ti], 0:b - a], func=AF.Sqrt, bias=float(EPS), scale=1.0 / 512.0),
                    reads=[self.b_ps[nb_[ti]]], writes=[self.b_rstd])
            P.op("dve", lambda h: h.reciprocal(out=self.rstd, in_=self.rstd), reads=[self.b_rstd], writes=[self.b_rstd])
            for q in range(4):
                P.op("dve", lambda h, q=q, g=g: h.scalar_tensor_tensor(
                    out=yT[:, q, :], in0=yT[:, q, :], scalar=gnw[:, g * 4 + q:g * 4 + q + 1], in1=self.rstd,
                    op0=ALU.mult, op1=ALU.mult), reads=[B_["yT"], B_["tab"], self.b_rstd], writes=[B_["yT"]])
            for nb in range(4):
                w, wb = self.wacquire()
                for q in range(4):
                    n_ = nb * 4 + q
                    for ti, (a, b) in enumerate(TTS):
                        bank = self.next_bank()
                        self.proj(w, wb, 4, q * 128, yT, [B_["yT"]], a, b, bank)
                        self.resid_add(n_, a, b, bank)
        self.mask_pre()

    def tok_transposes(self, srcT, b_src, dst_tok, b_dst, col0, kfac, hd, b_tab):
        P = self.P
        for (c0, c1, pb) in ((0, 8, 0), (8, NCH, 1)):
            def tr(h, c0=c0, c1=c1, pb=pb):
                ins = None
                for c in range(c0, c1):
                    ins = h.transpose(self.psb[0:CH, pb, (c - c0) * 128:(c - c0 + 1) * 128],
                                      srcT[:, c * CH:(c + 1) * CH], self.identb[:, :])
                return ins
            P.op("pe", tr, reads=[b_src, self.b_const], writes=[self.b_psb[pb]])
            nch = c1 - c0
            src = self.psb[0:CH, pb, 0:nch * 128].rearrange("p (c d) -> p c d", d=128)
            dst = dst_tok[:, c0:c1, col0:col0 + 128]
            if kfac is None:
                P.op("act", lambda h, src=src, dst=dst: h.activation(out=dst, in_=src, func=AF.Copy),
                     reads=[self.b_psb[pb]], writes=[b_dst])
            else:
                kf = kfac[:, hd * NCH + c0:hd * NCH + c1].unsqueeze(2).to_broadcast([CH, nch, 128])
                P.op("dve", lambda h, src=src, dst=dst, kf=kf: h.tensor_tensor(out=dst, in0=src, in1=kf, op=ALU.mult),
                     reads=[self.b_psb[pb], b_tab], writes=[b_dst])


def col_layout(v):
    return np.ascontiguousarray(v.reshape(-1, 128).T)


def bf16(a):
    import ml_dtypes
    return np.asarray(a, np.float32).astype(ml_dtypes.bfloat16)


def core_consts(c):
    p = c % 4
    ident = np.eye(128, dtype=np.float32)
    premask = np.full((128, PRE), 1.0 if p == 0 else 0.0, np.float32)
    halosel = np.zeros((128, 4), np.float32)
    if p > 0:
        halosel[:, p - 1] = 1.0
    return {"c_ident": ident, "c_premask": premask, "c_halosel": halosel}


def ret_consts(c):
    p = c % 4
    half = 128
    inv = (10000.0 ** (-np.arange(half, dtype=np.float32) / half)).astype(np.float32)
    pos = (p * OWN + np.arange(T)).astype(np.float32)
    ang = (pos[None, :] * inv[:, None]).astype(np.float32)
    lg = np.log1p(-np.exp2(-5.0 - np.arange(RH, dtype=np.float64)))
    s = np.arange(CH)[:, None]
    jj = np.arange(T)[None, :]
    G = np.where(jj >= s, np.exp((jj - s)[None] * lg[:, None, None]), 0.0) / 16.0
    Q = np.exp((np.arange(QB)[None, :] + 1.0) * lg[:, None])
    Qrep = np.broadcast_to(Q.reshape(1, RH * QB), (128, RH * QB))
    cc = np.arange(NCH)[None, None, :]
    Kf = np.exp((T - 1 - CH * cc - s[:, :, None]) * lg[None, :, None]) / 16.0
    coef = np.zeros((4, RH))
    for r in range(4):
        if r < p:
            coef[r] = np.exp((OWN * (p - 1 - r) - PRE) * lg)
    return {
        "c_cos": bf16(np.cos(ang)), "c_sin": bf16(np.sin(ang)),
        "c_retG": bf16(G), "c_retQ": bf16(Qrep),
        "c_retK": np.ascontiguousarray(Kf.reshape(CH, RH * NCH).astype(np.float32)),
        "c_retcoef": np.ascontiguousarray(np.broadcast_to(coef.reshape(1, 4 * RH), (128, 4 * RH)).astype(np.float32)),
    }


def ffn_inputs(idx, ffn_norm_w, ffn_w_up, ffn_conv_w, ffn_conv_b, ffn_w_down):
    return {
        "ffn_nw%d" % idx: col_layout(ffn_norm_w[idx]),
        "ffn_wup%d" % idx: ffn_w_up[idx],
        "ffn_cw%d" % idx: np.ascontiguousarray(ffn_conv_w[idx].reshape(3, FT, 128).transpose(2, 1, 0)),
        "ffn_cb%d" % idx: col_layout(ffn_conv_b[idx]),
        "ffn_wdn%d" % idx: ffn_w_down[idx],
    }


def ret_inputs(j, ret_norm_w, ret_w_in, ret_gn_w, ret_w_out):
    return {
        "ret_nw%d" % j: col_layout(ret_norm_w[j]),
        "ret_win%d" % j: ret_w_in[j],
        "ret_gn%d" % j: col_layout(ret_gn_w[j]),
        "ret_wout%d" % j: ret_w_out[j],
    }


def ssd_consts(c):
    p = c % 4
    tri = (np.arange(CH)[:, None] <= np.arange(CH)[None, :]).astype(np.float32)
    ones = np.ones((CH, 128), np.float32)
    sel = np.zeros((8, 8, 128), np.float32)
    for j in range(8):
        sel[j, j, :] = 1.0
    selr = np.zeros((128, 4), np.float32)
    selr[:, :p] = 1.0
    return {"c_tri": tri, "c_ones": ones, "c_sel": sel.reshape(8, 8 * 128), "c_selr": selr}


def ssd_inputs(j, ssd_norm_w, ssd_w_in, ssd_conv_w, ssd_conv_b, ssd_dt_bias, ssd_a_log, ssd_d, ssd_gnorm_w, ssd_w_out):
    return {
        "ssd_nw%d" % j: col_layout(ssd_norm_w[j]),
        "ssd_win%d" % j: ssd_w_in[j],
        "ssd_cw%d" % j: np.ascontiguousarray(ssd_conv_w[j].reshape(4, 48, 128).transpose(2, 1, 0).reshape(128, 48 * 4)),
        "ssd_cb%d" % j: col_layout(ssd_conv_b[j]),
        "ssd_dtb%d" % j: np.ascontiguousarray(ssd_dt_bias[j].reshape(8, 8).T),
        "ssd_alog%d" % j: np.ascontiguousarray(ssd_a_log[j].reshape(8, 8).T),
        "ssd_dcol%d" % j: col_layout(np.repeat(ssd_d[j], 64)),
        "ssd_gn%d" % j: col_layout(ssd_gnorm_w[j]),
        "ssd_wout%d" % j: ssd_w_out[j],
    }


FUSE_GROUPS = [[("ret", 0), ("ffn", 0), ("ssd", 0), ("ffn", 1), ("ret", 1), ("ffn", 2), ("ssd", 1), ("ffn", 3)]]


def _sub_inputs(kind, idx, name_idx, inp):
    if kind == "ffn":
        d = ffn_inputs(idx, inp["ffn_norm_w"], inp["ffn_w_up"], inp["ffn_conv_w"], inp["ffn_conv_b"], inp["ffn_w_down"])
    elif kind == "ret":
        d = ret_inputs(idx, inp["ret_norm_w"], inp["ret_w_in"], inp["ret_gn_w"], inp["ret_w_out"])
    else:
        d = ssd_inputs(idx, inp["ssd_norm_w"], inp["ssd_w_in"], inp["ssd_conv_w"], inp["ssd_conv_b"], inp["ssd_dt_bias"],
                       inp["ssd_a_log"], inp["ssd_d"], inp["ssd_gnorm_w"], inp["ssd_w_out"])
    if name_idx != idx:
        d = {k[:-len(str(idx))] + str(name_idx): v for k, v in d.items()}
    return d


def kernel(**inputs):
    inp = {k: np.ascontiguousarray(np.asarray(v, dtype=np.float32)) for k, v in inputs.items()}
    x, meta = inp["x"], inp["meta_tokens"]
    hs = []
    for c in range(NCORES):
        b, p = c // 4, c % 4
        xin = np.zeros((T, D), np.float32)
        if p == 0:
            xin[:PRE] = meta
        xin[PRE:] = x[b, p * OWN:(p + 1) * OWN]
        hs.append(xin)
    consts = []
    for c in range(NCORES):
        d = core_consts(c)
        d.update(ret_consts(c))
        d.update(ssd_consts(c))
        consts.append(d)
    progs = {}
    for gi, group in enumerate(FUSE_GROUPS):
        last = gi == len(FUSE_GROUPS) - 1
        slots, counts = [], {}
        for kind, idx in group:
            s = counts.get(kind, 0)
            counts[kind] = s + 1
            slots.append((kind, s))
        key = (tuple(slots), last)
        if key not in progs:
            progs[key] = Builder(slots, final_norm=last).build()
        nc = progs[key]
        names = set(a.memorylocations[0].name for a in nc.allocations
                    if isinstance(a, mybir.MemoryLocationSet) and a.kind == "ExternalInput")
        w = {}
        for (kind, idx), (_, s) in zip(group, slots):
            w.update(_sub_inputs(kind, idx, s, inp))
        if last:
            w["fin_nw"] = col_layout(inp["final_norm_w"])
        in_maps = []
        for c in range(NCORES):
            m = {"xin": hs[c]}
            m.update(consts[c])
            m.update(w)
            in_maps.append({k: v for k, v in m.items() if k in names})
        res = run_bass_kernel_spmd(nc, in_maps, core_ids=list(range(NCORES)))
        hs = [np.asarray(res.results[c]["out"]) for c in range(NCORES)]
    out = np.empty((2, 4 * OWN, D), np.float32)
    for c in range(NCORES):
        b, p = c // 4, c % 4
        out[b, p * OWN:(p + 1) * OWN] = hs[c][PRE:]
    return out
```

```python
import contextlib
import numpy as np
import concourse.bass as bass
import concourse.mybir as mybir
from concourse.bass_utils import run_bass_kernel_spmd

F32 = mybir.dt.float32
BF16 = mybir.dt.bfloat16
AF = mybir.ActivationFunctionType
ALU = mybir.AluOpType

NCORES = 8
D = 2048
KT = 16
PRE = 16
OWN = 1024
T = PRE + OWN
CH = 104
NCH = T // CH
TTS = [(0, 347), (347, 694), (694, 1040)]
EPS = 1e-6
FFN = 5632
FT = FFN // 128
FG = 4
FGT = FT // FG
DEPTH = 4
WSLOT = 2048
NWSLOT = 4


class Buf:
    __slots__ = ("name", "writer", "readers")

    def __init__(self, name):
        self.name = name
        self.writer = None
        self.readers = {}


class Prog:
    ENGS = ("pe", "act", "dve", "pool", "sp")

    def __init__(self, nc):
        self.nc = nc
        self.ops = {e: [] for e in self.ENGS}
        self.seen = {e: {} for e in self.ENGS}
        self.dma_count = {}
        self.dma_unit = {}

    def op(self, eng, fn, reads=(), writes=(), dma=None, unit=16):
        deps = []
        for b in reads:
            if b.writer is not None:
                deps.append(b.writer)
        for b in writes:
            if b.writer is not None:
                deps.append(b.writer)
            deps.extend(b.readers.values())
        waits = []
        seen = self.seen[eng]
        for d in deps:
            if d[0] == "eng":
                _, e2, idx = d
                if e2 == eng and eng in ("pe", "sp"):
                    continue
                if seen.get(e2, -1) >= idx:
                    continue
                seen[e2] = idx
            else:
                _, key, cnt = d
                k = ("dma", key)
                if seen.get(k, -1) >= cnt:
                    continue
                seen[k] = cnt
            waits.append(d)
        idx = len(self.ops[eng])
        if dma is not None:
            c = self.dma_count.get(dma, 0) + 1
            self.dma_count[dma] = c
            self.dma_unit[dma] = unit
            tok = ("dma", dma, c)
            rkey = ("dma", dma)
        else:
            tok = ("eng", eng, idx)
            rkey = eng
        self.ops[eng].append(dict(fn=fn, waits=waits, tok=tok, inc=False))
        for b in reads:
            b.readers[rkey] = tok
        for b in writes:
            b.writer = tok
            b.readers = {}
        return tok

    def emit(self, final_bufs=()):
        nc = self.nc
        self.op("sp", None, reads=list(final_bufs))
        for e in self.ENGS:
            for o in self.ops[e]:
                for w in o["waits"]:
                    if w[0] == "eng":
                        self.ops[w[1]][w[2]]["inc"] = True
        rank = {}
        for e in self.ENGS:
            r = 0
            rk = []
            for o in self.ops[e]:
                if o["inc"] and o["tok"][0] == "eng":
                    r += 1
                rk.append(r)
            rank[e] = rk
        with contextlib.ExitStack() as st:
            esem = {e: st.enter_context(nc.semaphore("s_" + e)) for e in ("pe", "act", "dve", "pool")}
            dsem = {k: st.enter_context(nc.semaphore("d_%s" % str(k))) for k in self.dma_count}
            block = st.enter_context(nc.Block())
            handles = {"pe": block.tensor, "act": block.scalar, "dve": block.vector,
                       "pool": block.gpsimd, "sp": block.sync}

            def make(e):
                def body(h):
                    for o in self.ops[e]:
                        for w in o["waits"]:
                            if w[0] == "eng":
                                h.wait_ge(esem[w[1]], rank[w[1]][w[2]])
                            else:
                                h.wait_ge(dsem[w[1]], self.dma_unit[w[1]] * w[2])
                        if o["fn"] is None:
                            continue
                        ins = o["fn"](h)
                        tok = o["tok"]
                        if tok[0] == "dma":
                            if self.dma_unit[tok[1]] == 1:
                                ins.then_inc(dsem[tok[1]])
                            else:
                                ins.then_inc(dsem[tok[1]], self.dma_unit[tok[1]])
                        elif o["inc"]:
                            ins.then_inc(esem[e], 1)
                return body
            for e in self.ENGS:
                handles[e](make(e))


NF_ARENA = 9088
NB_ARENA = 28160
RH = 8
QB = 208
NQB = T // QB
GAMMA = [1.0 - 2.0 ** (-5.0 - h) for h in range(RH)]


class Builder:
    def __init__(self, sublayers, final_norm):
        self.sublayers = sublayers
        self.final_norm = final_norm
        self.nc = bass.Bass("TRN2", target_bir_lowering=False)
        self.P = Prog(self.nc)
        self.st = contextlib.ExitStack()
        self.dram = {}
        self.wblocks = []
        self.w_issued = 0
        self.w_next = 0
        self.ring = [0, 1, 2, 3, 4, 5]
        self.ring_i = 0
        self.n_cc = 0
        self.arena_bufs = []

    def din(self, name, shape, dt=F32):
        if name in self.dram:
            return self.dram[name]
        t = self.nc.dram_tensor(name, list(shape), dt, kind="ExternalInput").ap()
        self.dram[name] = t
        return t

    def sb(self, name, shape, dt):
        return self.st.enter_context(self.nc.sbuf_tensor(name, list(shape), dt))

    def set_ring(self, banks):
        self.ring = list(banks)
        self.ring_i = 0

    def next_bank(self):
        b = self.ring[self.ring_i % len(self.ring)]
        self.ring_i += 1
        return b

    def arena_begin(self):
        self.oF = 0
        self.oB = 0
        old = self.arena_bufs
        self.arena_bufs = []
        self._old_arena = old

    def arena_end(self):
        bufs = self._old_arena + self.arena_bufs
        self.P.op("dve", lambda h: h.memset(self.dummy[:], 0.0), writes=bufs + [self.b_dummy])

    def cF(self, n, parts=128):
        ap = self.arenaF[0:parts, self.oF:self.oF + n]
        self.oF += n
        assert self.oF <= NF_ARENA, ("fp32 arena overflow", self.oF)
        return ap

    def cB(self, n, parts=128):
        ap = self.arenaB[0:parts, self.oB:self.oB + n]
        self.oB += n
        assert self.oB <= NB_ARENA, ("bf16 arena overflow", self.oB)
        return ap

    def abuf(self, name):
        b = Buf(name)
        self.arena_bufs.append(b)
        return b

    def wreg(self, view, kt, n):
        assert kt * n <= WSLOT
        self.wblocks.append((view, kt, n))

    def _wissue(self, i):
        view, kt, n = self.wblocks[i]
        slot = i % NWSLOT
        dst = self.wsb[:, slot, 0:kt * n].rearrange("p (k n) -> p k n", n=n)
        self.P.op("pool", lambda h, dst=dst, view=view: h.dma_start(out=dst, in_=view),
                  writes=[self.wbuf[slot]], dma="w%d" % slot)

    def wprefetch(self):
        while self.w_issued < min(len(self.wblocks), self.w_next + NWSLOT):
            self._wissue(self.w_issued)
            self.w_issued += 1

    def wacquire(self, count=1):
        i = self.w_next
        self.w_next += count
        while self.w_issued < min(len(self.wblocks), i + NWSLOT):
            self._wissue(self.w_issued)
            self.w_issued += 1
        res = []
        for ii in range(i, i + count):
            view, kt, n = self.wblocks[ii]
            slot = ii % NWSLOT
            res.append((self.wsb[:, slot, 0:kt * n].rearrange("p (k n) -> p k n", n=n), self.wbuf[slot]))
        return res[0] if count == 1 else res

    def build(self):
        nc, P = self.nc, self.P
        self.xin = self.din("xin", [T, D])
        self.c_ident = self.din("c_ident", [128, 128])
        self.c_premask = self.din("c_premask", [128, PRE])
        self.c_halosel = self.din("c_halosel", [128, 4])
        self.out = nc.dram_tensor("out", [T, D], F32, kind="ExternalOutput").ap()
        for kind, idx in self.sublayers:
            getattr(self, kind + "_inputs")(idx)
        if self.final_norm:
            self.din("fin_nw", [128, KT])

        self.hT = self.sb("hT", [128, KT, T], F32)
        self.hnT = self.sb("hnT", [128, KT, T], BF16)
        self.wsb = self.sb("wsb", [128, NWSLOT, WSLOT], BF16)
        self.ident = self.sb("ident", [128, 128], F32)
        self.identb = self.sb("identb", [128, 128], BF16)
        self.ones_bf = self.sb("ones_bf", [128, 128], BF16)
        self.premask = self.sb("premask", [128, PRE], F32)
        self.halosel = self.sb("halosel", [128, 4], F32)
        self.nwcol = self.sb("nwcol", [128, KT], F32)
        self.hg = self.sb("hg", [128, 4, KT * 3], BF16)
        self.halo = self.sb("halo", [128, KT * 3], F32)
        self.dummy = self.sb("dummy_t", [128, 2], F32)
        self.arenaF = self.sb("arenaF", [128, NF_ARENA], F32)
        self.arenaB = self.sb("arenaB", [128, NB_ARENA], BF16)
        self.ps = self.st.enter_context(nc.psum_tensor("ps", [128, 6, 512], F32))
        self.psb = self.st.enter_context(nc.psum_tensor("psb", [128, 2, 1024], BF16))

        self.b_hT = [Buf("hT%d" % k) for k in range(KT)]
        self.b_hn = Buf("hn")
        self.wbuf = [Buf("w%d" % s) for s in range(NWSLOT)]
        self.b_const = Buf("const")
        self.b_nw = Buf("nw")
        self.b_ps = [Buf("ps%d" % i) for i in range(6)]
        self.b_psb = [Buf("psb0"), Buf("psb1")]
        self.b_hg = Buf("hg")
        self.b_halo = Buf("halo")
        self.b_out = Buf("out")
        self.b_dummy = Buf("dummy")

        for kind, idx in self.sublayers:
            getattr(self, kind + "_wreg")(idx)

        P.op("sp", lambda h: h.dma_start(out=self.ident[:], in_=self.c_ident[:, :]), writes=[self.b_const], dma="c")
        P.op("sp", lambda h: h.dma_start(out=self.premask[:], in_=self.c_premask[:, :]), writes=[self.b_const], dma="c")
        P.op("sp", lambda h: h.dma_start(out=self.halosel[:], in_=self.c_halosel[:, :]), writes=[self.b_const], dma="c")
        P.op("dve", lambda h: h.memset(self.ones_bf[:], 1.0), writes=[self.b_const])
        P.op("dve", lambda h: h.tensor_copy(out=self.identb[:], in_=self.ident[:]), reads=[self.b_const], writes=[self.b_const])

        self.load_input()
        for kind, idx in self.sublayers:
            getattr(self, kind)(idx)
        self.store_output()
        P.emit(final_bufs=[self.b_out])
        self.st.close()
        return nc

    def io_arena(self):
        self.arena_begin()
        self.stage = self.cF(D, parts=CH)
        self.b_stage = self.abuf("stage")
        self.rstd = self.cF(T)
        self.b_rstd = self.abuf("rstd")
        self.sq = self.cB(T)
        self.b_sq = self.abuf("sq")
        self.arena_end()
        self.set_ring([0, 1, 2, 3, 4, 5])

    def load_input(self):
        P = self.P
        self.io_arena()
        for c in range(NCH):
            P.op("sp", lambda h, c=c: h.dma_start(out=self.stage, in_=self.xin[c * CH:(c + 1) * CH, :]),
                 writes=[self.b_stage], dma="st")
            for q in range(4):
                bank = self.next_bank()

                def tr(h, c=c, q=q, bank=bank):
                    ins = None
                    for i in range(4):
                        k = 4 * q + i
                        ins = h.transpose(self.ps[:, bank, i * CH:(i + 1) * CH],
                                          self.stage[:, k * 128:(k + 1) * 128], self.ident[0:CH, 0:CH])
                    return ins
                P.op("pe", tr, reads=[self.b_stage, self.b_const], writes=[self.b_ps[bank]])
                src = self.ps[:, bank, 0:4 * CH].rearrange("p (i t) -> p i t", t=CH)
                dst = self.hT[:, 4 * q:4 * q + 4, c * CH:(c + 1) * CH]
                if q % 2 == 0:
                    P.op("act", lambda h, src=src, dst=dst: h.activation(out=dst, in_=src, func=AF.Copy),
                         reads=[self.b_ps[bank]], writes=self.b_hT[4 * q:4 * q + 4])
                else:
                    P.op("dve", lambda h, src=src, dst=dst: h.tensor_copy(out=dst, in_=src),
                         reads=[self.b_ps[bank]], writes=self.b_hT[4 * q:4 * q + 4])

    def store_output(self):
        P = self.P
        self.io_arena()
        if self.final_norm:
            self.rmsnorm(self.dram["fin_nw"], to_hn=False)
        for c in range(NCH):
            for q in range(4):
                bank = self.next_bank()

                def tr(h, c=c, q=q, bank=bank):
                    ins = None
                    for i in range(4):
                        k = 4 * q + i
                        ins = h.transpose(self.ps[0:CH, bank, i * 128:(i + 1) * 128],
                                          self.hT[:, k, c * CH:(c + 1) * CH], self.ident[:, :])
                    return ins
                P.op("pe", tr, reads=self.b_hT[4 * q:4 * q + 4] + [self.b_const], writes=[self.b_ps[bank]])
                src = self.ps[0:CH, bank, :]
                dst = self.stage[:, q * 512:(q + 1) * 512]
                if q % 2 == 0:
                    P.op("act", lambda h, src=src, dst=dst: h.activation(out=dst, in_=src, func=AF.Copy),
                         reads=[self.b_ps[bank]], writes=[self.b_stage])
                else:
                    P.op("dve", lambda h, src=src, dst=dst: h.tensor_copy(out=dst, in_=src),
                         reads=[self.b_ps[bank]], writes=[self.b_stage])
            P.op("sp", lambda h, c=c: h.dma_start(out=self.out[c * CH:(c + 1) * CH, :], in_=self.stage),
                 reads=[self.b_stage], writes=[self.b_out], dma="o")

    def rmsnorm(self, nw_dram, to_hn=True):
        P = self.P
        P.op("sp", lambda h: h.dma_start(out=self.nwcol[:], in_=nw_dram[:, :]), writes=[self.b_nw], dma="nw")
        P.op("dve", lambda h: h.tensor_scalar_mul(out=self.nwcol[:], in0=self.nwcol[:], scalar1=float(np.sqrt(D))),
             reads=[self.b_nw], writes=[self.b_nw])
        banks = [self.next_bank() for _ in TTS]
        for k in range(KT):
            P.op("act", lambda h, k=k: h.activation(out=self.sq, in_=self.hT[:, k, :], func=AF.Square),
                 reads=[self.b_hT[k]], writes=[self.b_sq])

            def mm(h, k=k):
                ins = None
                for ti, (a, b) in enumerate(TTS):
                    ins = h.matmul(self.ps[:, banks[ti], 0:b - a], lhsT=self.ones_bf[:, :], rhs=self.sq[:, a:b],
                                   start=(k == 0), stop=(k == KT - 1))
                return ins
            P.op("pe", mm, reads=[self.b_sq, self.b_const], writes=[self.b_ps[b] for b in banks])
        for ti, (a, b) in enumerate(TTS):
            P.op("act", lambda h, ti=ti, a=a, b=b: h.activation(
                out=self.rstd[:, a:b], in_=self.ps[:, banks[ti], 0:b - a], func=AF.Sqrt, bias=float(D * EPS), scale=1.0),
                reads=[self.b_ps[banks[ti]]], writes=[self.b_rstd])
        P.op("dve", lambda h: h.reciprocal(out=self.rstd, in_=self.rstd), reads=[self.b_rstd], writes=[self.b_rstd])
        for k in range(KT):
            dst = self.hnT[:, k, :] if to_hn else self.hT[:, k, :]
            P.op("dve", lambda h, k=k, dst=dst: h.scalar_tensor_tensor(
                out=dst, in0=self.hT[:, k, :], scalar=self.nwcol[:, k:k + 1], in1=self.rstd,
                op0=ALU.mult, op1=ALU.mult), reads=[self.b_hT[k], self.b_nw, self.b_rstd],
                writes=[self.b_hn if to_hn else self.b_hT[k]])

    def halo_exchange(self):
        nc, P = self.nc, self.P
        i = self.n_cc
        self.n_cc += 1
        bounce = nc.dram_tensor("cc_b%d" % i, [128, KT * 3], BF16)
        gathered = nc.dram_tensor("cc_g%d" % i, [4 * 128, KT * 3], BF16)
        b_b, b_g = Buf("ccb%d" % i), Buf("ccg%d" % i)
        P.op("sp", lambda h: h.dma_start(out=bounce.ap().rearrange("p (k t) -> p k t", t=3), in_=self.hnT[:, :, T - 3:T]),
             reads=[self.b_hn], writes=[b_b], dma="cc_in")
        self.wprefetch()
        P.op("pool", lambda h: h.collective_compute("AllGather", ALU.bypass, replica_groups=[[0, 1, 2, 3], [4, 5, 6, 7]],
                                                    ins=[bounce.ap().opt()], outs=[gathered.ap().opt()]),
             reads=[b_b], writes=[b_g], dma="cc", unit=1)
        P.op("sp", lambda h: h.dma_start(out=self.hg[:], in_=gathered.ap().rearrange("(r p) f -> p r f", p=128)),
             reads=[b_g], writes=[self.b_hg], dma="cc_out")
        P.op("dve", lambda h: h.tensor_scalar_mul(out=self.halo[:], in0=self.hg[:, 0, :], scalar1=self.halosel[:, 0:1]),
             reads=[self.b_hg, self.b_const], writes=[self.b_halo])
        for r in range(1, 4):
            P.op("dve", lambda h, r=r: h.scalar_tensor_tensor(
                out=self.halo[:], in0=self.hg[:, r, :], scalar=self.halosel[:, r:r + 1], in1=self.halo[:],
                op0=ALU.mult, op1=ALU.add), reads=[self.b_hg, self.b_halo], writes=[self.b_halo])
        P.op("dve", lambda h: h.tensor_tensor(out=self.hnT[:, :, PRE - 3:PRE], in0=self.hnT[:, :, PRE - 3:PRE],
                                              in1=self.halo[:].rearrange("p (k t) -> p k t", t=3), op=ALU.add),
             reads=[self.b_hn, self.b_halo], writes=[self.b_hn])

    def mask_pre(self):
        self.P.op("dve", lambda h: h.tensor_tensor(
            out=self.hT[:, :, 0:PRE], in0=self.hT[:, :, 0:PRE],
            in1=self.premask[:].unsqueeze(1).to_broadcast([128, KT, PRE]), op=ALU.mult),
            reads=self.b_hT + [self.b_const], writes=self.b_hT)

    def proj(self, w, wb, kt, c0, rhsT, rhs_bufs, a, b, bank):
        def mm(h):
            ins = None
            for k in range(kt):
                ins = h.matmul(self.ps[:, bank, 0:b - a], lhsT=w[:, k, c0:c0 + 128], rhs=rhsT[:, k, a:b],
                               start=(k == 0), stop=(k == kt - 1))
            return ins
        self.P.op("pe", mm, reads=[wb] + list(rhs_bufs), writes=[self.b_ps[bank]])

    def resid_add(self, n, a, b, bank):
        self.P.op("dve", lambda h: h.tensor_tensor(
            out=self.hT[:, n, a:b], in0=self.hT[:, n, a:b], in1=self.ps[:, bank, 0:b - a], op=ALU.add),
            reads=[self.b_hT[n], self.b_ps[bank]], writes=[self.b_hT[n]])

    def ffn_inputs(self, idx):
        self.din("ffn_nw%d" % idx, [128, KT])
        self.din("ffn_wup%d" % idx, [D, 2 * FFN])
        self.din("ffn_cw%d" % idx, [128, FT, 3])
        self.din("ffn_cb%d" % idx, [128, FT])
        self.din("ffn_wdn%d" % idx, [FFN, D])

    def ffn_wreg(self, idx):
        wup = self.dram["ffn_wup%d" % idx]
        wdn = self.dram["ffn_wdn%d" % idx]
        for g in range(FG):
            for jj in range(FGT):
                j = g * FGT + jj
                self.wreg(wup[:, j * 128:(j + 1) * 128].rearrange("(k p) n -> p k n", p=128), KT, 128)
                self.wreg(wup[:, FFN + j * 128:FFN + (j + 1) * 128].rearrange("(k p) n -> p k n", p=128), KT, 128)
            for nb in range(16):
                self.wreg(wdn[g * FGT * 128:(g + 1) * FGT * 128, nb * 128:(nb + 1) * 128].rearrange(
                    "(k p) n -> p k n", p=128), FGT, 128)

    def ffn(self, idx):
        P = self.P
        self.arena_begin()
        self.rstd = self.cF(T)
        self.b_rstd = self.abuf("rstd")
        self.sq = self.cB(T)
        self.b_sq = self.abuf("sq")
        actT = self.cB(FGT * T).rearrange("p (j t) -> p j t", t=T)
        gpre = self.cF(2 * (T + 2)).rearrange("p (s t) -> p s t", s=2)
        gacc = self.cF(T)
        sg = self.cF(T)
        cw = self.cF(FT * 3).rearrange("p (j w) -> p j w", w=3)
        cb = self.cF(FT)
        b_act, b_gacc, b_sg, b_cw = self.abuf("actT"), self.abuf("gacc"), self.abuf("sg"), self.abuf("cw")
        b_gpre = [self.abuf("gpre0"), self.abuf("gpre1")]
        self.arena_end()
        self.set_ring([0, 1, 2, 3, 4, 5])

        P.op("dve", lambda h: h.memset(gpre[:, :, 0:2], 0.0), writes=b_gpre)
        P.op("sp", lambda h: h.dma_start(out=cw, in_=self.dram["ffn_cw%d" % idx][:, :, :]), writes=[b_cw], dma="cw")
        P.op("sp", lambda h: h.dma_start(out=cb, in_=self.dram["ffn_cb%d" % idx][:, :]), writes=[b_cw], dma="cw")
        self.rmsnorm(self.dram["ffn_nw%d" % idx])
        self.halo_exchange()
        for g in range(FG):
            for jj in range(FGT):
                j = g * FGT + jj
                s = j % 2
                wg, wgb = self.wacquire()
                gb = [self.next_bank() for _ in TTS]
                for ti, (a, b) in enumerate(TTS):
                    self.proj(wg, wgb, KT, 0, self.hnT, [self.b_hn], a, b, gb[ti])
                    P.op("act", lambda h, a=a, b=b, s=s, bank=gb[ti]: h.activation(
                        out=gpre[:, s, 2 + a:2 + b], in_=self.ps[:, bank, 0:b - a], func=AF.Copy),
                        reads=[self.b_ps[gb[ti]]], writes=[b_gpre[s]])
                wu, wub = self.wacquire()
                ub = [self.next_bank() for _ in TTS]
                for ti, (a, b) in enumerate(TTS):
                    self.proj(wu, wub, KT, 0, self.hnT, [self.b_hn], a, b, ub[ti])
                P.op("dve", lambda h, j=j, s=s: h.tensor_scalar(
                    out=gacc, in0=gpre[:, s, 2:T + 2], scalar1=cw[:, j, 2:3], scalar2=cb[:, j:j + 1],
                    op0=ALU.mult, op1=ALU.add), reads=[b_gpre[s], b_cw, b_sg], writes=[b_gacc])
                P.op("dve", lambda h, j=j, s=s: h.scalar_tensor_tensor(
                    out=gacc, in0=gpre[:, s, 1:T + 1], scalar=cw[:, j, 1:2], in1=gacc, op0=ALU.mult, op1=ALU.add),
                    reads=[b_gpre[s], b_cw, b_gacc], writes=[b_gacc])
                P.op("dve", lambda h, j=j, s=s: h.scalar_tensor_tensor(
                    out=gacc, in0=gpre[:, s, 0:T], scalar=cw[:, j, 0:1], in1=gacc, op0=ALU.mult, op1=ALU.add),
                    reads=[b_gpre[s], b_cw, b_gacc], writes=[b_gacc])
                P.op("act", lambda h: h.activation(out=sg, in_=gacc, func=AF.Silu), reads=[b_gacc], writes=[b_sg])
                for ti, (a, b) in enumerate(TTS):
                    P.op("dve", lambda h, a=a, b=b, jj=jj, bank=ub[ti]: h.tensor_tensor(
                        out=actT[:, jj, a:b], in0=sg[:, a:b], in1=self.ps[:, bank, 0:b - a], op=ALU.mult),
                        reads=[b_sg, self.b_ps[ub[ti]]], writes=[b_act])
            for n in range(16):
                wd, wdb = self.wacquire()
                for ti, (a, b) in enumerate(TTS):
                    bank = self.next_bank()
                    self.proj(wd, wdb, FGT, 0, actT, [b_act], a, b, bank)
                    self.resid_add(n, a, b, bank)
        self.mask_pre()

    def ret_inputs(self, j):
        self.din("ret_nw%d" % j, [128, KT])
        self.din("ret_win%d" % j, [D, 12288])
        self.din("ret_gn%d" % j, [128, 32])
        self.din("ret_wout%d" % j, [4096, D])
        self.din("c_cos", [128, T], BF16)
        self.din("c_sin", [128, T], BF16)
        self.din("c_retG", [RH, CH, T], BF16)
        self.din("c_retQ", [128, RH * QB], BF16)
        self.din("c_retK", [CH, RH * NCH])
        self.din("c_retcoef", [128, 4 * RH])

    def ret_wreg(self, j):
        win = self.dram["ret_win%d" % j]
        wout = self.dram["ret_wout%d" % j]
        for h in range(RH):
            def blk(c0):
                self.wreg(win[:, c0:c0 + 128].rearrange("(k p) n -> p k n", p=128), KT, 128)
                self.wreg(win[:, c0 + 128:c0 + 256].rearrange("(k p) n -> p k n", p=128), KT, 128)
            blk(2048 + h * 256)
            blk(h * 256)
            blk(4096 + h * 512)
            blk(4096 + h * 512 + 256)
            blk(8192 + h * 512)
            blk(8192 + h * 512 + 256)
            for nb in range(4):
                self.wreg(wout[h * 512:(h + 1) * 512, nb * 512:(nb + 1) * 512].rearrange(
                    "(k p) n -> p k n", p=128), 4, 512)

    def ret(self, j):
        nc, P = self.nc, self.P
        self.arena_begin()
        self.rstd = self.cF(T)
        self.b_rstd = self.abuf("rstd")
        self.sq = self.cB(T)
        self.b_sq = self.abuf("sq")
        cosT, sinT = self.cB(T), self.cB(T)
        G = self.cB(T, parts=CH)
        qdecB = self.cB(RH * QB).rearrange("p (h l) -> p h l", l=QB)
        kfac = self.cF(RH * NCH, parts=CH)
        coef = self.cF(4 * RH)
        gnw = self.cF(32)
        kT = self.cB(2 * T).rearrange("p (d t) -> p d t", t=T)
        qT = self.cB(2 * T).rearrange("p (d t) -> p d t", t=T)
        qd = self.cB(2 * 2 * QB).rearrange("p (s d l) -> p s d l", s=2, d=2)
        vtmp = self.cB(T)
        v_tok = self.cB(NCH * 512, parts=CH).rearrange("p (c e) -> p c e", e=512)
        k_tok = self.cB(NCH * 256, parts=CH).rearrange("p (c d) -> p c d", d=256)
        yT = self.cB(4 * T).rearrange("p (e t) -> p e t", t=T)
        sqo = self.cB(4 * QB).rearrange("p (e l) -> p e l", l=QB)
        PT = self.cB(2 * QB, parts=CH).rearrange("p (s l) -> p s l", s=2)
        SinB = self.cB(1024).rearrange("p (d e) -> p d e", e=512)
        oT = self.cF(4 * QB).rearrange("p (e l) -> p e l", l=QB)
        rt = self.cF(2 * 347).rearrange("p (s l) -> p s l", s=2)
        sloc = self.cF(1024).rearrange("p (d e) -> p d e", e=512)
        sgath = self.cF(1024).rearrange("p (d e) -> p d e", e=512)
        SinF = self.cF(1024).rearrange("p (d e) -> p d e", e=512)
        rs = self.cF(QB)
        b_tab, b_G, b_kT, b_qT, b_vtmp = self.abuf("tab"), self.abuf("G"), self.abuf("kT"), self.abuf("qT"), self.abuf("vtmp")
        b_qd = [self.abuf("qd0"), self.abuf("qd1")]
        b_vtok, b_ktok, b_yT, b_sqo = self.abuf("vtok"), self.abuf("ktok"), self.abuf("yT"), self.abuf("sqo")
        b_PT = [self.abuf("PT0"), self.abuf("PT1")]
        b_SinB, b_oT, b_rt, b_sloc, b_sgath, b_SinF, b_rs = (self.abuf(n) for n in
                                                             ("SinB", "oT", "rt", "sloc", "sgath", "SinF", "rs"))
        self.arena_end()

        dr = self.dram
        for dst, src in ((cosT, dr["c_cos"][:, :]), (sinT, dr["c_sin"][:, :]),
                         (qdecB, dr["c_retQ"].rearrange("p (h l) -> p h l", l=QB)),
                         (kfac, dr["c_retK"][:, :]), (coef, dr["c_retcoef"][:, :]), (gnw, dr["ret_gn%d" % j][:, :])):
            P.op("sp", lambda h, dst=dst, src=src: h.dma_start(out=dst, in_=src), writes=[b_tab], dma="tab")
        self.set_ring([0, 1, 2, 3, 4, 5])
        self.rmsnorm(dr["ret_nw%d" % j])

        for hd in range(RH):
            gam = GAMMA[hd]
            P.op("sp", lambda h, hd=hd: h.dma_start(out=G, in_=dr["c_retG"][hd, :, :]), writes=[b_G], dma="G")
            self.set_ring([0, 1, 2, 3, 4, 5])
            for dstT, b_dst in ((kT, b_kT), (qT, b_qT)):
                (w, wb), (w2, wb2) = self.wacquire(2)
                for ti, (a, b) in enumerate(TTS):
                    b1, b2 = self.next_bank(), self.next_bank()
                    self.proj(w, wb, KT, 0, self.hnT, [self.b_hn], a, b, b1)
                    self.proj(w2, wb2, KT, 0, self.hnT, [self.b_hn], a, b, b2)
                    n = b - a
                    x1, x2 = self.ps[:, b1, 0:n], self.ps[:, b2, 0:n]
                    t1, t2 = rt[:, 0, 0:n], rt[:, 1, 0:n]
                    seq = [(t1, x1, cosT[:, a:b], None), (t2, x2, sinT[:, a:b], None), (dstT[:, 0, a:b], t1, t2, ALU.subtract),
                           (t1, x1, sinT[:, a:b], None), (t2, x2, cosT[:, a:b], None), (dstT[:, 1, a:b], t1, t2, ALU.add)]
                    for si, (o_, i0, i1, op_) in enumerate(seq):
                        fin = op_ is not None
                        rd = [b_rt] if fin else [self.b_ps[b1 if i0 is x1 else b2], b_tab]
                        P.op("dve", lambda h, o_=o_, i0=i0, i1=i1, op_=op_: h.tensor_tensor(
                            out=o_, in0=i0, in1=i1, op=(op_ or ALU.mult)), reads=rd + ([b_rt] if not fin else []),
                            writes=[b_dst] if fin else [b_rt])
            for half in range(2):
                for q in range(2):
                    w, wb = self.wacquire()
                    e = half * 2 + q
                    for ti, (a, b) in enumerate(TTS):
                        bank = self.next_bank()
                        self.proj(w, wb, KT, 0, self.hnT, [self.b_hn], a, b, bank)
                        P.op("act", lambda h, a=a, b=b, bank=bank: h.activation(
                            out=vtmp[:, a:b], in_=self.ps[:, bank, 0:b - a], func=AF.Copy),
                            reads=[self.b_ps[bank]], writes=[b_vtmp])
                    self.tok_transposes(vtmp, b_vtmp, v_tok, b_vtok, e * 128, None, None, b_tab)
            for dt in range(2):
                self.tok_transposes(kT[:, dt, :], b_kT, k_tok, b_ktok, dt * 128, kfac, hd, b_tab)
            sb_ = [0, 1]
            for dt in range(2):
                def mm(h, dt=dt):
                    ins = None
                    for c in range(NCH):
                        ins = h.matmul(self.ps[:, sb_[dt], :], lhsT=k_tok[:, c, dt * 128:(dt + 1) * 128], rhs=v_tok[:, c, :],
                                       start=(c == 0), stop=(c == NCH - 1))
                    return ins
                P.op("pe", mm, reads=[b_ktok, b_vtok], writes=[self.b_ps[sb_[dt]]])
                P.op("act", lambda h, dt=dt: h.activation(out=sloc[:, dt, :], in_=self.ps[:, sb_[dt], :], func=AF.Copy),
                     reads=[self.b_ps[sb_[dt]]], writes=[b_sloc])
            i = self.n_cc
            self.n_cc += 1
            bounce = nc.dram_tensor("cc_b%d" % i, [256, 512], F32)
            gathered = nc.dram_tensor("cc_g%d" % i, [4 * 256, 512], F32)
            b_b, b_g = Buf("ccb%d" % i), Buf("ccg%d" % i)
            P.op("sp", lambda h, bounce=bounce: h.dma_start(out=bounce.ap().rearrange("(d p) e -> p d e", p=128), in_=sloc),
                 reads=[b_sloc], writes=[b_b], dma="cc_in")
            self.wprefetch()
            P.op("pool", lambda h, bounce=bounce, gathered=gathered: h.collective_compute(
                "AllGather", ALU.bypass, replica_groups=[[0, 1, 2, 3], [4, 5, 6, 7]],
                ins=[bounce.ap().opt()], outs=[gathered.ap().opt()]), reads=[b_b], writes=[b_g], dma="cc", unit=1)
            self.set_ring([2, 3, 4, 5])
            for half in range(2):
                for q in range(2):
                    w, wb = self.wacquire()
                    e = half * 2 + q
                    for ti, (a, b) in enumerate(TTS):
                        bank = self.next_bank()
                        self.proj(w, wb, KT, 0, self.hnT, [self.b_hn], a, b, bank)
                        P.op("act", lambda h, e=e, a=a, b=b, bank=bank: h.activation(
                            out=yT[:, e, a:b], in_=self.ps[:, bank, 0:b - a], func=AF.Silu),
                            reads=[self.b_ps[bank]], writes=[b_yT])
                    P.op("dve", lambda h, e=e, hd=hd: h.tensor_scalar_mul(
                        out=yT[:, e, :], in0=yT[:, e, :], scalar1=gnw[:, hd * 4 + e:hd * 4 + e + 1]),
                        reads=[b_yT, b_tab], writes=[b_yT])
            for r in range(4):
                P.op("sp", lambda h, r=r, gathered=gathered: h.dma_start(
                    out=sgath, in_=gathered.ap()[r * 256:(r + 1) * 256, :].rearrange("(d p) e -> p d e", p=128)),
                    reads=[b_g], writes=[b_sgath], dma="sg")
                cf = coef[:, r * RH + hd:r * RH + hd + 1]
                if r == 0:
                    P.op("dve", lambda h, cf=cf: h.tensor_scalar_mul(out=SinF, in0=sgath, scalar1=cf),
                         reads=[b_sgath, b_tab], writes=[b_SinF])
                else:
                    P.op("dve", lambda h, cf=cf: h.scalar_tensor_tensor(out=SinF, in0=sgath, scalar=cf, in1=SinF,
                                                                        op0=ALU.mult, op1=ALU.add),
                         reads=[b_sgath, b_tab, b_SinF], writes=[b_SinF])
            P.op("act", lambda h: h.activation(out=SinB, in_=SinF, func=AF.Copy), reads=[b_SinF], writes=[b_SinB])
            for qb in range(NQB):
                s = qb % 2
                q0b = qb * QB
                oa = [0, 1] if s == 0 else [2, 3]
                for dt in range(2):
                    P.op("dve", lambda h, dt=dt, s=s, q0b=q0b, qb=qb, hd=hd, gam=gam: h.scalar_tensor_tensor(
                        out=qd[:, s, dt, :], in0=qT[:, dt, q0b:q0b + QB], scalar=float(gam ** (QB * qb)),
                        in1=qdecB[:, hd, :], op0=ALU.mult, op1=ALU.mult),
                        reads=[b_qT, b_tab], writes=[b_qd[s]])
                nkc = 2 * qb + 2

                def emit_sc(kc):
                    k0 = kc * CH
                    q0 = max(q0b, k0)
                    n = q0b + QB - q0
                    sbank = 4 + (kc % 2)

                    def sc(h, k0=k0, q0=q0, n=n, sbank=sbank):
                        ins = None
                        for dt in range(2):
                            ins = h.matmul(self.ps[0:CH, sbank, 0:n], lhsT=kT[:, dt, k0:k0 + CH], rhs=qT[:, dt, q0:q0 + n],
                                           start=(dt == 0), stop=(dt == 1))
                        return ins
                    P.op("pe", sc, reads=[b_kT, b_qT], writes=[self.b_ps[sbank]])

                def emit_pv(kc):
                    k0 = kc * CH
                    q0 = max(q0b, k0)
                    n = q0b + QB - q0
                    sbank = 4 + (kc % 2)
                    ps_ = kc % 2
                    P.op("dve", lambda h, n=n, sbank=sbank, ps_=ps_, q0=q0, k0=k0: h.tensor_tensor(
                        out=PT[:, ps_, 0:n], in0=self.ps[0:CH, sbank, 0:n], in1=G[:, q0 - k0:q0 - k0 + n], op=ALU.mult),
                        reads=[self.b_ps[sbank], b_G], writes=[b_PT[ps_]])

                    def pv(h, kc=kc, q0=q0, n=n, ps_=ps_, oa=oa, q0b=q0b):
                        ins = None
                        off = q0 - q0b
                        for e in range(4):
                            ins = h.matmul(self.ps[:, oa[e // 2], (e % 2) * QB + off:(e % 2) * QB + off + n],
                                           lhsT=v_tok[:, kc, e * 128:(e + 1) * 128], rhs=PT[:, ps_, 0:n],
                                           start=(kc == 0 and e % 2 == 0), stop=False, skip_group_check=True)
                        return ins
                    P.op("pe", pv, reads=[b_vtok, b_PT[ps_]], writes=[self.b_ps[oa[0]], self.b_ps[oa[1]]])
                emit_sc(0)
                for kc in range(nkc):
                    if kc + 1 < nkc:
                        emit_sc(kc + 1)
                    emit_pv(kc)

                def corr(h, s=s, oa=oa):
                    ins = None
                    for e in range(4):
                        for dt in range(2):
                            ins = h.matmul(self.ps[:, oa[e // 2], (e % 2) * QB:(e % 2) * QB + QB],
                                           lhsT=SinB[:, dt, e * 128:(e + 1) * 128], rhs=qd[:, s, dt, :],
                                           start=False, stop=(dt == 1), skip_group_check=True)
                    return ins
                P.op("pe", corr, reads=[b_SinB, b_qd[s]], writes=[self.b_ps[oa[0]], self.b_ps[oa[1]]])
                for i2 in range(2):
                    P.op("act", lambda h, i2=i2, oa=oa: h.activation(
                        out=oT[:, 2 * i2:2 * i2 + 2, :], in_=self.ps[:, oa[i2], 0:2 * QB].rearrange("p (e l) -> p e l", l=QB),
                        func=AF.Copy), reads=[self.b_ps[oa[i2]]], writes=[b_oT])
                P.op("act", lambda h: h.activation(out=sqo, in_=oT, func=AF.Square), reads=[b_oT], writes=[b_sqo])
                nbank = 4 + (qb % 2)

                def nsum(h, nbank=nbank):
                    ins = None
                    for e in range(4):
                        ins = h.matmul(self.ps[:, nbank, 0:QB], lhsT=self.ones_bf[:, :], rhs=sqo[:, e, :],
                                       start=(e == 0), stop=(e == 3))
                    return ins
                P.op("pe", nsum, reads=[b_sqo, self.b_const], writes=[self.b_ps[nbank]])
                P.op("act", lambda h, nbank=nbank: h.activation(out=rs, in_=self.ps[:, nbank, 0:QB], func=AF.Sqrt,
                                                                bias=float(EPS), scale=1.0 / 512.0),
                     reads=[self.b_ps[nbank]], writes=[b_rs])
                P.op("dve", lambda h: h.reciprocal(out=rs, in_=rs), reads=[b_rs], writes=[b_rs])
                P.op("dve", lambda h: h.tensor_tensor(out=oT, in0=oT, in1=rs.unsqueeze(1).to_broadcast([128, 4, QB]),
                                                      op=ALU.mult), reads=[b_oT, b_rs], writes=[b_oT])
                P.op("dve", lambda h, q0b=q0b: h.tensor_tensor(out=yT[:, :, q0b:q0b + QB], in0=yT[:, :, q0b:q0b + QB],
                                                               in1=oT, op=ALU.mult), reads=[b_oT, b_yT], writes=[b_yT])
            self.set_ring([0, 1, 2, 3, 4, 5])
            for nb in range(4):
                w, wb = self.wacquire()
                for q in range(4):
                    n_ = nb * 4 + q
                    for ti, (a, b) in enumerate(TTS):
                        bank = self.next_bank()
                        self.proj(w, wb, 4, q * 128, yT, [b_yT], a, b, bank)
                        self.resid_add(n_, a, b, bank)
        if getattr(self, "debug", False):
            for name, ap, shape, dt, bufs in (("dbg_kT", kT, [128, 2, T], BF16, [b_kT]), ("dbg_qT", qT, [128, 2, T], BF16, [b_qT]),
                                              ("dbg_vtok", v_tok, [CH, NCH, 512], BF16, [b_vtok]),
                                              ("dbg_ktok", k_tok, [CH, NCH, 256], BF16, [b_ktok]),
                                              ("dbg_yT", yT, [128, 4, T], BF16, [b_yT]),
                                              ("dbg_SinF", SinF, [128, 2, 512], F32, [b_SinF]),
                                              ("dbg_hn", self.hnT[:, :, :], [128, KT, T], BF16, [self.b_hn])):
                o_ = nc.dram_tensor(name, shape, dt, kind="ExternalOutput").ap()
                P.op("sp", lambda h, o_=o_, ap=ap: h.dma_start(out=o_, in_=ap), reads=bufs, writes=[self.b_out], dma="dbg")
        self.mask_pre()

    def ssd_inputs(self, j):
        self.din("ssd_nw%d" % j, [128, KT])
        self.din("ssd_win%d" % j, [D, 10304])
        self.din("ssd_cw%d" % j, [128, 48 * 4])
        self.din("ssd_cb%d" % j, [128, 48])
        self.din("ssd_dtb%d" % j, [8, 8])
        self.din("ssd_alog%d" % j, [8, 8])
        self.din("ssd_dcol%d" % j, [128, 32])
        self.din("ssd_gn%d" % j, [128, 32])
        self.din("ssd_wout%d" % j, [4096, D])
        self.din("c_tri", [CH, CH])
        self.din("c_ones", [CH, 128])
        self.din("c_sel", [8, 8 * 128])
        self.din("c_selr", [128, 4])

    def ssd_wreg(self, j):
        win = self.dram["ssd_win%d" % j]
        wout = self.dram["ssd_wout%d" % j]

        def blk(c0, n=128):
            self.wreg(win[:, c0:c0 + n].rearrange("(k p) n -> p k n", p=128), KT, n)
        for g in range(8):
            for q in range(4):
                blk(4096 + g * 512 + q * 128)
            blk(8192 + g * 128)
            blk(9216 + g * 128)
            blk(10240 + g * 8, 8)
            for q in range(4):
                blk(g * 512 + q * 128)
            for nb in range(4):
                self.wreg(wout[g * 512:(g + 1) * 512, nb * 512:(nb + 1) * 512].rearrange(
                    "(k p) n -> p k n", p=128), 4, 512)

    def ssd(self, jl):
        nc, P = self.nc, self.P
        dr = self.dram
        NC8 = NCH * 8
        self.arena_begin()
        self.rstd = self.cF(T)
        self.b_rstd = self.abuf("rstd")
        self.sq = self.cB(T)
        self.b_sq = self.abuf("sq")
        xpre = self.cF(T + 3)
        acc = self.cF(347)
        dda = self.cF(T, parts=8)
        cumT = self.cF(T, parts=8)
        dt_tok = self.cF(NC8, parts=CH)
        da_tok = self.cF(NC8, parts=CH)
        cum_tok = self.cF(NC8, parts=CH)
        negcum = self.cF(NC8, parts=CH)
        tot = self.cF(NC8)
        sfx = self.cF(NC8)
        eend = self.cF(NC8)
        wloc = self.cF(NC8, parts=CH)
        wend = self.cF(NC8, parts=CH)
        tmp80 = self.cF(NC8)
        S = self.cF(512)
        Sg = self.cF(520)
        dmul = self.cF(8)
        tri = self.cF(CH, parts=CH)
        ones = self.cF(128, parts=CH)
        sel = self.cF(8 * 128, parts=8)
        selr = self.cF(4)
        cw = self.cF(48 * 4).rearrange("p (t w) -> p t w", w=4)
        cb = self.cF(48)
        dtb = self.cF(8, parts=8)
        nega = self.cF(8, parts=8)
        dcol = self.cF(32)
        gnw = self.cF(32)
        xs = self.cB(T)
        xDT = self.cB(4 * T).rearrange("p (q t) -> p q t", t=T)
        x_tok = self.cB(NCH * 512, parts=CH).rearrange("p (c f) -> p c f", f=512)
        BT = self.cB(T)
        CT = self.cB(T)
        B_tok = self.cB(NCH * 128, parts=CH).rearrange("p (c n) -> p c n", n=128)
        E = self.cB(8 * CH, parts=CH).rearrange("p (j l) -> p j l", l=CH)
        eR = self.cB(8 * CH).rearrange("p (j l) -> p j l", l=CH)
        CBm = self.cB(CH, parts=CH)
        xdt = self.cB(512, parts=CH)
        xw = self.cB(2 * 512, parts=CH).rearrange("p (s f) -> p s f", s=2)
        S_bf = self.cB(512)
        yT = self.cB(4 * T).rearrange("p (q t) -> p q t", t=T)
        zs = self.cB(4 * T).rearrange("p (q t) -> p q t", t=T)
        names = ("tab", "xpre", "acc", "dda", "cumT", "dttok", "datok", "cumtok", "tot", "wl", "S", "Sg", "dmul", "xs", "xDT",
                 "xtok", "BT", "CT", "Btok", "E", "eR", "CBm", "xdt", "xw0", "xw1", "Sbf", "yT", "zs", "tmp80")
        B_ = {n: self.abuf(n) for n in names}
        self.arena_end()

        loads = ((tri, dr["c_tri"][:, :]), (ones, dr["c_ones"][:, :]), (sel, dr["c_sel"][:, :]), (selr, dr["c_selr"][:, :]),
                 (cw, dr["ssd_cw%d" % jl].rearrange("p (t w) -> p t w", w=4)), (cb, dr["ssd_cb%d" % jl][:, :]),
                 (dtb, dr["ssd_dtb%d" % jl][:, :]), (nega, dr["ssd_alog%d" % jl][:, :]),
                 (dcol, dr["ssd_dcol%d" % jl][:, :]), (gnw, dr["ssd_gn%d" % jl][:, :]))
        for dst, src in loads:
            P.op("sp", lambda h, dst=dst, src=src: h.dma_start(out=dst, in_=src), writes=[B_["tab"]], dma="tab")
        P.op("act", lambda h: h.activation(out=nega, in_=nega, func=AF.Exp), reads=[B_["tab"]], writes=[B_["tab"]])
        P.op("dve", lambda h: h.tensor_scalar_mul(out=nega, in0=nega, scalar1=-1.0), reads=[B_["tab"]], writes=[B_["tab"]])
        P.op("dve", lambda h: h.memset(xpre[:, 0:3], 0.0), writes=[B_["xpre"]])
        self.set_ring([0, 1, 2, 3, 4, 5])
        self.rmsnorm(dr["ssd_nw%d" % jl])
        self.halo_exchange()

        def conv_tile(tile, dst, b_dst, mask_pre, after_proj=None):
            w, wb = self.wacquire()
            for ti, (a, b) in enumerate(TTS):
                bank = self.next_bank()
                self.proj(w, wb, KT, 0, self.hnT, [self.b_hn], a, b, bank)
                P.op("act", lambda h, a=a, b=b, bank=bank: h.activation(
                    out=xpre[:, 3 + a:3 + b], in_=self.ps[:, bank, 0:b - a], func=AF.Copy),
                    reads=[self.b_ps[bank]], writes=[B_["xpre"]])
            if after_proj is not None:
                after_proj()
            for ti, (a, b) in enumerate(TTS):
                n = b - a
                P.op("dve", lambda h, a=a, b=b, n=n: h.tensor_scalar(
                    out=acc[:, 0:n], in0=xpre[:, 3 + a:3 + b], scalar1=cw[:, tile, 3:4], scalar2=cb[:, tile:tile + 1],
                    op0=ALU.mult, op1=ALU.add), reads=[B_["xpre"], B_["tab"]], writes=[B_["acc"]])
                for wi in range(3):
                    P.op("dve", lambda h, a=a, b=b, n=n, wi=wi: h.scalar_tensor_tensor(
                        out=acc[:, 0:n], in0=xpre[:, wi + a:wi + b], scalar=cw[:, tile, wi:wi + 1], in1=acc[:, 0:n],
                        op0=ALU.mult, op1=ALU.add), reads=[B_["xpre"], B_["tab"], B_["acc"]], writes=[B_["acc"]])
                P.op("act", lambda h, a=a, b=b, n=n: h.activation(out=dst[:, a:b], in_=acc[:, 0:n], func=AF.Silu),
                     reads=[B_["acc"]], writes=[b_dst])
            if mask_pre:
                P.op("dve", lambda h: h.tensor_tensor(out=dst[:, 0:PRE], in0=dst[:, 0:PRE], in1=self.premask[:], op=ALU.mult),
                     reads=[b_dst, self.b_const], writes=[b_dst])

        for g in range(8):
            self.set_ring([0, 1, 2, 3, 4, 5])
            pending = None
            for q in range(4):
                conv_tile(g * 4 + q, xs, B_["xs"], True, pending)

                def pending(q=q, g=g):
                    self.tok_transposes(xs, B_["xs"], x_tok, B_["xtok"], q * 128, None, None, B_["tab"])
                    P.op("dve", lambda h, q=q, g=g: h.tensor_scalar_mul(out=xDT[:, q, :], in0=xs,
                                                                        scalar1=dcol[:, g * 4 + q:g * 4 + q + 1]),
                         reads=[B_["xs"], B_["tab"]], writes=[B_["xDT"]])
            conv_tile(32 + g, BT, B_["BT"], False, pending)
            conv_tile(40 + g, CT, B_["CT"], False,
                      lambda: self.tok_transposes(BT, B_["BT"], B_tok, B_["Btok"], 0, None, None, B_["tab"]))
            w, wb = self.wacquire()
            for ti, (a, b) in enumerate(TTS):
                bank = self.next_bank()

                def mm(h, a=a, b=b, bank=bank, w=w):
                    ins = None
                    for k in range(KT):
                        ins = h.matmul(self.ps[0:8, bank, 0:b - a], lhsT=w[:, k, 0:8], rhs=self.hnT[:, k, a:b],
                                       start=(k == 0), stop=(k == KT - 1))
                    return ins
                P.op("pe", mm, reads=[wb, self.b_hn], writes=[self.b_ps[bank]])
                P.op("act", lambda h, a=a, b=b, bank=bank, g=g: h.activation(
                    out=dda[:, a:b], in_=self.ps[0:8, bank, 0:b - a], func=AF.Exp, bias=dtb[:, g:g + 1], scale=1.0),
                    reads=[self.b_ps[bank], B_["tab"]], writes=[B_["dda"]])
            P.op("act", lambda h: h.activation(out=dda, in_=dda, func=AF.Ln, bias=1.0, scale=1.0),
                 reads=[B_["dda"]], writes=[B_["dda"]])

            def tok8(dst, b_dst):
                bank = self.next_bank()

                def tr(h, bank=bank):
                    ins = None
                    for c in range(NCH):
                        ins = h.transpose(self.ps[0:CH, bank, c * 8:(c + 1) * 8], dda[:, c * CH:(c + 1) * CH], self.ident[0:8, 0:8])
                    return ins
                P.op("pe", tr, reads=[B_["dda"], self.b_const], writes=[self.b_ps[bank]])
                P.op("act", lambda h, bank=bank: h.activation(out=dst, in_=self.ps[0:CH, bank, 0:NC8], func=AF.Copy),
                     reads=[self.b_ps[bank]], writes=[b_dst])
            tok8(dt_tok, B_["dttok"])
            P.op("dve", lambda h, g=g: h.tensor_scalar_mul(out=dda, in0=dda, scalar1=nega[:, g:g + 1]),
                 reads=[B_["dda"], B_["tab"]], writes=[B_["dda"]])
            P.op("dve", lambda h: h.tensor_tensor(out=dda[:, 0:PRE], in0=dda[:, 0:PRE], in1=self.premask[0:8, :], op=ALU.mult),
                 reads=[B_["dda"], self.b_const], writes=[B_["dda"]])
            tok8(da_tok, B_["datok"])
            b_cum, b_tot = self.next_bank(), self.next_bank()
            P.op("pe", lambda h, b_cum=b_cum: h.matmul(self.ps[0:CH, b_cum, 0:NC8], lhsT=tri, rhs=da_tok, start=True, stop=True),
                 reads=[B_["datok"], B_["tab"]], writes=[self.b_ps[b_cum]])
            P.op("pe", lambda h, b_tot=b_tot: h.matmul(self.ps[:, b_tot, 0:NC8], lhsT=ones, rhs=da_tok, start=True, stop=True),
                 reads=[B_["datok"], B_["tab"]], writes=[self.b_ps[b_tot]])
            P.op("act", lambda h, b_cum=b_cum: h.activation(out=cum_tok, in_=self.ps[0:CH, b_cum, 0:NC8], func=AF.Copy),
                 reads=[self.b_ps[b_cum]], writes=[B_["cumtok"]])
            P.op("act", lambda h, b_cum=b_cum: h.activation(out=negcum, in_=self.ps[0:CH, b_cum, 0:NC8], func=AF.Copy, scale=-1.0),
                 reads=[self.b_ps[b_cum]], writes=[B_["cumtok"]])
            P.op("act", lambda h, b_tot=b_tot: h.activation(out=tot, in_=self.ps[:, b_tot, 0:NC8], func=AF.Copy),
                 reads=[self.b_ps[b_tot]], writes=[B_["tot"]])
            P.op("act", lambda h, b_tot=b_tot: h.activation(out=eend, in_=self.ps[:, b_tot, 0:NC8], func=AF.Exp),
                 reads=[self.b_ps[b_tot]], writes=[B_["tot"]])
            P.op("dve", lambda h: h.tensor_copy(out=sfx[:, (NCH - 1) * 8:NC8], in_=tot[:, (NCH - 1) * 8:NC8]),
                 reads=[B_["tot"]], writes=[B_["tot"]])
            for c in range(NCH - 2, -1, -1):
                P.op("dve", lambda h, c=c: h.tensor_tensor(out=sfx[:, c * 8:(c + 1) * 8], in0=sfx[:, (c + 1) * 8:(c + 2) * 8],
                                                           in1=tot[:, c * 8:(c + 1) * 8], op=ALU.add),
                     reads=[B_["tot"]], writes=[B_["tot"]])
            for src, dst in ((sfx, wloc), (tot, wend)):
                P.op("dve", lambda h, src=src: h.tensor_tensor(out=tmp80[0:CH, :], in0=src[0:CH, :], in1=cum_tok, op=ALU.subtract),
                     reads=[B_["tot"], B_["cumtok"], B_["tmp80"]], writes=[B_["tmp80"]])
                P.op("act", lambda h: h.activation(out=tmp80[0:CH, :], in_=tmp80[0:CH, :], func=AF.Exp),
                     reads=[B_["tmp80"]], writes=[B_["tmp80"]])
                P.op("dve", lambda h, dst=dst: h.tensor_tensor(out=dst, in0=tmp80[0:CH, :], in1=dt_tok, op=ALU.mult),
                     reads=[B_["tmp80"], B_["dttok"]], writes=[B_["wl"]])
            cb_ = [self.next_bank() for _ in range(3)]
            for c in range(NCH):
                P.op("pe", lambda h, c=c: h.matmul(self.ps[0:8, cb_[c // 4], (c % 4) * CH:(c % 4 + 1) * CH],
                                                   lhsT=da_tok[:, c * 8:(c + 1) * 8], rhs=tri, start=(c % 4 == 0), stop=True,
                                                   skip_group_check=True),
                     reads=[B_["datok"], B_["tab"]], writes=[self.b_ps[cb_[c // 4]]])
            for i3 in range(3):
                nch = min(4, NCH - 4 * i3)
                P.op("act", lambda h, i3=i3, nch=nch: h.activation(
                    out=cumT[:, 4 * i3 * CH:(4 * i3 + nch) * CH], in_=self.ps[0:8, cb_[i3], 0:nch * CH], func=AF.Copy),
                    reads=[self.b_ps[cb_[i3]]], writes=[B_["cumT"]])
            ub = self.next_bank()
            for c in range(NCH):
                s2 = c % 2
                P.op("dve", lambda h, c=c, s2=s2: h.tensor_tensor(
                    out=xw[:, s2, :].rearrange("p (j d) -> p j d", d=64), in0=x_tok[:, c, :].rearrange("p (j d) -> p j d", d=64),
                    in1=wloc[:, c * 8:(c + 1) * 8].unsqueeze(2).to_broadcast([CH, 8, 64]), op=ALU.mult),
                    reads=[B_["xtok"], B_["wl"]], writes=[B_["xw%d" % s2]])
                P.op("pe", lambda h, c=c, s2=s2: h.matmul(self.ps[:, ub, :], lhsT=B_tok[:, c, :], rhs=xw[:, s2, :],
                                                          start=(c == 0), stop=(c == NCH - 1)),
                     reads=[B_["Btok"], B_["xw%d" % s2]], writes=[self.b_ps[ub]])
            P.op("act", lambda h: h.activation(out=Sg[:, 0:512], in_=self.ps[:, ub, :], func=AF.Copy),
                 reads=[self.b_ps[ub]], writes=[B_["Sg"]])
            P.op("act", lambda h: h.activation(out=Sg[:, 512:520], in_=sfx[:, 0:8], func=AF.Exp),
                 reads=[B_["tot"]], writes=[B_["Sg"]])
            i = self.n_cc
            self.n_cc += 1
            bounce = nc.dram_tensor("cc_b%d" % i, [128, 520], F32)
            gathered = nc.dram_tensor("cc_g%d" % i, [4 * 128, 520], F32)
            b_b, b_g = Buf("ccb%d" % i), Buf("ccg%d" % i)
            P.op("sp", lambda h, bounce=bounce: h.dma_start(out=bounce.ap(), in_=Sg), reads=[B_["Sg"]], writes=[b_b], dma="cc_in")
            self.wprefetch()
            P.op("pool", lambda h, bounce=bounce, gathered=gathered: h.collective_compute(
                "AllGather", ALU.bypass, replica_groups=[[0, 1, 2, 3], [4, 5, 6, 7]],
                ins=[bounce.ap().opt()], outs=[gathered.ap().opt()]), reads=[b_b], writes=[b_g], dma="cc", unit=1)
            self.set_ring([0, 1, 2, 3, 4, 5])
            for q in range(4):
                w, wb = self.wacquire()
                for ti, (a, b) in enumerate(TTS):
                    bank = self.next_bank()
                    self.proj(w, wb, KT, 0, self.hnT, [self.b_hn], a, b, bank)
                    P.op("act", lambda h, q=q, a=a, b=b, bank=bank: h.activation(
                        out=zs[:, q, a:b], in_=self.ps[:, bank, 0:b - a], func=AF.Silu),
                        reads=[self.b_ps[bank]], writes=[B_["zs"]])
            P.op("dve", lambda h: h.memset(S, 0.0), writes=[B_["S"]])
            for r in range(4):
                P.op("sp", lambda h, r=r, gathered=gathered: h.dma_start(out=Sg, in_=gathered.ap()[r * 128:(r + 1) * 128, :]),
                     reads=[b_g], writes=[B_["Sg"]], dma="sg")
                P.op("dve", lambda h, r=r: h.tensor_scalar(out=dmul, in0=Sg[:, 512:520], scalar1=-1.0, scalar2=selr[:, r:r + 1],
                                                           op0=ALU.add, op1=ALU.mult),
                     reads=[B_["Sg"], B_["tab"]], writes=[B_["dmul"]])
                P.op("dve", lambda h: h.tensor_scalar_add(out=dmul, in0=dmul, scalar1=1.0), reads=[B_["dmul"]], writes=[B_["dmul"]])
                P.op("dve", lambda h: h.tensor_tensor(
                    out=S.rearrange("p (j d) -> p j d", d=64), in0=S.rearrange("p (j d) -> p j d", d=64),
                    in1=dmul.unsqueeze(2).to_broadcast([128, 8, 64]), op=ALU.mult), reads=[B_["S"], B_["dmul"]], writes=[B_["S"]])
                P.op("dve", lambda h, r=r: h.scalar_tensor_tensor(out=S, in0=Sg[:, 0:512], scalar=selr[:, r:r + 1], in1=S,
                                                                  op0=ALU.mult, op1=ALU.add),
                     reads=[B_["Sg"], B_["tab"], B_["S"]], writes=[B_["S"]])
            P.op("act", lambda h: h.activation(out=S_bf, in_=S, func=AF.Copy), reads=[B_["S"]], writes=[B_["Sbf"]])
            rb, yb, cbk, ub = [0, 1], [2, 3], 4, 5
            for c in range(NCH):
                c0 = c * CH
                for bk in range(2):
                    def rmm(h, bk=bk, c0=c0):
                        ins = None
                        for jj in range(4):
                            j = bk * 4 + jj
                            ins = h.matmul(self.ps[:, rb[bk], jj * CH:(jj + 1) * CH], lhsT=sel[:, j * 128:(j + 1) * 128],
                                           rhs=cumT[:, c0:c0 + CH], start=(jj == 0), stop=True, skip_group_check=True)
                        return ins
                    P.op("pe", rmm, reads=[B_["cumT"], B_["tab"]], writes=[self.b_ps[rb[bk]]])
                P.op("pe", lambda h, c0=c0: h.matmul(self.ps[0:CH, cbk, 0:CH], lhsT=BT[:, c0:c0 + CH], rhs=CT[:, c0:c0 + CH],
                                                     start=True, stop=True), reads=[B_["BT"], B_["CT"]], writes=[self.b_ps[cbk]])
                for j in range(8):
                    P.op("act", lambda h, j=j, c=c: h.activation(
                        out=E[:, j, :], in_=self.ps[0:CH, rb[j // 4], (j % 4) * CH:(j % 4 + 1) * CH], func=AF.Exp,
                        bias=negcum[:, c * 8 + j:c * 8 + j + 1], scale=1.0),
                        reads=[self.b_ps[rb[j // 4]], B_["cumtok"]], writes=[B_["E"]])
                for bk in range(2):
                    P.op("act", lambda h, bk=bk: h.activation(
                        out=eR[:, bk * 4:bk * 4 + 4, :], in_=self.ps[:, rb[bk], 0:4 * CH].rearrange("p (j l) -> p j l", l=CH),
                        func=AF.Exp), reads=[self.b_ps[rb[bk]]], writes=[B_["eR"]])
                P.op("dve", lambda h: h.tensor_tensor(out=CBm, in0=self.ps[0:CH, cbk, 0:CH], in1=tri, op=ALU.mult),
                     reads=[self.b_ps[cbk], B_["tab"]], writes=[B_["CBm"]])
                P.op("dve", lambda h: h.scalar_tensor_tensor(
                    out=E, in0=E, scalar=1.0, in1=CBm.unsqueeze(1).to_broadcast([CH, 8, CH]), op0=ALU.min, op1=ALU.mult),
                    reads=[B_["E"], B_["CBm"]], writes=[B_["E"]])
                P.op("dve", lambda h, c0=c0: h.tensor_tensor(
                    out=eR, in0=eR, in1=CT[:, c0:c0 + CH].unsqueeze(1).to_broadcast([128, 8, CH]), op=ALU.mult),
                    reads=[B_["eR"], B_["CT"]], writes=[B_["eR"]])
                P.op("dve", lambda h, c=c: h.tensor_tensor(
                    out=xdt.rearrange("p (j d) -> p j d", d=64), in0=x_tok[:, c, :].rearrange("p (j d) -> p j d", d=64),
                    in1=dt_tok[:, c * 8:(c + 1) * 8].unsqueeze(2).to_broadcast([CH, 8, 64]), op=ALU.mult),
                    reads=[B_["xtok"], B_["dttok"]], writes=[B_["xdt"]])
                P.op("dve", lambda h, c=c: h.tensor_tensor(
                    out=xw[:, 0, :].rearrange("p (j d) -> p j d", d=64), in0=x_tok[:, c, :].rearrange("p (j d) -> p j d", d=64),
                    in1=wend[:, c * 8:(c + 1) * 8].unsqueeze(2).to_broadcast([CH, 8, 64]), op=ALU.mult),
                    reads=[B_["xtok"], B_["wl"]], writes=[B_["xw0"]])
                for bk in range(2):
                    def ymm(h, bk=bk):
                        ins = None
                        for jj in range(4):
                            j = bk * 4 + jj
                            o_ = self.ps[:, yb[bk], jj * CH:(jj + 1) * CH]
                            h.matmul(o_, lhsT=xdt[:, (j // 2) * 128:(j // 2 + 1) * 128], rhs=E[:, j, :],
                                     start=(jj == 0), stop=False, skip_group_check=True)
                            ins = h.matmul(o_, lhsT=S_bf[:, (j // 2) * 128:(j // 2 + 1) * 128], rhs=eR[:, j, :],
                                           start=False, stop=True, skip_group_check=True)
                        return ins
                    P.op("pe", ymm, reads=[B_["xdt"], B_["E"], B_["Sbf"], B_["eR"]], writes=[self.b_ps[yb[bk]]])
                P.op("pe", lambda h, c=c: h.matmul(self.ps[:, ub, :], lhsT=B_tok[:, c, :], rhs=xw[:, 0, :], start=True, stop=True),
                     reads=[B_["Btok"], B_["xw0"]], writes=[self.b_ps[ub]])
                for bk in range(2):
                    for par in range(2):
                        src = self.ps[64 * par:64 * par + 64, yb[bk], 0:4 * CH].rearrange("p (q r l) -> p q r l", r=2, l=CH)[:, :, par, :]
                        P.op("dve", lambda h, src=src, par=par, bk=bk, c0=c0: h.tensor_tensor(
                            out=yT[64 * par:64 * par + 64, 2 * bk:2 * bk + 2, c0:c0 + CH], in0=src,
                            in1=xDT[64 * par:64 * par + 64, 2 * bk:2 * bk + 2, c0:c0 + CH], op=ALU.add),
                            reads=[self.b_ps[yb[bk]], B_["xDT"]], writes=[B_["yT"]])
                P.op("dve", lambda h, c=c: h.tensor_tensor(
                    out=S.rearrange("p (j d) -> p j d", d=64), in0=S.rearrange("p (j d) -> p j d", d=64),
                    in1=eend[:, c * 8:(c + 1) * 8].unsqueeze(2).to_broadcast([128, 8, 64]), op=ALU.mult),
                    reads=[B_["S"], B_["tot"]], writes=[B_["S"]])
                P.op("dve", lambda h: h.tensor_tensor(out=S, in0=S, in1=self.ps[:, ub, :], op=ALU.add),
                     reads=[B_["S"], self.b_ps[ub]], writes=[B_["S"]])
                P.op("act", lambda h: h.activation(out=S_bf, in_=S, func=AF.Copy), reads=[B_["S"]], writes=[B_["Sbf"]])
            self.set_ring([0, 1, 2, 3, 4, 5])
            nb_ = [self.next_bank() for _ in TTS]
            for q in range(4):
                P.op("dve", lambda h, q=q: h.tensor_tensor(out=yT[:, q, :], in0=yT[:, q, :], in1=zs[:, q, :], op=ALU.mult),
                     reads=[B_["yT"], B_["zs"]], writes=[B_["yT"]])
                P.op("act", lambda h, q=q: h.activation(out=self.sq, in_=yT[:, q, :], func=AF.Square),
                     reads=[B_["yT"]], writes=[self.b_sq])

                def mm(h, q=q):
                    ins = None
                    for ti, (a, b) in enumerate(TTS):
                        ins = h.matmul(self.ps[:, nb_[ti], 0:b - a], lhsT=self.ones_bf[:, :], rhs=self.sq[:, a:b],
                                       start=(q == 0), stop=(q == 3))
                    return ins
                P.op("pe", mm, reads=[self.b_sq, self.b_const], writes=[self.b_ps[b] for b in nb_])
            for ti, (a, b) in enumerate(TTS):
                P.op("act", lambda h, ti=ti, a=a, b=b: h.activation(
                    out=self.rstd[:, a:b], in_=self.ps[:, nb_[ti], 0:b - a], func=AF.Sqrt, bias=float(EPS), scale=1.0 / 512.0),
                    reads=[self.b_ps[nb_[ti]]], writes=[self.b_rstd])
            P.op("dve", lambda h: h.reciprocal(out=self.rstd, in_=self.rstd), reads=[self.b_rstd], writes=[self.b_rstd])
            for q in range(4):
                P.op("dve", lambda h, q=q, g=g: h.scalar_tensor_tensor(
                    out=yT[:, q, :], in0=yT[:, q, :], scalar=gnw[:, g * 4 + q:g * 4 + q + 1], in1=self.rstd,
                    op0=ALU.mult, op1=ALU.mult), reads=[B_["yT"], B_["tab"], self.b_rstd], writes=[B_["yT"]])
            for nb in range(4):
                w, wb = self.wacquire()
                for q in range(4):
                    n_ = nb * 4 + q
                    for ti, (a, b) in enumerate(TTS):
                        bank = self.next_bank()
                        self.proj(w, wb, 4, q * 128, yT, [B_["yT"]], a, b, bank)
                        self.resid_add(n_, a, b, bank)
        self.mask_pre()

    def tok_transposes(self, srcT, b_src, dst_tok, b_dst, col0, kfac, hd, b_tab):
        P = self.P
        for (c0, c1, pb) in ((0, 8, 0), (8, NCH, 1)):
            def tr(h, c0=c0, c1=c1, pb=pb):
                ins = None
                for c in range(c0, c1):
                    ins = h.transpose(self.psb[0:CH, pb, (c - c0) * 128:(c - c0 + 1) * 128],
                                      srcT[:, c * CH:(c + 1) * CH], self.identb[:, :])
                return ins
            P.op("pe", tr, reads=[b_src, self.b_const], writes=[self.b_psb[pb]])
            nch = c1 - c0
            src = self.psb[0:CH, pb, 0:nch * 128].rearrange("p (c d) -> p c d", d=128)
            dst = dst_tok[:, c0:c1, col0:col0 + 128]
            if kfac is None:
                P.op("act", lambda h, src=src, dst=dst: h.activation(out=dst, in_=src, func=AF.Copy),
                     reads=[self.b_psb[pb]], writes=[b_dst])
            else:
                kf = kfac[:, hd * NCH + c0:hd * NCH + c1].unsqueeze(2).to_broadcast([CH, nch, 128])
                P.op("dve", lambda h, src=src, dst=dst, kf=kf: h.tensor_tensor(out=dst, in0=src, in1=kf, op=ALU.mult),
                     reads=[self.b_psb[pb], b_tab], writes=[b_dst])


def col_layout(v):
    return np.ascontiguousarray(v.reshape(-1, 128).T)


def bf16(a):
    import ml_dtypes
    return np.asarray(a, np.float32).astype(ml_dtypes.bfloat16)


def core_consts(c):
    p = c % 4
    ident = np.eye(128, dtype=np.float32)
    premask = np.full((128, PRE), 1.0 if p == 0 else 0.0, np.float32)
    halosel = np.zeros((128, 4), np.float32)
    if p > 0:
        halosel[:, p - 1] = 1.0
    return {"c_ident": ident, "c_premask": premask, "c_halosel": halosel}


def ret_consts(c):
    p = c % 4
    half = 128
    inv = (10000.0 ** (-np.arange(half, dtype=np.float32) / half)).astype(np.float32)
    pos = (p * OWN + np.arange(T)).astype(np.float32)
    ang = (pos[None, :] * inv[:, None]).astype(np.float32)
    lg = np.log1p(-np.exp2(-5.0 - np.arange(RH, dtype=np.float64)))
    s = np.arange(CH)[:, None]
    jj = np.arange(T)[None, :]
    G = np.where(jj >= s, np.exp((jj - s)[None] * lg[:, None, None]), 0.0) / 16.0
    Q = np.exp((np.arange(QB)[None, :] + 1.0) * lg[:, None])
    Qrep = np.broadcast_to(Q.reshape(1, RH * QB), (128, RH * QB))
    cc = np.arange(NCH)[None, None, :]
    Kf = np.exp((T - 1 - CH * cc - s[:, :, None]) * lg[None, :, None]) / 16.0
    coef = np.zeros((4, RH))
    for r in range(4):
        if r < p:
            coef[r] = np.exp((OWN * (p - 1 - r) - PRE) * lg)
    return {
        "c_cos": bf16(np.cos(ang)), "c_sin": bf16(np.sin(ang)),
        "c_retG": bf16(G), "c_retQ": bf16(Qrep),
        "c_retK": np.ascontiguousarray(Kf.reshape(CH, RH * NCH).astype(np.float32)),
        "c_retcoef": np.ascontiguousarray(np.broadcast_to(coef.reshape(1, 4 * RH), (128, 4 * RH)).astype(np.float32)),
    }


def ffn_inputs(idx, ffn_norm_w, ffn_w_up, ffn_conv_w, ffn_conv_b, ffn_w_down):
    return {
        "ffn_nw%d" % idx: col_layout(ffn_norm_w[idx]),
        "ffn_wup%d" % idx: ffn_w_up[idx],
        "ffn_cw%d" % idx: np.ascontiguousarray(ffn_conv_w[idx].reshape(3, FT, 128).transpose(2, 1, 0)),
        "ffn_cb%d" % idx: col_layout(ffn_conv_b[idx]),
        "ffn_wdn%d" % idx: ffn_w_down[idx],
    }


def ret_inputs(j, ret_norm_w, ret_w_in, ret_gn_w, ret_w_out):
    return {
        "ret_nw%d" % j: col_layout(ret_norm_w[j]),
        "ret_win%d" % j: ret_w_in[j],
        "ret_gn%d" % j: col_layout(ret_gn_w[j]),
        "ret_wout%d" % j: ret_w_out[j],
    }


def ssd_consts(c):
    p = c % 4
    tri = (np.arange(CH)[:, None] <= np.arange(CH)[None, :]).astype(np.float32)
    ones = np.ones((CH, 128), np.float32)
    sel = np.zeros((8, 8, 128), np.float32)
    for j in range(8):
        sel[j, j, :] = 1.0
    selr = np.zeros((128, 4), np.float32)
    selr[:, :p] = 1.0
    return {"c_tri": tri, "c_ones": ones, "c_sel": sel.reshape(8, 8 * 128), "c_selr": selr}


def ssd_inputs(j, ssd_norm_w, ssd_w_in, ssd_conv_w, ssd_conv_b, ssd_dt_bias, ssd_a_log, ssd_d, ssd_gnorm_w, ssd_w_out):
    return {
        "ssd_nw%d" % j: col_layout(ssd_norm_w[j]),
        "ssd_win%d" % j: ssd_w_in[j],
        "ssd_cw%d" % j: np.ascontiguousarray(ssd_conv_w[j].reshape(4, 48, 128).transpose(2, 1, 0).reshape(128, 48 * 4)),
        "ssd_cb%d" % j: col_layout(ssd_conv_b[j]),
        "ssd_dtb%d" % j: np.ascontiguousarray(ssd_dt_bias[j].reshape(8, 8).T),
        "ssd_alog%d" % j: np.ascontiguousarray(ssd_a_log[j].reshape(8, 8).T),
        "ssd_dcol%d" % j: col_layout(np.repeat(ssd_d[j], 64)),
        "ssd_gn%d" % j: col_layout(ssd_gnorm_w[j]),
        "ssd_wout%d" % j: ssd_w_out[j],
    }


FUSE_GROUPS = [[("ret", 0), ("ffn", 0), ("ssd", 0), ("ffn", 1), ("ret", 1), ("ffn", 2), ("ssd", 1), ("ffn", 3)]]


def _sub_inputs(kind, idx, name_idx, inp):
    if kind == "ffn":
        d = ffn_inputs(idx, inp["ffn_norm_w"], inp["ffn_w_up"], inp["ffn_conv_w"], inp["ffn_conv_b"], inp["ffn_w_down"])
    elif kind == "ret":
        d = ret_inputs(idx, inp["ret_norm_w"], inp["ret_w_in"], inp["ret_gn_w"], inp["ret_w_out"])
    else:
        d = ssd_inputs(idx, inp["ssd_norm_w"], inp["ssd_w_in"], inp["ssd_conv_w"], inp["ssd_conv_b"], inp["ssd_dt_bias"],
                       inp["ssd_a_log"], inp["ssd_d"], inp["ssd_gnorm_w"], inp["ssd_w_out"])
    if name_idx != idx:
        d = {k[:-len(str(idx))] + str(name_idx): v for k, v in d.items()}
    return d


def kernel(**inputs):
    inp = {k: np.ascontiguousarray(np.asarray(v, dtype=np.float32)) for k, v in inputs.items()}
    x, meta = inp["x"], inp["meta_tokens"]
    hs = []
    for c in range(NCORES):
        b, p = c // 4, c % 4
        xin = np.zeros((T, D), np.float32)
        if p == 0:
            xin[:PRE] = meta
        xin[PRE:] = x[b, p * OWN:(p + 1) * OWN]
        hs.append(xin)
    consts = []
    for c in range(NCORES):
        d = core_consts(c)
        d.update(ret_consts(c))
        d.update(ssd_consts(c))
        consts.append(d)
    progs = {}
    for gi, group in enumerate(FUSE_GROUPS):
        last = gi == len(FUSE_GROUPS) - 1
        slots, counts = [], {}
        for kind, idx in group:
            s = counts.get(kind, 0)
            counts[kind] = s + 1
            slots.append((kind, s))
        key = (tuple(slots), last)
        if key not in progs:
            progs[key] = Builder(slots, final_norm=last).build()
        nc = progs[key]
        names = set(a.memorylocations[0].name for a in nc.allocations
                    if isinstance(a, mybir.MemoryLocationSet) and a.kind == "ExternalInput")
        w = {}
        for (kind, idx), (_, s) in zip(group, slots):
            w.update(_sub_inputs(kind, idx, s, inp))
        if last:
            w["fin_nw"] = col_layout(inp["final_norm_w"])
        in_maps = []
        for c in range(NCORES):
            m = {"xin": hs[c]}
            m.update(consts[c])
            m.update(w)
            in_maps.append({k: v for k, v in m.items() if k in names})
        res = run_bass_kernel_spmd(nc, in_maps, core_ids=list(range(NCORES)))
        hs = [np.asarray(res.results[c]["out"]) for c in range(NCORES)]
    out = np.empty((2, 4 * OWN, D), np.float32)
    for c in range(NCORES):
        b, p = c // 4, c % 4
        out[b, p * OWN:(p + 1) * OWN] = hs[c][PRE:]
    return out
```

```python
import contextlib
import numpy as np
import concourse.bass as bass
import concourse.mybir as mybir
from concourse.bass_utils import run_bass_kernel_spmd

F32 = mybir.dt.float32
BF16 = mybir.dt.bfloat16
AF = mybir.ActivationFunctionType
ALU = mybir.AluOpType

NCORES = 8
D = 2048
KT = 16
PRE = 16
OWN = 1024
T = PRE + OWN
CH = 104
NCH = T // CH
TTS = [(0, 347), (347, 694), (694, 1040)]
EPS = 1e-6
FFN = 5632
FT = FFN // 128
FG = 4
FGT = FT // FG
DEPTH = 4
WSLOT = 2048
NWSLOT = 4


class Buf:
    __slots__ = ("name", "writer", "readers")

    def __init__(self, name):
        self.name = name
        self.writer = None
        self.readers = {}


class Prog:
    ENGS = ("pe", "act", "dve", "pool", "sp")

    def __init__(self, nc):
        self.nc = nc
        self.ops = {e: [] for e in self.ENGS}
        self.seen = {e: {} for e in self.ENGS}
        self.dma_count = {}
        self.dma_unit = {}

    def op(self, eng, fn, reads=(), writes=(), dma=None, unit=16):
        deps = []
        for b in reads:
            if b.writer is not None:
                deps.append(b.writer)
        for b in writes:
            if b.writer is not None:
                deps.append(b.writer)
            deps.extend(b.readers.values())
        waits = []
        seen = self.seen[eng]
        for d in deps:
            if d[0] == "eng":
                _, e2, idx = d
                if e2 == eng and eng in ("pe", "sp"):
                    continue
                if seen.get(e2, -1) >= idx:
                    continue
                seen[e2] = idx
            else:
                _, key, cnt = d
                k = ("dma", key)
                if seen.get(k, -1) >= cnt:
                    continue
                seen[k] = cnt
            waits.append(d)
        idx = len(self.ops[eng])
        if dma is not None:
            c = self.dma_count.get(dma, 0) + 1
            self.dma_count[dma] = c
            self.dma_unit[dma] = unit
            tok = ("dma", dma, c)
            rkey = ("dma", dma)
        else:
            tok = ("eng", eng, idx)
            rkey = eng
        self.ops[eng].append(dict(fn=fn, waits=waits, tok=tok, inc=False))
        for b in reads:
            b.readers[rkey] = tok
        for b in writes:
            b.writer = tok
            b.readers = {}
        return tok

    def emit(self, final_bufs=()):
        nc = self.nc
        self.op("sp", None, reads=list(final_bufs))
        for e in self.ENGS:
            for o in self.ops[e]:
                for w in o["waits"]:
                    if w[0] == "eng":
                        self.ops[w[1]][w[2]]["inc"] = True
        rank = {}
        for e in self.ENGS:
            r = 0
            rk = []
            for o in self.ops[e]:
                if o["inc"] and o["tok"][0] == "eng":
                    r += 1
                rk.append(r)
            rank[e] = rk
        with contextlib.ExitStack() as st:
            esem = {e: st.enter_context(nc.semaphore("s_" + e)) for e in ("pe", "act", "dve", "pool")}
            dsem = {k: st.enter_context(nc.semaphore("d_%s" % str(k))) for k in self.dma_count}
            block = st.enter_context(nc.Block())
            handles = {"pe": block.tensor, "act": block.scalar, "dve": block.vector,
                       "pool": block.gpsimd, "sp": block.sync}

            def make(e):
                def body(h):
                    for o in self.ops[e]:
                        for w in o["waits"]:
                            if w[0] == "eng":
                                h.wait_ge(esem[w[1]], rank[w[1]][w[2]])
                            else:
                                h.wait_ge(dsem[w[1]], self.dma_unit[w[1]] * w[2])
                        if o["fn"] is None:
                            continue
                        ins = o["fn"](h)
                        tok = o["tok"]
                        if tok[0] == "dma":
                            if self.dma_unit[tok[1]] == 1:
                                ins.then_inc(dsem[tok[1]])
                            else:
                                ins.then_inc(dsem[tok[1]], self.dma_unit[tok[1]])
                        elif o["inc"]:
                            ins.then_inc(esem[e], 1)
                return body
            for e in self.ENGS:
                handles[e](make(e))


NF_ARENA = 9088
NB_ARENA = 28544
RH = 8
QB = 208
NQB = T // QB
GAMMA = [1.0 - 2.0 ** (-5.0 - h) for h in range(RH)]


class Builder:
    def __init__(self, sublayers, final_norm):
        self.sublayers = sublayers
        self.final_norm = final_norm
        self.nc = bass.Bass("TRN2", target_bir_lowering=False)
        self.P = Prog(self.nc)
        self.st = contextlib.ExitStack()
        self.dram = {}
        self.wblocks = []
        self.w_issued = 0
        self.w_next = 0
        self.ring = [0, 1, 2, 3, 4, 5]
        self.ring_i = 0
        self.n_cc = 0
        self.arena_bufs = []

    def din(self, name, shape, dt=F32):
        if name in self.dram:
            return self.dram[name]
        t = self.nc.dram_tensor(name, list(shape), dt, kind="ExternalInput").ap()
        self.dram[name] = t
        return t

    def sb(self, name, shape, dt):
        return self.st.enter_context(self.nc.sbuf_tensor(name, list(shape), dt))

    def set_ring(self, banks):
        self.ring = list(banks)
        self.ring_i = 0

    def next_bank(self):
        b = self.ring[self.ring_i % len(self.ring)]
        self.ring_i += 1
        return b

    def arena_begin(self):
        self.oF = 0
        self.oB = 0
        old = self.arena_bufs
        self.arena_bufs = []
        self._old_arena = old

    def arena_end(self):
        bufs = self._old_arena + self.arena_bufs
        self.P.op("dve", lambda h: h.memset(self.dummy[:], 0.0), writes=bufs + [self.b_dummy])

    def cF(self, n, parts=128):
        ap = self.arenaF[0:parts, self.oF:self.oF + n]
        self.oF += n
        assert self.oF <= NF_ARENA, ("fp32 arena overflow", self.oF)
        return ap

    def cB(self, n, parts=128):
        ap = self.arenaB[0:parts, self.oB:self.oB + n]
        self.oB += n
        assert self.oB <= NB_ARENA, ("bf16 arena overflow", self.oB)
        return ap

    def abuf(self, name):
        b = Buf(name)
        self.arena_bufs.append(b)
        return b

    def wreg(self, view, kt, n):
        assert kt * n <= WSLOT
        self.wblocks.append((view, kt, n))

    def _wissue(self, i):
        view, kt, n = self.wblocks[i]
        slot = i % NWSLOT
        dst = self.wsb[:, slot, 0:kt * n].rearrange("p (k n) -> p k n", n=n)
        self.P.op("pool", lambda h, dst=dst, view=view: h.dma_start(out=dst, in_=view),
                  writes=[self.wbuf[slot]], dma="w%d" % slot)

    def wprefetch(self):
        while self.w_issued < min(len(self.wblocks), self.w_next + NWSLOT):
            self._wissue(self.w_issued)
            self.w_issued += 1

    def wacquire(self, count=1):
        i = self.w_next
        self.w_next += count
        while self.w_issued < min(len(self.wblocks), i + NWSLOT):
            self._wissue(self.w_issued)
            self.w_issued += 1
        res = []
        for ii in range(i, i + count):
            view, kt, n = self.wblocks[ii]
            slot = ii % NWSLOT
            res.append((self.wsb[:, slot, 0:kt * n].rearrange("p (k n) -> p k n", n=n), self.wbuf[slot]))
        return res[0] if count == 1 else res

    def build(self):
        nc, P = self.nc, self.P
        self.xin = self.din("xin", [T, D])
        self.c_ident = self.din("c_ident", [128, 128])
        self.c_premask = self.din("c_premask", [128, PRE])
        self.c_halosel = self.din("c_halosel", [128, 4])
        self.out = nc.dram_tensor("out", [T, D], F32, kind="ExternalOutput").ap()
        for kind, idx in self.sublayers:
            getattr(self, kind + "_inputs")(idx)
        if self.final_norm:
            self.din("fin_nw", [128, KT])

        self.hT = self.sb("hT", [128, KT, T], F32)
        self.hnT = self.sb("hnT", [128, KT, T], BF16)
        self.wsb = self.sb("wsb", [128, NWSLOT, WSLOT], BF16)
        self.ident = self.sb("ident", [128, 128], F32)
        self.identb = self.sb("identb", [128, 128], BF16)
        self.ones_bf = self.sb("ones_bf", [128, 128], BF16)
        self.premask = self.sb("premask", [128, PRE], F32)
        self.halosel = self.sb("halosel", [128, 4], F32)
        self.nwcol = self.sb("nwcol", [128, KT], F32)
        self.hg = self.sb("hg", [128, 4, KT * 3], BF16)
        self.halo = self.sb("halo", [128, KT * 3], F32)
        self.dummy = self.sb("dummy_t", [128, 2], F32)
        self.arenaF = self.sb("arenaF", [128, NF_ARENA], F32)
        self.arenaB = self.sb("arenaB", [128, NB_ARENA], BF16)
        self.ps = self.st.enter_context(nc.psum_tensor("ps", [128, 6, 512], F32))
        self.psb = self.st.enter_context(nc.psum_tensor("psb", [128, 2, 1024], BF16))

        self.b_hT = [Buf("hT%d" % k) for k in range(KT)]
        self.b_hn = Buf("hn")
        self.wbuf = [Buf("w%d" % s) for s in range(NWSLOT)]
        self.b_const = Buf("const")
        self.b_nw = Buf("nw")
        self.b_ps = [Buf("ps%d" % i) for i in range(6)]
        self.b_psb = [Buf("psb0"), Buf("psb1")]
        self.b_hg = Buf("hg")
        self.b_halo = Buf("halo")
        self.b_out = Buf("out")
        self.b_dummy = Buf("dummy")

        for kind, idx in self.sublayers:
            getattr(self, kind + "_wreg")(idx)

        P.op("sp", lambda h: h.dma_start(out=self.ident[:], in_=self.c_ident[:, :]), writes=[self.b_const], dma="c")
        P.op("sp", lambda h: h.dma_start(out=self.premask[:], in_=self.c_premask[:, :]), writes=[self.b_const], dma="c")
        P.op("sp", lambda h: h.dma_start(out=self.halosel[:], in_=self.c_halosel[:, :]), writes=[self.b_const], dma="c")
        P.op("dve", lambda h: h.memset(self.ones_bf[:], 1.0), writes=[self.b_const])
        P.op("dve", lambda h: h.tensor_copy(out=self.identb[:], in_=self.ident[:]), reads=[self.b_const], writes=[self.b_const])

        self.load_input()
        for kind, idx in self.sublayers:
            getattr(self, kind)(idx)
        self.store_output()
        P.emit(final_bufs=[self.b_out])
        self.st.close()
        return nc

    def io_arena(self):
        self.arena_begin()
        self.stage = self.cF(D, parts=CH)
        self.b_stage = self.abuf("stage")
        self.rstd = self.cF(T)
        self.b_rstd = self.abuf("rstd")
        self.sq = self.cB(T)
        self.b_sq = self.abuf("sq")
        self.arena_end()
        self.set_ring([0, 1, 2, 3, 4, 5])

    def load_input(self):
        P = self.P
        self.io_arena()
        for c in range(NCH):
            P.op("sp", lambda h, c=c: h.dma_start(out=self.stage, in_=self.xin[c * CH:(c + 1) * CH, :]),
                 writes=[self.b_stage], dma="st")
            for q in range(4):
                bank = self.next_bank()

                def tr(h, c=c, q=q, bank=bank):
                    ins = None
                    for i in range(4):
                        k = 4 * q + i
                        ins = h.transpose(self.ps[:, bank, i * CH:(i + 1) * CH],
                                          self.stage[:, k * 128:(k + 1) * 128], self.ident[0:CH, 0:CH])
                    return ins
                P.op("pe", tr, reads=[self.b_stage, self.b_const], writes=[self.b_ps[bank]])
                src = self.ps[:, bank, 0:4 * CH].rearrange("p (i t) -> p i t", t=CH)
                dst = self.hT[:, 4 * q:4 * q + 4, c * CH:(c + 1) * CH]
                if q % 2 == 0:
                    P.op("act", lambda h, src=src, dst=dst: h.activation(out=dst, in_=src, func=AF.Copy),
                         reads=[self.b_ps[bank]], writes=self.b_hT[4 * q:4 * q + 4])
                else:
                    P.op("dve", lambda h, src=src, dst=dst: h.tensor_copy(out=dst, in_=src),
                         reads=[self.b_ps[bank]], writes=self.b_hT[4 * q:4 * q + 4])

    def store_output(self):
        P = self.P
        self.io_arena()
        if self.final_norm:
            self.rmsnorm(self.dram["fin_nw"], to_hn=False)
        for c in range(NCH):
            for q in range(4):
                bank = self.next_bank()

                def tr(h, c=c, q=q, bank=bank):
                    ins = None
                    for i in range(4):
                        k = 4 * q + i
                        ins = h.transpose(self.ps[0:CH, bank, i * 128:(i + 1) * 128],
                                          self.hT[:, k, c * CH:(c + 1) * CH], self.ident[:, :])
                    return ins
                P.op("pe", tr, reads=self.b_hT[4 * q:4 * q + 4] + [self.b_const], writes=[self.b_ps[bank]])
                src = self.ps[0:CH, bank, :]
                dst = self.stage[:, q * 512:(q + 1) * 512]
                if q % 2 == 0:
                    P.op("act", lambda h, src=src, dst=dst: h.activation(out=dst, in_=src, func=AF.Copy),
                         reads=[self.b_ps[bank]], writes=[self.b_stage])
                else:
                    P.op("dve", lambda h, src=src, dst=dst: h.tensor_copy(out=dst, in_=src),
                         reads=[self.b_ps[bank]], writes=[self.b_stage])
            P.op("sp", lambda h, c=c: h.dma_start(out=self.out[c * CH:(c + 1) * CH, :], in_=self.stage),
                 reads=[self.b_stage], writes=[self.b_out], dma="o")

    def rmsnorm(self, nw_dram, to_hn=True):
        P = self.P
        P.op("sp", lambda h: h.dma_start(out=self.nwcol[:], in_=nw_dram[:, :]), writes=[self.b_nw], dma="nw")
        P.op("dve", lambda h: h.tensor_scalar_mul(out=self.nwcol[:], in0=self.nwcol[:], scalar1=float(np.sqrt(D))),
             reads=[self.b_nw], writes=[self.b_nw])
        banks = [self.next_bank() for _ in TTS]
        for k in range(KT):
            P.op("act", lambda h, k=k: h.activation(out=self.sq, in_=self.hT[:, k, :], func=AF.Square),
                 reads=[self.b_hT[k]], writes=[self.b_sq])

            def mm(h, k=k):
                ins = None
                for ti, (a, b) in enumerate(TTS):
                    ins = h.matmul(self.ps[:, banks[ti], 0:b - a], lhsT=self.ones_bf[:, :], rhs=self.sq[:, a:b],
                                   start=(k == 0), stop=(k == KT - 1))
                return ins
            P.op("pe", mm, reads=[self.b_sq, self.b_const], writes=[self.b_ps[b] for b in banks])
        for ti, (a, b) in enumerate(TTS):
            P.op("act", lambda h, ti=ti, a=a, b=b: h.activation(
                out=self.rstd[:, a:b], in_=self.ps[:, banks[ti], 0:b - a], func=AF.Sqrt, bias=float(D * EPS), scale=1.0),
                reads=[self.b_ps[banks[ti]]], writes=[self.b_rstd])
        P.op("dve", lambda h: h.reciprocal(out=self.rstd, in_=self.rstd), reads=[self.b_rstd], writes=[self.b_rstd])
        for k in range(KT):
            dst = self.hnT[:, k, :] if to_hn else self.hT[:, k, :]
            P.op("dve", lambda h, k=k, dst=dst: h.scalar_tensor_tensor(
                out=dst, in0=self.hT[:, k, :], scalar=self.nwcol[:, k:k + 1], in1=self.rstd,
                op0=ALU.mult, op1=ALU.mult), reads=[self.b_hT[k], self.b_nw, self.b_rstd],
                writes=[self.b_hn if to_hn else self.b_hT[k]])

    def halo_exchange(self):
        nc, P = self.nc, self.P
        i = self.n_cc
        self.n_cc += 1
        bounce = nc.dram_tensor("cc_b%d" % i, [128, KT * 3], BF16)
        gathered = nc.dram_tensor("cc_g%d" % i, [4 * 128, KT * 3], BF16)
        b_b, b_g = Buf("ccb%d" % i), Buf("ccg%d" % i)
        P.op("sp", lambda h: h.dma_start(out=bounce.ap().rearrange("p (k t) -> p k t", t=3), in_=self.hnT[:, :, T - 3:T]),
             reads=[self.b_hn], writes=[b_b], dma="cc_in")
        self.wprefetch()
        P.op("pool", lambda h: h.collective_compute("AllGather", ALU.bypass, replica_groups=[[0, 1, 2, 3], [4, 5, 6, 7]],
                                                    ins=[bounce.ap().opt()], outs=[gathered.ap().opt()]),
             reads=[b_b], writes=[b_g], dma="cc", unit=1)
        P.op("sp", lambda h: h.dma_start(out=self.hg[:], in_=gathered.ap().rearrange("(r p) f -> p r f", p=128)),
             reads=[b_g], writes=[self.b_hg], dma="cc_out")
        P.op("dve", lambda h: h.tensor_scalar_mul(out=self.halo[:], in0=self.hg[:, 0, :], scalar1=self.halosel[:, 0:1]),
             reads=[self.b_hg, self.b_const], writes=[self.b_halo])
        for r in range(1, 4):
            P.op("dve", lambda h, r=r: h.scalar_tensor_tensor(
                out=self.halo[:], in0=self.hg[:, r, :], scalar=self.halosel[:, r:r + 1], in1=self.halo[:],
                op0=ALU.mult, op1=ALU.add), reads=[self.b_hg, self.b_halo], writes=[self.b_halo])
        P.op("dve", lambda h: h.tensor_tensor(out=self.hnT[:, :, PRE - 3:PRE], in0=self.hnT[:, :, PRE - 3:PRE],
                                              in1=self.halo[:].rearrange("p (k t) -> p k t", t=3), op=ALU.add),
             reads=[self.b_hn, self.b_halo], writes=[self.b_hn])

    def mask_pre(self):
        self.P.op("dve", lambda h: h.tensor_tensor(
            out=self.hT[:, :, 0:PRE], in0=self.hT[:, :, 0:PRE],
            in1=self.premask[:].unsqueeze(1).to_broadcast([128, KT, PRE]), op=ALU.mult),
            reads=self.b_hT + [self.b_const], writes=self.b_hT)

    def proj(self, w, wb, kt, c0, rhsT, rhs_bufs, a, b, bank):
        def mm(h):
            ins = None
            for k in range(kt):
                ins = h.matmul(self.ps[:, bank, 0:b - a], lhsT=w[:, k, c0:c0 + 128], rhs=rhsT[:, k, a:b],
                               start=(k == 0), stop=(k == kt - 1))
            return ins
        self.P.op("pe", mm, reads=[wb] + list(rhs_bufs), writes=[self.b_ps[bank]])

    def resid_add(self, n, a, b, bank):
        self.P.op("dve", lambda h: h.tensor_tensor(
            out=self.hT[:, n, a:b], in0=self.hT[:, n, a:b], in1=self.ps[:, bank, 0:b - a], op=ALU.add),
            reads=[self.b_hT[n], self.b_ps[bank]], writes=[self.b_hT[n]])

    def ffn_inputs(self, idx):
        self.din("ffn_nw%d" % idx, [128, KT])
        self.din("ffn_wup%d" % idx, [D, 2 * FFN])
        self.din("ffn_cw%d" % idx, [128, FT, 3])
        self.din("ffn_cb%d" % idx, [128, FT])
        self.din("ffn_wdn%d" % idx, [FFN, D])

    def ffn_wreg(self, idx):
        wup = self.dram["ffn_wup%d" % idx]
        wdn = self.dram["ffn_wdn%d" % idx]
        for g in range(FG):
            for jj in range(FGT):
                j = g * FGT + jj
                self.wreg(wup[:, j * 128:(j + 1) * 128].rearrange("(k p) n -> p k n", p=128), KT, 128)
                self.wreg(wup[:, FFN + j * 128:FFN + (j + 1) * 128].rearrange("(k p) n -> p k n", p=128), KT, 128)
            for nb in range(16):
                self.wreg(wdn[g * FGT * 128:(g + 1) * FGT * 128, nb * 128:(nb + 1) * 128].rearrange(
                    "(k p) n -> p k n", p=128), FGT, 128)

    def ffn(self, idx):
        P = self.P
        self.arena_begin()
        self.rstd = self.cF(T)
        self.b_rstd = self.abuf("rstd")
        self.sq = self.cB(T)
        self.b_sq = self.abuf("sq")
        actT = self.cB(FGT * T).rearrange("p (j t) -> p j t", t=T)
        gpre = self.cF(2 * (T + 2)).rearrange("p (s t) -> p s t", s=2)
        gacc = self.cF(T)
        sg = self.cF(T)
        cw = self.cF(FT * 3).rearrange("p (j w) -> p j w", w=3)
        cb = self.cF(FT)
        b_act, b_gacc, b_sg, b_cw = self.abuf("actT"), self.abuf("gacc"), self.abuf("sg"), self.abuf("cw")
        b_gpre = [self.abuf("gpre0"), self.abuf("gpre1")]
        self.arena_end()
        self.set_ring([0, 1, 2, 3, 4, 5])

        P.op("dve", lambda h: h.memset(gpre[:, :, 0:2], 0.0), writes=b_gpre)
        P.op("sp", lambda h: h.dma_start(out=cw, in_=self.dram["ffn_cw%d" % idx][:, :, :]), writes=[b_cw], dma="cw")
        P.op("sp", lambda h: h.dma_start(out=cb, in_=self.dram["ffn_cb%d" % idx][:, :]), writes=[b_cw], dma="cw")
        self.rmsnorm(self.dram["ffn_nw%d" % idx])
        self.halo_exchange()
        for g in range(FG):
            for jj in range(FGT):
                j = g * FGT + jj
                s = j % 2
                wg, wgb = self.wacquire()
                gb = [self.next_bank() for _ in TTS]
                for ti, (a, b) in enumerate(TTS):
                    self.proj(wg, wgb, KT, 0, self.hnT, [self.b_hn], a, b, gb[ti])
                    P.op("act", lambda h, a=a, b=b, s=s, bank=gb[ti]: h.activation(
                        out=gpre[:, s, 2 + a:2 + b], in_=self.ps[:, bank, 0:b - a], func=AF.Copy),
                        reads=[self.b_ps[gb[ti]]], writes=[b_gpre[s]])
                wu, wub = self.wacquire()
                ub = [self.next_bank() for _ in TTS]
                for ti, (a, b) in enumerate(TTS):
                    self.proj(wu, wub, KT, 0, self.hnT, [self.b_hn], a, b, ub[ti])
                P.op("dve", lambda h, j=j, s=s: h.tensor_scalar(
                    out=gacc, in0=gpre[:, s, 2:T + 2], scalar1=cw[:, j, 2:3], scalar2=cb[:, j:j + 1],
                    op0=ALU.mult, op1=ALU.add), reads=[b_gpre[s], b_cw, b_sg], writes=[b_gacc])
                P.op("dve", lambda h, j=j, s=s: h.scalar_tensor_tensor(
                    out=gacc, in0=gpre[:, s, 1:T + 1], scalar=cw[:, j, 1:2], in1=gacc, op0=ALU.mult, op1=ALU.add),
                    reads=[b_gpre[s], b_cw, b_gacc], writes=[b_gacc])
                P.op("dve", lambda h, j=j, s=s: h.scalar_tensor_tensor(
                    out=gacc, in0=gpre[:, s, 0:T], scalar=cw[:, j, 0:1], in1=gacc, op0=ALU.mult, op1=ALU.add),
                    reads=[b_gpre[s], b_cw, b_gacc], writes=[b_gacc])
                P.op("act", lambda h: h.activation(out=sg, in_=gacc, func=AF.Silu), reads=[b_gacc], writes=[b_sg])
                for ti, (a, b) in enumerate(TTS):
                    P.op("dve", lambda h, a=a, b=b, jj=jj, bank=ub[ti]: h.tensor_tensor(
                        out=actT[:, jj, a:b], in0=sg[:, a:b], in1=self.ps[:, bank, 0:b - a], op=ALU.mult),
                        reads=[b_sg, self.b_ps[ub[ti]]], writes=[b_act])
            for n in range(16):
                wd, wdb = self.wacquire()
                for ti, (a, b) in enumerate(TTS):
                    bank = self.next_bank()
                    self.proj(wd, wdb, FGT, 0, actT, [b_act], a, b, bank)
                    self.resid_add(n, a, b, bank)
        self.mask_pre()

    def ret_inputs(self, j):
        self.din("ret_nw%d" % j, [128, KT])
        self.din("ret_win%d" % j, [D, 12288])
        self.din("ret_gn%d" % j, [128, 32])
        self.din("ret_wout%d" % j, [4096, D])
        self.din("c_cos", [128, T], BF16)
        self.din("c_sin", [128, T], BF16)
        self.din("c_retG", [RH, CH, T], BF16)
        self.din("c_retQ", [128, RH * QB], BF16)
        self.din("c_retK", [CH, RH * NCH])
        self.din("c_retcoef", [128, 4 * RH])

    def ret_wreg(self, j):
        win = self.dram["ret_win%d" % j]
        wout = self.dram["ret_wout%d" % j]
        for h in range(RH):
            def blk(c0):
                self.wreg(win[:, c0:c0 + 128].rearrange("(k p) n -> p k n", p=128), KT, 128)
                self.wreg(win[:, c0 + 128:c0 + 256].rearrange("(k p) n -> p k n", p=128), KT, 128)
            blk(4096 + h * 512)
            blk(4096 + h * 512 + 256)
            blk(2048 + h * 256)
            blk(h * 256)
            blk(8192 + h * 512)
            blk(8192 + h * 512 + 256)
            for nb in range(4):
                self.wreg(wout[h * 512:(h + 1) * 512, nb * 512:(nb + 1) * 512].rearrange(
                    "(k p) n -> p k n", p=128), 4, 512)

    def ret(self, j):
        nc, P = self.nc, self.P
        self.arena_begin()
        self.rstd = self.cF(T)
        self.b_rstd = self.abuf("rstd")
        self.sq = self.cB(T)
        self.b_sq = self.abuf("sq")
        cosT, sinT = self.cB(T), self.cB(T)
        G = self.cB(T, parts=CH)
        qdecB = self.cB(RH * QB).rearrange("p (h l) -> p h l", l=QB)
        kfac = self.cF(RH * NCH, parts=CH)
        coef = self.cF(4 * RH)
        gnw = self.cF(32)
        kT = self.cB(2 * T).rearrange("p (d t) -> p d t", t=T)
        qT = self.cB(2 * T).rearrange("p (d t) -> p d t", t=T)
        qd = self.cB(2 * 2 * QB).rearrange("p (s d l) -> p s d l", s=2, d=2)
        vtmp = self.cB(T)
        v_tok = self.cB(NCH * 512, parts=CH).rearrange("p (c e) -> p c e", e=512)
        k_tok = self.cB(NCH * 256, parts=CH).rearrange("p (c d) -> p c d", d=256)
        yT = self.cB(4 * T).rearrange("p (e t) -> p e t", t=T)
        sqo = self.cB(4 * QB).rearrange("p (e l) -> p e l", l=QB)
        PT = self.cB(2 * QB, parts=CH).rearrange("p (s l) -> p s l", s=2)
        SinB = self.cB(1024).rearrange("p (d e) -> p d e", e=512)
        oT = self.cF(4 * QB).rearrange("p (e l) -> p e l", l=QB)
        rt = self.cF(2 * 347).rearrange("p (s l) -> p s l", s=2)
        sloc = self.cF(1024).rearrange("p (d e) -> p d e", e=512)
        sgath = self.cF(1024).rearrange("p (d e) -> p d e", e=512)
        SinF = self.cF(1024).rearrange("p (d e) -> p d e", e=512)
        rs = self.cF(QB)
        b_tab, b_G, b_kT, b_qT, b_vtmp = self.abuf("tab"), self.abuf("G"), self.abuf("kT"), self.abuf("qT"), self.abuf("vtmp")
        b_qd = [self.abuf("qd0"), self.abuf("qd1")]
        b_vtok, b_ktok, b_yT, b_sqo = self.abuf("vtok"), self.abuf("ktok"), self.abuf("yT"), self.abuf("sqo")
        b_PT = [self.abuf("PT0"), self.abuf("PT1")]
        b_SinB, b_oT, b_rt, b_sloc, b_sgath, b_SinF, b_rs = (self.abuf(n) for n in
                                                             ("SinB", "oT", "rt", "sloc", "sgath", "SinF", "rs"))
        self.arena_end()

        dr = self.dram
        for dst, src in ((cosT, dr["c_cos"][:, :]), (sinT, dr["c_sin"][:, :]),
                         (qdecB, dr["c_retQ"].rearrange("p (h l) -> p h l", l=QB)),
                         (kfac, dr["c_retK"][:, :]), (coef, dr["c_retcoef"][:, :]), (gnw, dr["ret_gn%d" % j][:, :])):
            P.op("sp", lambda h, dst=dst, src=src: h.dma_start(out=dst, in_=src), writes=[b_tab], dma="tab")
        self.set_ring([0, 1, 2, 3, 4, 5])
        self.rmsnorm(dr["ret_nw%d" % j])

        for hd in range(RH):
            gam = GAMMA[hd]
            P.op("sp", lambda h, hd=hd: h.dma_start(out=G, in_=dr["c_retG"][hd, :, :]), writes=[b_G], dma="G")
            self.set_ring([0, 1, 2, 3, 4, 5])
            def rot_proj(dstT, b_dst):
                (w, wb), (w2, wb2) = self.wacquire(2)
                for ti, (a, b) in enumerate(TTS):
                    b1, b2 = self.next_bank(), self.next_bank()
                    self.proj(w, wb, KT, 0, self.hnT, [self.b_hn], a, b, b1)
                    self.proj(w2, wb2, KT, 0, self.hnT, [self.b_hn], a, b, b2)
                    n = b - a
                    x1, x2 = self.ps[:, b1, 0:n], self.ps[:, b2, 0:n]
                    t1, t2 = rt[:, 0, 0:n], rt[:, 1, 0:n]
                    seq = [(t1, x1, cosT[:, a:b], None), (t2, x2, sinT[:, a:b], None), (dstT[:, 0, a:b], t1, t2, ALU.subtract),
                           (t1, x1, sinT[:, a:b], None), (t2, x2, cosT[:, a:b], None), (dstT[:, 1, a:b], t1, t2, ALU.add)]
                    for si, (o_, i0, i1, op_) in enumerate(seq):
                        fin = op_ is not None
                        rd = [b_rt] if fin else [self.b_ps[b1 if i0 is x1 else b2], b_tab]
                        P.op("dve", lambda h, o_=o_, i0=i0, i1=i1, op_=op_: h.tensor_tensor(
                            out=o_, in0=i0, in1=i1, op=(op_ or ALU.mult)), reads=rd + ([b_rt] if not fin else []),
                            writes=[b_dst] if fin else [b_rt])
            for half in range(2):
                for q in range(2):
                    w, wb = self.wacquire()
                    e = half * 2 + q
                    for ti, (a, b) in enumerate(TTS):
                        bank = self.next_bank()
                        self.proj(w, wb, KT, 0, self.hnT, [self.b_hn], a, b, bank)
                        P.op("act", lambda h, a=a, b=b, bank=bank: h.activation(
                            out=vtmp[:, a:b], in_=self.ps[:, bank, 0:b - a], func=AF.Copy),
                            reads=[self.b_ps[bank]], writes=[b_vtmp])
                    self.tok_transposes(vtmp, b_vtmp, v_tok, b_vtok, e * 128, None, None, b_tab)
            rot_proj(kT, b_kT)
            for dt in range(2):
                self.tok_transposes(kT[:, dt, :], b_kT, k_tok, b_ktok, dt * 128, kfac, hd, b_tab)
            sb_ = [0, 1]
            for dt in range(2):
                def mm(h, dt=dt):
                    ins = None
                    for c in range(NCH):
                        ins = h.matmul(self.ps[:, sb_[dt], :], lhsT=k_tok[:, c, dt * 128:(dt + 1) * 128], rhs=v_tok[:, c, :],
                                       start=(c == 0), stop=(c == NCH - 1))
                    return ins
                P.op("pe", mm, reads=[b_ktok, b_vtok], writes=[self.b_ps[sb_[dt]]])
                P.op("act", lambda h, dt=dt: h.activation(out=sloc[:, dt, :], in_=self.ps[:, sb_[dt], :], func=AF.Copy),
                     reads=[self.b_ps[sb_[dt]]], writes=[b_sloc])
            i = self.n_cc
            self.n_cc += 1
            bounce = nc.dram_tensor("cc_b%d" % i, [256, 512], F32)
            gathered = nc.dram_tensor("cc_g%d" % i, [4 * 256, 512], F32)
            b_b, b_g = Buf("ccb%d" % i), Buf("ccg%d" % i)
            P.op("sp", lambda h, bounce=bounce: h.dma_start(out=bounce.ap().rearrange("(d p) e -> p d e", p=128), in_=sloc),
                 reads=[b_sloc], writes=[b_b], dma="cc_in")
            self.wprefetch()
            P.op("pool", lambda h, bounce=bounce, gathered=gathered: h.collective_compute(
                "AllGather", ALU.bypass, replica_groups=[[0, 1, 2, 3], [4, 5, 6, 7]],
                ins=[bounce.ap().opt()], outs=[gathered.ap().opt()]), reads=[b_b], writes=[b_g], dma="cc", unit=1)
            rot_proj(qT, b_qT)
            self.set_ring([2, 3, 4, 5])
            for half in range(2):
                for q in range(2):
                    w, wb = self.wacquire()
                    e = half * 2 + q
                    for ti, (a, b) in enumerate(TTS):
                        bank = self.next_bank()
                        self.proj(w, wb, KT, 0, self.hnT, [self.b_hn], a, b, bank)
                        P.op("act", lambda h, e=e, a=a, b=b, bank=bank: h.activation(
                            out=yT[:, e, a:b], in_=self.ps[:, bank, 0:b - a], func=AF.Silu),
                            reads=[self.b_ps[bank]], writes=[b_yT])
                    P.op("dve", lambda h, e=e, hd=hd: h.tensor_scalar_mul(
                        out=yT[:, e, :], in0=yT[:, e, :], scalar1=gnw[:, hd * 4 + e:hd * 4 + e + 1]),
                        reads=[b_yT, b_tab], writes=[b_yT])
            for r in range(4):
                P.op("sp", lambda h, r=r, gathered=gathered: h.dma_start(
                    out=sgath, in_=gathered.ap()[r * 256:(r + 1) * 256, :].rearrange("(d p) e -> p d e", p=128)),
                    reads=[b_g], writes=[b_sgath], dma="sg")
                cf = coef[:, r * RH + hd:r * RH + hd + 1]
                if r == 0:
                    P.op("dve", lambda h, cf=cf: h.tensor_scalar_mul(out=SinF, in0=sgath, scalar1=cf),
                         reads=[b_sgath, b_tab], writes=[b_SinF])
                else:
                    P.op("dve", lambda h, cf=cf: h.scalar_tensor_tensor(out=SinF, in0=sgath, scalar=cf, in1=SinF,
                                                                        op0=ALU.mult, op1=ALU.add),
                         reads=[b_sgath, b_tab, b_SinF], writes=[b_SinF])
            P.op("act", lambda h: h.activation(out=SinB, in_=SinF, func=AF.Copy), reads=[b_SinF], writes=[b_SinB])
            for qb in range(NQB):
                s = qb % 2
                q0b = qb * QB
                oa = [0, 1] if s == 0 else [2, 3]
                for dt in range(2):
                    P.op("dve", lambda h, dt=dt, s=s, q0b=q0b, qb=qb, hd=hd, gam=gam: h.scalar_tensor_tensor(
                        out=qd[:, s, dt, :], in0=qT[:, dt, q0b:q0b + QB], scalar=float(gam ** (QB * qb)),
                        in1=qdecB[:, hd, :], op0=ALU.mult, op1=ALU.mult),
                        reads=[b_qT, b_tab], writes=[b_qd[s]])
                nkc = 2 * qb + 2

                def emit_sc(kc):
                    k0 = kc * CH
                    q0 = max(q0b, k0)
                    n = q0b + QB - q0
                    sbank = 4 + (kc % 2)

                    def sc(h, k0=k0, q0=q0, n=n, sbank=sbank):
                        ins = None
                        for dt in range(2):
                            ins = h.matmul(self.ps[0:CH, sbank, 0:n], lhsT=kT[:, dt, k0:k0 + CH], rhs=qT[:, dt, q0:q0 + n],
                                           start=(dt == 0), stop=(dt == 1))
                        return ins
                    P.op("pe", sc, reads=[b_kT, b_qT], writes=[self.b_ps[sbank]])

                def emit_pv(kc):
                    k0 = kc * CH
                    q0 = max(q0b, k0)
                    n = q0b + QB - q0
                    sbank = 4 + (kc % 2)
                    ps_ = kc % 2
                    P.op("dve", lambda h, n=n, sbank=sbank, ps_=ps_, q0=q0, k0=k0: h.tensor_tensor(
                        out=PT[:, ps_, 0:n], in0=self.ps[0:CH, sbank, 0:n], in1=G[:, q0 - k0:q0 - k0 + n], op=ALU.mult),
                        reads=[self.b_ps[sbank], b_G], writes=[b_PT[ps_]])

                    def pv(h, kc=kc, q0=q0, n=n, ps_=ps_, oa=oa, q0b=q0b):
                        ins = None
                        off = q0 - q0b
                        for e in range(4):
                            ins = h.matmul(self.ps[:, oa[e // 2], (e % 2) * QB + off:(e % 2) * QB + off + n],
                                           lhsT=v_tok[:, kc, e * 128:(e + 1) * 128], rhs=PT[:, ps_, 0:n],
                                           start=(kc == 0 and e % 2 == 0), stop=False, skip_group_check=True)
                        return ins
                    P.op("pe", pv, reads=[b_vtok, b_PT[ps_]], writes=[self.b_ps[oa[0]], self.b_ps[oa[1]]])
                emit_sc(0)
                for kc in range(nkc):
                    if kc + 1 < nkc:
                        emit_sc(kc + 1)
                    emit_pv(kc)

                def corr(h, s=s, oa=oa):
                    ins = None
                    for e in range(4):
                        for dt in range(2):
                            ins = h.matmul(self.ps[:, oa[e // 2], (e % 2) * QB:(e % 2) * QB + QB],
                                           lhsT=SinB[:, dt, e * 128:(e + 1) * 128], rhs=qd[:, s, dt, :],
                                           start=False, stop=(dt == 1), skip_group_check=True)
                    return ins
                P.op("pe", corr, reads=[b_SinB, b_qd[s]], writes=[self.b_ps[oa[0]], self.b_ps[oa[1]]])
                for i2 in range(2):
                    P.op("act", lambda h, i2=i2, oa=oa: h.activation(
                        out=oT[:, 2 * i2:2 * i2 + 2, :], in_=self.ps[:, oa[i2], 0:2 * QB].rearrange("p (e l) -> p e l", l=QB),
                        func=AF.Copy), reads=[self.b_ps[oa[i2]]], writes=[b_oT])
                P.op("act", lambda h: h.activation(out=sqo, in_=oT, func=AF.Square), reads=[b_oT], writes=[b_sqo])
                nbank = 4 + (qb % 2)

                def nsum(h, nbank=nbank):
                    ins = None
                    for e in range(4):
                        ins = h.matmul(self.ps[:, nbank, 0:QB], lhsT=self.ones_bf[:, :], rhs=sqo[:, e, :],
                                       start=(e == 0), stop=(e == 3))
                    return ins
                P.op("pe", nsum, reads=[b_sqo, self.b_const], writes=[self.b_ps[nbank]])
                P.op("act", lambda h, nbank=nbank: h.activation(out=rs, in_=self.ps[:, nbank, 0:QB], func=AF.Sqrt,
                                                                bias=float(EPS), scale=1.0 / 512.0),
                     reads=[self.b_ps[nbank]], writes=[b_rs])
                P.op("dve", lambda h: h.reciprocal(out=rs, in_=rs), reads=[b_rs], writes=[b_rs])
                P.op("dve", lambda h: h.tensor_tensor(out=oT, in0=oT, in1=rs.unsqueeze(1).to_broadcast([128, 4, QB]),
                                                      op=ALU.mult), reads=[b_oT, b_rs], writes=[b_oT])
                P.op("dve", lambda h, q0b=q0b: h.tensor_tensor(out=yT[:, :, q0b:q0b + QB], in0=yT[:, :, q0b:q0b + QB],
                                                               in1=oT, op=ALU.mult), reads=[b_oT, b_yT], writes=[b_yT])
            self.set_ring([0, 1, 2, 3, 4, 5])
            for nb in range(4):
                w, wb = self.wacquire()
                for q in range(4):
                    n_ = nb * 4 + q
                    for ti, (a, b) in enumerate(TTS):
                        bank = self.next_bank()
                        self.proj(w, wb, 4, q * 128, yT, [b_yT], a, b, bank)
                        self.resid_add(n_, a, b, bank)
        if getattr(self, "debug", False):
            for name, ap, shape, dt, bufs in (("dbg_kT", kT, [128, 2, T], BF16, [b_kT]), ("dbg_qT", qT, [128, 2, T], BF16, [b_qT]),
                                              ("dbg_vtok", v_tok, [CH, NCH, 512], BF16, [b_vtok]),
                                              ("dbg_ktok", k_tok, [CH, NCH, 256], BF16, [b_ktok]),
                                              ("dbg_yT", yT, [128, 4, T], BF16, [b_yT]),
                                              ("dbg_SinF", SinF, [128, 2, 512], F32, [b_SinF]),
                                              ("dbg_hn", self.hnT[:, :, :], [128, KT, T], BF16, [self.b_hn])):
                o_ = nc.dram_tensor(name, shape, dt, kind="ExternalOutput").ap()
                P.op("sp", lambda h, o_=o_, ap=ap: h.dma_start(out=o_, in_=ap), reads=bufs, writes=[self.b_out], dma="dbg")
        self.mask_pre()

    def ssd_inputs(self, j):
        self.din("ssd_nw%d" % j, [128, KT])
        self.din("ssd_win%d" % j, [D, 10304])
        self.din("ssd_cw%d" % j, [128, 48 * 4])
        self.din("ssd_cb%d" % j, [128, 48])
        self.din("ssd_dtb%d" % j, [8, 8])
        self.din("ssd_alog%d" % j, [8, 8])
        self.din("ssd_dcol%d" % j, [128, 32])
        self.din("ssd_gn%d" % j, [128, 32])
        self.din("ssd_wout%d" % j, [4096, D])
        self.din("c_tri", [CH, CH])
        self.din("c_ones", [CH, 128])
        self.din("c_sel", [8, 8 * 128])
        self.din("c_selr", [128, 4])

    def ssd_wreg(self, j):
        win = self.dram["ssd_win%d" % j]
        wout = self.dram["ssd_wout%d" % j]

        def blk(c0, n=128):
            self.wreg(win[:, c0:c0 + n].rearrange("(k p) n -> p k n", p=128), KT, n)
        for g in range(8):
            for q in range(4):
                blk(4096 + g * 512 + q * 128)
            blk(8192 + g * 128)
            blk(9216 + g * 128)
            blk(10240 + g * 8, 8)
            for q in range(4):
                blk(g * 512 + q * 128)
            for nb in range(4):
                self.wreg(wout[g * 512:(g + 1) * 512, nb * 512:(nb + 1) * 512].rearrange(
                    "(k p) n -> p k n", p=128), 4, 512)

    def ssd(self, jl):
        nc, P = self.nc, self.P
        dr = self.dram
        NC8 = NCH * 8
        self.arena_begin()
        self.rstd = self.cF(T)
        self.b_rstd = self.abuf("rstd")
        self.sq = self.cB(T)
        self.b_sq = self.abuf("sq")
        xpre = self.cF(T + 3)
        acc = self.cF(347)
        dda = self.cF(T, parts=8)
        cumT = self.cF(T, parts=8)
        dt_tok = self.cF(NC8, parts=CH)
        da_tok = self.cF(NC8, parts=CH)
        cum_tok = self.cF(NC8, parts=CH)
        negcum = self.cF(NC8, parts=CH)
        tot = self.cF(NC8)
        sfx = self.cF(NC8)
        eend = self.cF(NC8)
        wloc = self.cF(NC8, parts=CH)
        wend = self.cF(NC8, parts=CH)
        tmp80 = self.cF(NC8)
        S = self.cF(512)
        Sg = self.cF(520)
        dmul = self.cF(8)
        tri = self.cF(CH, parts=CH)
        ones = self.cF(128, parts=CH)
        sel = self.cF(8 * 128, parts=8)
        selr = self.cF(4)
        cw = self.cF(48 * 4).rearrange("p (t w) -> p t w", w=4)
        cb = self.cF(48)
        dtb = self.cF(8, parts=8)
        nega = self.cF(8, parts=8)
        dcol = self.cF(32)
        gnw = self.cF(32)
        xs = self.cB(T)
        xDT = self.cB(4 * T).rearrange("p (q t) -> p q t", t=T)
        x_tok = self.cB(NCH * 512, parts=CH).rearrange("p (c f) -> p c f", f=512)
        BT = self.cB(T)
        CT = self.cB(T)
        B_tok = self.cB(NCH * 128, parts=CH).rearrange("p (c n) -> p c n", n=128)
        E = self.cB(2 * 8 * CH, parts=CH).rearrange("p (s j l) -> p s j l", s=2, l=CH)
        eR = self.cB(2 * 8 * CH).rearrange("p (s j l) -> p s j l", s=2, l=CH)
        CBm = self.cB(CH, parts=CH)
        xdt = self.cB(512, parts=CH)
        xw = self.cB(2 * 512, parts=CH).rearrange("p (s f) -> p s f", s=2)
        S_bf = self.cB(512)
        yT = self.cB(4 * T).rearrange("p (q t) -> p q t", t=T)
        zs = self.cB(4 * T).rearrange("p (q t) -> p q t", t=T)
        names = ("tab", "xpre", "acc", "dda", "cumT", "dttok", "datok", "cumtok", "tot", "wl", "S", "Sg", "dmul", "xs", "xDT",
                 "xtok", "BT", "CT", "Btok", "E0", "E1", "eR0", "eR1", "CBm", "xdt", "xw0", "xw1", "Sbf", "yT", "zs", "tmp80")
        B_ = {n: self.abuf(n) for n in names}
        self.arena_end()

        loads = ((tri, dr["c_tri"][:, :]), (ones, dr["c_ones"][:, :]), (sel, dr["c_sel"][:, :]), (selr, dr["c_selr"][:, :]),
                 (cw, dr["ssd_cw%d" % jl].rearrange("p (t w) -> p t w", w=4)), (cb, dr["ssd_cb%d" % jl][:, :]),
                 (dtb, dr["ssd_dtb%d" % jl][:, :]), (nega, dr["ssd_alog%d" % jl][:, :]),
                 (dcol, dr["ssd_dcol%d" % jl][:, :]), (gnw, dr["ssd_gn%d" % jl][:, :]))
        for dst, src in loads:
            P.op("sp", lambda h, dst=dst, src=src: h.dma_start(out=dst, in_=src), writes=[B_["tab"]], dma="tab")
        P.op("act", lambda h: h.activation(out=nega, in_=nega, func=AF.Exp), reads=[B_["tab"]], writes=[B_["tab"]])
        P.op("dve", lambda h: h.tensor_scalar_mul(out=nega, in0=nega, scalar1=-1.0), reads=[B_["tab"]], writes=[B_["tab"]])
        P.op("dve", lambda h: h.memset(xpre[:, 0:3], 0.0), writes=[B_["xpre"]])
        self.set_ring([0, 1, 2, 3, 4, 5])
        self.rmsnorm(dr["ssd_nw%d" % jl])
        self.halo_exchange()

        def conv_tile(tile, dst, b_dst, mask_pre, after_proj=None):
            w, wb = self.wacquire()
            for ti, (a, b) in enumerate(TTS):
                bank = self.next_bank()
                self.proj(w, wb, KT, 0, self.hnT, [self.b_hn], a, b, bank)
                P.op("act", lambda h, a=a, b=b, bank=bank: h.activation(
                    out=xpre[:, 3 + a:3 + b], in_=self.ps[:, bank, 0:b - a], func=AF.Copy),
                    reads=[self.b_ps[bank]], writes=[B_["xpre"]])
            if after_proj is not None:
                after_proj()
            for ti, (a, b) in enumerate(TTS):
                n = b - a
                P.op("dve", lambda h, a=a, b=b, n=n: h.tensor_scalar(
                    out=acc[:, 0:n], in0=xpre[:, 3 + a:3 + b], scalar1=cw[:, tile, 3:4], scalar2=cb[:, tile:tile + 1],
                    op0=ALU.mult, op1=ALU.add), reads=[B_["xpre"], B_["tab"]], writes=[B_["acc"]])
                for wi in range(3):
                    P.op("dve", lambda h, a=a, b=b, n=n, wi=wi: h.scalar_tensor_tensor(
                        out=acc[:, 0:n], in0=xpre[:, wi + a:wi + b], scalar=cw[:, tile, wi:wi + 1], in1=acc[:, 0:n],
                        op0=ALU.mult, op1=ALU.add), reads=[B_["xpre"], B_["tab"], B_["acc"]], writes=[B_["acc"]])
                P.op("act", lambda h, a=a, b=b, n=n: h.activation(out=dst[:, a:b], in_=acc[:, 0:n], func=AF.Silu),
                     reads=[B_["acc"]], writes=[b_dst])
            if mask_pre:
                P.op("dve", lambda h: h.tensor_tensor(out=dst[:, 0:PRE], in0=dst[:, 0:PRE], in1=self.premask[:], op=ALU.mult),
                     reads=[b_dst, self.b_const], writes=[b_dst])

        for g in range(8):
            self.set_ring([0, 1, 2, 3, 4, 5])
            pending = None
            for q in range(4):
                conv_tile(g * 4 + q, xs, B_["xs"], True, pending)

                def pending(q=q, g=g):
                    self.tok_transposes(xs, B_["xs"], x_tok, B_["xtok"], q * 128, None, None, B_["tab"])
                    P.op("dve", lambda h, q=q, g=g: h.tensor_scalar_mul(out=xDT[:, q, :], in0=xs,
                                                                        scalar1=dcol[:, g * 4 + q:g * 4 + q + 1]),
                         reads=[B_["xs"], B_["tab"]], writes=[B_["xDT"]])
            conv_tile(32 + g, BT, B_["BT"], False, pending)
            conv_tile(40 + g, CT, B_["CT"], False,
                      lambda: self.tok_transposes(BT, B_["BT"], B_tok, B_["Btok"], 0, None, None, B_["tab"]))
            w, wb = self.wacquire()
            for ti, (a, b) in enumerate(TTS):
                bank = self.next_bank()

                def mm(h, a=a, b=b, bank=bank, w=w):
                    ins = None
                    for k in range(KT):
                        ins = h.matmul(self.ps[0:8, bank, 0:b - a], lhsT=w[:, k, 0:8], rhs=self.hnT[:, k, a:b],
                                       start=(k == 0), stop=(k == KT - 1))
                    return ins
                P.op("pe", mm, reads=[wb, self.b_hn], writes=[self.b_ps[bank]])
                P.op("act", lambda h, a=a, b=b, bank=bank, g=g: h.activation(
                    out=dda[:, a:b], in_=self.ps[0:8, bank, 0:b - a], func=AF.Exp, bias=dtb[:, g:g + 1], scale=1.0),
                    reads=[self.b_ps[bank], B_["tab"]], writes=[B_["dda"]])
            P.op("act", lambda h: h.activation(out=dda, in_=dda, func=AF.Ln, bias=1.0, scale=1.0),
                 reads=[B_["dda"]], writes=[B_["dda"]])

            def tok8(dst, b_dst):
                bank = self.next_bank()

                def tr(h, bank=bank):
                    ins = None
                    for c in range(NCH):
                        ins = h.transpose(self.ps[0:CH, bank, c * 8:(c + 1) * 8], dda[:, c * CH:(c + 1) * CH], self.ident[0:8, 0:8])
                    return ins
                P.op("pe", tr, reads=[B_["dda"], self.b_const], writes=[self.b_ps[bank]])
                P.op("act", lambda h, bank=bank: h.activation(out=dst, in_=self.ps[0:CH, bank, 0:NC8], func=AF.Copy),
                     reads=[self.b_ps[bank]], writes=[b_dst])
            tok8(dt_tok, B_["dttok"])
            P.op("dve", lambda h, g=g: h.tensor_scalar_mul(out=dda, in0=dda, scalar1=nega[:, g:g + 1]),
                 reads=[B_["dda"], B_["tab"]], writes=[B_["dda"]])
            P.op("dve", lambda h: h.tensor_tensor(out=dda[:, 0:PRE], in0=dda[:, 0:PRE], in1=self.premask[0:8, :], op=ALU.mult),
                 reads=[B_["dda"], self.b_const], writes=[B_["dda"]])
            tok8(da_tok, B_["datok"])
            b_cum, b_tot = self.next_bank(), self.next_bank()
            P.op("pe", lambda h, b_cum=b_cum: h.matmul(self.ps[0:CH, b_cum, 0:NC8], lhsT=tri, rhs=da_tok, start=True, stop=True),
                 reads=[B_["datok"], B_["tab"]], writes=[self.b_ps[b_cum]])
            P.op("pe", lambda h, b_tot=b_tot: h.matmul(self.ps[:, b_tot, 0:NC8], lhsT=ones, rhs=da_tok, start=True, stop=True),
                 reads=[B_["datok"], B_["tab"]], writes=[self.b_ps[b_tot]])
            P.op("act", lambda h, b_cum=b_cum: h.activation(out=cum_tok, in_=self.ps[0:CH, b_cum, 0:NC8], func=AF.Copy),
                 reads=[self.b_ps[b_cum]], writes=[B_["cumtok"]])
            P.op("act", lambda h, b_cum=b_cum: h.activation(out=negcum, in_=self.ps[0:CH, b_cum, 0:NC8], func=AF.Copy, scale=-1.0),
                 reads=[self.b_ps[b_cum]], writes=[B_["cumtok"]])
            P.op("act", lambda h, b_tot=b_tot: h.activation(out=tot, in_=self.ps[:, b_tot, 0:NC8], func=AF.Copy),
                 reads=[self.b_ps[b_tot]], writes=[B_["tot"]])
            P.op("act", lambda h, b_tot=b_tot: h.activation(out=eend, in_=self.ps[:, b_tot, 0:NC8], func=AF.Exp),
                 reads=[self.b_ps[b_tot]], writes=[B_["tot"]])
            P.op("dve", lambda h: h.tensor_copy(out=sfx[:, (NCH - 1) * 8:NC8], in_=tot[:, (NCH - 1) * 8:NC8]),
                 reads=[B_["tot"]], writes=[B_["tot"]])
            for c in range(NCH - 2, -1, -1):
                P.op("dve", lambda h, c=c: h.tensor_tensor(out=sfx[:, c * 8:(c + 1) * 8], in0=sfx[:, (c + 1) * 8:(c + 2) * 8],
                                                           in1=tot[:, c * 8:(c + 1) * 8], op=ALU.add),
                     reads=[B_["tot"]], writes=[B_["tot"]])
            for src, dst in ((sfx, wloc), (tot, wend)):
                P.op("dve", lambda h, src=src: h.tensor_tensor(out=tmp80[0:CH, :], in0=src[0:CH, :], in1=cum_tok, op=ALU.subtract),
                     reads=[B_["tot"], B_["cumtok"], B_["tmp80"]], writes=[B_["tmp80"]])
                P.op("act", lambda h: h.activation(out=tmp80[0:CH, :], in_=tmp80[0:CH, :], func=AF.Exp),
                     reads=[B_["tmp80"]], writes=[B_["tmp80"]])
                P.op("dve", lambda h, dst=dst: h.tensor_tensor(out=dst, in0=tmp80[0:CH, :], in1=dt_tok, op=ALU.mult),
                     reads=[B_["tmp80"], B_["dttok"]], writes=[B_["wl"]])
            cb_ = [self.next_bank() for _ in range(3)]
            for c in range(NCH):
                P.op("pe", lambda h, c=c: h.matmul(self.ps[0:8, cb_[c // 4], (c % 4) * CH:(c % 4 + 1) * CH],
                                                   lhsT=da_tok[:, c * 8:(c + 1) * 8], rhs=tri, start=(c % 4 == 0), stop=True,
                                                   skip_group_check=True),
                     reads=[B_["datok"], B_["tab"]], writes=[self.b_ps[cb_[c // 4]]])
            for i3 in range(3):
                nch = min(4, NCH - 4 * i3)
                P.op("act", lambda h, i3=i3, nch=nch: h.activation(
                    out=cumT[:, 4 * i3 * CH:(4 * i3 + nch) * CH], in_=self.ps[0:8, cb_[i3], 0:nch * CH], func=AF.Copy),
                    reads=[self.b_ps[cb_[i3]]], writes=[B_["cumT"]])
            ub = self.next_bank()
            for c in range(NCH):
                s2 = c % 2
                P.op("dve", lambda h, c=c, s2=s2: h.tensor_tensor(
                    out=xw[:, s2, :].rearrange("p (j d) -> p j d", d=64), in0=x_tok[:, c, :].rearrange("p (j d) -> p j d", d=64),
                    in1=wloc[:, c * 8:(c + 1) * 8].unsqueeze(2).to_broadcast([CH, 8, 64]), op=ALU.mult),
                    reads=[B_["xtok"], B_["wl"]], writes=[B_["xw%d" % s2]])
                P.op("pe", lambda h, c=c, s2=s2: h.matmul(self.ps[:, ub, :], lhsT=B_tok[:, c, :], rhs=xw[:, s2, :],
                                                          start=(c == 0), stop=(c == NCH - 1)),
                     reads=[B_["Btok"], B_["xw%d" % s2]], writes=[self.b_ps[ub]])
            P.op("act", lambda h: h.activation(out=Sg[:, 0:512], in_=self.ps[:, ub, :], func=AF.Copy),
                 reads=[self.b_ps[ub]], writes=[B_["Sg"]])
            P.op("act", lambda h: h.activation(out=Sg[:, 512:520], in_=sfx[:, 0:8], func=AF.Exp),
                 reads=[B_["tot"]], writes=[B_["Sg"]])
            i = self.n_cc
            self.n_cc += 1
            bounce = nc.dram_tensor("cc_b%d" % i, [128, 520], F32)
            gathered = nc.dram_tensor("cc_g%d" % i, [4 * 128, 520], F32)
            b_b, b_g = Buf("ccb%d" % i), Buf("ccg%d" % i)
            P.op("sp", lambda h, bounce=bounce: h.dma_start(out=bounce.ap(), in_=Sg), reads=[B_["Sg"]], writes=[b_b], dma="cc_in")
            self.wprefetch()
            P.op("pool", lambda h, bounce=bounce, gathered=gathered: h.collective_compute(
                "AllGather", ALU.bypass, replica_groups=[[0, 1, 2, 3], [4, 5, 6, 7]],
                ins=[bounce.ap().opt()], outs=[gathered.ap().opt()]), reads=[b_b], writes=[b_g], dma="cc", unit=1)
            self.set_ring([0, 1, 2, 3, 4, 5])
            for q in range(4):
                w, wb = self.wacquire()
                for ti, (a, b) in enumerate(TTS):
                    bank = self.next_bank()
                    self.proj(w, wb, KT, 0, self.hnT, [self.b_hn], a, b, bank)
                    P.op("act", lambda h, q=q, a=a, b=b, bank=bank: h.activation(
                        out=zs[:, q, a:b], in_=self.ps[:, bank, 0:b - a], func=AF.Silu),
                        reads=[self.b_ps[bank]], writes=[B_["zs"]])
            P.op("dve", lambda h: h.memset(S, 0.0), writes=[B_["S"]])
            for r in range(4):
                P.op("sp", lambda h, r=r, gathered=gathered: h.dma_start(out=Sg, in_=gathered.ap()[r * 128:(r + 1) * 128, :]),
                     reads=[b_g], writes=[B_["Sg"]], dma="sg")
                P.op("dve", lambda h, r=r: h.tensor_scalar(out=dmul, in0=Sg[:, 512:520], scalar1=-1.0, scalar2=selr[:, r:r + 1],
                                                           op0=ALU.add, op1=ALU.mult),
                     reads=[B_["Sg"], B_["tab"]], writes=[B_["dmul"]])
                P.op("dve", lambda h: h.tensor_scalar_add(out=dmul, in0=dmul, scalar1=1.0), reads=[B_["dmul"]], writes=[B_["dmul"]])
                P.op("dve", lambda h: h.tensor_tensor(
                    out=S.rearrange("p (j d) -> p j d", d=64), in0=S.rearrange("p (j d) -> p j d", d=64),
                    in1=dmul.unsqueeze(2).to_broadcast([128, 8, 64]), op=ALU.mult), reads=[B_["S"], B_["dmul"]], writes=[B_["S"]])
                P.op("dve", lambda h, r=r: h.scalar_tensor_tensor(out=S, in0=Sg[:, 0:512], scalar=selr[:, r:r + 1], in1=S,
                                                                  op0=ALU.mult, op1=ALU.add),
                     reads=[B_["Sg"], B_["tab"], B_["S"]], writes=[B_["S"]])
            P.op("act", lambda h: h.activation(out=S_bf, in_=S, func=AF.Copy), reads=[B_["S"]], writes=[B_["Sbf"]])
            rb, yb, cbk, ub = [0, 1], [2, 3], 4, 5

            def front_a(c):
                c0, s = c * CH, c % 2
                for bk in range(2):
                    def rmm(h, bk=bk, c0=c0):
                        ins = None
                        for jj in range(4):
                            j = bk * 4 + jj
                            ins = h.matmul(self.ps[:, rb[bk], jj * CH:(jj + 1) * CH], lhsT=sel[:, j * 128:(j + 1) * 128],
                                           rhs=cumT[:, c0:c0 + CH], start=(jj == 0), stop=True, skip_group_check=True)
                        return ins
                    P.op("pe", rmm, reads=[B_["cumT"], B_["tab"]], writes=[self.b_ps[rb[bk]]])
                P.op("pe", lambda h, c0=c0: h.matmul(self.ps[0:CH, cbk, 0:CH], lhsT=BT[:, c0:c0 + CH], rhs=CT[:, c0:c0 + CH],
                                                     start=True, stop=True), reads=[B_["BT"], B_["CT"]], writes=[self.b_ps[cbk]])
                for j in range(8):
                    P.op("act", lambda h, j=j, c=c, s=s: h.activation(
                        out=E[:, s, j, :], in_=self.ps[0:CH, rb[j // 4], (j % 4) * CH:(j % 4 + 1) * CH], func=AF.Exp,
                        bias=negcum[:, c * 8 + j:c * 8 + j + 1], scale=1.0),
                        reads=[self.b_ps[rb[j // 4]], B_["cumtok"]], writes=[B_["E%d" % s]])
                for bk in range(2):
                    P.op("act", lambda h, bk=bk, s=s: h.activation(
                        out=eR[:, s, bk * 4:bk * 4 + 4, :], in_=self.ps[:, rb[bk], 0:4 * CH].rearrange("p (j l) -> p j l", l=CH),
                        func=AF.Exp), reads=[self.b_ps[rb[bk]]], writes=[B_["eR%d" % s]])

            def front_b(c):
                c0, s = c * CH, c % 2
                P.op("dve", lambda h: h.tensor_tensor(out=CBm, in0=self.ps[0:CH, cbk, 0:CH], in1=tri, op=ALU.mult),
                     reads=[self.b_ps[cbk], B_["tab"]], writes=[B_["CBm"]])
                P.op("dve", lambda h, s=s: h.scalar_tensor_tensor(
                    out=E[:, s, :, :], in0=E[:, s, :, :], scalar=1.0, in1=CBm.unsqueeze(1).to_broadcast([CH, 8, CH]),
                    op0=ALU.min, op1=ALU.mult), reads=[B_["E%d" % s], B_["CBm"]], writes=[B_["E%d" % s]])
                P.op("dve", lambda h, c0=c0, s=s: h.tensor_tensor(
                    out=eR[:, s, :, :], in0=eR[:, s, :, :], in1=CT[:, c0:c0 + CH].unsqueeze(1).to_broadcast([128, 8, CH]), op=ALU.mult),
                    reads=[B_["eR%d" % s], B_["CT"]], writes=[B_["eR%d" % s]])
                P.op("dve", lambda h, c=c: h.tensor_tensor(
                    out=xdt.rearrange("p (j d) -> p j d", d=64), in0=x_tok[:, c, :].rearrange("p (j d) -> p j d", d=64),
                    in1=dt_tok[:, c * 8:(c + 1) * 8].unsqueeze(2).to_broadcast([CH, 8, 64]), op=ALU.mult),
                    reads=[B_["xtok"], B_["dttok"]], writes=[B_["xdt"]])
                P.op("dve", lambda h, c=c, s=s: h.tensor_tensor(
                    out=xw[:, s, :].rearrange("p (j d) -> p j d", d=64), in0=x_tok[:, c, :].rearrange("p (j d) -> p j d", d=64),
                    in1=wend[:, c * 8:(c + 1) * 8].unsqueeze(2).to_broadcast([CH, 8, 64]), op=ALU.mult),
                    reads=[B_["xtok"], B_["wl"]], writes=[B_["xw%d" % s]])

            def back_pe(c):
                s = c % 2
                for bk in range(2):
                    def ymm(h, bk=bk, s=s):
                        ins = None
                        for jj in range(4):
                            j = bk * 4 + jj
                            o_ = self.ps[:, yb[bk], jj * CH:(jj + 1) * CH]
                            h.matmul(o_, lhsT=xdt[:, (j // 2) * 128:(j // 2 + 1) * 128], rhs=E[:, s, j, :],
                                     start=(jj == 0), stop=False, skip_group_check=True)
                            ins = h.matmul(o_, lhsT=S_bf[:, (j // 2) * 128:(j // 2 + 1) * 128], rhs=eR[:, s, j, :],
                                           start=False, stop=True, skip_group_check=True)
                        return ins
                    P.op("pe", ymm, reads=[B_["xdt"], B_["E%d" % s], B_["Sbf"], B_["eR%d" % s]], writes=[self.b_ps[yb[bk]]])
                P.op("pe", lambda h, c=c, s=s: h.matmul(self.ps[:, ub, :], lhsT=B_tok[:, c, :], rhs=xw[:, s, :], start=True, stop=True),
                     reads=[B_["Btok"], B_["xw%d" % s]], writes=[self.b_ps[ub]])

            def back_rest(c):
                c0 = c * CH
                for bk in range(2):
                    for par in range(2):
                        src = self.ps[64 * par:64 * par + 64, yb[bk], 0:4 * CH].rearrange("p (q r l) -> p q r l", r=2, l=CH)[:, :, par, :]
                        P.op("dve", lambda h, src=src, par=par, bk=bk, c0=c0: h.tensor_tensor(
                            out=yT[64 * par:64 * par + 64, 2 * bk:2 * bk + 2, c0:c0 + CH], in0=src,
                            in1=xDT[64 * par:64 * par + 64, 2 * bk:2 * bk + 2, c0:c0 + CH], op=ALU.add),
                            reads=[self.b_ps[yb[bk]], B_["xDT"]], writes=[B_["yT"]])
                P.op("dve", lambda h, c=c: h.tensor_tensor(
                    out=S.rearrange("p (j d) -> p j d", d=64), in0=S.rearrange("p (j d) -> p j d", d=64),
                    in1=eend[:, c * 8:(c + 1) * 8].unsqueeze(2).to_broadcast([128, 8, 64]), op=ALU.mult),
                    reads=[B_["S"], B_["tot"]], writes=[B_["S"]])
                P.op("dve", lambda h: h.tensor_tensor(out=S, in0=S, in1=self.ps[:, ub, :], op=ALU.add),
                     reads=[B_["S"], self.b_ps[ub]], writes=[B_["S"]])
                P.op("act", lambda h: h.activation(out=S_bf, in_=S, func=AF.Copy), reads=[B_["S"]], writes=[B_["Sbf"]])

            front_a(0)
            front_b(0)
            for c in range(NCH):
                if c + 1 < NCH:
                    front_a(c + 1)
                back_pe(c)
                back_rest(c)
                if c + 1 < NCH:
                    front_b(c + 1)
            self.set_ring([0, 1, 2, 3, 4, 5])
            nb_ = [self.next_bank() for _ in TTS]
            for q in range(4):
                P.op("dve", lambda h, q=q: h.tensor_tensor(out=yT[:, q, :], in0=yT[:, q, :], in1=zs[:, q, :], op=ALU.mult),
                     reads=[B_["yT"], B_["zs"]], writes=[B_["yT"]])
                P.op("act", lambda h, q=q: h.activation(out=self.sq, in_=yT[:, q, :], func=AF.Square),
                     reads=[B_["yT"]], writes=[self.b_sq])

                def mm(h, q=q):
                    ins = None
                    for ti, (a, b) in enumerate(TTS):
                        ins = h.matmul(self.ps[:, nb_[ti], 0:b - a], lhsT=self.ones_bf[:, :], rhs=self.sq[:, a:b],
                                       start=(q == 0), stop=(q == 3))
                    return ins
                P.op("pe", mm, reads=[self.b_sq, self.b_const], writes=[self.b_ps[b] for b in nb_])
            for ti, (a, b) in enumerate(TTS):
                P.op("act", lambda h, ti=ti, a=a, b=b: h.activation(
                    out=self.rstd[:, a:b], in_=self.ps[:, nb_[ti], 0:b - a], func=AF.Sqrt, bias=float(EPS), scale=1.0 / 512.0),
                    reads=[self.b_ps[nb_[ti]]], writes=[self.b_rstd])
            P.op("dve", lambda h: h.reciprocal(out=self.rstd, in_=self.rstd), reads=[self.b_rstd], writes=[self.b_rstd])
            for q in range(4):
                P.op("dve", lambda h, q=q, g=g: h.scalar_tensor_tensor(
                    out=yT[:, q, :], in0=yT[:, q, :], scalar=gnw[:, g * 4 + q:g * 4 + q + 1], in1=self.rstd,
                    op0=ALU.mult, op1=ALU.mult), reads=[B_["yT"], B_["tab"], self.b_rstd], writes=[B_["yT"]])
            for nb in range(4):
                w, wb = self.wacquire()
                for q in range(4):
                    n_ = nb * 4 + q
                    for ti, (a, b) in enumerate(TTS):
                        bank = self.next_bank()
                        self.proj(w, wb, 4, q * 128, yT, [B_["yT"]], a, b, bank)
                        self.resid_add(n_, a, b, bank)
        self.mask_pre()

    def tok_transposes(self, srcT, b_src, dst_tok, b_dst, col0, kfac, hd, b_tab):
        P = self.P
        for (c0, c1, pb) in ((0, 8, 0), (8, NCH, 1)):
            def tr(h, c0=c0, c1=c1, pb=pb):
                ins = None
                for c in range(c0, c1):
                    ins = h.transpose(self.psb[0:CH, pb, (c - c0) * 128:(c - c0 + 1) * 128],
                                      srcT[:, c * CH:(c + 1) * CH], self.identb[:, :])
                return ins
            P.op("pe", tr, reads=[b_src, self.b_const], writes=[self.b_psb[pb]])
            nch = c1 - c0
            src = self.psb[0:CH, pb, 0:nch * 128].rearrange("p (c d) -> p c d", d=128)
            dst = dst_tok[:, c0:c1, col0:col0 + 128]
            if kfac is None:
                P.op("act", lambda h, src=src, dst=dst: h.activation(out=dst, in_=src, func=AF.Copy),
                     reads=[self.b_psb[pb]], writes=[b_dst])
            else:
                kf = kfac[:, hd * NCH + c0:hd * NCH + c1].unsqueeze(2).to_broadcast([CH, nch, 128])
                P.op("dve", lambda h, src=src, dst=dst, kf=kf: h.tensor_tensor(out=dst, in0=src, in1=kf, op=ALU.mult),
                     reads=[self.b_psb[pb], b_tab], writes=[b_dst])


def col_layout(v):
    return np.ascontiguousarray(v.reshape(-1, 128).T)


def bf16(a):
    import ml_dtypes
    return np.asarray(a, np.float32).astype(ml_dtypes.bfloat16)


def core_consts(c):
    p = c % 4
    ident = np.eye(128, dtype=np.float32)
    premask = np.full((128, PRE), 1.0 if p == 0 else 0.0, np.float32)
    halosel = np.zeros((128, 4), np.float32)
    if p > 0:
        halosel[:, p - 1] = 1.0
    return {"c_ident": ident, "c_premask": premask, "c_halosel": halosel}


def ret_consts(c):
    p = c % 4
    half = 128
    inv = (10000.0 ** (-np.arange(half, dtype=np.float32) / half)).astype(np.float32)
    pos = (p * OWN + np.arange(T)).astype(np.float32)
    ang = (pos[None, :] * inv[:, None]).astype(np.float32)
    lg = np.log1p(-np.exp2(-5.0 - np.arange(RH, dtype=np.float64)))
    s = np.arange(CH)[:, None]
    jj = np.arange(T)[None, :]
    G = np.where(jj >= s, np.exp((jj - s)[None] * lg[:, None, None]), 0.0) / 16.0
    Q = np.exp((np.arange(QB)[None, :] + 1.0) * lg[:, None])
    Qrep = np.broadcast_to(Q.reshape(1, RH * QB), (128, RH * QB))
    cc = np.arange(NCH)[None, None, :]
    Kf = np.exp((T - 1 - CH * cc - s[:, :, None]) * lg[None, :, None]) / 16.0
    coef = np.zeros((4, RH))
    for r in range(4):
        if r < p:
            coef[r] = np.exp((OWN * (p - 1 - r) - PRE) * lg)
    return {
        "c_cos": bf16(np.cos(ang)), "c_sin": bf16(np.sin(ang)),
        "c_retG": bf16(G), "c_retQ": bf16(Qrep),
        "c_retK": np.ascontiguousarray(Kf.reshape(CH, RH * NCH).astype(np.float32)),
        "c_retcoef": np.ascontiguousarray(np.broadcast_to(coef.reshape(1, 4 * RH), (128, 4 * RH)).astype(np.float32)),
    }


def ffn_inputs(idx, ffn_norm_w, ffn_w_up, ffn_conv_w, ffn_conv_b, ffn_w_down):
    return {
        "ffn_nw%d" % idx: col_layout(ffn_norm_w[idx]),
        "ffn_wup%d" % idx: ffn_w_up[idx],
        "ffn_cw%d" % idx: np.ascontiguousarray(ffn_conv_w[idx].reshape(3, FT, 128).transpose(2, 1, 0)),
        "ffn_cb%d" % idx: col_layout(ffn_conv_b[idx]),
        "ffn_wdn%d" % idx: ffn_w_down[idx],
    }


def ret_inputs(j, ret_norm_w, ret_w_in, ret_gn_w, ret_w_out):
    return {
        "ret_nw%d" % j: col_layout(ret_norm_w[j]),
        "ret_win%d" % j: ret_w_in[j],
        "ret_gn%d" % j: col_layout(ret_gn_w[j]),
        "ret_wout%d" % j: ret_w_out[j],
    }


def ssd_consts(c):
    p = c % 4
    tri = (np.arange(CH)[:, None] <= np.arange(CH)[None, :]).astype(np.float32)
    ones = np.ones((CH, 128), np.float32)
    sel = np.zeros((8, 8, 128), np.float32)
    for j in range(8):
        sel[j, j, :] = 1.0
    selr = np.zeros((128, 4), np.float32)
    selr[:, :p] = 1.0
    return {"c_tri": tri, "c_ones": ones, "c_sel": sel.reshape(8, 8 * 128), "c_selr": selr}


def ssd_inputs(j, ssd_norm_w, ssd_w_in, ssd_conv_w, ssd_conv_b, ssd_dt_bias, ssd_a_log, ssd_d, ssd_gnorm_w, ssd_w_out):
    return {
        "ssd_nw%d" % j: col_layout(ssd_norm_w[j]),
        "ssd_win%d" % j: ssd_w_in[j],
        "ssd_cw%d" % j: np.ascontiguousarray(ssd_conv_w[j].reshape(4, 48, 128).transpose(2, 1, 0).reshape(128, 48 * 4)),
        "ssd_cb%d" % j: col_layout(ssd_conv_b[j]),
        "ssd_dtb%d" % j: np.ascontiguousarray(ssd_dt_bias[j].reshape(8, 8).T),
        "ssd_alog%d" % j: np.ascontiguousarray(ssd_a_log[j].reshape(8, 8).T),
        "ssd_dcol%d" % j: col_layout(np.repeat(ssd_d[j], 64)),
        "ssd_gn%d" % j: col_layout(ssd_gnorm_w[j]),
        "ssd_wout%d" % j: ssd_w_out[j],
    }


FUSE_GROUPS = [[("ret", 0), ("ffn", 0), ("ssd", 0), ("ffn", 1), ("ret", 1), ("ffn", 2), ("ssd", 1), ("ffn", 3)]]


def _sub_inputs(kind, idx, name_idx, inp):
    if kind == "ffn":
        d = ffn_inputs(idx, inp["ffn_norm_w"], inp["ffn_w_up"], inp["ffn_conv_w"], inp["ffn_conv_b"], inp["ffn_w_down"])
    elif kind == "ret":
        d = ret_inputs(idx, inp["ret_norm_w"], inp["ret_w_in"], inp["ret_gn_w"], inp["ret_w_out"])
    else:
        d = ssd_inputs(idx, inp["ssd_norm_w"], inp["ssd_w_in"], inp["ssd_conv_w"], inp["ssd_conv_b"], inp["ssd_dt_bias"],
                       inp["ssd_a_log"], inp["ssd_d"], inp["ssd_gnorm_w"], inp["ssd_w_out"])
    if name_idx != idx:
        d = {k[:-len(str(idx))] + str(name_idx): v for k, v in d.items()}
    return d


def kernel(**inputs):
    inp = {k: np.ascontiguousarray(np.asarray(v, dtype=np.float32)) for k, v in inputs.items()}
    x, meta = inp["x"], inp["meta_tokens"]
    hs = []
    for c in range(NCORES):
        b, p = c // 4, c % 4
        xin = np.zeros((T, D), np.float32)
        if p == 0:
            xin[:PRE] = meta
        xin[PRE:] = x[b, p * OWN:(p + 1) * OWN]
        hs.append(xin)
    consts = []
    for c in range(NCORES):
        d = core_consts(c)
        d.update(ret_consts(c))
        d.update(ssd_consts(c))
        consts.append(d)
    progs = {}
    for gi, group in enumerate(FUSE_GROUPS):
        last = gi == len(FUSE_GROUPS) - 1
        slots, counts = [], {}
        for kind, idx in group:
            s = counts.get(kind, 0)
            counts[kind] = s + 1
            slots.append((kind, s))
        key = (tuple(slots), last)
        if key not in progs:
            progs[key] = Builder(slots, final_norm=last).build()
        nc = progs[key]
        names = set(a.memorylocations[0].name for a in nc.allocations
                    if isinstance(a, mybir.MemoryLocationSet) and a.kind == "ExternalInput")
        w = {}
        for (kind, idx), (_, s) in zip(group, slots):
            w.update(_sub_inputs(kind, idx, s, inp))
        if last:
            w["fin_nw"] = col_layout(inp["final_norm_w"])
        in_maps = []
        for c in range(NCORES):
            m = {"xin": hs[c]}
            m.update(consts[c])
            m.update(w)
            in_maps.append({k: v for k, v in m.items() if k in names})
        res = run_bass_kernel_spmd(nc, in_maps, core_ids=list(range(NCORES)))
        hs = [np.asarray(res.results[c]["out"]) for c in range(NCORES)]
    out = np.empty((2, 4 * OWN, D), np.float32)
    for c in range(NCORES):
        b, p = c // 4, c % 4
        out[b, p * OWN:(p + 1) * OWN] = hs[c][PRE:]
    return out
```

```python
import contextlib
import numpy as np
import concourse.bass as bass
import concourse.mybir as mybir
from concourse.bass_utils import run_bass_kernel_spmd

F32 = mybir.dt.float32
BF16 = mybir.dt.bfloat16
AF = mybir.ActivationFunctionType
ALU = mybir.AluOpType

NCORES = 8
D = 2048
KT = 16
PRE = 16
OWN = 1024
T = PRE + OWN
CH = 104
NCH = T // CH
TTS = [(0, 347), (347, 694), (694, 1040)]
EPS = 1e-6
FFN = 5632
FT = FFN // 128
FG = 4
FGT = FT // FG
DEPTH = 4
WSLOT = 2048
NWSLOT = 4


class Buf:
    __slots__ = ("name", "writer", "readers")

    def __init__(self, name):
        self.name = name
        self.writer = None
        self.readers = {}


class Prog:
    ENGS = ("pe", "act", "dve", "pool", "sp")

    def __init__(self, nc):
        self.nc = nc
        self.ops = {e: [] for e in self.ENGS}
        self.seen = {e: {} for e in self.ENGS}
        self.dma_count = {}
        self.dma_unit = {}

    def op(self, eng, fn, reads=(), writes=(), dma=None, unit=16):
        deps = []
        for b in reads:
            if b.writer is not None:
                deps.append(b.writer)
        for b in writes:
            if b.writer is not None:
                deps.append(b.writer)
            deps.extend(b.readers.values())
        waits = []
        seen = self.seen[eng]
        for d in deps:
            if d[0] == "eng":
                _, e2, idx = d
                if e2 == eng and eng in ("pe", "sp"):
                    continue
                if seen.get(e2, -1) >= idx:
                    continue
                seen[e2] = idx
            else:
                _, key, cnt = d
                k = ("dma", key)
                if seen.get(k, -1) >= cnt:
                    continue
                seen[k] = cnt
            waits.append(d)
        idx = len(self.ops[eng])
        if dma is not None:
            c = self.dma_count.get(dma, 0) + 1
            self.dma_count[dma] = c
            self.dma_unit[dma] = unit
            tok = ("dma", dma, c)
            rkey = ("dma", dma)
        else:
            tok = ("eng", eng, idx)
            rkey = eng
        self.ops[eng].append(dict(fn=fn, waits=waits, tok=tok, inc=False))
        for b in reads:
            b.readers[rkey] = tok
        for b in writes:
            b.writer = tok
            b.readers = {}
        return tok

    def emit(self, final_bufs=()):
        nc = self.nc
        self.op("sp", None, reads=list(final_bufs))
        for e in self.ENGS:
            for o in self.ops[e]:
                for w in o["waits"]:
                    if w[0] == "eng":
                        self.ops[w[1]][w[2]]["inc"] = True
        rank = {}
        for e in self.ENGS:
            r = 0
            rk = []
            for o in self.ops[e]:
                if o["inc"] and o["tok"][0] == "eng":
                    r += 1
                rk.append(r)
            rank[e] = rk
        with contextlib.ExitStack() as st:
            esem = {e: st.enter_context(nc.semaphore("s_" + e)) for e in ("pe", "act", "dve", "pool")}
            dsem = {k: st.enter_context(nc.semaphore("d_%s" % str(k))) for k in self.dma_count}
            block = st.enter_context(nc.Block())
            handles = {"pe": block.tensor, "act": block.scalar, "dve": block.vector,
                       "pool": block.gpsimd, "sp": block.sync}

            def make(e):
                def body(h):
                    for o in self.ops[e]:
                        for w in o["waits"]:
                            if w[0] == "eng":
                                h.wait_ge(esem[w[1]], rank[w[1]][w[2]])
                            else:
                                h.wait_ge(dsem[w[1]], self.dma_unit[w[1]] * w[2])
                        if o["fn"] is None:
                            continue
                        ins = o["fn"](h)
                        tok = o["tok"]
                        if tok[0] == "dma":
                            if self.dma_unit[tok[1]] == 1:
                                ins.then_inc(dsem[tok[1]])
                            else:
                                ins.then_inc(dsem[tok[1]], self.dma_unit[tok[1]])
                        elif o["inc"]:
                            ins.then_inc(esem[e], 1)
                return body
            for e in self.ENGS:
                handles[e](make(e))


NF_ARENA = 9088
NB_ARENA = 28544
RH = 8
QB = 208
NQB = T // QB
GAMMA = [1.0 - 2.0 ** (-5.0 - h) for h in range(RH)]


class Builder:
    def __init__(self, sublayers, final_norm):
        self.sublayers = sublayers
        self.final_norm = final_norm
        self.nc = bass.Bass("TRN2", target_bir_lowering=False)
        self.P = Prog(self.nc)
        self.st = contextlib.ExitStack()
        self.dram = {}
        self.wblocks = []
        self.w_issued = 0
        self.w_next = 0
        self.ring = [0, 1, 2, 3, 4, 5]
        self.ring_i = 0
        self.n_cc = 0
        self.arena_bufs = []

    def din(self, name, shape, dt=F32):
        if name in self.dram:
            return self.dram[name]
        t = self.nc.dram_tensor(name, list(shape), dt, kind="ExternalInput").ap()
        self.dram[name] = t
        return t

    def sb(self, name, shape, dt):
        return self.st.enter_context(self.nc.sbuf_tensor(name, list(shape), dt))

    def set_ring(self, banks):
        self.ring = list(banks)
        self.ring_i = 0

    def next_bank(self):
        b = self.ring[self.ring_i % len(self.ring)]
        self.ring_i += 1
        return b

    def arena_begin(self):
        self.oF = 0
        self.oB = 0
        old = self.arena_bufs
        self.arena_bufs = []
        self._old_arena = old

    def arena_end(self):
        bufs = self._old_arena + self.arena_bufs
        self.P.op("dve", lambda h: h.memset(self.dummy[:], 0.0), writes=bufs + [self.b_dummy])

    def cF(self, n, parts=128):
        ap = self.arenaF[0:parts, self.oF:self.oF + n]
        self.oF += n
        assert self.oF <= NF_ARENA, ("fp32 arena overflow", self.oF)
        return ap

    def cB(self, n, parts=128):
        ap = self.arenaB[0:parts, self.oB:self.oB + n]
        self.oB += n
        assert self.oB <= NB_ARENA, ("bf16 arena overflow", self.oB)
        return ap

    def abuf(self, name):
        b = Buf(name)
        self.arena_bufs.append(b)
        return b

    def wreg(self, view, kt, n):
        assert kt * n <= WSLOT
        self.wblocks.append((view, kt, n))

    def _wissue(self, i):
        view, kt, n = self.wblocks[i]
        slot = i % NWSLOT
        dst = self.wsb[:, slot, 0:kt * n].rearrange("p (k n) -> p k n", n=n)
        self.P.op("pool", lambda h, dst=dst, view=view: h.dma_start(out=dst, in_=view),
                  writes=[self.wbuf[slot]], dma="w%d" % slot)

    def wprefetch(self):
        while self.w_issued < min(len(self.wblocks), self.w_next + NWSLOT):
            self._wissue(self.w_issued)
            self.w_issued += 1

    def wacquire(self, count=1):
        i = self.w_next
        self.w_next += count
        while self.w_issued < min(len(self.wblocks), i + NWSLOT):
            self._wissue(self.w_issued)
            self.w_issued += 1
        res = []
        for ii in range(i, i + count):
            view, kt, n = self.wblocks[ii]
            slot = ii % NWSLOT
            res.append((self.wsb[:, slot, 0:kt * n].rearrange("p (k n) -> p k n", n=n), self.wbuf[slot]))
        return res[0] if count == 1 else res

    def build(self):
        nc, P = self.nc, self.P
        self.xin = self.din("xin", [T, D])
        self.c_ident = self.din("c_ident", [128, 128])
        self.c_premask = self.din("c_premask", [128, PRE])
        self.c_halosel = self.din("c_halosel", [128, 4])
        self.out = nc.dram_tensor("out", [T, D], F32, kind="ExternalOutput").ap()
        for kind, idx in self.sublayers:
            getattr(self, kind + "_inputs")(idx)
        if self.final_norm:
            self.din("fin_nw", [128, KT])

        self.hT = self.sb("hT", [128, KT, T], F32)
        self.hnT = self.sb("hnT", [128, KT, T], BF16)
        self.wsb = self.sb("wsb", [128, NWSLOT, WSLOT], BF16)
        self.ident = self.sb("ident", [128, 128], F32)
        self.identb = self.sb("identb", [128, 128], BF16)
        self.ones_bf = self.sb("ones_bf", [128, 128], BF16)
        self.premask = self.sb("premask", [128, PRE], F32)
        self.halosel = self.sb("halosel", [128, 4], F32)
        self.nwcol = self.sb("nwcol", [128, KT], F32)
        self.hg = self.sb("hg", [128, 4, KT * 3], BF16)
        self.halo = self.sb("halo", [128, KT * 3], F32)
        self.dummy = self.sb("dummy_t", [128, 2], F32)
        self.arenaF = self.sb("arenaF", [128, NF_ARENA], F32)
        self.arenaB = self.sb("arenaB", [128, NB_ARENA], BF16)
        self.ps = self.st.enter_context(nc.psum_tensor("ps", [128, 6, 512], F32))
        self.psb = self.st.enter_context(nc.psum_tensor("psb", [128, 2, 1024], BF16))

        self.b_hT = [Buf("hT%d" % k) for k in range(KT)]
        self.b_hn = Buf("hn")
        self.wbuf = [Buf("w%d" % s) for s in range(NWSLOT)]
        self.b_const = Buf("const")
        self.b_nw = Buf("nw")
        self.b_ps = [Buf("ps%d" % i) for i in range(6)]
        self.b_psb = [Buf("psb0"), Buf("psb1")]
        self.b_hg = Buf("hg")
        self.b_halo = Buf("halo")
        self.b_out = Buf("out")
        self.b_dummy = Buf("dummy")

        for kind, idx in self.sublayers:
            getattr(self, kind + "_wreg")(idx)

        P.op("sp", lambda h: h.dma_start(out=self.ident[:], in_=self.c_ident[:, :]), writes=[self.b_const], dma="c")
        P.op("sp", lambda h: h.dma_start(out=self.premask[:], in_=self.c_premask[:, :]), writes=[self.b_const], dma="c")
        P.op("sp", lambda h: h.dma_start(out=self.halosel[:], in_=self.c_halosel[:, :]), writes=[self.b_const], dma="c")
        P.op("dve", lambda h: h.memset(self.ones_bf[:], 1.0), writes=[self.b_const])
        P.op("dve", lambda h: h.tensor_copy(out=self.identb[:], in_=self.ident[:]), reads=[self.b_const], writes=[self.b_const])

        self.load_input()
        for kind, idx in self.sublayers:
            getattr(self, kind)(idx)
        self.store_output()
        P.emit(final_bufs=[self.b_out])
        self.st.close()
        return nc

    def io_arena(self):
        self.arena_begin()
        self.stage = self.cF(D, parts=CH)
        self.b_stage = self.abuf("stage")
        self.rstd = self.cF(T)
        self.b_rstd = self.abuf("rstd")
        self.sq = self.cB(T)
        self.b_sq = self.abuf("sq")
        self.arena_end()
        self.set_ring([0, 1, 2, 3, 4, 5])

    def load_input(self):
        P = self.P
        self.io_arena()
        for c in range(NCH):
            P.op("sp", lambda h, c=c: h.dma_start(out=self.stage, in_=self.xin[c * CH:(c + 1) * CH, :]),
                 writes=[self.b_stage], dma="st")
            for q in range(4):
                bank = self.next_bank()

                def tr(h, c=c, q=q, bank=bank):
                    ins = None
                    for i in range(4):
                        k = 4 * q + i
                        ins = h.transpose(self.ps[:, bank, i * CH:(i + 1) * CH],
                                          self.stage[:, k * 128:(k + 1) * 128], self.ident[0:CH, 0:CH])
                    return ins
                P.op("pe", tr, reads=[self.b_stage, self.b_const], writes=[self.b_ps[bank]])
                src = self.ps[:, bank, 0:4 * CH].rearrange("p (i t) -> p i t", t=CH)
                dst = self.hT[:, 4 * q:4 * q + 4, c * CH:(c + 1) * CH]
                if q % 2 == 0:
                    P.op("act", lambda h, src=src, dst=dst: h.activation(out=dst, in_=src, func=AF.Copy),
                         reads=[self.b_ps[bank]], writes=self.b_hT[4 * q:4 * q + 4])
                else:
                    P.op("dve", lambda h, src=src, dst=dst: h.tensor_copy(out=dst, in_=src),
                         reads=[self.b_ps[bank]], writes=self.b_hT[4 * q:4 * q + 4])

    def store_output(self):
        P = self.P
        self.io_arena()
        if self.final_norm:
            self.rmsnorm(self.dram["fin_nw"], to_hn=False)
        for c in range(NCH):
            for q in range(4):
                bank = self.next_bank()

                def tr(h, c=c, q=q, bank=bank):
                    ins = None
                    for i in range(4):
                        k = 4 * q + i
                        ins = h.transpose(self.ps[0:CH, bank, i * 128:(i + 1) * 128],
                                          self.hT[:, k, c * CH:(c + 1) * CH], self.ident[:, :])
                    return ins
                P.op("pe", tr, reads=self.b_hT[4 * q:4 * q + 4] + [self.b_const], writes=[self.b_ps[bank]])
                src = self.ps[0:CH, bank, :]
                dst = self.stage[:, q * 512:(q + 1) * 512]
                if q % 2 == 0:
                    P.op("act", lambda h, src=src, dst=dst: h.activation(out=dst, in_=src, func=AF.Copy),
                         reads=[self.b_ps[bank]], writes=[self.b_stage])
                else:
                    P.op("dve", lambda h, src=src, dst=dst: h.tensor_copy(out=dst, in_=src),
                         reads=[self.b_ps[bank]], writes=[self.b_stage])
            P.op("sp", lambda h, c=c: h.dma_start(out=self.out[c * CH:(c + 1) * CH, :], in_=self.stage),
                 reads=[self.b_stage], writes=[self.b_out], dma="o")

    def rmsnorm(self, nw_dram, to_hn=True):
        P = self.P
        P.op("sp", lambda h: h.dma_start(out=self.nwcol[:], in_=nw_dram[:, :]), writes=[self.b_nw], dma="nw")
        P.op("dve", lambda h: h.tensor_scalar_mul(out=self.nwcol[:], in0=self.nwcol[:], scalar1=float(np.sqrt(D))),
             reads=[self.b_nw], writes=[self.b_nw])
        banks = [self.next_bank() for _ in TTS]
        for k in range(KT):
            P.op("act", lambda h, k=k: h.activation(out=self.sq, in_=self.hT[:, k, :], func=AF.Square),
                 reads=[self.b_hT[k]], writes=[self.b_sq])

            def mm(h, k=k):
                ins = None
                for ti, (a, b) in enumerate(TTS):
                    ins = h.matmul(self.ps[:, banks[ti], 0:b - a], lhsT=self.ones_bf[:, :], rhs=self.sq[:, a:b],
                                   start=(k == 0), stop=(k == KT - 1))
                return ins
            P.op("pe", mm, reads=[self.b_sq, self.b_const], writes=[self.b_ps[b] for b in banks])
        for ti, (a, b) in enumerate(TTS):
            P.op("act", lambda h, ti=ti, a=a, b=b: h.activation(
                out=self.rstd[:, a:b], in_=self.ps[:, banks[ti], 0:b - a], func=AF.Sqrt, bias=float(D * EPS), scale=1.0),
                reads=[self.b_ps[banks[ti]]], writes=[self.b_rstd])
        P.op("dve", lambda h: h.reciprocal(out=self.rstd, in_=self.rstd), reads=[self.b_rstd], writes=[self.b_rstd])
        for k in range(KT):
            dst = self.hnT[:, k, :] if to_hn else self.hT[:, k, :]
            P.op("dve", lambda h, k=k, dst=dst: h.scalar_tensor_tensor(
                out=dst, in0=self.hT[:, k, :], scalar=self.nwcol[:, k:k + 1], in1=self.rstd,
                op0=ALU.mult, op1=ALU.mult), reads=[self.b_hT[k], self.b_nw, self.b_rstd],
                writes=[self.b_hn if to_hn else self.b_hT[k]])

    def halo_exchange(self):
        nc, P = self.nc, self.P
        i = self.n_cc
        self.n_cc += 1
        bounce = nc.dram_tensor("cc_b%d" % i, [128, KT * 3], BF16)
        gathered = nc.dram_tensor("cc_g%d" % i, [4 * 128, KT * 3], BF16)
        b_b, b_g = Buf("ccb%d" % i), Buf("ccg%d" % i)
        P.op("sp", lambda h: h.dma_start(out=bounce.ap().rearrange("p (k t) -> p k t", t=3), in_=self.hnT[:, :, T - 3:T]),
             reads=[self.b_hn], writes=[b_b], dma="cc_in")
        self.wprefetch()
        P.op("pool", lambda h: h.collective_compute("AllGather", ALU.bypass, replica_groups=[[0, 1, 2, 3], [4, 5, 6, 7]],
                                                    ins=[bounce.ap().opt()], outs=[gathered.ap().opt()]),
             reads=[b_b], writes=[b_g], dma="cc", unit=1)
        P.op("sp", lambda h: h.dma_start(out=self.hg[:], in_=gathered.ap().rearrange("(r p) f -> p r f", p=128)),
             reads=[b_g], writes=[self.b_hg], dma="cc_out")
        P.op("dve", lambda h: h.tensor_scalar_mul(out=self.halo[:], in0=self.hg[:, 0, :], scalar1=self.halosel[:, 0:1]),
             reads=[self.b_hg, self.b_const], writes=[self.b_halo])
        for r in range(1, 4):
            P.op("dve", lambda h, r=r: h.scalar_tensor_tensor(
                out=self.halo[:], in0=self.hg[:, r, :], scalar=self.halosel[:, r:r + 1], in1=self.halo[:],
                op0=ALU.mult, op1=ALU.add), reads=[self.b_hg, self.b_halo], writes=[self.b_halo])
        P.op("dve", lambda h: h.tensor_tensor(out=self.hnT[:, :, PRE - 3:PRE], in0=self.hnT[:, :, PRE - 3:PRE],
                                              in1=self.halo[:].rearrange("p (k t) -> p k t", t=3), op=ALU.add),
             reads=[self.b_hn, self.b_halo], writes=[self.b_hn])

    def mask_pre(self):
        self.P.op("dve", lambda h: h.tensor_tensor(
            out=self.hT[:, :, 0:PRE], in0=self.hT[:, :, 0:PRE],
            in1=self.premask[:].unsqueeze(1).to_broadcast([128, KT, PRE]), op=ALU.mult),
            reads=self.b_hT + [self.b_const], writes=self.b_hT)

    def proj(self, w, wb, kt, c0, rhsT, rhs_bufs, a, b, bank):
        def mm(h):
            ins = None
            for k in range(kt):
                ins = h.matmul(self.ps[:, bank, 0:b - a], lhsT=w[:, k, c0:c0 + 128], rhs=rhsT[:, k, a:b],
                               start=(k == 0), stop=(k == kt - 1))
            return ins
        self.P.op("pe", mm, reads=[wb] + list(rhs_bufs), writes=[self.b_ps[bank]])

    def resid_add(self, n, a, b, bank):
        self.P.op("dve", lambda h: h.tensor_tensor(
            out=self.hT[:, n, a:b], in0=self.hT[:, n, a:b], in1=self.ps[:, bank, 0:b - a], op=ALU.add),
            reads=[self.b_hT[n], self.b_ps[bank]], writes=[self.b_hT[n]])

    def ffn_inputs(self, idx):
        self.din("ffn_nw%d" % idx, [128, KT])
        self.din("ffn_wup%d" % idx, [D, 2 * FFN])
        self.din("ffn_cw%d" % idx, [128, FT, 3])
        self.din("ffn_cb%d" % idx, [128, FT])
        self.din("ffn_wdn%d" % idx, [FFN, D])

    def ffn_wreg(self, idx):
        wup = self.dram["ffn_wup%d" % idx]
        wdn = self.dram["ffn_wdn%d" % idx]
        for g in range(FG):
            for jj in range(FGT):
                j = g * FGT + jj
                self.wreg(wup[:, j * 128:(j + 1) * 128].rearrange("(k p) n -> p k n", p=128), KT, 128)
                self.wreg(wup[:, FFN + j * 128:FFN + (j + 1) * 128].rearrange("(k p) n -> p k n", p=128), KT, 128)
            for nb in range(16):
                self.wreg(wdn[g * FGT * 128:(g + 1) * FGT * 128, nb * 128:(nb + 1) * 128].rearrange(
                    "(k p) n -> p k n", p=128), FGT, 128)

    def ffn(self, idx):
        P = self.P
        self.arena_begin()
        self.rstd = self.cF(T)
        self.b_rstd = self.abuf("rstd")
        self.sq = self.cB(T)
        self.b_sq = self.abuf("sq")
        actT = self.cB(FGT * T).rearrange("p (j t) -> p j t", t=T)
        gpre = self.cF(2 * (T + 2)).rearrange("p (s t) -> p s t", s=2)
        gacc = self.cF(T)
        sg = self.cF(T)
        cw = self.cF(FT * 3).rearrange("p (j w) -> p j w", w=3)
        cb = self.cF(FT)
        b_act, b_gacc, b_sg, b_cw = self.abuf("actT"), self.abuf("gacc"), self.abuf("sg"), self.abuf("cw")
        b_gpre = [self.abuf("gpre0"), self.abuf("gpre1")]
        self.arena_end()
        self.set_ring([0, 1, 2, 3, 4, 5])

        P.op("dve", lambda h: h.memset(gpre[:, :, 0:2], 0.0), writes=b_gpre)
        P.op("sp", lambda h: h.dma_start(out=cw, in_=self.dram["ffn_cw%d" % idx][:, :, :]), writes=[b_cw], dma="cw")
        P.op("sp", lambda h: h.dma_start(out=cb, in_=self.dram["ffn_cb%d" % idx][:, :]), writes=[b_cw], dma="cw")
        self.rmsnorm(self.dram["ffn_nw%d" % idx])
        self.halo_exchange()
        for g in range(FG):
            for jj in range(FGT):
                j = g * FGT + jj
                s = j % 2
                wg, wgb = self.wacquire()
                gb = [self.next_bank() for _ in TTS]
                for ti, (a, b) in enumerate(TTS):
                    self.proj(wg, wgb, KT, 0, self.hnT, [self.b_hn], a, b, gb[ti])
                    P.op("act", lambda h, a=a, b=b, s=s, bank=gb[ti]: h.activation(
                        out=gpre[:, s, 2 + a:2 + b], in_=self.ps[:, bank, 0:b - a], func=AF.Copy),
                        reads=[self.b_ps[gb[ti]]], writes=[b_gpre[s]])
                wu, wub = self.wacquire()
                ub = [self.next_bank() for _ in TTS]
                for ti, (a, b) in enumerate(TTS):
                    self.proj(wu, wub, KT, 0, self.hnT, [self.b_hn], a, b, ub[ti])
                P.op("dve", lambda h, j=j, s=s: h.tensor_scalar(
                    out=gacc, in0=gpre[:, s, 2:T + 2], scalar1=cw[:, j, 2:3], scalar2=cb[:, j:j + 1],
                    op0=ALU.mult, op1=ALU.add), reads=[b_gpre[s], b_cw, b_sg], writes=[b_gacc])
                P.op("dve", lambda h, j=j, s=s: h.scalar_tensor_tensor(
                    out=gacc, in0=gpre[:, s, 1:T + 1], scalar=cw[:, j, 1:2], in1=gacc, op0=ALU.mult, op1=ALU.add),
                    reads=[b_gpre[s], b_cw, b_gacc], writes=[b_gacc])
                P.op("dve", lambda h, j=j, s=s: h.scalar_tensor_tensor(
                    out=gacc, in0=gpre[:, s, 0:T], scalar=cw[:, j, 0:1], in1=gacc, op0=ALU.mult, op1=ALU.add),
                    reads=[b_gpre[s], b_cw, b_gacc], writes=[b_gacc])
                P.op("act", lambda h: h.activation(out=sg, in_=gacc, func=AF.Silu), reads=[b_gacc], writes=[b_sg])
                for ti, (a, b) in enumerate(TTS):
                    P.op("dve", lambda h, a=a, b=b, jj=jj, bank=ub[ti]: h.tensor_tensor(
                        out=actT[:, jj, a:b], in0=sg[:, a:b], in1=self.ps[:, bank, 0:b - a], op=ALU.mult),
                        reads=[b_sg, self.b_ps[ub[ti]]], writes=[b_act])
            for n in range(16):
                wd, wdb = self.wacquire()
                for ti, (a, b) in enumerate(TTS):
                    bank = self.next_bank()
                    self.proj(wd, wdb, FGT, 0, actT, [b_act], a, b, bank)
                    self.resid_add(n, a, b, bank)
        self.mask_pre()

    def ret_inputs(self, j):
        self.din("ret_nw%d" % j, [128, KT])
        self.din("ret_win%d" % j, [D, 12288])
        self.din("ret_gn%d" % j, [128, 32])
        self.din("ret_wout%d" % j, [4096, D])
        self.din("c_cos", [128, T], BF16)
        self.din("c_sin", [128, T], BF16)
        self.din("c_retG", [RH, CH, T], BF16)
        self.din("c_retQ", [128, RH * QB], BF16)
        self.din("c_retK", [CH, RH * NCH])
        self.din("c_retcoef", [128, 4 * RH])

    def ret_wreg(self, j):
        win = self.dram["ret_win%d" % j]
        wout = self.dram["ret_wout%d" % j]
        for h in range(RH):
            def blk(c0):
                self.wreg(win[:, c0:c0 + 128].rearrange("(k p) n -> p k n", p=128), KT, 128)
                self.wreg(win[:, c0 + 128:c0 + 256].rearrange("(k p) n -> p k n", p=128), KT, 128)
            blk(4096 + h * 512)
            blk(4096 + h * 512 + 256)
            blk(2048 + h * 256)
            blk(h * 256)
            def outblk(hh):
                for nb in range(4):
                    self.wreg(wout[hh * 512:(hh + 1) * 512, nb * 512:(nb + 1) * 512].rearrange(
                        "(k p) n -> p k n", p=128), 4, 512)
            if h > 0:
                outblk(h - 1)
            blk(8192 + h * 512)
            blk(8192 + h * 512 + 256)
            if h == RH - 1:
                outblk(h)

    def ret(self, j):
        nc, P = self.nc, self.P
        self.arena_begin()
        self.rstd = self.cF(T)
        self.b_rstd = self.abuf("rstd")
        self.sq = self.cB(T)
        self.b_sq = self.abuf("sq")
        cosT, sinT = self.cB(T), self.cB(T)
        G = self.cB(T, parts=CH)
        qdecB = self.cB(RH * QB).rearrange("p (h l) -> p h l", l=QB)
        kfac = self.cF(RH * NCH, parts=CH)
        coef = self.cF(4 * RH)
        gnw = self.cF(32)
        kT = self.cB(2 * T).rearrange("p (d t) -> p d t", t=T)
        qT = self.cB(2 * T).rearrange("p (d t) -> p d t", t=T)
        qd = self.cB(2 * 2 * QB).rearrange("p (s d l) -> p s d l", s=2, d=2)
        vtmp = self.cB(T)
        v_tok = self.cB(NCH * 512, parts=CH).rearrange("p (c e) -> p c e", e=512)
        k_tok = self.cB(NCH * 256, parts=CH).rearrange("p (c d) -> p c d", d=256)
        yT = self.cB(4 * T).rearrange("p (e t) -> p e t", t=T)
        sqo = self.cB(4 * QB).rearrange("p (e l) -> p e l", l=QB)
        PT = self.cB(2 * QB, parts=CH).rearrange("p (s l) -> p s l", s=2)
        SinB = self.cB(1024).rearrange("p (d e) -> p d e", e=512)
        oT = self.cF(4 * QB).rearrange("p (e l) -> p e l", l=QB)
        rt = self.cF(2 * 347).rearrange("p (s l) -> p s l", s=2)
        sloc = self.cF(1024).rearrange("p (d e) -> p d e", e=512)
        sgath = self.cF(1024).rearrange("p (d e) -> p d e", e=512)
        SinF = self.cF(1024).rearrange("p (d e) -> p d e", e=512)
        rs = self.cF(QB)
        b_tab, b_G, b_kT, b_qT, b_vtmp = self.abuf("tab"), self.abuf("G"), self.abuf("kT"), self.abuf("qT"), self.abuf("vtmp")
        b_qd = [self.abuf("qd0"), self.abuf("qd1")]
        b_vtok, b_ktok, b_yT, b_sqo = self.abuf("vtok"), self.abuf("ktok"), self.abuf("yT"), self.abuf("sqo")
        b_PT = [self.abuf("PT0"), self.abuf("PT1")]
        b_SinB, b_oT, b_rt, b_sloc, b_sgath, b_SinF, b_rs = (self.abuf(n) for n in
                                                             ("SinB", "oT", "rt", "sloc", "sgath", "SinF", "rs"))
        self.arena_end()

        dr = self.dram
        for dst, src in ((cosT, dr["c_cos"][:, :]), (sinT, dr["c_sin"][:, :]),
                         (qdecB, dr["c_retQ"].rearrange("p (h l) -> p h l", l=QB)),
                         (kfac, dr["c_retK"][:, :]), (coef, dr["c_retcoef"][:, :]), (gnw, dr["ret_gn%d" % j][:, :])):
            P.op("sp", lambda h, dst=dst, src=src: h.dma_start(out=dst, in_=src), writes=[b_tab], dma="tab")
        self.set_ring([0, 1, 2, 3, 4, 5])
        self.rmsnorm(dr["ret_nw%d" % j])

        def out_proj():
            self.set_ring([0, 1, 2, 3, 4, 5])
            for nb in range(4):
                w, wb = self.wacquire()
                for q in range(4):
                    n_ = nb * 4 + q
                    for ti, (a, b) in enumerate(TTS):
                        bank = self.next_bank()
                        self.proj(w, wb, 4, q * 128, yT, [b_yT], a, b, bank)
                        self.resid_add(n_, a, b, bank)

        for hd in range(RH):
            gam = GAMMA[hd]
            P.op("sp", lambda h, hd=hd: h.dma_start(out=G, in_=dr["c_retG"][hd, :, :]), writes=[b_G], dma="G")
            self.set_ring([0, 1, 2, 3, 4, 5])
            def rot_proj(dstT, b_dst):
                (w, wb), (w2, wb2) = self.wacquire(2)
                for ti, (a, b) in enumerate(TTS):
                    b1, b2 = self.next_bank(), self.next_bank()
                    self.proj(w, wb, KT, 0, self.hnT, [self.b_hn], a, b, b1)
                    self.proj(w2, wb2, KT, 0, self.hnT, [self.b_hn], a, b, b2)
                    n = b - a
                    x1, x2 = self.ps[:, b1, 0:n], self.ps[:, b2, 0:n]
                    t1, t2 = rt[:, 0, 0:n], rt[:, 1, 0:n]
                    seq = [(t1, x1, cosT[:, a:b], None), (t2, x2, sinT[:, a:b], None), (dstT[:, 0, a:b], t1, t2, ALU.subtract),
                           (t1, x1, sinT[:, a:b], None), (t2, x2, cosT[:, a:b], None), (dstT[:, 1, a:b], t1, t2, ALU.add)]
                    for si, (o_, i0, i1, op_) in enumerate(seq):
                        fin = op_ is not None
                        rd = [b_rt] if fin else [self.b_ps[b1 if i0 is x1 else b2], b_tab]
                        P.op("dve", lambda h, o_=o_, i0=i0, i1=i1, op_=op_: h.tensor_tensor(
                            out=o_, in0=i0, in1=i1, op=(op_ or ALU.mult)), reads=rd + ([b_rt] if not fin else []),
                            writes=[b_dst] if fin else [b_rt])
            for half in range(2):
                for q in range(2):
                    w, wb = self.wacquire()
                    e = half * 2 + q
                    for ti, (a, b) in enumerate(TTS):
                        bank = self.next_bank()
                        self.proj(w, wb, KT, 0, self.hnT, [self.b_hn], a, b, bank)
                        P.op("act", lambda h, a=a, b=b, bank=bank: h.activation(
                            out=vtmp[:, a:b], in_=self.ps[:, bank, 0:b - a], func=AF.Copy),
                            reads=[self.b_ps[bank]], writes=[b_vtmp])
                    self.tok_transposes(vtmp, b_vtmp, v_tok, b_vtok, e * 128, None, None, b_tab)
            rot_proj(kT, b_kT)
            for dt in range(2):
                self.tok_transposes(kT[:, dt, :], b_kT, k_tok, b_ktok, dt * 128, kfac, hd, b_tab)
            sb_ = [0, 1]
            for dt in range(2):
                def mm(h, dt=dt):
                    ins = None
                    for c in range(NCH):
                        ins = h.matmul(self.ps[:, sb_[dt], :], lhsT=k_tok[:, c, dt * 128:(dt + 1) * 128], rhs=v_tok[:, c, :],
                                       start=(c == 0), stop=(c == NCH - 1))
                    return ins
                P.op("pe", mm, reads=[b_ktok, b_vtok], writes=[self.b_ps[sb_[dt]]])
                P.op("act", lambda h, dt=dt: h.activation(out=sloc[:, dt, :], in_=self.ps[:, sb_[dt], :], func=AF.Copy),
                     reads=[self.b_ps[sb_[dt]]], writes=[b_sloc])
            i = self.n_cc
            self.n_cc += 1
            bounce = nc.dram_tensor("cc_b%d" % i, [256, 512], F32)
            gathered = nc.dram_tensor("cc_g%d" % i, [4 * 256, 512], F32)
            b_b, b_g = Buf("ccb%d" % i), Buf("ccg%d" % i)
            P.op("sp", lambda h, bounce=bounce: h.dma_start(out=bounce.ap().rearrange("(d p) e -> p d e", p=128), in_=sloc),
                 reads=[b_sloc], writes=[b_b], dma="cc_in")
            self.wprefetch()
            P.op("pool", lambda h, bounce=bounce, gathered=gathered: h.collective_compute(
                "AllGather", ALU.bypass, replica_groups=[[0, 1, 2, 3], [4, 5, 6, 7]],
                ins=[bounce.ap().opt()], outs=[gathered.ap().opt()]), reads=[b_b], writes=[b_g], dma="cc", unit=1)
            rot_proj(qT, b_qT)
            if hd > 0:
                out_proj()
            self.set_ring([2, 3, 4, 5])
            for half in range(2):
                for q in range(2):
                    w, wb = self.wacquire()
                    e = half * 2 + q
                    for ti, (a, b) in enumerate(TTS):
                        bank = self.next_bank()
                        self.proj(w, wb, KT, 0, self.hnT, [self.b_hn], a, b, bank)
                        P.op("act", lambda h, e=e, a=a, b=b, bank=bank: h.activation(
                            out=yT[:, e, a:b], in_=self.ps[:, bank, 0:b - a], func=AF.Silu),
                            reads=[self.b_ps[bank]], writes=[b_yT])
                    P.op("dve", lambda h, e=e, hd=hd: h.tensor_scalar_mul(
                        out=yT[:, e, :], in0=yT[:, e, :], scalar1=gnw[:, hd * 4 + e:hd * 4 + e + 1]),
                        reads=[b_yT, b_tab], writes=[b_yT])
            for r in range(4):
                P.op("sp", lambda h, r=r, gathered=gathered: h.dma_start(
                    out=sgath, in_=gathered.ap()[r * 256:(r + 1) * 256, :].rearrange("(d p) e -> p d e", p=128)),
                    reads=[b_g], writes=[b_sgath], dma="sg")
                cf = coef[:, r * RH + hd:r * RH + hd + 1]
                if r == 0:
                    P.op("dve", lambda h, cf=cf: h.tensor_scalar_mul(out=SinF, in0=sgath, scalar1=cf),
                         reads=[b_sgath, b_tab], writes=[b_SinF])
                else:
                    P.op("dve", lambda h, cf=cf: h.scalar_tensor_tensor(out=SinF, in0=sgath, scalar=cf, in1=SinF,
                                                                        op0=ALU.mult, op1=ALU.add),
                         reads=[b_sgath, b_tab, b_SinF], writes=[b_SinF])
            P.op("act", lambda h: h.activation(out=SinB, in_=SinF, func=AF.Copy), reads=[b_SinF], writes=[b_SinB])
            pending_tail = None
            for qb in range(NQB):
                s = qb % 2
                q0b = qb * QB
                oa = [0, 1] if s == 0 else [2, 3]
                for dt in range(2):
                    P.op("dve", lambda h, dt=dt, s=s, q0b=q0b, qb=qb, hd=hd, gam=gam: h.scalar_tensor_tensor(
                        out=qd[:, s, dt, :], in0=qT[:, dt, q0b:q0b + QB], scalar=float(gam ** (QB * qb)),
                        in1=qdecB[:, hd, :], op0=ALU.mult, op1=ALU.mult),
                        reads=[b_qT, b_tab], writes=[b_qd[s]])
                nkc = 2 * qb + 2

                def emit_sc(kc):
                    k0 = kc * CH
                    q0 = max(q0b, k0)
                    n = q0b + QB - q0
                    sbank = 4 + (kc % 2)

                    def sc(h, k0=k0, q0=q0, n=n, sbank=sbank):
                        ins = None
                        for dt in range(2):
                            ins = h.matmul(self.ps[0:CH, sbank, 0:n], lhsT=kT[:, dt, k0:k0 + CH], rhs=qT[:, dt, q0:q0 + n],
                                           start=(dt == 0), stop=(dt == 1))
                        return ins
                    P.op("pe", sc, reads=[b_kT, b_qT], writes=[self.b_ps[sbank]])

                def emit_pv(kc):
                    k0 = kc * CH
                    q0 = max(q0b, k0)
                    n = q0b + QB - q0
                    sbank = 4 + (kc % 2)
                    ps_ = kc % 2
                    P.op("dve", lambda h, n=n, sbank=sbank, ps_=ps_, q0=q0, k0=k0: h.tensor_tensor(
                        out=PT[:, ps_, 0:n], in0=self.ps[0:CH, sbank, 0:n], in1=G[:, q0 - k0:q0 - k0 + n], op=ALU.mult),
                        reads=[self.b_ps[sbank], b_G], writes=[b_PT[ps_]])

                    def pv(h, kc=kc, q0=q0, n=n, ps_=ps_, oa=oa, q0b=q0b):
                        ins = None
                        off = q0 - q0b
                        for e in range(4):
                            ins = h.matmul(self.ps[:, oa[e // 2], (e % 2) * QB + off:(e % 2) * QB + off + n],
                                           lhsT=v_tok[:, kc, e * 128:(e + 1) * 128], rhs=PT[:, ps_, 0:n],
                                           start=(kc == 0 and e % 2 == 0), stop=False, skip_group_check=True)
                        return ins
                    P.op("pe", pv, reads=[b_vtok, b_PT[ps_]], writes=[self.b_ps[oa[0]], self.b_ps[oa[1]]])
                emit_sc(0)
                for kc in range(nkc):
                    if kc + 1 < nkc:
                        emit_sc(kc + 1)
                    emit_pv(kc)
                if pending_tail is not None:
                    pending_tail()
                    pending_tail = None

                def corr(h, s=s, oa=oa):
                    ins = None
                    for e in range(4):
                        for dt in range(2):
                            ins = h.matmul(self.ps[:, oa[e // 2], (e % 2) * QB:(e % 2) * QB + QB],
                                           lhsT=SinB[:, dt, e * 128:(e + 1) * 128], rhs=qd[:, s, dt, :],
                                           start=False, stop=(dt == 1), skip_group_check=True)
                    return ins
                P.op("pe", corr, reads=[b_SinB, b_qd[s]], writes=[self.b_ps[oa[0]], self.b_ps[oa[1]]])
                for i2 in range(2):
                    P.op("act", lambda h, i2=i2, oa=oa: h.activation(
                        out=oT[:, 2 * i2:2 * i2 + 2, :], in_=self.ps[:, oa[i2], 0:2 * QB].rearrange("p (e l) -> p e l", l=QB),
                        func=AF.Copy), reads=[self.b_ps[oa[i2]]], writes=[b_oT])
                P.op("act", lambda h: h.activation(out=sqo, in_=oT, func=AF.Square), reads=[b_oT], writes=[b_sqo])
                nbank = 4 + (qb % 2)

                def nsum(h, nbank=nbank):
                    ins = None
                    for e in range(4):
                        ins = h.matmul(self.ps[:, nbank, 0:QB], lhsT=self.ones_bf[:, :], rhs=sqo[:, e, :],
                                       start=(e == 0), stop=(e == 3))
                    return ins
                P.op("pe", nsum, reads=[b_sqo, self.b_const], writes=[self.b_ps[nbank]])
                P.op("act", lambda h, nbank=nbank: h.activation(out=rs, in_=self.ps[:, nbank, 0:QB], func=AF.Sqrt,
                                                                bias=float(EPS), scale=1.0 / 512.0),
                     reads=[self.b_ps[nbank]], writes=[b_rs])

                def tail(q0b=q0b):
                    P.op("dve", lambda h: h.reciprocal(out=rs, in_=rs), reads=[b_rs], writes=[b_rs])
                    P.op("dve", lambda h: h.tensor_tensor(out=oT, in0=oT, in1=rs.unsqueeze(1).to_broadcast([128, 4, QB]),
                                                          op=ALU.mult), reads=[b_oT, b_rs], writes=[b_oT])
                    P.op("dve", lambda h, q0b=q0b: h.tensor_tensor(out=yT[:, :, q0b:q0b + QB], in0=yT[:, :, q0b:q0b + QB],
                                                                   in1=oT, op=ALU.mult), reads=[b_oT, b_yT], writes=[b_yT])
                pending_tail = tail
            pending_tail()
            if hd == RH - 1:
                out_proj()
        if getattr(self, "debug", False):
            for name, ap, shape, dt, bufs in (("dbg_kT", kT, [128, 2, T], BF16, [b_kT]), ("dbg_qT", qT, [128, 2, T], BF16, [b_qT]),
                                              ("dbg_vtok", v_tok, [CH, NCH, 512], BF16, [b_vtok]),
                                              ("dbg_ktok", k_tok, [CH, NCH, 256], BF16, [b_ktok]),
                                              ("dbg_yT", yT, [128, 4, T], BF16, [b_yT]),
                                              ("dbg_SinF", SinF, [128, 2, 512], F32, [b_SinF]),
                                              ("dbg_hn", self.hnT[:, :, :], [128, KT, T], BF16, [self.b_hn])):
                o_ = nc.dram_tensor(name, shape, dt, kind="ExternalOutput").ap()
                P.op("sp", lambda h, o_=o_, ap=ap: h.dma_start(out=o_, in_=ap), reads=bufs, writes=[self.b_out], dma="dbg")
        self.mask_pre()

    def ssd_inputs(self, j):
        self.din("ssd_nw%d" % j, [128, KT])
        self.din("ssd_win%d" % j, [D, 10304])
        self.din("ssd_cw%d" % j, [128, 48 * 4])
        self.din("ssd_cb%d" % j, [128, 48])
        self.din("ssd_dtb%d" % j, [8, 8])
        self.din("ssd_alog%d" % j, [8, 8])
        self.din("ssd_dcol%d" % j, [128, 32])
        self.din("ssd_gn%d" % j, [128, 32])
        self.din("ssd_wout%d" % j, [4096, D])
        self.din("c_tri", [CH, CH])
        self.din("c_ones", [CH, 128])
        self.din("c_sel", [8, 8 * 128])
        self.din("c_selr", [128, 4])

    def ssd_wreg(self, j):
        win = self.dram["ssd_win%d" % j]
        wout = self.dram["ssd_wout%d" % j]

        def blk(c0, n=128):
            self.wreg(win[:, c0:c0 + n].rearrange("(k p) n -> p k n", p=128), KT, n)
        for g in range(8):
            for q in range(4):
                blk(4096 + g * 512 + q * 128)
            blk(8192 + g * 128)
            blk(9216 + g * 128)
            blk(10240 + g * 8, 8)
            for q in range(4):
                blk(g * 512 + q * 128)
            for nb in range(4):
                self.wreg(wout[g * 512:(g + 1) * 512, nb * 512:(nb + 1) * 512].rearrange(
                    "(k p) n -> p k n", p=128), 4, 512)

    def ssd(self, jl):
        nc, P = self.nc, self.P
        dr = self.dram
        NC8 = NCH * 8
        self.arena_begin()
        self.rstd = self.cF(T)
        self.b_rstd = self.abuf("rstd")
        self.sq = self.cB(T)
        self.b_sq = self.abuf("sq")
        xpre = self.cF(T + 3)
        acc = self.cF(347)
        dda = self.cF(T, parts=8)
        cumT = self.cF(T, parts=8)
        dt_tok = self.cF(NC8, parts=CH)
        da_tok = self.cF(NC8, parts=CH)
        cum_tok = self.cF(NC8, parts=CH)
        negcum = self.cF(NC8, parts=CH)
        tot = self.cF(NC8)
        sfx = self.cF(NC8)
        eend = self.cF(NC8)
        wloc = self.cF(NC8, parts=CH)
        wend = self.cF(NC8, parts=CH)
        tmp80 = self.cF(NC8)
        S = self.cF(512)
        Sg = self.cF(520)
        dmul = self.cF(8)
        tri = self.cF(CH, parts=CH)
        ones = self.cF(128, parts=CH)
        sel = self.cF(8 * 128, parts=8)
        selr = self.cF(4)
        cw = self.cF(48 * 4).rearrange("p (t w) -> p t w", w=4)
        cb = self.cF(48)
        dtb = self.cF(8, parts=8)
        nega = self.cF(8, parts=8)
        dcol = self.cF(32)
        gnw = self.cF(32)
        xs = self.cB(T)
        xDT = self.cB(4 * T).rearrange("p (q t) -> p q t", t=T)
        x_tok = self.cB(NCH * 512, parts=CH).rearrange("p (c f) -> p c f", f=512)
        BT = self.cB(T)
        CT = self.cB(T)
        B_tok = self.cB(NCH * 128, parts=CH).rearrange("p (c n) -> p c n", n=128)
        E = self.cB(2 * 8 * CH, parts=CH).rearrange("p (s j l) -> p s j l", s=2, l=CH)
        eR = self.cB(2 * 8 * CH).rearrange("p (s j l) -> p s j l", s=2, l=CH)
        CBm = self.cB(CH, parts=CH)
        xdt = self.cB(512, parts=CH)
        xw = self.cB(2 * 512, parts=CH).rearrange("p (s f) -> p s f", s=2)
        S_bf = self.cB(512)
        yT = self.cB(4 * T).rearrange("p (q t) -> p q t", t=T)
        zs = self.cB(4 * T).rearrange("p (q t) -> p q t", t=T)
        names = ("tab", "xpre", "acc", "dda", "cumT", "dttok", "datok", "cumtok", "tot", "wl", "S", "Sg", "dmul", "xs", "xDT",
                 "xtok", "BT", "CT", "Btok", "E0", "E1", "eR0", "eR1", "CBm", "xdt", "xw0", "xw1", "Sbf", "yT", "zs", "tmp80")
        B_ = {n: self.abuf(n) for n in names}
        self.arena_end()

        loads = ((tri, dr["c_tri"][:, :]), (ones, dr["c_ones"][:, :]), (sel, dr["c_sel"][:, :]), (selr, dr["c_selr"][:, :]),
                 (cw, dr["ssd_cw%d" % jl].rearrange("p (t w) -> p t w", w=4)), (cb, dr["ssd_cb%d" % jl][:, :]),
                 (dtb, dr["ssd_dtb%d" % jl][:, :]), (nega, dr["ssd_alog%d" % jl][:, :]),
                 (dcol, dr["ssd_dcol%d" % jl][:, :]), (gnw, dr["ssd_gn%d" % jl][:, :]))
        for dst, src in loads:
            P.op("sp", lambda h, dst=dst, src=src: h.dma_start(out=dst, in_=src), writes=[B_["tab"]], dma="tab")
        P.op("act", lambda h: h.activation(out=nega, in_=nega, func=AF.Exp), reads=[B_["tab"]], writes=[B_["tab"]])
        P.op("dve", lambda h: h.tensor_scalar_mul(out=nega, in0=nega, scalar1=-1.0), reads=[B_["tab"]], writes=[B_["tab"]])
        P.op("dve", lambda h: h.memset(xpre[:, 0:3], 0.0), writes=[B_["xpre"]])
        self.set_ring([0, 1, 2, 3, 4, 5])
        self.rmsnorm(dr["ssd_nw%d" % jl])
        self.halo_exchange()

        def conv_tile(tile, dst, b_dst, mask_pre, after_proj=None):
            w, wb = self.wacquire()
            for ti, (a, b) in enumerate(TTS):
                bank = self.next_bank()
                self.proj(w, wb, KT, 0, self.hnT, [self.b_hn], a, b, bank)
                P.op("act", lambda h, a=a, b=b, bank=bank: h.activation(
                    out=xpre[:, 3 + a:3 + b], in_=self.ps[:, bank, 0:b - a], func=AF.Copy),
                    reads=[self.b_ps[bank]], writes=[B_["xpre"]])
            if after_proj is not None:
                after_proj()
            for ti, (a, b) in enumerate(TTS):
                n = b - a
                P.op("dve", lambda h, a=a, b=b, n=n: h.tensor_scalar(
                    out=acc[:, 0:n], in0=xpre[:, 3 + a:3 + b], scalar1=cw[:, tile, 3:4], scalar2=cb[:, tile:tile + 1],
                    op0=ALU.mult, op1=ALU.add), reads=[B_["xpre"], B_["tab"]], writes=[B_["acc"]])
                for wi in range(3):
                    P.op("dve", lambda h, a=a, b=b, n=n, wi=wi: h.scalar_tensor_tensor(
                        out=acc[:, 0:n], in0=xpre[:, wi + a:wi + b], scalar=cw[:, tile, wi:wi + 1], in1=acc[:, 0:n],
                        op0=ALU.mult, op1=ALU.add), reads=[B_["xpre"], B_["tab"], B_["acc"]], writes=[B_["acc"]])
                P.op("act", lambda h, a=a, b=b, n=n: h.activation(out=dst[:, a:b], in_=acc[:, 0:n], func=AF.Silu),
                     reads=[B_["acc"]], writes=[b_dst])
            if mask_pre:
                P.op("dve", lambda h: h.tensor_tensor(out=dst[:, 0:PRE], in0=dst[:, 0:PRE], in1=self.premask[:], op=ALU.mult),
                     reads=[b_dst, self.b_const], writes=[b_dst])

        for g in range(8):
            self.set_ring([0, 1, 2, 3, 4, 5])
            pending = None
            for q in range(4):
                conv_tile(g * 4 + q, xs, B_["xs"], True, pending)

                def pending(q=q, g=g):
                    self.tok_transposes(xs, B_["xs"], x_tok, B_["xtok"], q * 128, None, None, B_["tab"])
                    P.op("dve", lambda h, q=q, g=g: h.tensor_scalar_mul(out=xDT[:, q, :], in0=xs,
                                                                        scalar1=dcol[:, g * 4 + q:g * 4 + q + 1]),
                         reads=[B_["xs"], B_["tab"]], writes=[B_["xDT"]])
            conv_tile(32 + g, BT, B_["BT"], False, pending)
            conv_tile(40 + g, CT, B_["CT"], False,
                      lambda: self.tok_transposes(BT, B_["BT"], B_tok, B_["Btok"], 0, None, None, B_["tab"]))
            w, wb = self.wacquire()
            for ti, (a, b) in enumerate(TTS):
                bank = self.next_bank()

                def mm(h, a=a, b=b, bank=bank, w=w):
                    ins = None
                    for k in range(KT):
                        ins = h.matmul(self.ps[0:8, bank, 0:b - a], lhsT=w[:, k, 0:8], rhs=self.hnT[:, k, a:b],
                                       start=(k == 0), stop=(k == KT - 1))
                    return ins
                P.op("pe", mm, reads=[wb, self.b_hn], writes=[self.b_ps[bank]])
                P.op("act", lambda h, a=a, b=b, bank=bank, g=g: h.activation(
                    out=dda[:, a:b], in_=self.ps[0:8, bank, 0:b - a], func=AF.Exp, bias=dtb[:, g:g + 1], scale=1.0),
                    reads=[self.b_ps[bank], B_["tab"]], writes=[B_["dda"]])
            P.op("act", lambda h: h.activation(out=dda, in_=dda, func=AF.Ln, bias=1.0, scale=1.0),
                 reads=[B_["dda"]], writes=[B_["dda"]])

            def tok8(dst, b_dst):
                bank = self.next_bank()

                def tr(h, bank=bank):
                    ins = None
                    for c in range(NCH):
                        ins = h.transpose(self.ps[0:CH, bank, c * 8:(c + 1) * 8], dda[:, c * CH:(c + 1) * CH], self.ident[0:8, 0:8])
                    return ins
                P.op("pe", tr, reads=[B_["dda"], self.b_const], writes=[self.b_ps[bank]])
                P.op("act", lambda h, bank=bank: h.activation(out=dst, in_=self.ps[0:CH, bank, 0:NC8], func=AF.Copy),
                     reads=[self.b_ps[bank]], writes=[b_dst])
            tok8(dt_tok, B_["dttok"])
            P.op("dve", lambda h, g=g: h.tensor_scalar_mul(out=dda, in0=dda, scalar1=nega[:, g:g + 1]),
                 reads=[B_["dda"], B_["tab"]], writes=[B_["dda"]])
            P.op("dve", lambda h: h.tensor_tensor(out=dda[:, 0:PRE], in0=dda[:, 0:PRE], in1=self.premask[0:8, :], op=ALU.mult),
                 reads=[B_["dda"], self.b_const], writes=[B_["dda"]])
            tok8(da_tok, B_["datok"])
            b_cum, b_tot = self.next_bank(), self.next_bank()
            P.op("pe", lambda h, b_cum=b_cum: h.matmul(self.ps[0:CH, b_cum, 0:NC8], lhsT=tri, rhs=da_tok, start=True, stop=True),
                 reads=[B_["datok"], B_["tab"]], writes=[self.b_ps[b_cum]])
            P.op("pe", lambda h, b_tot=b_tot: h.matmul(self.ps[:, b_tot, 0:NC8], lhsT=ones, rhs=da_tok, start=True, stop=True),
                 reads=[B_["datok"], B_["tab"]], writes=[self.b_ps[b_tot]])
            P.op("act", lambda h, b_cum=b_cum: h.activation(out=cum_tok, in_=self.ps[0:CH, b_cum, 0:NC8], func=AF.Copy),
                 reads=[self.b_ps[b_cum]], writes=[B_["cumtok"]])
            P.op("act", lambda h, b_cum=b_cum: h.activation(out=negcum, in_=self.ps[0:CH, b_cum, 0:NC8], func=AF.Copy, scale=-1.0),
                 reads=[self.b_ps[b_cum]], writes=[B_["cumtok"]])
            P.op("act", lambda h, b_tot=b_tot: h.activation(out=tot, in_=self.ps[:, b_tot, 0:NC8], func=AF.Copy),
                 reads=[self.b_ps[b_tot]], writes=[B_["tot"]])
            P.op("act", lambda h, b_tot=b_tot: h.activation(out=eend, in_=self.ps[:, b_tot, 0:NC8], func=AF.Exp),
                 reads=[self.b_ps[b_tot]], writes=[B_["tot"]])
            P.op("dve", lambda h: h.tensor_copy(out=sfx[:, (NCH - 1) * 8:NC8], in_=tot[:, (NCH - 1) * 8:NC8]),
                 reads=[B_["tot"]], writes=[B_["tot"]])
            for c in range(NCH - 2, -1, -1):
                P.op("dve", lambda h, c=c: h.tensor_tensor(out=sfx[:, c * 8:(c + 1) * 8], in0=sfx[:, (c + 1) * 8:(c + 2) * 8],
                                                           in1=tot[:, c * 8:(c + 1) * 8], op=ALU.add),
                     reads=[B_["tot"]], writes=[B_["tot"]])
            for src, dst in ((sfx, wloc), (tot, wend)):
                P.op("dve", lambda h, src=src: h.tensor_tensor(out=tmp80[0:CH, :], in0=src[0:CH, :], in1=cum_tok, op=ALU.subtract),
                     reads=[B_["tot"], B_["cumtok"], B_["tmp80"]], writes=[B_["tmp80"]])
                P.op("act", lambda h: h.activation(out=tmp80[0:CH, :], in_=tmp80[0:CH, :], func=AF.Exp),
                     reads=[B_["tmp80"]], writes=[B_["tmp80"]])
                P.op("dve", lambda h, dst=dst: h.tensor_tensor(out=dst, in0=tmp80[0:CH, :], in1=dt_tok, op=ALU.mult),
                     reads=[B_["tmp80"], B_["dttok"]], writes=[B_["wl"]])
            cb_ = [self.next_bank() for _ in range(3)]
            for c in range(NCH):
                P.op("pe", lambda h, c=c: h.matmul(self.ps[0:8, cb_[c // 4], (c % 4) * CH:(c % 4 + 1) * CH],
                                                   lhsT=da_tok[:, c * 8:(c + 1) * 8], rhs=tri, start=(c % 4 == 0), stop=True,
                                                   skip_group_check=True),
                     reads=[B_["datok"], B_["tab"]], writes=[self.b_ps[cb_[c // 4]]])
            for i3 in range(3):
                nch = min(4, NCH - 4 * i3)
                P.op("act", lambda h, i3=i3, nch=nch: h.activation(
                    out=cumT[:, 4 * i3 * CH:(4 * i3 + nch) * CH], in_=self.ps[0:8, cb_[i3], 0:nch * CH], func=AF.Copy),
                    reads=[self.b_ps[cb_[i3]]], writes=[B_["cumT"]])
            ub = self.next_bank()
            for c in range(NCH):
                s2 = c % 2
                P.op("dve", lambda h, c=c, s2=s2: h.tensor_tensor(
                    out=xw[:, s2, :].rearrange("p (j d) -> p j d", d=64), in0=x_tok[:, c, :].rearrange("p (j d) -> p j d", d=64),
                    in1=wloc[:, c * 8:(c + 1) * 8].unsqueeze(2).to_broadcast([CH, 8, 64]), op=ALU.mult),
                    reads=[B_["xtok"], B_["wl"]], writes=[B_["xw%d" % s2]])
                P.op("pe", lambda h, c=c, s2=s2: h.matmul(self.ps[:, ub, :], lhsT=B_tok[:, c, :], rhs=xw[:, s2, :],
                                                          start=(c == 0), stop=(c == NCH - 1)),
                     reads=[B_["Btok"], B_["xw%d" % s2]], writes=[self.b_ps[ub]])
            P.op("act", lambda h: h.activation(out=Sg[:, 0:512], in_=self.ps[:, ub, :], func=AF.Copy),
                 reads=[self.b_ps[ub]], writes=[B_["Sg"]])
            P.op("act", lambda h: h.activation(out=Sg[:, 512:520], in_=sfx[:, 0:8], func=AF.Exp),
                 reads=[B_["tot"]], writes=[B_["Sg"]])
            i = self.n_cc
            self.n_cc += 1
            bounce = nc.dram_tensor("cc_b%d" % i, [128, 520], F32)
            gathered = nc.dram_tensor("cc_g%d" % i, [4 * 128, 520], F32)
            b_b, b_g = Buf("ccb%d" % i), Buf("ccg%d" % i)
            P.op("sp", lambda h, bounce=bounce: h.dma_start(out=bounce.ap(), in_=Sg), reads=[B_["Sg"]], writes=[b_b], dma="cc_in")
            self.wprefetch()
            P.op("pool", lambda h, bounce=bounce, gathered=gathered: h.collective_compute(
                "AllGather", ALU.bypass, replica_groups=[[0, 1, 2, 3], [4, 5, 6, 7]],
                ins=[bounce.ap().opt()], outs=[gathered.ap().opt()]), reads=[b_b], writes=[b_g], dma="cc", unit=1)
            self.set_ring([0, 1, 2, 3, 4, 5])
            for q in range(4):
                w, wb = self.wacquire()
                for ti, (a, b) in enumerate(TTS):
                    bank = self.next_bank()
                    self.proj(w, wb, KT, 0, self.hnT, [self.b_hn], a, b, bank)
                    P.op("act", lambda h, q=q, a=a, b=b, bank=bank: h.activation(
                        out=zs[:, q, a:b], in_=self.ps[:, bank, 0:b - a], func=AF.Silu),
                        reads=[self.b_ps[bank]], writes=[B_["zs"]])
            P.op("dve", lambda h: h.memset(S, 0.0), writes=[B_["S"]])
            for r in range(4):
                P.op("sp", lambda h, r=r, gathered=gathered: h.dma_start(out=Sg, in_=gathered.ap()[r * 128:(r + 1) * 128, :]),
                     reads=[b_g], writes=[B_["Sg"]], dma="sg")
                P.op("dve", lambda h, r=r: h.tensor_scalar(out=dmul, in0=Sg[:, 512:520], scalar1=-1.0, scalar2=selr[:, r:r + 1],
                                                           op0=ALU.add, op1=ALU.mult),
                     reads=[B_["Sg"], B_["tab"]], writes=[B_["dmul"]])
                P.op("dve", lambda h: h.tensor_scalar_add(out=dmul, in0=dmul, scalar1=1.0), reads=[B_["dmul"]], writes=[B_["dmul"]])
                P.op("dve", lambda h: h.tensor_tensor(
                    out=S.rearrange("p (j d) -> p j d", d=64), in0=S.rearrange("p (j d) -> p j d", d=64),
                    in1=dmul.unsqueeze(2).to_broadcast([128, 8, 64]), op=ALU.mult), reads=[B_["S"], B_["dmul"]], writes=[B_["S"]])
                P.op("dve", lambda h, r=r: h.scalar_tensor_tensor(out=S, in0=Sg[:, 0:512], scalar=selr[:, r:r + 1], in1=S,
                                                                  op0=ALU.mult, op1=ALU.add),
                     reads=[B_["Sg"], B_["tab"], B_["S"]], writes=[B_["S"]])
            P.op("act", lambda h: h.activation(out=S_bf, in_=S, func=AF.Copy), reads=[B_["S"]], writes=[B_["Sbf"]])
            rb, yb, cbk, ub = [0, 1], [2, 3], 4, 5

            def front_a(c):
                c0, s = c * CH, c % 2
                for bk in range(2):
                    def rmm(h, bk=bk, c0=c0):
                        ins = None
                        for jj in range(4):
                            j = bk * 4 + jj
                            ins = h.matmul(self.ps[:, rb[bk], jj * CH:(jj + 1) * CH], lhsT=sel[:, j * 128:(j + 1) * 128],
                                           rhs=cumT[:, c0:c0 + CH], start=(jj == 0), stop=True, skip_group_check=True)
                        return ins
                    P.op("pe", rmm, reads=[B_["cumT"], B_["tab"]], writes=[self.b_ps[rb[bk]]])
                P.op("pe", lambda h, c0=c0: h.matmul(self.ps[0:CH, cbk, 0:CH], lhsT=BT[:, c0:c0 + CH], rhs=CT[:, c0:c0 + CH],
                                                     start=True, stop=True), reads=[B_["BT"], B_["CT"]], writes=[self.b_ps[cbk]])
                for j in range(8):
                    P.op("act", lambda h, j=j, c=c, s=s: h.activation(
                        out=E[:, s, j, :], in_=self.ps[0:CH, rb[j // 4], (j % 4) * CH:(j % 4 + 1) * CH], func=AF.Exp,
                        bias=negcum[:, c * 8 + j:c * 8 + j + 1], scale=1.0),
                        reads=[self.b_ps[rb[j // 4]], B_["cumtok"]], writes=[B_["E%d" % s]])
                for bk in range(2):
                    P.op("act", lambda h, bk=bk, s=s: h.activation(
                        out=eR[:, s, bk * 4:bk * 4 + 4, :], in_=self.ps[:, rb[bk], 0:4 * CH].rearrange("p (j l) -> p j l", l=CH),
                        func=AF.Exp), reads=[self.b_ps[rb[bk]]], writes=[B_["eR%d" % s]])

            def front_b(c):
                c0, s = c * CH, c % 2
                P.op("dve", lambda h: h.tensor_tensor(out=CBm, in0=self.ps[0:CH, cbk, 0:CH], in1=tri, op=ALU.mult),
                     reads=[self.b_ps[cbk], B_["tab"]], writes=[B_["CBm"]])
                P.op("dve", lambda h, s=s: h.scalar_tensor_tensor(
                    out=E[:, s, :, :], in0=E[:, s, :, :], scalar=1.0, in1=CBm.unsqueeze(1).to_broadcast([CH, 8, CH]),
                    op0=ALU.min, op1=ALU.mult), reads=[B_["E%d" % s], B_["CBm"]], writes=[B_["E%d" % s]])
                P.op("dve", lambda h, c0=c0, s=s: h.tensor_tensor(
                    out=eR[:, s, :, :], in0=eR[:, s, :, :], in1=CT[:, c0:c0 + CH].unsqueeze(1).to_broadcast([128, 8, CH]), op=ALU.mult),
                    reads=[B_["eR%d" % s], B_["CT"]], writes=[B_["eR%d" % s]])
                P.op("dve", lambda h, c=c: h.tensor_tensor(
                    out=xdt.rearrange("p (j d) -> p j d", d=64), in0=x_tok[:, c, :].rearrange("p (j d) -> p j d", d=64),
                    in1=dt_tok[:, c * 8:(c + 1) * 8].unsqueeze(2).to_broadcast([CH, 8, 64]), op=ALU.mult),
                    reads=[B_["xtok"], B_["dttok"]], writes=[B_["xdt"]])
                P.op("dve", lambda h, c=c, s=s: h.tensor_tensor(
                    out=xw[:, s, :].rearrange("p (j d) -> p j d", d=64), in0=x_tok[:, c, :].rearrange("p (j d) -> p j d", d=64),
                    in1=wend[:, c * 8:(c + 1) * 8].unsqueeze(2).to_broadcast([CH, 8, 64]), op=ALU.mult),
                    reads=[B_["xtok"], B_["wl"]], writes=[B_["xw%d" % s]])

            def back_pe(c):
                s = c % 2
                for bk in range(2):
                    def ymm(h, bk=bk, s=s):
                        ins = None
                        for jj in range(4):
                            j = bk * 4 + jj
                            o_ = self.ps[:, yb[bk], jj * CH:(jj + 1) * CH]
                            h.matmul(o_, lhsT=xdt[:, (j // 2) * 128:(j // 2 + 1) * 128], rhs=E[:, s, j, :],
                                     start=(jj == 0), stop=False, skip_group_check=True)
                            ins = h.matmul(o_, lhsT=S_bf[:, (j // 2) * 128:(j // 2 + 1) * 128], rhs=eR[:, s, j, :],
                                           start=False, stop=True, skip_group_check=True)
                        return ins
                    P.op("pe", ymm, reads=[B_["xdt"], B_["E%d" % s], B_["Sbf"], B_["eR%d" % s]], writes=[self.b_ps[yb[bk]]])
                P.op("pe", lambda h, c=c, s=s: h.matmul(self.ps[:, ub, :], lhsT=B_tok[:, c, :], rhs=xw[:, s, :], start=True, stop=True),
                     reads=[B_["Btok"], B_["xw%d" % s]], writes=[self.b_ps[ub]])

            def back_rest(c):
                c0 = c * CH
                for bk in range(2):
                    for par in range(2):
                        src = self.ps[64 * par:64 * par + 64, yb[bk], 0:4 * CH].rearrange("p (q r l) -> p q r l", r=2, l=CH)[:, :, par, :]
                        P.op("dve", lambda h, src=src, par=par, bk=bk, c0=c0: h.tensor_tensor(
                            out=yT[64 * par:64 * par + 64, 2 * bk:2 * bk + 2, c0:c0 + CH], in0=src,
                            in1=xDT[64 * par:64 * par + 64, 2 * bk:2 * bk + 2, c0:c0 + CH], op=ALU.add),
                            reads=[self.b_ps[yb[bk]], B_["xDT"]], writes=[B_["yT"]])
                P.op("dve", lambda h, c=c: h.tensor_tensor(
                    out=S.rearrange("p (j d) -> p j d", d=64), in0=S.rearrange("p (j d) -> p j d", d=64),
                    in1=eend[:, c * 8:(c + 1) * 8].unsqueeze(2).to_broadcast([128, 8, 64]), op=ALU.mult),
                    reads=[B_["S"], B_["tot"]], writes=[B_["S"]])
                P.op("dve", lambda h: h.tensor_tensor(out=S, in0=S, in1=self.ps[:, ub, :], op=ALU.add),
                     reads=[B_["S"], self.b_ps[ub]], writes=[B_["S"]])
                P.op("act", lambda h: h.activation(out=S_bf, in_=S, func=AF.Copy), reads=[B_["S"]], writes=[B_["Sbf"]])

            front_a(0)
            front_b(0)
            for c in range(NCH):
                if c + 1 < NCH:
                    front_a(c + 1)
                back_pe(c)
                back_rest(c)
                if c + 1 < NCH:
                    front_b(c + 1)
            self.set_ring([0, 1, 2, 3, 4, 5])
            nb_ = [self.next_bank() for _ in TTS]
            for q in range(4):
                P.op("dve", lambda h, q=q: h.tensor_tensor(out=yT[:, q, :], in0=yT[:, q, :], in1=zs[:, q, :], op=ALU.mult),
                     reads=[B_["yT"], B_["zs"]], writes=[B_["yT"]])
                P.op("act", lambda h, q=q: h.activation(out=self.sq, in_=yT[:, q, :], func=AF.Square),
                     reads=[B_["yT"]], writes=[self.b_sq])

                def mm(h, q=q):
                    ins = None
                    for ti, (a, b) in enumerate(TTS):
                        ins = h.matmul(self.ps[:, nb_[ti], 0:b - a], lhsT=self.ones_bf[:, :], rhs=self.sq[:, a:b],
                                       start=(q == 0), stop=(q == 3))
                    return ins
                P.op("pe", mm, reads=[self.b_sq, self.b_const], writes=[self.b_ps[b] for b in nb_])
            for ti, (a, b) in enumerate(TTS):
                P.op("act", lambda h, ti=ti, a=a, b=b: h.activation(
                    out=self.rstd[:, a:b], in_=self.ps[:, nb_[ti], 0:b - a], func=AF.Sqrt, bias=float(EPS), scale=1.0 / 512.0),
                    reads=[self.b_ps[nb_[ti]]], writes=[self.b_rstd])
            P.op("dve", lambda h: h.reciprocal(out=self.rstd, in_=self.rstd), reads=[self.b_rstd], writes=[self.b_rstd])
            for q in range(4):
                P.op("dve", lambda h, q=q, g=g: h.scalar_tensor_tensor(
                    out=yT[:, q, :], in0=yT[:, q, :], scalar=gnw[:, g * 4 + q:g * 4 + q + 1], in1=self.rstd,
                    op0=ALU.mult, op1=ALU.mult), reads=[B_["yT"], B_["tab"], self.b_rstd], writes=[B_["yT"]])
            for nb in range(4):
                w, wb = self.wacquire()
                for q in range(4):
                    n_ = nb * 4 + q
                    for ti, (a, b) in enumerate(TTS):
                        bank = self.next_bank()
                        self.proj(w, wb, 4, q * 128, yT, [B_["yT"]], a, b, bank)
                        self.resid_add(n_, a, b, bank)
        self.mask_pre()

    def tok_transposes(self, srcT, b_src, dst_tok, b_dst, col0, kfac, hd, b_tab):
        P = self.P
        for (c0, c1, pb) in ((0, 8, 0), (8, NCH, 1)):
            def tr(h, c0=c0, c1=c1, pb=pb):
                ins = None
                for c in range(c0, c1):
                    ins = h.transpose(self.psb[0:CH, pb, (c - c0) * 128:(c - c0 + 1) * 128],
                                      srcT[:, c * CH:(c + 1) * CH], self.identb[:, :])
                return ins
            P.op("pe", tr, reads=[b_src, self.b_const], writes=[self.b_psb[pb]])
            nch = c1 - c0
            src = self.psb[0:CH, pb, 0:nch * 128].rearrange("p (c d) -> p c d", d=128)
            dst = dst_tok[:, c0:c1, col0:col0 + 128]
            if kfac is None:
                P.op("act", lambda h, src=src, dst=dst: h.activation(out=dst, in_=src, func=AF.Copy),
                     reads=[self.b_psb[pb]], writes=[b_dst])
            else:
                kf = kfac[:, hd * NCH + c0:hd * NCH + c1].unsqueeze(2).to_broadcast([CH, nch, 128])
                P.op("dve", lambda h, src=src, dst=dst, kf=kf: h.tensor_tensor(out=dst, in0=src, in1=kf, op=ALU.mult),
                     reads=[self.b_psb[pb], b_tab], writes=[b_dst])


def col_layout(v):
    return np.ascontiguousarray(v.reshape(-1, 128).T)


def bf16(a):
    import ml_dtypes
    return np.asarray(a, np.float32).astype(ml_dtypes.bfloat16)


def core_consts(c):
    p = c % 4
    ident = np.eye(128, dtype=np.float32)
    premask = np.full((128, PRE), 1.0 if p == 0 else 0.0, np.float32)
    halosel = np.zeros((128, 4), np.float32)
    if p > 0:
        halosel[:, p - 1] = 1.0
    return {"c_ident": ident, "c_premask": premask, "c_halosel": halosel}


def ret_consts(c):
    p = c % 4
    half = 128
    inv = (10000.0 ** (-np.arange(half, dtype=np.float32) / half)).astype(np.float32)
    pos = (p * OWN + np.arange(T)).astype(np.float32)
    ang = (pos[None, :] * inv[:, None]).astype(np.float32)
    lg = np.log1p(-np.exp2(-5.0 - np.arange(RH, dtype=np.float64)))
    s = np.arange(CH)[:, None]
    jj = np.arange(T)[None, :]
    G = np.where(jj >= s, np.exp((jj - s)[None] * lg[:, None, None]), 0.0) / 16.0
    Q = np.exp((np.arange(QB)[None, :] + 1.0) * lg[:, None])
    Qrep = np.broadcast_to(Q.reshape(1, RH * QB), (128, RH * QB))
    cc = np.arange(NCH)[None, None, :]
    Kf = np.exp((T - 1 - CH * cc - s[:, :, None]) * lg[None, :, None]) / 16.0
    coef = np.zeros((4, RH))
    for r in range(4):
        if r < p:
            coef[r] = np.exp((OWN * (p - 1 - r) - PRE) * lg)
    return {
        "c_cos": bf16(np.cos(ang)), "c_sin": bf16(np.sin(ang)),
        "c_retG": bf16(G), "c_retQ": bf16(Qrep),
        "c_retK": np.ascontiguousarray(Kf.reshape(CH, RH * NCH).astype(np.float32)),
        "c_retcoef": np.ascontiguousarray(np.broadcast_to(coef.reshape(1, 4 * RH), (128, 4 * RH)).astype(np.float32)),
    }


def ffn_inputs(idx, ffn_norm_w, ffn_w_up, ffn_conv_w, ffn_conv_b, ffn_w_down):
    return {
        "ffn_nw%d" % idx: col_layout(ffn_norm_w[idx]),
        "ffn_wup%d" % idx: ffn_w_up[idx],
        "ffn_cw%d" % idx: np.ascontiguousarray(ffn_conv_w[idx].reshape(3, FT, 128).transpose(2, 1, 0)),
        "ffn_cb%d" % idx: col_layout(ffn_conv_b[idx]),
        "ffn_wdn%d" % idx: ffn_w_down[idx],
    }


def ret_inputs(j, ret_norm_w, ret_w_in, ret_gn_w, ret_w_out):
    return {
        "ret_nw%d" % j: col_layout(ret_norm_w[j]),
        "ret_win%d" % j: ret_w_in[j],
        "ret_gn%d" % j: col_layout(ret_gn_w[j]),
        "ret_wout%d" % j: ret_w_out[j],
    }


def ssd_consts(c):
    p = c % 4
    tri = (np.arange(CH)[:, None] <= np.arange(CH)[None, :]).astype(np.float32)
    ones = np.ones((CH, 128), np.float32)
    sel = np.zeros((8, 8, 128), np.float32)
    for j in range(8):
        sel[j, j, :] = 1.0
    selr = np.zeros((128, 4), np.float32)
    selr[:, :p] = 1.0
    return {"c_tri": tri, "c_ones": ones, "c_sel": sel.reshape(8, 8 * 128), "c_selr": selr}


def ssd_inputs(j, ssd_norm_w, ssd_w_in, ssd_conv_w, ssd_conv_b, ssd_dt_bias, ssd_a_log, ssd_d, ssd_gnorm_w, ssd_w_out):
    return {
        "ssd_nw%d" % j: col_layout(ssd_norm_w[j]),
        "ssd_win%d" % j: ssd_w_in[j],
        "ssd_cw%d" % j: np.ascontiguousarray(ssd_conv_w[j].reshape(4, 48, 128).transpose(2, 1, 0).reshape(128, 48 * 4)),
        "ssd_cb%d" % j: col_layout(ssd_conv_b[j]),
        "ssd_dtb%d" % j: np.ascontiguousarray(ssd_dt_bias[j].reshape(8, 8).T),
        "ssd_alog%d" % j: np.ascontiguousarray(ssd_a_log[j].reshape(8, 8).T),
        "ssd_dcol%d" % j: col_layout(np.repeat(ssd_d[j], 64)),
        "ssd_gn%d" % j: col_layout(ssd_gnorm_w[j]),
        "ssd_wout%d" % j: ssd_w_out[j],
    }


FUSE_GROUPS = [[("ret", 0), ("ffn", 0), ("ssd", 0), ("ffn", 1), ("ret", 1), ("ffn", 2), ("ssd", 1), ("ffn", 3)]]


def _sub_inputs(kind, idx, name_idx, inp):
    if kind == "ffn":
        d = ffn_inputs(idx, inp["ffn_norm_w"], inp["ffn_w_up"], inp["ffn_conv_w"], inp["ffn_conv_b"], inp["ffn_w_down"])
    elif kind == "ret":
        d = ret_inputs(idx, inp["ret_norm_w"], inp["ret_w_in"], inp["ret_gn_w"], inp["ret_w_out"])
    else:
        d = ssd_inputs(idx, inp["ssd_norm_w"], inp["ssd_w_in"], inp["ssd_conv_w"], inp["ssd_conv_b"], inp["ssd_dt_bias"],
                       inp["ssd_a_log"], inp["ssd_d"], inp["ssd_gnorm_w"], inp["ssd_w_out"])
    if name_idx != idx:
        d = {k[:-len(str(idx))] + str(name_idx): v for k, v in d.items()}
    return d


def kernel(**inputs):
    inp = {k: np.ascontiguousarray(np.asarray(v, dtype=np.float32)) for k, v in inputs.items()}
    x, meta = inp["x"], inp["meta_tokens"]
    hs = []
    for c in range(NCORES):
        b, p = c // 4, c % 4
        xin = np.zeros((T, D), np.float32)
        if p == 0:
            xin[:PRE] = meta
        xin[PRE:] = x[b, p * OWN:(p + 1) * OWN]
        hs.append(xin)
    consts = []
    for c in range(NCORES):
        d = core_consts(c)
        d.update(ret_consts(c))
        d.update(ssd_consts(c))
        consts.append(d)
    progs = {}
    for gi, group in enumerate(FUSE_GROUPS):
        last = gi == len(FUSE_GROUPS) - 1
        slots, counts = [], {}
        for kind, idx in group:
            s = counts.get(kind, 0)
            counts[kind] = s + 1
            slots.append((kind, s))
        key = (tuple(slots), last)
        if key not in progs:
            progs[key] = Builder(slots, final_norm=last).build()
        nc = progs[key]
        names = set(a.memorylocations[0].name for a in nc.allocations
                    if isinstance(a, mybir.MemoryLocationSet) and a.kind == "ExternalInput")
        w = {}
        for (kind, idx), (_, s) in zip(group, slots):
            w.update(_sub_inputs(kind, idx, s, inp))
        if last:
            w["fin_nw"] = col_layout(inp["final_norm_w"])
        in_maps = []
        for c in range(NCORES):
            m = {"xin": hs[c]}
            m.update(consts[c])
            m.update(w)
            in_maps.append({k: v for k, v in m.items() if k in names})
        res = run_bass_kernel_spmd(nc, in_maps, core_ids=list(range(NCORES)))
        hs = [np.asarray(res.results[c]["out"]) for c in range(NCORES)]
    out = np.empty((2, 4 * OWN, D), np.float32)
    for c in range(NCORES):
        b, p = c // 4, c % 4
        out[b, p * OWN:(p + 1) * OWN] = hs[c][PRE:]
    return out
```

```python
import contextlib
import numpy as np
import concourse.bass as bass
import concourse.mybir as mybir
from concourse.bass_utils import run_bass_kernel_spmd

F32 = mybir.dt.float32
BF16 = mybir.dt.bfloat16
AF = mybir.ActivationFunctionType
ALU = mybir.AluOpType

NCORES = 8
D = 2048
KT = 16
PRE = 16
OWN = 1024
T = PRE + OWN
CH = 104
NCH = T // CH
TTS = [(0, 347), (347, 694), (694, 1040)]
EPS = 1e-6
FFN = 5632
FT = FFN // 128
FG = 4
FGT = FT // FG
DEPTH = 4
WSLOT = 2048
NWSLOT = 4


class Buf:
    __slots__ = ("name", "writer", "readers")

    def __init__(self, name):
        self.name = name
        self.writer = None
        self.readers = {}


class Prog:
    ENGS = ("pe", "act", "dve", "pool", "sp")

    def __init__(self, nc):
        self.nc = nc
        self.ops = {e: [] for e in self.ENGS}
        self.seen = {e: {} for e in self.ENGS}
        self.dma_count = {}
        self.dma_unit = {}

    def op(self, eng, fn, reads=(), writes=(), dma=None, unit=16):
        deps = []
        for b in reads:
            if b.writer is not None:
                deps.append(b.writer)
        for b in writes:
            if b.writer is not None:
                deps.append(b.writer)
            deps.extend(b.readers.values())
        waits = []
        seen = self.seen[eng]
        for d in deps:
            if d[0] == "eng":
                _, e2, idx = d
                if e2 == eng and eng in ("pe", "sp"):
                    continue
                if seen.get(e2, -1) >= idx:
                    continue
                seen[e2] = idx
            else:
                _, key, cnt = d
                k = ("dma", key)
                if seen.get(k, -1) >= cnt:
                    continue
                seen[k] = cnt
            waits.append(d)
        idx = len(self.ops[eng])
        if dma is not None:
            c = self.dma_count.get(dma, 0) + 1
            self.dma_count[dma] = c
            self.dma_unit[dma] = unit
            tok = ("dma", dma, c)
            rkey = ("dma", dma)
        else:
            tok = ("eng", eng, idx)
            rkey = eng
        self.ops[eng].append(dict(fn=fn, waits=waits, tok=tok, inc=False))
        for b in reads:
            b.readers[rkey] = tok
        for b in writes:
            b.writer = tok
            b.readers = {}
        return tok

    def emit(self, final_bufs=()):
        nc = self.nc
        self.op("sp", None, reads=list(final_bufs))
        for e in self.ENGS:
            for o in self.ops[e]:
                for w in o["waits"]:
                    if w[0] == "eng":
                        self.ops[w[1]][w[2]]["inc"] = True
        rank = {}
        for e in self.ENGS:
            r = 0
            rk = []
            for o in self.ops[e]:
                if o["inc"] and o["tok"][0] == "eng":
                    r += 1
                rk.append(r)
            rank[e] = rk
        with contextlib.ExitStack() as st:
            esem = {e: st.enter_context(nc.semaphore("s_" + e)) for e in ("pe", "act", "dve", "pool")}
            dsem = {k: st.enter_context(nc.semaphore("d_%s" % str(k))) for k in self.dma_count}
            block = st.enter_context(nc.Block())
            handles = {"pe": block.tensor, "act": block.scalar, "dve": block.vector,
                       "pool": block.gpsimd, "sp": block.sync}

            def make(e):
                def body(h):
                    for o in self.ops[e]:
                        for w in o["waits"]:
                            if w[0] == "eng":
                                h.wait_ge(esem[w[1]], rank[w[1]][w[2]])
                            else:
                                h.wait_ge(dsem[w[1]], self.dma_unit[w[1]] * w[2])
                        if o["fn"] is None:
                            continue
                        ins = o["fn"](h)
                        tok = o["tok"]
                        if tok[0] == "dma":
                            if self.dma_unit[tok[1]] == 1:
                                ins.then_inc(dsem[tok[1]])
                            else:
                                ins.then_inc(dsem[tok[1]], self.dma_unit[tok[1]])
                        elif o["inc"]:
                            ins.then_inc(esem[e], 1)
                return body
            for e in self.ENGS:
                handles[e](make(e))


NF_ARENA = 9088
NB_ARENA = 28544
RH = 8
QB = 208
NQB = T // QB
GAMMA = [1.0 - 2.0 ** (-5.0 - h) for h in range(RH)]


class Builder:
    def __init__(self, sublayers, final_norm):
        self.sublayers = sublayers
        self.final_norm = final_norm
        self.nc = bass.Bass("TRN2", target_bir_lowering=False)
        self.P = Prog(self.nc)
        self.st = contextlib.ExitStack()
        self.dram = {}
        self.wblocks = []
        self.w_issued = 0
        self.w_next = 0
        self.ring = [0, 1, 2, 3, 4, 5]
        self.ring_i = 0
        self.n_cc = 0
        self.arena_bufs = []

    def din(self, name, shape, dt=F32):
        if name in self.dram:
            return self.dram[name]
        t = self.nc.dram_tensor(name, list(shape), dt, kind="ExternalInput").ap()
        self.dram[name] = t
        return t

    def sb(self, name, shape, dt):
        return self.st.enter_context(self.nc.sbuf_tensor(name, list(shape), dt))

    def set_ring(self, banks):
        self.ring = list(banks)
        self.ring_i = 0

    def next_bank(self):
        b = self.ring[self.ring_i % len(self.ring)]
        self.ring_i += 1
        return b

    def arena_begin(self):
        self.oF = 0
        self.oB = 0
        old = self.arena_bufs
        self.arena_bufs = []
        self._old_arena = old

    def arena_end(self):
        bufs = self._old_arena + self.arena_bufs
        self.P.op("dve", lambda h: h.memset(self.dummy[:], 0.0), writes=bufs + [self.b_dummy])

    def cF(self, n, parts=128):
        ap = self.arenaF[0:parts, self.oF:self.oF + n]
        self.oF += n
        assert self.oF <= NF_ARENA, ("fp32 arena overflow", self.oF)
        return ap

    def cB(self, n, parts=128):
        ap = self.arenaB[0:parts, self.oB:self.oB + n]
        self.oB += n
        assert self.oB <= NB_ARENA, ("bf16 arena overflow", self.oB)
        return ap

    def abuf(self, name):
        b = Buf(name)
        self.arena_bufs.append(b)
        return b

    def wreg(self, view, kt, n):
        assert kt * n <= WSLOT
        self.wblocks.append((view, kt, n))

    def _wissue(self, i):
        view, kt, n = self.wblocks[i]
        slot = i % NWSLOT
        dst = self.wsb[:, slot, 0:kt * n].rearrange("p (k n) -> p k n", n=n)
        self.P.op("pool", lambda h, dst=dst, view=view: h.dma_start(out=dst, in_=view),
                  writes=[self.wbuf[slot]], dma="w%d" % slot)

    def wprefetch(self):
        while self.w_issued < min(len(self.wblocks), self.w_next + NWSLOT):
            self._wissue(self.w_issued)
            self.w_issued += 1

    def wacquire(self, count=1):
        i = self.w_next
        self.w_next += count
        while self.w_issued < min(len(self.wblocks), i + NWSLOT):
            self._wissue(self.w_issued)
            self.w_issued += 1
        res = []
        for ii in range(i, i + count):
            view, kt, n = self.wblocks[ii]
            slot = ii % NWSLOT
            res.append((self.wsb[:, slot, 0:kt * n].rearrange("p (k n) -> p k n", n=n), self.wbuf[slot]))
        return res[0] if count == 1 else res

    def build(self):
        nc, P = self.nc, self.P
        self.xin = self.din("xin", [T, D])
        self.c_ident = self.din("c_ident", [128, 128])
        self.c_premask = self.din("c_premask", [128, PRE])
        self.c_halosel = self.din("c_halosel", [128, 4])
        self.out = nc.dram_tensor("out", [T, D], F32, kind="ExternalOutput").ap()
        for kind, idx in self.sublayers:
            getattr(self, kind + "_inputs")(idx)
        if self.final_norm:
            self.din("fin_nw", [128, KT])

        self.hT = self.sb("hT", [128, KT, T], F32)
        self.hnT = self.sb("hnT", [128, KT, T], BF16)
        self.wsb = self.sb("wsb", [128, NWSLOT, WSLOT], BF16)
        self.ident = self.sb("ident", [128, 128], F32)
        self.identb = self.sb("identb", [128, 128], BF16)
        self.ones_bf = self.sb("ones_bf", [128, 128], BF16)
        self.premask = self.sb("premask", [128, PRE], F32)
        self.halosel = self.sb("halosel", [128, 4], F32)
        self.nwcol = self.sb("nwcol", [128, KT], F32)
        self.hg = self.sb("hg", [128, 4, KT * 3], BF16)
        self.halo = self.sb("halo", [128, KT * 3], F32)
        self.dummy = self.sb("dummy_t", [128, 2], F32)
        self.arenaF = self.sb("arenaF", [128, NF_ARENA], F32)
        self.arenaB = self.sb("arenaB", [128, NB_ARENA], BF16)
        self.ps = self.st.enter_context(nc.psum_tensor("ps", [128, 6, 512], F32))
        self.psb = self.st.enter_context(nc.psum_tensor("psb", [128, 2, 1024], BF16))

        self.b_hT = [Buf("hT%d" % k) for k in range(KT)]
        self.b_hn = Buf("hn")
        self.wbuf = [Buf("w%d" % s) for s in range(NWSLOT)]
        self.b_const = Buf("const")
        self.b_nw = Buf("nw")
        self.b_ps = [Buf("ps%d" % i) for i in range(6)]
        self.b_psb = [Buf("psb0"), Buf("psb1")]
        self.b_hg = Buf("hg")
        self.b_halo = Buf("halo")
        self.b_out = Buf("out")
        self.b_dummy = Buf("dummy")

        for kind, idx in self.sublayers:
            getattr(self, kind + "_wreg")(idx)

        P.op("sp", lambda h: h.dma_start(out=self.ident[:], in_=self.c_ident[:, :]), writes=[self.b_const], dma="c")
        P.op("sp", lambda h: h.dma_start(out=self.premask[:], in_=self.c_premask[:, :]), writes=[self.b_const], dma="c")
        P.op("sp", lambda h: h.dma_start(out=self.halosel[:], in_=self.c_halosel[:, :]), writes=[self.b_const], dma="c")
        P.op("dve", lambda h: h.memset(self.ones_bf[:], 1.0), writes=[self.b_const])
        P.op("dve", lambda h: h.tensor_copy(out=self.identb[:], in_=self.ident[:]), reads=[self.b_const], writes=[self.b_const])

        self.load_input()
        for kind, idx in self.sublayers:
            getattr(self, kind)(idx)
        self.store_output()
        P.emit(final_bufs=[self.b_out])
        self.st.close()
        return nc

    def io_arena(self):
        self.arena_begin()
        self.stage = self.cF(D, parts=CH)
        self.b_stage = self.abuf("stage")
        self.rstd = self.cF(T)
        self.b_rstd = self.abuf("rstd")
        self.sq = self.cB(T)
        self.b_sq = self.abuf("sq")
        self.sq2 = self.cB(T)
        self.b_sq2 = self.abuf("sq2")
        self.arena_end()
        self.set_ring([0, 1, 2, 3, 4, 5])

    def load_input(self):
        P = self.P
        self.io_arena()
        for c in range(NCH):
            P.op("sp", lambda h, c=c: h.dma_start(out=self.stage, in_=self.xin[c * CH:(c + 1) * CH, :]),
                 writes=[self.b_stage], dma="st")
            for q in range(4):
                bank = self.next_bank()

                def tr(h, c=c, q=q, bank=bank):
                    ins = None
                    for i in range(4):
                        k = 4 * q + i
                        ins = h.transpose(self.ps[:, bank, i * CH:(i + 1) * CH],
                                          self.stage[:, k * 128:(k + 1) * 128], self.ident[0:CH, 0:CH])
                    return ins
                P.op("pe", tr, reads=[self.b_stage, self.b_const], writes=[self.b_ps[bank]])
                src = self.ps[:, bank, 0:4 * CH].rearrange("p (i t) -> p i t", t=CH)
                dst = self.hT[:, 4 * q:4 * q + 4, c * CH:(c + 1) * CH]
                if q % 2 == 0:
                    P.op("act", lambda h, src=src, dst=dst: h.activation(out=dst, in_=src, func=AF.Copy),
                         reads=[self.b_ps[bank]], writes=self.b_hT[4 * q:4 * q + 4])
                else:
                    P.op("dve", lambda h, src=src, dst=dst: h.tensor_copy(out=dst, in_=src),
                         reads=[self.b_ps[bank]], writes=self.b_hT[4 * q:4 * q + 4])

    def store_output(self):
        P = self.P
        self.io_arena()
        if self.final_norm:
            self.rmsnorm(self.dram["fin_nw"], to_hn=False)
        for c in range(NCH):
            for q in range(4):
                bank = self.next_bank()

                def tr(h, c=c, q=q, bank=bank):
                    ins = None
                    for i in range(4):
                        k = 4 * q + i
                        ins = h.transpose(self.ps[0:CH, bank, i * 128:(i + 1) * 128],
                                          self.hT[:, k, c * CH:(c + 1) * CH], self.ident[:, :])
                    return ins
                P.op("pe", tr, reads=self.b_hT[4 * q:4 * q + 4] + [self.b_const], writes=[self.b_ps[bank]])
                src = self.ps[0:CH, bank, :]
                dst = self.stage[:, q * 512:(q + 1) * 512]
                if q % 2 == 0:
                    P.op("act", lambda h, src=src, dst=dst: h.activation(out=dst, in_=src, func=AF.Copy),
                         reads=[self.b_ps[bank]], writes=[self.b_stage])
                else:
                    P.op("dve", lambda h, src=src, dst=dst: h.tensor_copy(out=dst, in_=src),
                         reads=[self.b_ps[bank]], writes=[self.b_stage])
            P.op("sp", lambda h, c=c: h.dma_start(out=self.out[c * CH:(c + 1) * CH, :], in_=self.stage),
                 reads=[self.b_stage], writes=[self.b_out], dma="o")

    def rmsnorm(self, nw_dram, to_hn=True):
        P = self.P
        P.op("sp", lambda h: h.dma_start(out=self.nwcol[:], in_=nw_dram[:, :]), writes=[self.b_nw], dma="nw")
        P.op("dve", lambda h: h.tensor_scalar_mul(out=self.nwcol[:], in0=self.nwcol[:], scalar1=float(np.sqrt(D))),
             reads=[self.b_nw], writes=[self.b_nw])
        banks = [self.next_bank() for _ in TTS]
        sq2 = getattr(self, "sq2", None)
        for k in range(KT):
            if sq2 is not None and k % 2 == 1:
                sq_, b_sq_ = sq2, self.b_sq2
                P.op("dve", lambda h, k=k, sq_=sq_: h.tensor_tensor(out=sq_, in0=self.hT[:, k, :], in1=self.hT[:, k, :], op=ALU.mult),
                     reads=[self.b_hT[k]], writes=[b_sq_])
            else:
                sq_, b_sq_ = self.sq, self.b_sq
                P.op("act", lambda h, k=k, sq_=sq_: h.activation(out=sq_, in_=self.hT[:, k, :], func=AF.Square),
                     reads=[self.b_hT[k]], writes=[b_sq_])

            def mm(h, k=k, sq_=sq_):
                ins = None
                for ti, (a, b) in enumerate(TTS):
                    ins = h.matmul(self.ps[:, banks[ti], 0:b - a], lhsT=self.ones_bf[:, :], rhs=sq_[:, a:b],
                                   start=(k == 0), stop=(k == KT - 1))
                return ins
            P.op("pe", mm, reads=[b_sq_, self.b_const], writes=[self.b_ps[b] for b in banks])
        for ti, (a, b) in enumerate(TTS):
            P.op("act", lambda h, ti=ti, a=a, b=b: h.activation(
                out=self.rstd[:, a:b], in_=self.ps[:, banks[ti], 0:b - a], func=AF.Sqrt, bias=float(D * EPS), scale=1.0),
                reads=[self.b_ps[banks[ti]]], writes=[self.b_rstd])
        P.op("dve", lambda h: h.reciprocal(out=self.rstd, in_=self.rstd), reads=[self.b_rstd], writes=[self.b_rstd])
        for k in range(KT):
            dst = self.hnT[:, k, :] if to_hn else self.hT[:, k, :]
            P.op("dve", lambda h, k=k, dst=dst: h.scalar_tensor_tensor(
                out=dst, in0=self.hT[:, k, :], scalar=self.nwcol[:, k:k + 1], in1=self.rstd,
                op0=ALU.mult, op1=ALU.mult), reads=[self.b_hT[k], self.b_nw, self.b_rstd],
                writes=[self.b_hn if to_hn else self.b_hT[k]])

    def halo_exchange(self):
        nc, P = self.nc, self.P
        i = self.n_cc
        self.n_cc += 1
        bounce = nc.dram_tensor("cc_b%d" % i, [128, KT * 3], BF16)
        gathered = nc.dram_tensor("cc_g%d" % i, [4 * 128, KT * 3], BF16)
        b_b, b_g = Buf("ccb%d" % i), Buf("ccg%d" % i)
        P.op("sp", lambda h: h.dma_start(out=bounce.ap().rearrange("p (k t) -> p k t", t=3), in_=self.hnT[:, :, T - 3:T]),
             reads=[self.b_hn], writes=[b_b], dma="cc_in")
        self.wprefetch()
        P.op("pool", lambda h: h.collective_compute("AllGather", ALU.bypass, replica_groups=[[0, 1, 2, 3], [4, 5, 6, 7]],
                                                    ins=[bounce.ap().opt()], outs=[gathered.ap().opt()]),
             reads=[b_b], writes=[b_g], dma="cc", unit=1)
        P.op("sp", lambda h: h.dma_start(out=self.hg[:], in_=gathered.ap().rearrange("(r p) f -> p r f", p=128)),
             reads=[b_g], writes=[self.b_hg], dma="cc_out")
        P.op("dve", lambda h: h.tensor_scalar_mul(out=self.halo[:], in0=self.hg[:, 0, :], scalar1=self.halosel[:, 0:1]),
             reads=[self.b_hg, self.b_const], writes=[self.b_halo])
        for r in range(1, 4):
            P.op("dve", lambda h, r=r: h.scalar_tensor_tensor(
                out=self.halo[:], in0=self.hg[:, r, :], scalar=self.halosel[:, r:r + 1], in1=self.halo[:],
                op0=ALU.mult, op1=ALU.add), reads=[self.b_hg, self.b_halo], writes=[self.b_halo])
        P.op("dve", lambda h: h.tensor_tensor(out=self.hnT[:, :, PRE - 3:PRE], in0=self.hnT[:, :, PRE - 3:PRE],
                                              in1=self.halo[:].rearrange("p (k t) -> p k t", t=3), op=ALU.add),
             reads=[self.b_hn, self.b_halo], writes=[self.b_hn])

    def mask_pre(self):
        self.P.op("dve", lambda h: h.tensor_tensor(
            out=self.hT[:, :, 0:PRE], in0=self.hT[:, :, 0:PRE],
            in1=self.premask[:].unsqueeze(1).to_broadcast([128, KT, PRE]), op=ALU.mult),
            reads=self.b_hT + [self.b_const], writes=self.b_hT)

    def proj(self, w, wb, kt, c0, rhsT, rhs_bufs, a, b, bank):
        def mm(h):
            ins = None
            for k in range(kt):
                ins = h.matmul(self.ps[:, bank, 0:b - a], lhsT=w[:, k, c0:c0 + 128], rhs=rhsT[:, k, a:b],
                               start=(k == 0), stop=(k == kt - 1))
            return ins
        self.P.op("pe", mm, reads=[wb] + list(rhs_bufs), writes=[self.b_ps[bank]])

    def resid_add(self, n, a, b, bank):
        self.P.op("dve", lambda h: h.tensor_tensor(
            out=self.hT[:, n, a:b], in0=self.hT[:, n, a:b], in1=self.ps[:, bank, 0:b - a], op=ALU.add),
            reads=[self.b_hT[n], self.b_ps[bank]], writes=[self.b_hT[n]])

    def ffn_inputs(self, idx):
        self.din("ffn_nw%d" % idx, [128, KT])
        self.din("ffn_wup%d" % idx, [D, 2 * FFN])
        self.din("ffn_cw%d" % idx, [128, FT, 3])
        self.din("ffn_cb%d" % idx, [128, FT])
        self.din("ffn_wdn%d" % idx, [FFN, D])

    def ffn_wreg(self, idx):
        wup = self.dram["ffn_wup%d" % idx]
        wdn = self.dram["ffn_wdn%d" % idx]
        for g in range(FG):
            for jj in range(FGT):
                j = g * FGT + jj
                self.wreg(wup[:, j * 128:(j + 1) * 128].rearrange("(k p) n -> p k n", p=128), KT, 128)
                self.wreg(wup[:, FFN + j * 128:FFN + (j + 1) * 128].rearrange("(k p) n -> p k n", p=128), KT, 128)
            for nb in range(16):
                self.wreg(wdn[g * FGT * 128:(g + 1) * FGT * 128, nb * 128:(nb + 1) * 128].rearrange(
                    "(k p) n -> p k n", p=128), FGT, 128)

    def ffn(self, idx):
        P = self.P
        self.arena_begin()
        self.rstd = self.cF(T)
        self.b_rstd = self.abuf("rstd")
        self.sq = self.cB(T)
        self.b_sq = self.abuf("sq")
        self.sq2 = self.cB(T)
        self.b_sq2 = self.abuf("sq2")
        actT = self.cB(FGT * T).rearrange("p (j t) -> p j t", t=T)
        gpre = self.cF(2 * (T + 2)).rearrange("p (s t) -> p s t", s=2)
        gacc = self.cF(T)
        sg = self.cF(T)
        cw = self.cF(FT * 3).rearrange("p (j w) -> p j w", w=3)
        cb = self.cF(FT)
        b_act, b_gacc, b_sg, b_cw = self.abuf("actT"), self.abuf("gacc"), self.abuf("sg"), self.abuf("cw")
        b_gpre = [self.abuf("gpre0"), self.abuf("gpre1")]
        self.arena_end()
        self.set_ring([0, 1, 2, 3, 4, 5])

        P.op("dve", lambda h: h.memset(gpre[:, :, 0:2], 0.0), writes=b_gpre)
        P.op("sp", lambda h: h.dma_start(out=cw, in_=self.dram["ffn_cw%d" % idx][:, :, :]), writes=[b_cw], dma="cw")
        P.op("sp", lambda h: h.dma_start(out=cb, in_=self.dram["ffn_cb%d" % idx][:, :]), writes=[b_cw], dma="cw")
        self.rmsnorm(self.dram["ffn_nw%d" % idx])
        self.halo_exchange()
        for g in range(FG):
            for jj in range(FGT):
                j = g * FGT + jj
                s = j % 2
                wg, wgb = self.wacquire()
                gb = [self.next_bank() for _ in TTS]
                for ti, (a, b) in enumerate(TTS):
                    self.proj(wg, wgb, KT, 0, self.hnT, [self.b_hn], a, b, gb[ti])
                    P.op("act", lambda h, a=a, b=b, s=s, bank=gb[ti]: h.activation(
                        out=gpre[:, s, 2 + a:2 + b], in_=self.ps[:, bank, 0:b - a], func=AF.Copy),
                        reads=[self.b_ps[gb[ti]]], writes=[b_gpre[s]])
                wu, wub = self.wacquire()
                ub = [self.next_bank() for _ in TTS]
                for ti, (a, b) in enumerate(TTS):
                    self.proj(wu, wub, KT, 0, self.hnT, [self.b_hn], a, b, ub[ti])
                P.op("dve", lambda h, j=j, s=s: h.tensor_scalar(
                    out=gacc, in0=gpre[:, s, 2:T + 2], scalar1=cw[:, j, 2:3], scalar2=cb[:, j:j + 1],
                    op0=ALU.mult, op1=ALU.add), reads=[b_gpre[s], b_cw, b_sg], writes=[b_gacc])
                P.op("dve", lambda h, j=j, s=s: h.scalar_tensor_tensor(
                    out=gacc, in0=gpre[:, s, 1:T + 1], scalar=cw[:, j, 1:2], in1=gacc, op0=ALU.mult, op1=ALU.add),
                    reads=[b_gpre[s], b_cw, b_gacc], writes=[b_gacc])
                P.op("dve", lambda h, j=j, s=s: h.scalar_tensor_tensor(
                    out=gacc, in0=gpre[:, s, 0:T], scalar=cw[:, j, 0:1], in1=gacc, op0=ALU.mult, op1=ALU.add),
                    reads=[b_gpre[s], b_cw, b_gacc], writes=[b_gacc])
                P.op("act", lambda h: h.activation(out=sg, in_=gacc, func=AF.Silu), reads=[b_gacc], writes=[b_sg])
                for ti, (a, b) in enumerate(TTS):
                    P.op("dve", lambda h, a=a, b=b, jj=jj, bank=ub[ti]: h.tensor_tensor(
                        out=actT[:, jj, a:b], in0=sg[:, a:b], in1=self.ps[:, bank, 0:b - a], op=ALU.mult),
                        reads=[b_sg, self.b_ps[ub[ti]]], writes=[b_act])
            for n in range(16):
                wd, wdb = self.wacquire()
                for ti, (a, b) in enumerate(TTS):
                    bank = self.next_bank()
                    self.proj(wd, wdb, FGT, 0, actT, [b_act], a, b, bank)
                    self.resid_add(n, a, b, bank)
        self.mask_pre()

    def ret_inputs(self, j):
        self.din("ret_nw%d" % j, [128, KT])
        self.din("ret_win%d" % j, [D, 12288])
        self.din("ret_gn%d" % j, [128, 32])
        self.din("ret_wout%d" % j, [4096, D])
        self.din("c_cos", [128, T], BF16)
        self.din("c_sin", [128, T], BF16)
        self.din("c_retG", [RH, CH, T], BF16)
        self.din("c_retQ", [128, RH * QB], BF16)
        self.din("c_retK", [CH, RH * NCH])
        self.din("c_retcoef", [128, 4 * RH])

    def ret_wreg(self, j):
        win = self.dram["ret_win%d" % j]
        wout = self.dram["ret_wout%d" % j]
        for h in range(RH):
            def blk(c0):
                self.wreg(win[:, c0:c0 + 128].rearrange("(k p) n -> p k n", p=128), KT, 128)
                self.wreg(win[:, c0 + 128:c0 + 256].rearrange("(k p) n -> p k n", p=128), KT, 128)
            blk(4096 + h * 512)
            blk(4096 + h * 512 + 256)
            blk(2048 + h * 256)
            blk(h * 256)
            def outblk(hh):
                for nb in range(4):
                    self.wreg(wout[hh * 512:(hh + 1) * 512, nb * 512:(nb + 1) * 512].rearrange(
                        "(k p) n -> p k n", p=128), 4, 512)
            if h > 0:
                outblk(h - 1)
            blk(8192 + h * 512)
            blk(8192 + h * 512 + 256)
            if h == RH - 1:
                outblk(h)

    def ret(self, j):
        nc, P = self.nc, self.P
        self.arena_begin()
        self.rstd = self.cF(T)
        self.b_rstd = self.abuf("rstd")
        self.sq = self.cB(T)
        self.b_sq = self.abuf("sq")
        self.sq2 = self.cB(T)
        self.b_sq2 = self.abuf("sq2")
        cosT, sinT = self.cB(T), self.cB(T)
        G = self.cB(T, parts=CH)
        qdecB = self.cB(RH * QB).rearrange("p (h l) -> p h l", l=QB)
        kfac = self.cF(RH * NCH, parts=CH)
        coef = self.cF(4 * RH)
        gnw = self.cF(32)
        kT = self.cB(2 * T).rearrange("p (d t) -> p d t", t=T)
        qT = self.cB(2 * T).rearrange("p (d t) -> p d t", t=T)
        qd = self.cB(2 * 2 * QB).rearrange("p (s d l) -> p s d l", s=2, d=2)
        vtmp = self.cB(T)
        v_tok = self.cB(NCH * 512, parts=CH).rearrange("p (c e) -> p c e", e=512)
        k_tok = self.cB(NCH * 256, parts=CH).rearrange("p (c d) -> p c d", d=256)
        yT = self.cB(4 * T).rearrange("p (e t) -> p e t", t=T)
        sqo = self.cB(4 * QB).rearrange("p (e l) -> p e l", l=QB)
        PT = self.cB(2 * QB, parts=CH).rearrange("p (s l) -> p s l", s=2)
        SinB = self.cB(1024).rearrange("p (d e) -> p d e", e=512)
        oT = self.cF(4 * QB).rearrange("p (e l) -> p e l", l=QB)
        rt = self.cF(2 * 347).rearrange("p (s l) -> p s l", s=2)
        sloc = self.cF(1024).rearrange("p (d e) -> p d e", e=512)
        sgath = self.cF(1024).rearrange("p (d e) -> p d e", e=512)
        SinF = self.cF(1024).rearrange("p (d e) -> p d e", e=512)
        rs = self.cF(QB)
        b_tab, b_G, b_kT, b_qT, b_vtmp = self.abuf("tab"), self.abuf("G"), self.abuf("kT"), self.abuf("qT"), self.abuf("vtmp")
        b_qd = [self.abuf("qd0"), self.abuf("qd1")]
        b_vtok, b_ktok, b_yT, b_sqo = self.abuf("vtok"), self.abuf("ktok"), self.abuf("yT"), self.abuf("sqo")
        b_PT = [self.abuf("PT0"), self.abuf("PT1")]
        b_SinB, b_oT, b_rt, b_sloc, b_sgath, b_SinF, b_rs = (self.abuf(n) for n in
                                                             ("SinB", "oT", "rt", "sloc", "sgath", "SinF", "rs"))
        self.arena_end()

        dr = self.dram
        for dst, src in ((cosT, dr["c_cos"][:, :]), (sinT, dr["c_sin"][:, :]),
                         (qdecB, dr["c_retQ"].rearrange("p (h l) -> p h l", l=QB)),
                         (kfac, dr["c_retK"][:, :]), (coef, dr["c_retcoef"][:, :]), (gnw, dr["ret_gn%d" % j][:, :])):
            P.op("sp", lambda h, dst=dst, src=src: h.dma_start(out=dst, in_=src), writes=[b_tab], dma="tab")
        self.set_ring([0, 1, 2, 3, 4, 5])
        self.rmsnorm(dr["ret_nw%d" % j])

        def out_proj():
            self.set_ring([0, 1, 2, 3, 4, 5])
            for nb in range(4):
                w, wb = self.wacquire()
                for q in range(4):
                    n_ = nb * 4 + q
                    for ti, (a, b) in enumerate(TTS):
                        bank = self.next_bank()
                        self.proj(w, wb, 4, q * 128, yT, [b_yT], a, b, bank)
                        self.resid_add(n_, a, b, bank)

        for hd in range(RH):
            gam = GAMMA[hd]
            P.op("sp", lambda h, hd=hd: h.dma_start(out=G, in_=dr["c_retG"][hd, :, :]), writes=[b_G], dma="G")
            self.set_ring([0, 1, 2, 3, 4, 5])
            def rot_proj(dstT, b_dst):
                (w, wb), (w2, wb2) = self.wacquire(2)
                for ti, (a, b) in enumerate(TTS):
                    b1, b2 = self.next_bank(), self.next_bank()
                    self.proj(w, wb, KT, 0, self.hnT, [self.b_hn], a, b, b1)
                    self.proj(w2, wb2, KT, 0, self.hnT, [self.b_hn], a, b, b2)
                    n = b - a
                    x1, x2 = self.ps[:, b1, 0:n], self.ps[:, b2, 0:n]
                    t1, t2 = rt[:, 0, 0:n], rt[:, 1, 0:n]
                    seq = [(t1, x1, cosT[:, a:b], None), (t2, x2, sinT[:, a:b], None), (dstT[:, 0, a:b], t1, t2, ALU.subtract),
                           (t1, x1, sinT[:, a:b], None), (t2, x2, cosT[:, a:b], None), (dstT[:, 1, a:b], t1, t2, ALU.add)]
                    for si, (o_, i0, i1, op_) in enumerate(seq):
                        fin = op_ is not None
                        rd = [b_rt] if fin else [self.b_ps[b1 if i0 is x1 else b2], b_tab]
                        P.op("dve", lambda h, o_=o_, i0=i0, i1=i1, op_=op_: h.tensor_tensor(
                            out=o_, in0=i0, in1=i1, op=(op_ or ALU.mult)), reads=rd + ([b_rt] if not fin else []),
                            writes=[b_dst] if fin else [b_rt])
            for half in range(2):
                for q in range(2):
                    w, wb = self.wacquire()
                    e = half * 2 + q
                    for ti, (a, b) in enumerate(TTS):
                        bank = self.next_bank()
                        self.proj(w, wb, KT, 0, self.hnT, [self.b_hn], a, b, bank)
                        P.op("act", lambda h, a=a, b=b, bank=bank: h.activation(
                            out=vtmp[:, a:b], in_=self.ps[:, bank, 0:b - a], func=AF.Copy),
                            reads=[self.b_ps[bank]], writes=[b_vtmp])
                    self.tok_transposes(vtmp, b_vtmp, v_tok, b_vtok, e * 128, None, None, b_tab)
            rot_proj(kT, b_kT)
            for dt in range(2):
                self.tok_transposes(kT[:, dt, :], b_kT, k_tok, b_ktok, dt * 128, kfac, hd, b_tab)
            sb_ = [0, 1]
            for dt in range(2):
                def mm(h, dt=dt):
                    ins = None
                    for c in range(NCH):
                        ins = h.matmul(self.ps[:, sb_[dt], :], lhsT=k_tok[:, c, dt * 128:(dt + 1) * 128], rhs=v_tok[:, c, :],
                                       start=(c == 0), stop=(c == NCH - 1))
                    return ins
                P.op("pe", mm, reads=[b_ktok, b_vtok], writes=[self.b_ps[sb_[dt]]])
                P.op("act", lambda h, dt=dt: h.activation(out=sloc[:, dt, :], in_=self.ps[:, sb_[dt], :], func=AF.Copy),
                     reads=[self.b_ps[sb_[dt]]], writes=[b_sloc])
            i = self.n_cc
            self.n_cc += 1
            bounce = nc.dram_tensor("cc_b%d" % i, [256, 512], F32)
            gathered = nc.dram_tensor("cc_g%d" % i, [4 * 256, 512], F32)
            b_b, b_g = Buf("ccb%d" % i), Buf("ccg%d" % i)
            P.op("sp", lambda h, bounce=bounce: h.dma_start(out=bounce.ap().rearrange("(d p) e -> p d e", p=128), in_=sloc),
                 reads=[b_sloc], writes=[b_b], dma="cc_in")
            self.wprefetch()
            P.op("pool", lambda h, bounce=bounce, gathered=gathered: h.collective_compute(
                "AllGather", ALU.bypass, replica_groups=[[0, 1, 2, 3], [4, 5, 6, 7]],
                ins=[bounce.ap().opt()], outs=[gathered.ap().opt()]), reads=[b_b], writes=[b_g], dma="cc", unit=1)
            rot_proj(qT, b_qT)
            if hd > 0:
                out_proj()
            for r in range(4):
                P.op("sp", lambda h, r=r, gathered=gathered: h.dma_start(
                    out=sgath, in_=gathered.ap()[r * 256:(r + 1) * 256, :].rearrange("(d p) e -> p d e", p=128)),
                    reads=[b_g], writes=[b_sgath], dma="sg")
                cf = coef[:, r * RH + hd:r * RH + hd + 1]
                if r == 0:
                    P.op("dve", lambda h, cf=cf: h.tensor_scalar_mul(out=SinF, in0=sgath, scalar1=cf),
                         reads=[b_sgath, b_tab], writes=[b_SinF])
                else:
                    P.op("dve", lambda h, cf=cf: h.scalar_tensor_tensor(out=SinF, in0=sgath, scalar=cf, in1=SinF,
                                                                        op0=ALU.mult, op1=ALU.add),
                         reads=[b_sgath, b_tab, b_SinF], writes=[b_SinF])
            P.op("act", lambda h: h.activation(out=SinB, in_=SinF, func=AF.Copy), reads=[b_SinF], writes=[b_SinB])
            self.set_ring([2, 3, 4, 5])
            for half in range(2):
                for q in range(2):
                    w, wb = self.wacquire()
                    e = half * 2 + q
                    for ti, (a, b) in enumerate(TTS):
                        bank = self.next_bank()
                        self.proj(w, wb, KT, 0, self.hnT, [self.b_hn], a, b, bank)
                        P.op("act", lambda h, e=e, a=a, b=b, bank=bank: h.activation(
                            out=yT[:, e, a:b], in_=self.ps[:, bank, 0:b - a], func=AF.Silu),
                            reads=[self.b_ps[bank]], writes=[b_yT])
                    P.op("dve", lambda h, e=e, hd=hd: h.tensor_scalar_mul(
                        out=yT[:, e, :], in0=yT[:, e, :], scalar1=gnw[:, hd * 4 + e:hd * 4 + e + 1]),
                        reads=[b_yT, b_tab], writes=[b_yT])
            pending_tail = None
            for qb in range(NQB):
                s = qb % 2
                q0b = qb * QB
                oa = [0, 1] if s == 0 else [2, 3]
                for dt in range(2):
                    P.op("dve", lambda h, dt=dt, s=s, q0b=q0b, qb=qb, hd=hd, gam=gam: h.scalar_tensor_tensor(
                        out=qd[:, s, dt, :], in0=qT[:, dt, q0b:q0b + QB], scalar=float(gam ** (QB * qb)),
                        in1=qdecB[:, hd, :], op0=ALU.mult, op1=ALU.mult),
                        reads=[b_qT, b_tab], writes=[b_qd[s]])
                nkc = 2 * qb + 2

                def emit_sc(kc):
                    k0 = kc * CH
                    q0 = max(q0b, k0)
                    n = q0b + QB - q0
                    sbank = 4 + (kc % 2)

                    def sc(h, k0=k0, q0=q0, n=n, sbank=sbank):
                        ins = None
                        for dt in range(2):
                            ins = h.matmul(self.ps[0:CH, sbank, 0:n], lhsT=kT[:, dt, k0:k0 + CH], rhs=qT[:, dt, q0:q0 + n],
                                           start=(dt == 0), stop=(dt == 1))
                        return ins
                    P.op("pe", sc, reads=[b_kT, b_qT], writes=[self.b_ps[sbank]])

                def emit_pv(kc):
                    k0 = kc * CH
                    q0 = max(q0b, k0)
                    n = q0b + QB - q0
                    sbank = 4 + (kc % 2)
                    ps_ = kc % 2
                    P.op("dve", lambda h, n=n, sbank=sbank, ps_=ps_, q0=q0, k0=k0: h.tensor_tensor(
                        out=PT[:, ps_, 0:n], in0=self.ps[0:CH, sbank, 0:n], in1=G[:, q0 - k0:q0 - k0 + n], op=ALU.mult),
                        reads=[self.b_ps[sbank], b_G], writes=[b_PT[ps_]])

                    def pv(h, kc=kc, q0=q0, n=n, ps_=ps_, oa=oa, q0b=q0b):
                        ins = None
                        off = q0 - q0b
                        for e in range(4):
                            ins = h.matmul(self.ps[:, oa[e // 2], (e % 2) * QB + off:(e % 2) * QB + off + n],
                                           lhsT=v_tok[:, kc, e * 128:(e + 1) * 128], rhs=PT[:, ps_, 0:n],
                                           start=(kc == 0 and e % 2 == 0), stop=False, skip_group_check=True)
                        return ins
                    P.op("pe", pv, reads=[b_vtok, b_PT[ps_]], writes=[self.b_ps[oa[0]], self.b_ps[oa[1]]])
                emit_sc(0)
                for kc in range(nkc):
                    if kc + 1 < nkc:
                        emit_sc(kc + 1)
                    emit_pv(kc)
                if pending_tail is not None:
                    pending_tail()
                    pending_tail = None

                def corr(h, s=s, oa=oa):
                    ins = None
                    for e in range(4):
                        for dt in range(2):
                            ins = h.matmul(self.ps[:, oa[e // 2], (e % 2) * QB:(e % 2) * QB + QB],
                                           lhsT=SinB[:, dt, e * 128:(e + 1) * 128], rhs=qd[:, s, dt, :],
                                           start=False, stop=(dt == 1), skip_group_check=True)
                    return ins
                P.op("pe", corr, reads=[b_SinB, b_qd[s]], writes=[self.b_ps[oa[0]], self.b_ps[oa[1]]])
                for i2 in range(2):
                    P.op("act", lambda h, i2=i2, oa=oa: h.activation(
                        out=oT[:, 2 * i2:2 * i2 + 2, :], in_=self.ps[:, oa[i2], 0:2 * QB].rearrange("p (e l) -> p e l", l=QB),
                        func=AF.Copy), reads=[self.b_ps[oa[i2]]], writes=[b_oT])
                P.op("act", lambda h: h.activation(out=sqo, in_=oT, func=AF.Square), reads=[b_oT], writes=[b_sqo])
                nbank = 4 + (qb % 2)

                def nsum(h, nbank=nbank):
                    ins = None
                    for e in range(4):
                        ins = h.matmul(self.ps[:, nbank, 0:QB], lhsT=self.ones_bf[:, :], rhs=sqo[:, e, :],
                                       start=(e == 0), stop=(e == 3))
                    return ins
                P.op("pe", nsum, reads=[b_sqo, self.b_const], writes=[self.b_ps[nbank]])
                P.op("act", lambda h, nbank=nbank: h.activation(out=rs, in_=self.ps[:, nbank, 0:QB], func=AF.Sqrt,
                                                                bias=float(EPS), scale=1.0 / 512.0),
                     reads=[self.b_ps[nbank]], writes=[b_rs])

                def tail(q0b=q0b):
                    P.op("dve", lambda h: h.reciprocal(out=rs, in_=rs), reads=[b_rs], writes=[b_rs])
                    P.op("dve", lambda h: h.tensor_tensor(out=oT, in0=oT, in1=rs.unsqueeze(1).to_broadcast([128, 4, QB]),
                                                          op=ALU.mult), reads=[b_oT, b_rs], writes=[b_oT])
                    P.op("dve", lambda h, q0b=q0b: h.tensor_tensor(out=yT[:, :, q0b:q0b + QB], in0=yT[:, :, q0b:q0b + QB],
                                                                   in1=oT, op=ALU.mult), reads=[b_oT, b_yT], writes=[b_yT])
                pending_tail = tail
            pending_tail()
            if hd == RH - 1:
                out_proj()
        if getattr(self, "debug", False):
            for name, ap, shape, dt, bufs in (("dbg_kT", kT, [128, 2, T], BF16, [b_kT]), ("dbg_qT", qT, [128, 2, T], BF16, [b_qT]),
                                              ("dbg_vtok", v_tok, [CH, NCH, 512], BF16, [b_vtok]),
                                              ("dbg_ktok", k_tok, [CH, NCH, 256], BF16, [b_ktok]),
                                              ("dbg_yT", yT, [128, 4, T], BF16, [b_yT]),
                                              ("dbg_SinF", SinF, [128, 2, 512], F32, [b_SinF]),
                                              ("dbg_hn", self.hnT[:, :, :], [128, KT, T], BF16, [self.b_hn])):
                o_ = nc.dram_tensor(name, shape, dt, kind="ExternalOutput").ap()
                P.op("sp", lambda h, o_=o_, ap=ap: h.dma_start(out=o_, in_=ap), reads=bufs, writes=[self.b_out], dma="dbg")
        self.mask_pre()

    def ssd_inputs(self, j):
        self.din("ssd_nw%d" % j, [128, KT])
        self.din("ssd_win%d" % j, [D, 10304])
        self.din("ssd_cw%d" % j, [128, 48 * 4])
        self.din("ssd_cb%d" % j, [128, 48])
        self.din("ssd_dtb%d" % j, [8, 8])
        self.din("ssd_alog%d" % j, [8, 8])
        self.din("ssd_dcol%d" % j, [128, 32])
        self.din("ssd_gn%d" % j, [128, 32])
        self.din("ssd_wout%d" % j, [4096, D])
        self.din("c_tri", [CH, CH])
        self.din("c_ones", [CH, 128])
        self.din("c_sel", [8, 8 * 128])
        self.din("c_selr", [128, 4])

    def ssd_wreg(self, j):
        win = self.dram["ssd_win%d" % j]
        wout = self.dram["ssd_wout%d" % j]

        def blk(c0, n=128):
            self.wreg(win[:, c0:c0 + n].rearrange("(k p) n -> p k n", p=128), KT, n)
        for g in range(8):
            for q in range(4):
                blk(4096 + g * 512 + q * 128)
            blk(8192 + g * 128)
            blk(9216 + g * 128)
            blk(10240 + g * 8, 8)
            for q in range(4):
                blk(g * 512 + q * 128)
            for nb in range(4):
                self.wreg(wout[g * 512:(g + 1) * 512, nb * 512:(nb + 1) * 512].rearrange(
                    "(k p) n -> p k n", p=128), 4, 512)

    def ssd(self, jl):
        nc, P = self.nc, self.P
        dr = self.dram
        NC8 = NCH * 8
        self.arena_begin()
        self.rstd = self.cF(T)
        self.b_rstd = self.abuf("rstd")
        self.sq = self.cB(T)
        self.b_sq = self.abuf("sq")
        self.sq2 = None
        xpre = self.cF(T + 3)
        acc = self.cF(347)
        dda = self.cF(T, parts=8)
        cumT = self.cF(T, parts=8)
        dt_tok = self.cF(NC8, parts=CH)
        da_tok = self.cF(NC8, parts=CH)
        cum_tok = self.cF(NC8, parts=CH)
        negcum = self.cF(NC8, parts=CH)
        tot = self.cF(NC8)
        sfx = self.cF(NC8)
        eend = self.cF(NC8)
        wloc = self.cF(NC8, parts=CH)
        wend = self.cF(NC8, parts=CH)
        tmp80 = self.cF(NC8)
        S = self.cF(512)
        Sg = self.cF(520)
        dmul = self.cF(8)
        tri = self.cF(CH, parts=CH)
        ones = self.cF(128, parts=CH)
        sel = self.cF(8 * 128, parts=8)
        selr = self.cF(4)
        cw = self.cF(48 * 4).rearrange("p (t w) -> p t w", w=4)
        cb = self.cF(48)
        dtb = self.cF(8, parts=8)
        nega = self.cF(8, parts=8)
        dcol = self.cF(32)
        gnw = self.cF(32)
        xs = self.cB(T)
        xDT = self.cB(4 * T).rearrange("p (q t) -> p q t", t=T)
        x_tok = self.cB(NCH * 512, parts=CH).rearrange("p (c f) -> p c f", f=512)
        BT = self.cB(T)
        CT = self.cB(T)
        B_tok = self.cB(NCH * 128, parts=CH).rearrange("p (c n) -> p c n", n=128)
        E = self.cB(2 * 8 * CH, parts=CH).rearrange("p (s j l) -> p s j l", s=2, l=CH)
        eR = self.cB(2 * 8 * CH).rearrange("p (s j l) -> p s j l", s=2, l=CH)
        CBm = self.cB(CH, parts=CH)
        xdt = self.cB(512, parts=CH)
        xw = self.cB(2 * 512, parts=CH).rearrange("p (s f) -> p s f", s=2)
        S_bf = self.cB(512)
        yT = self.cB(4 * T).rearrange("p (q t) -> p q t", t=T)
        zs = self.cB(4 * T).rearrange("p (q t) -> p q t", t=T)
        names = ("tab", "xpre", "acc", "dda", "cumT", "dttok", "datok", "cumtok", "tot", "wl", "S", "Sg", "dmul", "xs", "xDT",
                 "xtok", "BT", "CT", "Btok", "E0", "E1", "eR0", "eR1", "CBm", "xdt", "xw0", "xw1", "Sbf", "yT", "zs", "tmp80")
        B_ = {n: self.abuf(n) for n in names}
        self.arena_end()

        loads = ((tri, dr["c_tri"][:, :]), (ones, dr["c_ones"][:, :]), (sel, dr["c_sel"][:, :]), (selr, dr["c_selr"][:, :]),
                 (cw, dr["ssd_cw%d" % jl].rearrange("p (t w) -> p t w", w=4)), (cb, dr["ssd_cb%d" % jl][:, :]),
                 (dtb, dr["ssd_dtb%d" % jl][:, :]), (nega, dr["ssd_alog%d" % jl][:, :]),
                 (dcol, dr["ssd_dcol%d" % jl][:, :]), (gnw, dr["ssd_gn%d" % jl][:, :]))
        for dst, src in loads:
            P.op("sp", lambda h, dst=dst, src=src: h.dma_start(out=dst, in_=src), writes=[B_["tab"]], dma="tab")
        P.op("act", lambda h: h.activation(out=nega, in_=nega, func=AF.Exp), reads=[B_["tab"]], writes=[B_["tab"]])
        P.op("dve", lambda h: h.tensor_scalar_mul(out=nega, in0=nega, scalar1=-1.0), reads=[B_["tab"]], writes=[B_["tab"]])
        P.op("dve", lambda h: h.memset(xpre[:, 0:3], 0.0), writes=[B_["xpre"]])
        self.set_ring([0, 1, 2, 3, 4, 5])
        self.rmsnorm(dr["ssd_nw%d" % jl])
        self.halo_exchange()

        def conv_tile(tile, dst, b_dst, mask_pre, after_proj=None):
            w, wb = self.wacquire()
            for ti, (a, b) in enumerate(TTS):
                bank = self.next_bank()
                self.proj(w, wb, KT, 0, self.hnT, [self.b_hn], a, b, bank)
                P.op("act", lambda h, a=a, b=b, bank=bank: h.activation(
                    out=xpre[:, 3 + a:3 + b], in_=self.ps[:, bank, 0:b - a], func=AF.Copy),
                    reads=[self.b_ps[bank]], writes=[B_["xpre"]])
            if after_proj is not None:
                after_proj()
            for ti, (a, b) in enumerate(TTS):
                n = b - a
                P.op("dve", lambda h, a=a, b=b, n=n: h.tensor_scalar(
                    out=acc[:, 0:n], in0=xpre[:, 3 + a:3 + b], scalar1=cw[:, tile, 3:4], scalar2=cb[:, tile:tile + 1],
                    op0=ALU.mult, op1=ALU.add), reads=[B_["xpre"], B_["tab"]], writes=[B_["acc"]])
                for wi in range(3):
                    P.op("dve", lambda h, a=a, b=b, n=n, wi=wi: h.scalar_tensor_tensor(
                        out=acc[:, 0:n], in0=xpre[:, wi + a:wi + b], scalar=cw[:, tile, wi:wi + 1], in1=acc[:, 0:n],
                        op0=ALU.mult, op1=ALU.add), reads=[B_["xpre"], B_["tab"], B_["acc"]], writes=[B_["acc"]])
                P.op("act", lambda h, a=a, b=b, n=n: h.activation(out=dst[:, a:b], in_=acc[:, 0:n], func=AF.Silu),
                     reads=[B_["acc"]], writes=[b_dst])
            if mask_pre:
                P.op("dve", lambda h: h.tensor_tensor(out=dst[:, 0:PRE], in0=dst[:, 0:PRE], in1=self.premask[:], op=ALU.mult),
                     reads=[b_dst, self.b_const], writes=[b_dst])

        for g in range(8):
            self.set_ring([0, 1, 2, 3, 4, 5])
            pending = None
            for q in range(4):
                conv_tile(g * 4 + q, xs, B_["xs"], True, pending)

                def pending(q=q, g=g):
                    self.tok_transposes(xs, B_["xs"], x_tok, B_["xtok"], q * 128, None, None, B_["tab"])
                    P.op("dve", lambda h, q=q, g=g: h.tensor_scalar_mul(out=xDT[:, q, :], in0=xs,
                                                                        scalar1=dcol[:, g * 4 + q:g * 4 + q + 1]),
                         reads=[B_["xs"], B_["tab"]], writes=[B_["xDT"]])
            conv_tile(32 + g, BT, B_["BT"], False, pending)
            conv_tile(40 + g, CT, B_["CT"], False,
                      lambda: self.tok_transposes(BT, B_["BT"], B_tok, B_["Btok"], 0, None, None, B_["tab"]))
            w, wb = self.wacquire()
            for ti, (a, b) in enumerate(TTS):
                bank = self.next_bank()

                def mm(h, a=a, b=b, bank=bank, w=w):
                    ins = None
                    for k in range(KT):
                        ins = h.matmul(self.ps[0:8, bank, 0:b - a], lhsT=w[:, k, 0:8], rhs=self.hnT[:, k, a:b],
                                       start=(k == 0), stop=(k == KT - 1))
                    return ins
                P.op("pe", mm, reads=[wb, self.b_hn], writes=[self.b_ps[bank]])
                P.op("act", lambda h, a=a, b=b, bank=bank, g=g: h.activation(
                    out=dda[:, a:b], in_=self.ps[0:8, bank, 0:b - a], func=AF.Exp, bias=dtb[:, g:g + 1], scale=1.0),
                    reads=[self.b_ps[bank], B_["tab"]], writes=[B_["dda"]])
            P.op("act", lambda h: h.activation(out=dda, in_=dda, func=AF.Ln, bias=1.0, scale=1.0),
                 reads=[B_["dda"]], writes=[B_["dda"]])

            def tok8(dst, b_dst):
                bank = self.next_bank()

                def tr(h, bank=bank):
                    ins = None
                    for c in range(NCH):
                        ins = h.transpose(self.ps[0:CH, bank, c * 8:(c + 1) * 8], dda[:, c * CH:(c + 1) * CH], self.ident[0:8, 0:8])
                    return ins
                P.op("pe", tr, reads=[B_["dda"], self.b_const], writes=[self.b_ps[bank]])
                P.op("act", lambda h, bank=bank: h.activation(out=dst, in_=self.ps[0:CH, bank, 0:NC8], func=AF.Copy),
                     reads=[self.b_ps[bank]], writes=[b_dst])
            tok8(dt_tok, B_["dttok"])
            P.op("dve", lambda h, g=g: h.tensor_scalar_mul(out=dda, in0=dda, scalar1=nega[:, g:g + 1]),
                 reads=[B_["dda"], B_["tab"]], writes=[B_["dda"]])
            P.op("dve", lambda h: h.tensor_tensor(out=dda[:, 0:PRE], in0=dda[:, 0:PRE], in1=self.premask[0:8, :], op=ALU.mult),
                 reads=[B_["dda"], self.b_const], writes=[B_["dda"]])
            tok8(da_tok, B_["datok"])
            b_cum, b_tot = self.next_bank(), self.next_bank()
            P.op("pe", lambda h, b_cum=b_cum: h.matmul(self.ps[0:CH, b_cum, 0:NC8], lhsT=tri, rhs=da_tok, start=True, stop=True),
                 reads=[B_["datok"], B_["tab"]], writes=[self.b_ps[b_cum]])
            P.op("pe", lambda h, b_tot=b_tot: h.matmul(self.ps[:, b_tot, 0:NC8], lhsT=ones, rhs=da_tok, start=True, stop=True),
                 reads=[B_["datok"], B_["tab"]], writes=[self.b_ps[b_tot]])
            P.op("act", lambda h, b_cum=b_cum: h.activation(out=cum_tok, in_=self.ps[0:CH, b_cum, 0:NC8], func=AF.Copy),
                 reads=[self.b_ps[b_cum]], writes=[B_["cumtok"]])
            P.op("act", lambda h, b_cum=b_cum: h.activation(out=negcum, in_=self.ps[0:CH, b_cum, 0:NC8], func=AF.Copy, scale=-1.0),
                 reads=[self.b_ps[b_cum]], writes=[B_["cumtok"]])
            P.op("act", lambda h, b_tot=b_tot: h.activation(out=tot, in_=self.ps[:, b_tot, 0:NC8], func=AF.Copy),
                 reads=[self.b_ps[b_tot]], writes=[B_["tot"]])
            P.op("act", lambda h, b_tot=b_tot: h.activation(out=eend, in_=self.ps[:, b_tot, 0:NC8], func=AF.Exp),
                 reads=[self.b_ps[b_tot]], writes=[B_["tot"]])
            P.op("dve", lambda h: h.tensor_copy(out=sfx[:, (NCH - 1) * 8:NC8], in_=tot[:, (NCH - 1) * 8:NC8]),
                 reads=[B_["tot"]], writes=[B_["tot"]])
            for c in range(NCH - 2, -1, -1):
                P.op("dve", lambda h, c=c: h.tensor_tensor(out=sfx[:, c * 8:(c + 1) * 8], in0=sfx[:, (c + 1) * 8:(c + 2) * 8],
                                                           in1=tot[:, c * 8:(c + 1) * 8], op=ALU.add),
                     reads=[B_["tot"]], writes=[B_["tot"]])
            for src, dst in ((sfx, wloc), (tot, wend)):
                P.op("dve", lambda h, src=src: h.tensor_tensor(out=tmp80[0:CH, :], in0=src[0:CH, :], in1=cum_tok, op=ALU.subtract),
                     reads=[B_["tot"], B_["cumtok"], B_["tmp80"]], writes=[B_["tmp80"]])
                P.op("act", lambda h: h.activation(out=tmp80[0:CH, :], in_=tmp80[0:CH, :], func=AF.Exp),
                     reads=[B_["tmp80"]], writes=[B_["tmp80"]])
                P.op("dve", lambda h, dst=dst: h.tensor_tensor(out=dst, in0=tmp80[0:CH, :], in1=dt_tok, op=ALU.mult),
                     reads=[B_["tmp80"], B_["dttok"]], writes=[B_["wl"]])
            cb_ = [self.next_bank() for _ in range(3)]
            for c in range(NCH):
                P.op("pe", lambda h, c=c: h.matmul(self.ps[0:8, cb_[c // 4], (c % 4) * CH:(c % 4 + 1) * CH],
                                                   lhsT=da_tok[:, c * 8:(c + 1) * 8], rhs=tri, start=(c % 4 == 0), stop=True,
                                                   skip_group_check=True),
                     reads=[B_["datok"], B_["tab"]], writes=[self.b_ps[cb_[c // 4]]])
            for i3 in range(3):
                nch = min(4, NCH - 4 * i3)
                P.op("act", lambda h, i3=i3, nch=nch: h.activation(
                    out=cumT[:, 4 * i3 * CH:(4 * i3 + nch) * CH], in_=self.ps[0:8, cb_[i3], 0:nch * CH], func=AF.Copy),
                    reads=[self.b_ps[cb_[i3]]], writes=[B_["cumT"]])
            ub = self.next_bank()
            for c in range(NCH):
                s2 = c % 2
                P.op("dve", lambda h, c=c, s2=s2: h.tensor_tensor(
                    out=xw[:, s2, :].rearrange("p (j d) -> p j d", d=64), in0=x_tok[:, c, :].rearrange("p (j d) -> p j d", d=64),
                    in1=wloc[:, c * 8:(c + 1) * 8].unsqueeze(2).to_broadcast([CH, 8, 64]), op=ALU.mult),
                    reads=[B_["xtok"], B_["wl"]], writes=[B_["xw%d" % s2]])
                P.op("pe", lambda h, c=c, s2=s2: h.matmul(self.ps[:, ub, :], lhsT=B_tok[:, c, :], rhs=xw[:, s2, :],
                                                          start=(c == 0), stop=(c == NCH - 1)),
                     reads=[B_["Btok"], B_["xw%d" % s2]], writes=[self.b_ps[ub]])
            P.op("act", lambda h: h.activation(out=Sg[:, 0:512], in_=self.ps[:, ub, :], func=AF.Copy),
                 reads=[self.b_ps[ub]], writes=[B_["Sg"]])
            P.op("act", lambda h: h.activation(out=Sg[:, 512:520], in_=sfx[:, 0:8], func=AF.Exp),
                 reads=[B_["tot"]], writes=[B_["Sg"]])
            i = self.n_cc
            self.n_cc += 1
            bounce = nc.dram_tensor("cc_b%d" % i, [128, 520], F32)
            gathered = nc.dram_tensor("cc_g%d" % i, [4 * 128, 520], F32)
            b_b, b_g = Buf("ccb%d" % i), Buf("ccg%d" % i)
            P.op("sp", lambda h, bounce=bounce: h.dma_start(out=bounce.ap(), in_=Sg), reads=[B_["Sg"]], writes=[b_b], dma="cc_in")
            self.wprefetch()
            P.op("pool", lambda h, bounce=bounce, gathered=gathered: h.collective_compute(
                "AllGather", ALU.bypass, replica_groups=[[0, 1, 2, 3], [4, 5, 6, 7]],
                ins=[bounce.ap().opt()], outs=[gathered.ap().opt()]), reads=[b_b], writes=[b_g], dma="cc", unit=1)
            self.set_ring([0, 1, 2, 3, 4, 5])
            for q in range(4):
                w, wb = self.wacquire()
                for ti, (a, b) in enumerate(TTS):
                    bank = self.next_bank()
                    self.proj(w, wb, KT, 0, self.hnT, [self.b_hn], a, b, bank)
                    P.op("act", lambda h, q=q, a=a, b=b, bank=bank: h.activation(
                        out=zs[:, q, a:b], in_=self.ps[:, bank, 0:b - a], func=AF.Silu),
                        reads=[self.b_ps[bank]], writes=[B_["zs"]])
            P.op("dve", lambda h: h.memset(S, 0.0), writes=[B_["S"]])
            for r in range(4):
                P.op("sp", lambda h, r=r, gathered=gathered: h.dma_start(out=Sg, in_=gathered.ap()[r * 128:(r + 1) * 128, :]),
                     reads=[b_g], writes=[B_["Sg"]], dma="sg")
                P.op("dve", lambda h, r=r: h.tensor_scalar(out=dmul, in0=Sg[:, 512:520], scalar1=-1.0, scalar2=selr[:, r:r + 1],
                                                           op0=ALU.add, op1=ALU.mult),
                     reads=[B_["Sg"], B_["tab"]], writes=[B_["dmul"]])
                P.op("dve", lambda h: h.tensor_scalar_add(out=dmul, in0=dmul, scalar1=1.0), reads=[B_["dmul"]], writes=[B_["dmul"]])
                P.op("dve", lambda h: h.tensor_tensor(
                    out=S.rearrange("p (j d) -> p j d", d=64), in0=S.rearrange("p (j d) -> p j d", d=64),
                    in1=dmul.unsqueeze(2).to_broadcast([128, 8, 64]), op=ALU.mult), reads=[B_["S"], B_["dmul"]], writes=[B_["S"]])
                P.op("dve", lambda h, r=r: h.scalar_tensor_tensor(out=S, in0=Sg[:, 0:512], scalar=selr[:, r:r + 1], in1=S,
                                                                  op0=ALU.mult, op1=ALU.add),
                     reads=[B_["Sg"], B_["tab"], B_["S"]], writes=[B_["S"]])
            P.op("act", lambda h: h.activation(out=S_bf, in_=S, func=AF.Copy), reads=[B_["S"]], writes=[B_["Sbf"]])
            rb, yb, cbk, ub = [0, 1], [2, 3], 4, 5

            def front_a(c):
                c0, s = c * CH, c % 2
                for bk in range(2):
                    def rmm(h, bk=bk, c0=c0):
                        ins = None
                        for jj in range(4):
                            j = bk * 4 + jj
                            ins = h.matmul(self.ps[:, rb[bk], jj * CH:(jj + 1) * CH], lhsT=sel[:, j * 128:(j + 1) * 128],
                                           rhs=cumT[:, c0:c0 + CH], start=(jj == 0), stop=True, skip_group_check=True)
                        return ins
                    P.op("pe", rmm, reads=[B_["cumT"], B_["tab"]], writes=[self.b_ps[rb[bk]]])
                P.op("pe", lambda h, c0=c0: h.matmul(self.ps[0:CH, cbk, 0:CH], lhsT=BT[:, c0:c0 + CH], rhs=CT[:, c0:c0 + CH],
                                                     start=True, stop=True), reads=[B_["BT"], B_["CT"]], writes=[self.b_ps[cbk]])
                for j in range(8):
                    P.op("act", lambda h, j=j, c=c, s=s: h.activation(
                        out=E[:, s, j, :], in_=self.ps[0:CH, rb[j // 4], (j % 4) * CH:(j % 4 + 1) * CH], func=AF.Exp,
                        bias=negcum[:, c * 8 + j:c * 8 + j + 1], scale=1.0),
                        reads=[self.b_ps[rb[j // 4]], B_["cumtok"]], writes=[B_["E%d" % s]])
                for bk in range(2):
                    P.op("act", lambda h, bk=bk, s=s: h.activation(
                        out=eR[:, s, bk * 4:bk * 4 + 4, :], in_=self.ps[:, rb[bk], 0:4 * CH].rearrange("p (j l) -> p j l", l=CH),
                        func=AF.Exp), reads=[self.b_ps[rb[bk]]], writes=[B_["eR%d" % s]])

            def front_b(c):
                c0, s = c * CH, c % 2
                P.op("dve", lambda h: h.tensor_tensor(out=CBm, in0=self.ps[0:CH, cbk, 0:CH], in1=tri, op=ALU.mult),
                     reads=[self.b_ps[cbk], B_["tab"]], writes=[B_["CBm"]])
                P.op("dve", lambda h, s=s: h.scalar_tensor_tensor(
                    out=E[:, s, :, :], in0=E[:, s, :, :], scalar=1.0, in1=CBm.unsqueeze(1).to_broadcast([CH, 8, CH]),
                    op0=ALU.min, op1=ALU.mult), reads=[B_["E%d" % s], B_["CBm"]], writes=[B_["E%d" % s]])
                P.op("dve", lambda h, c0=c0, s=s: h.tensor_tensor(
                    out=eR[:, s, :, :], in0=eR[:, s, :, :], in1=CT[:, c0:c0 + CH].unsqueeze(1).to_broadcast([128, 8, CH]), op=ALU.mult),
                    reads=[B_["eR%d" % s], B_["CT"]], writes=[B_["eR%d" % s]])
                P.op("dve", lambda h, c=c: h.tensor_tensor(
                    out=xdt.rearrange("p (j d) -> p j d", d=64), in0=x_tok[:, c, :].rearrange("p (j d) -> p j d", d=64),
                    in1=dt_tok[:, c * 8:(c + 1) * 8].unsqueeze(2).to_broadcast([CH, 8, 64]), op=ALU.mult),
                    reads=[B_["xtok"], B_["dttok"]], writes=[B_["xdt"]])
                P.op("dve", lambda h, c=c, s=s: h.tensor_tensor(
                    out=xw[:, s, :].rearrange("p (j d) -> p j d", d=64), in0=x_tok[:, c, :].rearrange("p (j d) -> p j d", d=64),
                    in1=wend[:, c * 8:(c + 1) * 8].unsqueeze(2).to_broadcast([CH, 8, 64]), op=ALU.mult),
                    reads=[B_["xtok"], B_["wl"]], writes=[B_["xw%d" % s]])

            def back_pe(c):
                s = c % 2
                for bk in range(2):
                    def ymm(h, bk=bk, s=s):
                        ins = None
                        for jj in range(4):
                            j = bk * 4 + jj
                            o_ = self.ps[:, yb[bk], jj * CH:(jj + 1) * CH]
                            h.matmul(o_, lhsT=xdt[:, (j // 2) * 128:(j // 2 + 1) * 128], rhs=E[:, s, j, :],
                                     start=(jj == 0), stop=False, skip_group_check=True)
                            ins = h.matmul(o_, lhsT=S_bf[:, (j // 2) * 128:(j // 2 + 1) * 128], rhs=eR[:, s, j, :],
                                           start=False, stop=True, skip_group_check=True)
                        return ins
                    P.op("pe", ymm, reads=[B_["xdt"], B_["E%d" % s], B_["Sbf"], B_["eR%d" % s]], writes=[self.b_ps[yb[bk]]])
                P.op("pe", lambda h, c=c, s=s: h.matmul(self.ps[:, ub, :], lhsT=B_tok[:, c, :], rhs=xw[:, s, :], start=True, stop=True),
                     reads=[B_["Btok"], B_["xw%d" % s]], writes=[self.b_ps[ub]])

            def back_rest(c):
                c0 = c * CH
                for bk in range(2):
                    for par in range(2):
                        src = self.ps[64 * par:64 * par + 64, yb[bk], 0:4 * CH].rearrange("p (q r l) -> p q r l", r=2, l=CH)[:, :, par, :]
                        P.op("dve", lambda h, src=src, par=par, bk=bk, c0=c0: h.tensor_tensor(
                            out=yT[64 * par:64 * par + 64, 2 * bk:2 * bk + 2, c0:c0 + CH], in0=src,
                            in1=xDT[64 * par:64 * par + 64, 2 * bk:2 * bk + 2, c0:c0 + CH], op=ALU.add),
                            reads=[self.b_ps[yb[bk]], B_["xDT"]], writes=[B_["yT"]])
                P.op("dve", lambda h, c=c: h.tensor_tensor(
                    out=S.rearrange("p (j d) -> p j d", d=64), in0=S.rearrange("p (j d) -> p j d", d=64),
                    in1=eend[:, c * 8:(c + 1) * 8].unsqueeze(2).to_broadcast([128, 8, 64]), op=ALU.mult),
                    reads=[B_["S"], B_["tot"]], writes=[B_["S"]])
                P.op("dve", lambda h: h.tensor_tensor(out=S, in0=S, in1=self.ps[:, ub, :], op=ALU.add),
                     reads=[B_["S"], self.b_ps[ub]], writes=[B_["S"]])
                P.op("act", lambda h: h.activation(out=S_bf, in_=S, func=AF.Copy), reads=[B_["S"]], writes=[B_["Sbf"]])

            front_a(0)
            front_b(0)
            for c in range(NCH):
                if c + 1 < NCH:
                    front_a(c + 1)
                back_pe(c)
                back_rest(c)
                if c + 1 < NCH:
                    front_b(c + 1)
            self.set_ring([0, 1, 2, 3, 4, 5])
            nb_ = [self.next_bank() for _ in TTS]
            for q in range(4):
                P.op("dve", lambda h, q=q: h.tensor_tensor(out=yT[:, q, :], in0=yT[:, q, :], in1=zs[:, q, :], op=ALU.mult),
                     reads=[B_["yT"], B_["zs"]], writes=[B_["yT"]])
                P.op("act", lambda h, q=q: h.activation(out=self.sq, in_=yT[:, q, :], func=AF.Square),
                     reads=[B_["yT"]], writes=[self.b_sq])

                def mm(h, q=q):
                    ins = None
                    for ti, (a, b) in enumerate(TTS):
                        ins = h.matmul(self.ps[:, nb_[ti], 0:b - a], lhsT=self.ones_bf[:, :], rhs=self.sq[:, a:b],
                                       start=(q == 0), stop=(q == 3))
                    return ins
                P.op("pe", mm, reads=[self.b_sq, self.b_const], writes=[self.b_ps[b] for b in nb_])
            for ti, (a, b) in enumerate(TTS):
                P.op("act", lambda h, ti=ti, a=a, b=b: h.activation(
                    out=self.rstd[:, a:b], in_=self.ps[:, nb_[ti], 0:b - a], func=AF.Sqrt, bias=float(EPS), scale=1.0 / 512.0),
                    reads=[self.b_ps[nb_[ti]]], writes=[self.b_rstd])
            P.op("dve", lambda h: h.reciprocal(out=self.rstd, in_=self.rstd), reads=[self.b_rstd], writes=[self.b_rstd])
            for q in range(4):
                P.op("dve", lambda h, q=q, g=g: h.scalar_tensor_tensor(
                    out=yT[:, q, :], in0=yT[:, q, :], scalar=gnw[:, g * 4 + q:g * 4 + q + 1], in1=self.rstd,
                    op0=ALU.mult, op1=ALU.mult), reads=[B_["yT"], B_["tab"], self.b_rstd], writes=[B_["yT"]])
            for nb in range(4):
                w, wb = self.wacquire()
                for q in range(4):
                    n_ = nb * 4 + q
                    for ti, (a, b) in enumerate(TTS):
                        bank = self.next_bank()
                        self.proj(w, wb, 4, q * 128, yT, [B_["yT"]], a, b, bank)
                        self.resid_add(n_, a, b, bank)
        self.mask_pre()

    def tok_transposes(self, srcT, b_src, dst_tok, b_dst, col0, kfac, hd, b_tab):
        P = self.P
        for (c0, c1, pb) in ((0, 8, 0), (8, NCH, 1)):
            def tr(h, c0=c0, c1=c1, pb=pb):
                ins = None
                for c in range(c0, c1):
                    ins = h.transpose(self.psb[0:CH, pb, (c - c0) * 128:(c - c0 + 1) * 128],
                                      srcT[:, c * CH:(c + 1) * CH], self.identb[:, :])
                return ins
            P.op("pe", tr, reads=[b_src, self.b_const], writes=[self.b_psb[pb]])
            nch = c1 - c0
            src = self.psb[0:CH, pb, 0:nch * 128].rearrange("p (c d) -> p c d", d=128)
            dst = dst_tok[:, c0:c1, col0:col0 + 128]
            if kfac is None:
                P.op("act", lambda h, src=src, dst=dst: h.activation(out=dst, in_=src, func=AF.Copy),
                     reads=[self.b_psb[pb]], writes=[b_dst])
            else:
                kf = kfac[:, hd * NCH + c0:hd * NCH + c1].unsqueeze(2).to_broadcast([CH, nch, 128])
                P.op("dve", lambda h, src=src, dst=dst, kf=kf: h.tensor_tensor(out=dst, in0=src, in1=kf, op=ALU.mult),
                     reads=[self.b_psb[pb], b_tab], writes=[b_dst])


def col_layout(v):
    return np.ascontiguousarray(v.reshape(-1, 128).T)


def bf16(a):
    import ml_dtypes
    return np.asarray(a, np.float32).astype(ml_dtypes.bfloat16)


def core_consts(c):
    p = c % 4
    ident = np.eye(128, dtype=np.float32)
    premask = np.full((128, PRE), 1.0 if p == 0 else 0.0, np.float32)
    halosel = np.zeros((128, 4), np.float32)
    if p > 0:
        halosel[:, p - 1] = 1.0
    return {"c_ident": ident, "c_premask": premask, "c_halosel": halosel}


def ret_consts(c):
    p = c % 4
    half = 128
    inv = (10000.0 ** (-np.arange(half, dtype=np.float32) / half)).astype(np.float32)
    pos = (p * OWN + np.arange(T)).astype(np.float32)
    ang = (pos[None, :] * inv[:, None]).astype(np.float32)
    lg = np.log1p(-np.exp2(-5.0 - np.arange(RH, dtype=np.float64)))
    s = np.arange(CH)[:, None]
    jj = np.arange(T)[None, :]
    G = np.where(jj >= s, np.exp((jj - s)[None] * lg[:, None, None]), 0.0) / 16.0
    Q = np.exp((np.arange(QB)[None, :] + 1.0) * lg[:, None])
    Qrep = np.broadcast_to(Q.reshape(1, RH * QB), (128, RH * QB))
    cc = np.arange(NCH)[None, None, :]
    Kf = np.exp((T - 1 - CH * cc - s[:, :, None]) * lg[None, :, None]) / 16.0
    coef = np.zeros((4, RH))
    for r in range(4):
        if r < p:
            coef[r] = np.exp((OWN * (p - 1 - r) - PRE) * lg)
    return {
        "c_cos": bf16(np.cos(ang)), "c_sin": bf16(np.sin(ang)),
        "c_retG": bf16(G), "c_retQ": bf16(Qrep),
        "c_retK": np.ascontiguousarray(Kf.reshape(CH, RH * NCH).astype(np.float32)),
        "c_retcoef": np.ascontiguousarray(np.broadcast_to(coef.reshape(1, 4 * RH), (128, 4 * RH)).astype(np.float32)),
    }


def ffn_inputs(idx, ffn_norm_w, ffn_w_up, ffn_conv_w, ffn_conv_b, ffn_w_down):
    return {
        "ffn_nw%d" % idx: col_layout(ffn_norm_w[idx]),
        "ffn_wup%d" % idx: ffn_w_up[idx],
        "ffn_cw%d" % idx: np.ascontiguousarray(ffn_conv_w[idx].reshape(3, FT, 128).transpose(2, 1, 0)),
        "ffn_cb%d" % idx: col_layout(ffn_conv_b[idx]),
        "ffn_wdn%d" % idx: ffn_w_down[idx],
    }


def ret_inputs(j, ret_norm_w, ret_w_in, ret_gn_w, ret_w_out):
    return {
        "ret_nw%d" % j: col_layout(ret_norm_w[j]),
        "ret_win%d" % j: ret_w_in[j],
        "ret_gn%d" % j: col_layout(ret_gn_w[j]),
        "ret_wout%d" % j: ret_w_out[j],
    }


def ssd_consts(c):
    p = c % 4
    tri = (np.arange(CH)[:, None] <= np.arange(CH)[None, :]).astype(np.float32)
    ones = np.ones((CH, 128), np.float32)
    sel = np.zeros((8, 8, 128), np.float32)
    for j in range(8):
        sel[j, j, :] = 1.0
    selr = np.zeros((128, 4), np.float32)
    selr[:, :p] = 1.0
    return {"c_tri": tri, "c_ones": ones, "c_sel": sel.reshape(8, 8 * 128), "c_selr": selr}


def ssd_inputs(j, ssd_norm_w, ssd_w_in, ssd_conv_w, ssd_conv_b, ssd_dt_bias, ssd_a_log, ssd_d, ssd_gnorm_w, ssd_w_out):
    return {
        "ssd_nw%d" % j: col_layout(ssd_norm_w[j]),
        "ssd_win%d" % j: ssd_w_in[j],
        "ssd_cw%d" % j: np.ascontiguousarray(ssd_conv_w[j].reshape(4, 48, 128).transpose(2, 1, 0).reshape(128, 48 * 4)),
        "ssd_cb%d" % j: col_layout(ssd_conv_b[j]),
        "ssd_dtb%d" % j: np.ascontiguousarray(ssd_dt_bias[j].reshape(8, 8).T),
        "ssd_alog%d" % j: np.ascontiguousarray(ssd_a_log[j].reshape(8, 8).T),
        "ssd_dcol%d" % j: col_layout(np.repeat(ssd_d[j], 64)),
        "ssd_gn%d" % j: col_layout(ssd_gnorm_w[j]),
        "ssd_wout%d" % j: ssd_w_out[j],
    }


FUSE_GROUPS = [[("ret", 0), ("ffn", 0), ("ssd", 0), ("ffn", 1), ("ret", 1), ("ffn", 2), ("ssd", 1), ("ffn", 3)]]


def _sub_inputs(kind, idx, name_idx, inp):
    if kind == "ffn":
        d = ffn_inputs(idx, inp["ffn_norm_w"], inp["ffn_w_up"], inp["ffn_conv_w"], inp["ffn_conv_b"], inp["ffn_w_down"])
    elif kind == "ret":
        d = ret_inputs(idx, inp["ret_norm_w"], inp["ret_w_in"], inp["ret_gn_w"], inp["ret_w_out"])
    else:
        d = ssd_inputs(idx, inp["ssd_norm_w"], inp["ssd_w_in"], inp["ssd_conv_w"], inp["ssd_conv_b"], inp["ssd_dt_bias"],
                       inp["ssd_a_log"], inp["ssd_d"], inp["ssd_gnorm_w"], inp["ssd_w_out"])
    if name_idx != idx:
        d = {k[:-len(str(idx))] + str(name_idx): v for k, v in d.items()}
    return d


def kernel(**inputs):
    inp = {k: np.ascontiguousarray(np.asarray(v, dtype=np.float32)) for k, v in inputs.items()}
    x, meta = inp["x"], inp["meta_tokens"]
    hs = []
    for c in range(NCORES):
        b, p = c // 4, c % 4
        xin = np.zeros((T, D), np.float32)
        if p == 0:
            xin[:PRE] = meta
        xin[PRE:] = x[b, p * OWN:(p + 1) * OWN]
        hs.append(xin)
    consts = []
    for c in range(NCORES):
        d = core_consts(c)
        d.update(ret_consts(c))
        d.update(ssd_consts(c))
        consts.append(d)
    progs = {}
    for gi, group in enumerate(FUSE_GROUPS):
        last = gi == len(FUSE_GROUPS) - 1
        slots, counts = [], {}
        for kind, idx in group:
            s = counts.get(kind, 0)
            counts[kind] = s + 1
            slots.append((kind, s))
        key = (tuple(slots), last)
        if key not in progs:
            progs[key] = Builder(slots, final_norm=last).build()
        nc = progs[key]
        names = set(a.memorylocations[0].name for a in nc.allocations
                    if isinstance(a, mybir.MemoryLocationSet) and a.kind == "ExternalInput")
        w = {}
        for (kind, idx), (_, s) in zip(group, slots):
            w.update(_sub_inputs(kind, idx, s, inp))
        if last:
            w["fin_nw"] = col_layout(inp["final_norm_w"])
        in_maps = []
        for c in range(NCORES):
            m = {"xin": hs[c]}
            m.update(consts[c])
            m.update(w)
            in_maps.append({k: v for k, v in m.items() if k in names})
        res = run_bass_kernel_spmd(nc, in_maps, core_ids=list(range(NCORES)))
        hs = [np.asarray(res.results[c]["out"]) for c in range(NCORES)]
    out = np.empty((2, 4 * OWN, D), np.float32)
    for c in range(NCORES):
        b, p = c // 4, c % 4
        out[b, p * OWN:(p + 1) * OWN] = hs[c][PRE:]
    return out
```
